# Optimizing a Trainium2 kernel written in Bass

```python
import jax, jax.numpy as jnp
import numpy as np

D_MODEL = 1024
BATCH = 8
SEQ = 4096
DEPTH = 2
DEC_BATCH = 128
DEC_SEQ = 1
PAST_LEN = 16384
PAGE_SIZE = 128

HEAD_DIM = 64
A_GROUPS = ((128, 1), (512, 4), (2048, 16))
A_HEADS_PER_GROUP = 4
A_HEADS = A_HEADS_PER_GROUP * len(A_GROUPS)
A_WIDTH = A_HEADS * HEAD_DIM
B_Q_HEADS = D_MODEL // HEAD_DIM
B_KV_HEADS = 2
B_WIDTH = B_Q_HEADS * HEAD_DIM
B_WINDOW = 128
BLOCK = 128
N_A = DEPTH // 2
N_B = DEPTH - N_A
ROPE_THETA = 10000.0
EPS = 1e-6
NEG = -1e30
ADA_STD = 0.5

kernel_name = "yoco_dilated_swa_sink_decoder_step"


def rmsnorm(x, g):
    xf = x.astype(jnp.float32)
    y = xf * jax.lax.rsqrt(jnp.mean(xf * xf, axis=-1, keepdims=True) + EPS) * g.astype(jnp.float32)
    return y.astype(x.dtype)


def modulate(h, shift, scale):
    return h * (1 + scale[:, None]) + shift[:, None]


def rope(x, pos):
    half = HEAD_DIM // 2
    inv = ROPE_THETA ** (-jnp.arange(half, dtype=jnp.float32) / half)
    ang = pos.astype(jnp.float32)[:, None] * inv[None, :]
    cos = jnp.cos(ang)[None, :, None, :]
    sin = jnp.sin(ang)[None, :, None, :]
    xf = x.astype(jnp.float32)
    x1, x2 = xf[..., :half], xf[..., half:]
    return jnp.concatenate([x1 * cos - x2 * sin, x2 * cos + x1 * sin], axis=-1).astype(x.dtype)


def masked_softmax_stats(s, mask):
    s = jnp.where(mask, s, NEG)
    m = jnp.max(s, axis=-1)
    p = jnp.exp(s - m[..., None])
    return p, jnp.sum(p, axis=-1), m


def sink_softmax(s, mask, sink):
    s = jnp.where(mask, s, NEG)
    m = jnp.maximum(jnp.max(s, axis=-1), sink)
    p = jnp.exp(s - m[..., None])
    return p, jnp.sum(p, axis=-1) + jnp.exp(sink - m)


def dilated_attn_prompt(q, k, v, dil, win):
    B, T, H, hd = q.shape
    M = T // dil
    Mp = -(-M // BLOCK) * BLOCK
    nb = Mp // BLOCK

    def to_blocks(a):
        a = a.reshape(B, M, dil, H, hd).transpose(0, 2, 1, 3, 4)
        a = jnp.pad(a, ((0, 0), (0, 0), (0, Mp - M), (0, 0), (0, 0)))
        return a.reshape(B, dil, nb, BLOCK, H, hd)

    def with_prev(a):
        prev = jnp.pad(a[:, :, :-1], ((0, 0), (0, 0), (1, 0), (0, 0), (0, 0), (0, 0)))
        return jnp.concatenate([prev, a], axis=3)

    qb = to_blocks(q)
    kk = with_prev(to_blocks(k))
    vv = with_prev(to_blocks(v))
    s = jnp.einsum('brnqhd,brnkhd->brnhqk', qb, kk, preferred_element_type=jnp.float32) * hd ** -0.5
    qi = jnp.arange(BLOCK)[:, None]
    kj = jnp.arange(2 * BLOCK)[None, :]
    dist = qi + BLOCK - kj
    kpos = (jnp.arange(nb) * BLOCK - BLOCK)[:, None, None] + kj[None]
    mask = ((dist >= 0) & (dist <= win // dil))[None] & (kpos >= 0)
    p, l, m = masked_softmax_stats(s, mask[:, None])
    o = jnp.einsum('brnhqk,brnkhd->brnqhd', p, vv.astype(jnp.float32)) / jnp.swapaxes(l, -1, -2)[..., None]
    lse = jnp.swapaxes(m + jnp.log(l), -1, -2)
    o = o.reshape(B, dil, Mp, H, hd)[:, :, :M].transpose(0, 2, 1, 3, 4).reshape(B, T, H, hd)
    lse = lse.reshape(B, dil, Mp, H)[:, :, :M].transpose(0, 2, 1, 3).reshape(B, T, H)
    return o, lse


def dilated_attn_sample(q, k, v, kv_cache, dil, win):
    L = kv_cache.shape[1]
    S = q.shape[1]
    ext = jnp.concatenate([kv_cache.astype(k.dtype), jnp.stack([k, v], axis=2)], axis=1)
    n = win // dil + 1
    idx = L + jnp.arange(S)[:, None] - dil * jnp.arange(n)[None, :]
    valid = idx >= 0
    g = ext[:, jnp.maximum(idx, 0)]
    s = jnp.einsum('bshd,bsnhd->bshn', q, g[:, :, :, 0], preferred_element_type=jnp.float32) * HEAD_DIM ** -0.5
    p, l, m = masked_softmax_stats(s, valid[None, :, None, :])
    o = jnp.einsum('bshn,bsnhd->bshd', p, g[:, :, :, 1].astype(jnp.float32)) / l[..., None]
    keep = min(win, L + S)
    return o, m + jnp.log(l), ext[:, L + S - keep:]


def swa_sink_prompt(q, k, v, sinks):
    B, T, HQ, hd = q.shape
    HKV = k.shape[2]
    G = HQ // HKV
    nb = T // BLOCK
    qb = q.reshape(B, nb, BLOCK, HKV, G, hd)

    def with_prev(a):
        a = a.reshape(B, nb, BLOCK, HKV, hd)
        prev = jnp.pad(a[:, :-1], ((0, 0), (1, 0), (0, 0), (0, 0), (0, 0)))
        return jnp.concatenate([prev, a], axis=2)

    kk, vv = with_prev(k), with_prev(v)
    s = jnp.einsum('bnqkgd,bnjkd->bnkgqj', qb, kk, preferred_element_type=jnp.float32) * hd ** -0.5
    qi = jnp.arange(BLOCK)[:, None]
    kj = jnp.arange(2 * BLOCK)[None, :]
    dist = qi + BLOCK - kj
    kpos = (jnp.arange(nb) * BLOCK - BLOCK)[:, None, None] + kj[None]
    mask = ((dist >= 0) & (dist < B_WINDOW))[None] & (kpos >= 0)
    sink = sinks.astype(jnp.float32).reshape(HKV, G, 1)
    p, den = sink_softmax(s, mask[:, None, None], sink)
    o = jnp.einsum('bnkgqj,bnjkd->bnqkgd', p, vv.astype(jnp.float32)) / jnp.moveaxis(den, -1, 2)[..., None]
    return o.reshape(B, T, HQ * hd)


def swa_sink_sample(q, kv_ext, n_past, sinks):
    B, S, HQ, hd = q.shape
    HKV = kv_ext.shape[3]
    G = HQ // HKV
    N = kv_ext.shape[1]
    qg = q.reshape(B, S, HKV, G, hd)
    s = jnp.einsum('bskgd,bjkd->bkgsj', qg, kv_ext[:, :, 0], preferred_element_type=jnp.float32) * hd ** -0.5
    dist = n_past + jnp.arange(S)[:, None] - jnp.arange(N)[None, :]
    mask = (dist >= 0) & (dist < B_WINDOW)
    sink = sinks.astype(jnp.float32).reshape(HKV, G, 1)
    p, den = sink_softmax(s, mask, sink)
    o = jnp.einsum('bkgsj,bjkd->bskgd', p, kv_ext[:, :, 1].astype(jnp.float32)) / jnp.moveaxis(den, -1, 1)[..., None]
    return o.reshape(B, S, HQ * hd)


def trunk(x, c, pos, a_caches, b_cache, ada_w, ada_b, g_pre, g_post, w_in_a, w_o_a,
          w_in_b, w_o_b, sinks_b, ada_kv_w, ada_kv_b, g_kv, w_kv):
    B, T, _ = x.shape
    new_a = [[] for _ in A_GROUPS]
    new_b = None
    kv_ext, n_past, kb, vb = None, 0, None, None
    for layer in range(DEPTH):
        shift, scale, gate = jnp.split(jax.nn.silu(c) @ ada_w[layer] + ada_b[layer], 3, axis=-1)
        h = modulate(rmsnorm(x, g_pre[layer]), shift, scale)
        if layer < N_A:
            i = layer
            q, k, v, z = jnp.split(h @ w_in_a[i], 4, axis=-1)
            q = rope(q.reshape(B, T, A_HEADS, HEAD_DIM), pos)
            k = rope(k.reshape(B, T, A_HEADS, HEAD_DIM), pos)
            v = v.reshape(B, T, A_HEADS, HEAD_DIM)
            outs, lses = [], []
            for g, (win, dil) in enumerate(A_GROUPS):
                sl = slice(g * A_HEADS_PER_GROUP, (g + 1) * A_HEADS_PER_GROUP)
                if a_caches is None:
                    o, lse = dilated_attn_prompt(q[:, :, sl], k[:, :, sl], v[:, :, sl], dil, win)
                    st = jnp.stack([k[:, :, sl], v[:, :, sl]], axis=2)[:, T - min(win, T):]
                else:
                    o, lse, st = dilated_attn_sample(q[:, :, sl], k[:, :, sl], v[:, :, sl], a_caches[g][i], dil, win)
                outs.append(o)
                lses.append(lse)
                new_a[g].append(st)
            alpha = jax.nn.softmax(jnp.stack(lses, axis=2), axis=2)
            o = (jnp.stack(outs, axis=2) * alpha[..., None]).reshape(B, T, A_WIDTH)
            mix = (o.astype(x.dtype) * jax.nn.silu(z)) @ w_o_a[i]
        else:
            j = layer - N_A
            if layer == N_A:
                kv_shift, kv_scale = jnp.split(jax.nn.silu(c) @ ada_kv_w + ada_kv_b, 2, axis=-1)
                hk = modulate(rmsnorm(x, g_kv), kv_shift, kv_scale)
                kvp = (hk @ w_kv).reshape(B, T, 2, B_KV_HEADS, HEAD_DIM)
                kb = rope(kvp[:, :, 0], pos)
                vb = kvp[:, :, 1]
                kv_new = jnp.stack([kb, vb], axis=2)
                if b_cache is None:
                    kv_ext, n_past = kv_new, 0
                else:
                    kv_ext = jnp.concatenate([b_cache.astype(kv_new.dtype), kv_new], axis=1)
                    n_past = b_cache.shape[1]
                n_ext = kv_ext.shape[1]
                new_b = kv_ext[:, n_ext - min(B_WINDOW, n_ext):]
            q, z = jnp.split(h @ w_in_b[j], 2, axis=-1)
            q = rope(q.reshape(B, T, B_Q_HEADS, HEAD_DIM), pos)
            if b_cache is None:
                o = swa_sink_prompt(q, kb, vb, sinks_b[j])
            else:
                o = swa_sink_sample(q, kv_ext, n_past, sinks_b[j])
            mix = (o.astype(x.dtype) * jax.nn.silu(z)) @ w_o_b[j]
        x = x + gate[:, None] * rmsnorm(mix, g_post[layer])
    return x, [jnp.stack(s, axis=0) for s in new_a], new_b


def setup_inputs(seed: int = 0) -> dict:
    key = jax.random.key(seed)
    ks = jax.random.split(key, 24)
    D = D_MODEL

    def nrm(k, shape, std):
        return jax.random.normal(k, shape, jnp.float32) * std

    inputs = {
        "x_prompt": nrm(ks[0], (BATCH, SEQ, D), 1.0),
        "x_sample": nrm(ks[1], (DEC_BATCH, DEC_SEQ, D), 1.0),
        "c_prompt": nrm(ks[2], (BATCH, D), 1.0),
        "c_sample": nrm(ks[3], (DEC_BATCH, D), 1.0),
        "cache_a_kv_g0": nrm(ks[4], (N_A, DEC_BATCH, min(A_GROUPS[0][0], PAST_LEN), 2, A_HEADS_PER_GROUP, HEAD_DIM), 1.0),
        "cache_a_kv_g1": nrm(ks[5], (N_A, DEC_BATCH, min(A_GROUPS[1][0], PAST_LEN), 2, A_HEADS_PER_GROUP, HEAD_DIM), 1.0),
        "cache_a_kv_g2": nrm(ks[6], (N_A, DEC_BATCH, min(A_GROUPS[2][0], PAST_LEN), 2, A_HEADS_PER_GROUP, HEAD_DIM), 1.0),
        "cache_b_kv": nrm(ks[7], (DEC_BATCH, min(B_WINDOW, PAST_LEN), 2, B_KV_HEADS, HEAD_DIM), 1.0),
        "ada_w": nrm(ks[8], (DEPTH, D, 3 * D), ADA_STD * D ** -0.5),
        "ada_b": nrm(ks[9], (DEPTH, 3 * D), 0.02),
        "g_pre": 1.0 + nrm(ks[10], (DEPTH, D), 0.02),
        "g_post": 1.0 + nrm(ks[11], (DEPTH, D), 0.02),
        "w_in_a": nrm(ks[12], (N_A, D, 4 * A_WIDTH), D ** -0.5),
        "w_o_a": nrm(ks[13], (N_A, A_WIDTH, D), A_WIDTH ** -0.5),
        "w_in_b": nrm(ks[14], (N_B, D, 2 * B_WIDTH), D ** -0.5),
        "w_o_b": nrm(ks[15], (N_B, B_WIDTH, D), B_WIDTH ** -0.5),
        "sinks_b": nrm(ks[16], (N_B, B_Q_HEADS), 1.0),
        "ada_kv_w": nrm(ks[17], (D, 2 * D), ADA_STD * D ** -0.5),
        "ada_kv_b": nrm(ks[18], (2 * D,), 0.02),
        "g_kv": 1.0 + nrm(ks[19], (D,), 0.02),
        "w_kv": nrm(ks[20], (D, 2 * B_KV_HEADS * HEAD_DIM), D ** -0.5),
    }
    return inputs


def reference(x_prompt, x_sample, c_prompt, c_sample, cache_a_kv_g0, cache_a_kv_g1, cache_a_kv_g2,
              cache_b_kv, ada_w, ada_b, g_pre, g_post, w_in_a, w_o_a, w_in_b, w_o_b, sinks_b,
              ada_kv_w, ada_kv_b, g_kv, w_kv):
    pos_prompt = jnp.arange(x_prompt.shape[1])
    pos_sample = PAST_LEN + jnp.arange(x_sample.shape[1])
    y_prompt, a_p, b_p = trunk(x_prompt, c_prompt, pos_prompt, None, None,
                               ada_w, ada_b, g_pre, g_post, w_in_a, w_o_a, w_in_b, w_o_b,
                               sinks_b, ada_kv_w, ada_kv_b, g_kv, w_kv)
    y_sample, a_s, b_s = trunk(x_sample, c_sample, pos_sample,
                               (cache_a_kv_g0, cache_a_kv_g1, cache_a_kv_g2), cache_b_kv,
                               ada_w, ada_b, g_pre, g_post, w_in_a, w_o_a, w_in_b, w_o_b,
                               sinks_b, ada_kv_w, ada_kv_b, g_kv, w_kv)
    return (y_prompt, y_sample, a_p[0], a_p[1], a_p[2], b_p, a_s[0], a_s[1], a_s[2], b_s)
```

```python
import os
import numpy as np
from contextlib import ExitStack
import concourse.bass as bass
import concourse.mybir as mybir
from concourse.bass_utils import run_bass_kernel_spmd

F32 = mybir.dt.float32
BF16 = mybir.dt.bfloat16
ALU = mybir.AluOpType
AF = mybir.ActivationFunctionType

NCORES = 8
T = 4096
D = 1024
NT = 32
NS = 8
EPS = 1e-6
PAST = 16384
DB = 16


class Prog:
    CE = ("pe", "act", "dve", "pool")
    ENG = ("pe", "act", "dve", "pool", "sp")

    def __init__(self, nc, stack, ndsem=20):
        self.nc = nc
        self.eng = dict(pe=nc.tensor, act=nc.scalar, dve=nc.vector, pool=nc.gpsimd, sp=nc.sync)
        self.ops = []
        self.last_w = {}
        self.readers = {}
        self.esem = {e: stack.enter_context(nc.semaphore("S_" + e)) for e in self.CE}
        self.cnt = {e: 0 for e in self.CE}
        self.dsem = {e: [stack.enter_context(nc.semaphore("D_%s%d" % (e, i))) for i in range(ndsem)]
                     for e in ("sp", "pool", "act")}
        self.dcum = {e: [0] * ndsem for e in self.dsem}
        self.dnext = {e: 0 for e in self.dsem}
        self.waited = {e: {} for e in self.ENG}
        self.nwait = 0
        self.nops = 0

    @staticmethod
    def _isps(k):
        n = k[0] if isinstance(k, tuple) else k
        return isinstance(n, str) and n.startswith("ps")

    def op(self, eng, fn, reads=(), writes=(), dma=False, rg=None):
        writes = list(writes) + [r for r in reads if self._isps(r)]
        reads = [r for r in reads if not self._isps(r)]
        i = len(self.ops)
        raw = set()
        oth = set()
        for r in reads:
            if r in self.last_w:
                raw.add(self.last_w[r])
        for w in writes:
            if w in self.last_w:
                oth.add(self.last_w[w])
            for j in self.readers.get(w, {}).values():
                oth.add(j)
        raw.discard(i)
        oth.discard(i)
        self.ops.append(dict(eng=eng, fn=fn, raw=raw, oth=oth - raw, dma=dma, rg=rg))
        for r in reads:
            self.readers.setdefault(r, {})[(eng if not dma else ("dma", i))] = i
        for w in writes:
            self.last_w[w] = i
            self.readers[w] = {}
        return i

    def _needs_wait(self, o, p, is_raw):
        if p["dma"] or o["dma"]:
            return True
        if p["eng"] != o["eng"]:
            return True
        if o["eng"] == "pe":
            return o["rg"] is not None and p["rg"] is not None and o["rg"] != p["rg"]
        return is_raw

    def _wait(self, e, sem, val):
        k = id(sem)
        if self.waited[e].get(k, 0) >= val:
            return
        self.eng[e].wait_ge(sem, val)
        self.waited[e][k] = val
        self.nwait += 1

    def flush(self):
        ops = self.ops
        n = len(ops)
        need = [False] * n
        for o in ops:
            for d in o["raw"]:
                if self._needs_wait(o, ops[d], True):
                    need[d] = True
            for d in o["oth"]:
                if self._needs_wait(o, ops[d], False):
                    need[d] = True
        last = {}
        for i, o in enumerate(ops):
            if not o["dma"]:
                last[o["eng"]] = i
        for i in last.values():
            need[i] = True
        sig = [None] * n
        for i, o in enumerate(ops):
            e = o["eng"]
            eo = self.eng[e]
            deps = [(d, True) for d in o["raw"]] + [(d, False) for d in o["oth"]]
            for d, is_raw in sorted(deps):
                p = ops[d]
                if not self._needs_wait(o, p, is_raw):
                    continue
                s, v = sig[d]
                self._wait(e, s, v)
            if o["dma"]:
                k = self.dnext[e]
                self.dnext[e] = (k + 1) % len(self.dsem[e])
                s = self.dsem[e][k]
                self._wait(e, s, self.dcum[e][k])
                ins = o["fn"](eo)
                self.dcum[e][k] += 16
                ins.then_inc(s, 16)
                sig[i] = (s, self.dcum[e][k])
            else:
                ins = o["fn"](eo)
                if need[i]:
                    self.cnt[e] += 1
                    ins.then_inc(self.esem[e], 1)
                    sig[i] = (self.esem[e], self.cnt[e])
        for e in self.ENG:
            for x in self.CE:
                if x != e and self.cnt[x] > 0:
                    self._wait(e, self.esem[x], self.cnt[x])
            for q in self.dsem:
                for k, s in enumerate(self.dsem[q]):
                    if self.dcum[q][k] > 0:
                        self._wait(e, s, self.dcum[q][k])
        self.nops += n
        print("[flush] ops", n, "cnt", self.cnt, "dcum max", {q: max(v) for q, v in self.dcum.items()})
        self.ops = []
        self.last_w = {}
        self.readers = {}


def _consts():
    half = 32
    inv = (10000.0 ** (-np.arange(half, dtype=np.float32) / half)).astype(np.float32)
    pos = np.arange(T, dtype=np.float32)
    ang = pos[:, None] * inv[None, :]
    cos = np.cos(ang).astype(np.float32)
    sin = np.sin(ang).astype(np.float32)
    fi = (np.arange(128) % 64) % 32
    cosT = np.ascontiguousarray(cos[:, fi].T)
    sinT = np.ascontiguousarray(sin[:, fi].T)
    angs = (np.float32(PAST) * inv).astype(np.float32)
    cs = np.cos(angs).astype(np.float32)
    ss = np.sin(angs).astype(np.float32)
    cs_tok = np.tile(np.concatenate([cs, cs])[None, :], (DB, 1)).astype(np.float32)
    sn_tok = np.tile(np.concatenate([-ss, ss])[None, :], (DB, 1)).astype(np.float32)
    rmat = np.zeros((128, 128), np.float32)
    for po in range(128):
        if (po % 64) < 32:
            rmat[po + 32, po] = -1.0
        else:
            rmat[po - 32, po] = 1.0
    k = np.arange(128)[:, None]
    q = np.arange(128)[None, :]
    cur = (k <= q).astype(np.float32)
    prev0 = (k >= q).astype(np.float32)
    prev1 = (k > q).astype(np.float32)
    m0 = np.concatenate([prev0, cur, prev0, cur], axis=1)
    m1 = np.concatenate([prev1, cur, prev1, cur], axis=1)
    masks = np.stack([m0, m1]).astype(np.float32)
    rowb = np.zeros((128, 1), np.float32)
    rowb[0, 0] = -30000.0
    return dict(rowb=rowb, cosT=cosT, sinT=sinT, cs_tok=cs_tok, sn_tok=sn_tok, rmat=rmat, masks=masks,
                ident=np.eye(128, dtype=np.float32))


class _Stop(Exception):
    pass


def build_nc():
    nc = bass.Bass("TRN2", target_bir_lowering=False)
    try:
        return _build(nc)
    except _Stop:
        return nc


def _build(nc):

    def din(name, shape, dt=F32):
        return nc.dram_tensor(name, list(shape), dt, kind="ExternalInput").ap()

    def dout(name, shape, dt=F32):
        return nc.dram_tensor(name, list(shape), dt, kind="ExternalOutput").ap()

    xp = din("xp", [T, D])
    crows = din("crows", [33, D])
    xs = din("xs", [DB, D])
    ca = [din("ca0", [DB, 128, 512]), din("ca1", [DB, 512, 512]), din("ca2", [DB, 2048, 512])]
    cb = din("cb", [DB, 128, 256])
    ada_w0 = din("ada_w0", [D, 3072])
    ada_w1 = din("ada_w1", [D, 3072])
    ada_kv_w = din("ada_kv_w", [D, 2048])
    brow = din("brow", [1, 8192])
    vecs = din("vecs", [40, 128])
    gpost_rows = din("gpost_rows", [2, D])
    w_in_a = din("w_in_a", [D, 3072])
    w_o_a = din("w_o_a", [768, D])
    w_in_b = din("w_in_b", [D, 2048])
    w_o_b = din("w_o_b", [D, D])
    w_kvx = din("w_kvx", [D, 384])
    sink_fm = din("sink_fm", [128, 8])
    sink_tm = din("sink_tm", [DB, 16])
    ident_d = din("ident", [128, 128])
    rmat_d = din("rmat", [128, 128])
    cosT_d = din("cosT", [128, T])
    sinT_d = din("sinT", [128, T])
    masks_d = din("masks", [2, 128, 512])
    cs_tok_d = din("cs_tok", [DB, 64])
    sn_tok_d = din("sn_tok", [DB, 64])
    rowb_d = din("rowb_in", [128, 1])
    gvec_tm = din("gvec_tm", [5, DB, D])
    w_kv_d = din("w_kv", [D, 256])

    yp = dout("yp", [T, D])
    ys = dout("ys", [DB, D])
    aop = [dout("a0p", [128, 512]), dout("a1p", [512, 512]), dout("a2p", [2048, 512])]
    bp = dout("bp", [128, 256])
    aos = [dout("a0s", [DB, 128, 512]), dout("a1s", [DB, 512, 512]), dout("a2s", [DB, 2048, 512])]
    bs = dout("bs", [DB, 128, 256])
    x1s = nc.dram_tensor("x1s", [T, D], F32, kind="Internal").ap()
    g2s = nc.dram_tensor("g2s", [4, 128, T], BF16, kind="Internal").ap()

    with ExitStack() as top:
        P = Prog(nc, top)
        op = P.op

        def stop(tag):
            if os.environ.get("KSTOP") == tag:
                P.flush()
                raise _Stop()

        def sbt(st, name, shape, dt):
            return st.enter_context(nc.sbuf_tensor(name, list(shape), dt))

        def pst(st, name, shape, dt):
            return st.enter_context(nc.psum_tensor(name, list(shape), dt))

        identf = sbt(top, "identf", [128, 128], F32)
        identb = sbt(top, "identb", [128, 128], BF16)
        rmf = sbt(top, "rmf", [128, 128], F32)
        rmb = sbt(top, "rmb", [128, 128], BF16)
        maskb = [sbt(top, "maskb%d" % l, [128, 512], BF16) for l in range(2)]
        vecT = sbt(top, "vecT", [128, 40], F32)
        modT = sbt(top, "modT", [128, 64], F32)
        amod = sbt(top, "amod", [128, 24], F32)
        ggbc = [sbt(top, "ggbc%d" % l, [128, D], F32) for l in range(2)]
        esink = sbt(top, "esink", [128, 8], F32)
        ones_b = sbt(top, "ones_b", [128, 64], BF16)
        msq = sbt(top, "msq", [128, 4 * NT], F32)
        rstd = sbt(top, "rstd", [128, 4 * NT], F32)
        junk = sbt(top, "junk", [128, D], BF16)

        psT = [pst(top, "psT%d" % i, [128, 8, 128], BF16) for i in range(2)]
        psP = pst(top, "psP", [128, 2, 512], F32)
        psR = pst(top, "psR", [128, 512], F32)
        psS = [pst(top, "psS%d" % i, [128, 512], F32) for i in range(2)]
        psO = pst(top, "psO", [128, 512], F32)
        psOf = psO

        def dma(q, out, in_, reads=(), writes=()):
            return op(q, lambda e, o=out, i=in_: e.dma_start(out=o, in_=i), reads, writes, dma=True)

        def mm(out, lhsT, rhs, start, stop, reads, writes, rg=None):
            return op("pe", lambda e, o=out, l=lhsT, r=rhs, a=start, b=stop: e.matmul(o, lhsT=l, rhs=r, start=a, stop=b),
                      reads, writes, rg=rg)

        def tr(out, in_, ident, reads, writes):
            return op("pe", lambda e, o=out, i=in_, d=ident: e.transpose(out=o, in_=i, identity=d), reads, writes)

        def act(out, in_, func, reads, writes, **kw):
            return op("act", lambda e, o=out, i=in_, f=func, k=kw: e.activation(out=o, in_=i, func=f, **k), reads, writes)

        def cp(eng, out, in_, reads, writes):
            if eng == "act":
                return act(out, in_, AF.Copy, reads, writes)
            return op(eng, lambda e, o=out, i=in_: e.tensor_copy(out=o, in_=i), reads, writes)

        def tt(eng, out, in0, in1, alu, reads, writes):
            return op(eng, lambda e, o=out, a=in0, b=in1, f=alu: e.tensor_tensor(out=o, in0=a, in1=b, op=f), reads, writes)

        def ts(eng, out, in0, s1, s2, op0, op1, reads, writes):
            if s2 is None:
                return op(eng, lambda e, o=out, a=in0, x=s1, f=op0: e.tensor_scalar(out=o, in0=a, scalar1=x, scalar2=None, op0=f),
                          reads, writes)
            return op(eng, lambda e, o=out, a=in0, x=s1, y=s2, f=op0, g=op1:
                      e.tensor_scalar(out=o, in0=a, scalar1=x, scalar2=y, op0=f, op1=g), reads, writes)

        def stt(eng, out, in0, scalar, in1, op0, op1, reads, writes):
            return op(eng, lambda e, o=out, a=in0, s=scalar, b=in1, f=op0, g=op1:
                      e.scalar_tensor_tensor(out=o, in0=a, scalar=s, in1=b, op0=f, op1=g), reads, writes)

        def memset(eng, ap, val, writes):
            return op(eng, lambda e, a=ap, v=val: e.memset(a, v), (), writes)

        cast_rr = [0]

        def load_w(st_slots, dst, src, rows, col0, ncols, dcol0, key):
            nk = rows // 128
            for kc in range(nk):
                for c0 in range(0, ncols, 1024):
                    w = min(1024, ncols - c0)
                    sl = cast_rr[0] % len(st_slots)
                    cast_rr[0] += 1
                    stg = st_slots[sl]
                    dma("sp", stg[:, 0:w], src[kc * 128:(kc + 1) * 128, col0 + c0:col0 + c0 + w], (), [("wst", sl)])
                    eng = ("pool", "dve", "pool", "act")[cast_rr[0] % 4]
                    cp(eng, dst[:, kc, dcol0 + c0:dcol0 + c0 + w], stg[:, 0:w], [("wst", sl)], [key])

        rr = dict(p=0, r=0, s=0, o=0, e=0, w=0)

        def nxt(k, n=2):
            v = rr[k] % n
            rr[k] += 1
            return v

        def evac_eng():
            return ("act", "dve")[nxt("e")]

        AX = mybir.AxisListType

        with ExitStack() as st:
            crow_t = sbt(st, "crow_t", [33, D], F32)
            sc_t = sbt(st, "sc_t", [33, D], F32)
            cT = sbt(st, "cT", [128, 8, 33], F32)
            vec_t = sbt(st, "vec_t", [40, 128], F32)
            ones33 = sbt(st, "ones33", [33, 128], F32)
            adat = [sbt(st, "adat0", [33, 3072], F32), sbt(st, "adat1", [33, 2048], F32)]
            gprow = sbt(st, "gprow", [33, D], F32)
            wstg = [sbt(st, "adaw%d" % i, [128, 3072], F32) for i in range(2)]
            psA = [psS[0], psS[1], psR, psOf, psP[:, 0, :], psP[:, 1, :]]
            sink_t = sbt(st, "sink_t", [128, 8], F32)
            maskf = sbt(st, "maskf", [128, 512], F32)
            xs_t = sbt(st, "xs_t", [DB, D], F32)
            x1_t = sbt(st, "x1_t", [DB, D], F32)
            gv = [sbt(st, "gv%d" % i, [DB, D], F32) for i in range(3)]
            cs_t = sbt(st, "cs_t", [DB, 64], F32)
            sn_t = sbt(st, "sn_t", [DB, 64], F32)
            stm = sbt(st, "stm", [DB, 16], F32)
            rowb = sbt(st, "rowb", [128, 1], F32)
            hs = sbt(st, "hs", [DB, D], F32)
            hs2 = sbt(st, "hs2", [DB, D], F32)
            hsT = sbt(st, "hsT", [128, 8, DB], F32)
            hkTs = sbt(st, "hkTs", [128, 8, DB], F32)
            proj = sbt(st, "proj", [33, 3072], F32)
            brow_t = proj
            kvp = sbt(st, "kvp", [DB, 256], F32)
            rop = sbt(st, "rop", [DB, 1536], F32)
            swp = sbt(st, "swp", [DB, 1536], F32)
            krow = [sbt(st, "krow%d" % i, [DB, 512], F32) for i in range(4)]
            KVflat = sbt(st, "KVflat", [128, DB * 520], F32)
            prod = sbt(st, "prod", [128, 1024], F32)
            scr = sbt(st, "scr", [128, 16], F32)
            PZ = sbt(st, "PZ", [128, DB, 16, DB], F32)
            Oc = sbt(st, "Oc", [DB, 16, 65], F32)
            Ot = sbt(st, "Ot", [DB, 16, 64], F32)
            sml = sbt(st, "sml", [DB, 128], F32)

            for g, L in enumerate((128, 512, 2048)):
                for b in range(DB):
                    dma("pool", aos[g][b, 0:L - 1, :], ca[g][b, 1:L, :], (), ())
            dma("pool", bs[:, 0:127, :], cb[:, 1:128, :], (), ())

            dma("sp", identf[:], ident_d[:, :], (), ["identf"])
            dma("sp", rmf[:], rmat_d[:, :], (), ["rmf"])
            dma("sp", crow_t[:], crows[:, :], (), ["crow"])
            dma("sp", vec_t[:], vecs[:, :], (), ["vec_t"])
            dma("sp", sink_t[:], sink_fm[:, :], (), ["sink_t"])
            dma("sp", xs_t[:], xs[:, :], (), ["xs_t"])
            dma("sp", cs_t[:], cs_tok_d[:, :], (), ["cs_t"])
            dma("sp", sn_t[:], sn_tok_d[:, :], (), ["sn_t"])
            dma("sp", stm[:], sink_tm[:, :], (), ["stm"])
            dma("sp", rowb[:], rowb_d[:, :], (), ["rowb"])
            for l in range(2):
                dma("sp", maskf[:], masks_d[l], (), ["maskf"])
                cp("dve", maskb[l][:], maskf[:], ["maskf"], ["maskb%d" % l])
            cp("dve", identb[:], identf[:], ["identf"], ["identb"])
            cp("dve", rmb[:], rmf[:], ["rmf"], ["rmb"])
            memset("dve", ones_b[:], 1.0, ["ones_b"])
            memset("dve", ones33[:], 1.0, ["ones33"])
            memset("pool", PZ[:], 0.0, ["PZ"])
            act(esink[:], sink_t[:], AF.Exp, ["sink_t"], ["esink"])
            act(stm[:], stm[:], AF.Exp, ["stm"], ["stm"])
            tr(psOf[:, 0:40], vec_t[:, :], identf[0:40, 0:40], ["vec_t", "identf"], [("psA", 3)])
            cp("dve", vecT[:], psOf[:, 0:40], [("psA", 3)], ["vecT"])
            act(sc_t[:], crow_t[:], AF.Silu, ["crow"], ["sc_t"])
            for kc in range(8):
                tr(psOf[:, 64 + kc * 33:64 + (kc + 1) * 33], sc_t[:, kc * 128:(kc + 1) * 128], identf[0:33, 0:33],
                   ["sc_t", "identf"], [("psA", 3)])
            cp("dve", cT[:].rearrange("p k t -> p (k t)"), psOf[:, 64:64 + 8 * 33], [("psA", 3)], ["cT"])

            def stream_mm(lhsT_of, lkey, M, wd, nk, ncols, dst, dkey, bias_off=None):
                nb = (ncols + 511) // 512
                if bias_off is not None:
                    dma("sp", brow_t[32:33, 0:ncols], brow[:, bias_off:bias_off + ncols], (), ["brow_t"])
                for kc in range(nk):
                    sl = nxt("w")
                    dma("sp", wstg[sl][:, 0:ncols], wd[kc * 128:(kc + 1) * 128, 0:ncols], (), [("adaw", sl)])
                    for b in range(nb):
                        w = min(512, ncols - b * 512)
                        mm(psA[b][0:M, 0:w], lhsT_of(kc), wstg[sl][:, b * 512:b * 512 + w], kc == 0,
                           (kc == nk - 1) and bias_off is None, [lkey, ("adaw", sl)], [("psA", b)])
                for b in range(nb):
                    w = min(512, ncols - b * 512)
                    if bias_off is not None:
                        mm(psA[b][0:M, 0:w], ones33[32:33, 0:M], brow_t[32:33, b * 512:b * 512 + w], False, True,
                           ["ones33", "brow_t"], [("psA", b)])
                    cp("dve" if b % 2 else "act", dst[0:M, b * 512:b * 512 + w], psA[b][0:M, 0:w], [("psA", b)], [(dkey, b)])
                return [(dkey, b) for b in range(nb)]

            def prompt_cols(ad, akeys, ncols, mcol):
                nch = ncols // 128
                for j in range(nch):
                    mm(psS[0][:, j:j + 1], ad[32:33, j * 128:(j + 1) * 128], ones33[32:33, 0:1], True, True,
                       akeys + ["ones33"], [("psA", 0)])
                cp("dve", modT[:, mcol:mcol + nch], psS[0][:, 0:nch], [("psA", 0)], ["modT"])

            def gate_bcast(ad, akeys, li):
                dma("sp", gprow[32:33, :], gpost_rows[li:li + 1, :], (), ["gprow"])
                tt("dve", gprow[32:33, :], gprow[32:33, :], ad[32:33, 2048:3072], ALU.mult, akeys + ["gprow"], ["gprow"])
                for b in range(2):
                    mm(psS[1][:], ones33[32:33, :], gprow[32:33, b * 512:(b + 1) * 512], True, True,
                       ["ones33", "gprow"], [("psA", 1)])
                    cp("act", ggbc[li][:, b * 512:(b + 1) * 512], psS[1][:], [("psA", 1)], ["ggbc%d" % li])

            scol = [100]

            def rms_tok(x_t, xkeys):
                c = scol[0]
                scol[0] += 1
                act(junk[0:DB, :], x_t[:], AF.Square, xkeys, [("msq", c)], scale=1.0 / 32, accum_out=msq[0:DB, c:c + 1])
                act(rstd[0:DB, c:c + 1], msq[0:DB, c:c + 1], AF.Ln, [("msq", c)], [("rstd", c)], bias=EPS)
                act(rstd[0:DB, c:c + 1], rstd[0:DB, c:c + 1], AF.Exp, [("rstd", c)], [("rstd", c)], scale=-0.5)
                return rstd[0:DB, c:c + 1], ("rstd", c)

            def modulate_tok(dst, dkey, x_t, xkeys, rs, rkey, g_t, gkey, scale_ap, shift_ap, akeys):
                ts("dve", dst[:], x_t[:], rs, None, ALU.mult, None, xkeys + [rkey], [dkey])
                tt("dve", dst[:], dst[:], g_t[:], ALU.mult, [dkey, gkey], [dkey])
                stt("dve", dst[:], scale_ap, 1.0, dst[:], ALU.add, ALU.mult, akeys + [dkey], [dkey])
                tt("dve", dst[:], dst[:], shift_ap, ALU.add, akeys + [dkey], [dkey])

            def transpose_tok(dstT, dkey, src, skeys, nk):
                for kc in range(nk):
                    tr(psOf[:, kc * DB:(kc + 1) * DB], src[0:DB, kc * 128:(kc + 1) * 128], identf[0:DB, 0:DB], skeys, [("psA", 3)])
                cp("dve", dstT[:, 0:nk, :].rearrange("p k t -> p (k t)"), psOf[:, 0:nk * DB], [("psA", 3)], [dkey])

            def rope_tok(dst, dkey, src, skeys, nh, sw):
                X = src.rearrange("p (h d) -> p h d", d=64)
                R = dst.rearrange("p (h d) -> p h d", d=64)
                S = sw.rearrange("p (h d) -> p h d", d=64)
                bc = lambda a: a.unsqueeze(1).to_broadcast([DB, nh, a.shape[1]])
                tt("dve", R, X, bc(cs_t[:, :]), ALU.mult, skeys + ["cs_t"], [dkey])
                tt("pool", S[:, :, 0:32], X[:, :, 32:64], bc(sn_t[:, 0:32]), ALU.mult, skeys + ["sn_t"], ["swp"])
                tt("pool", S[:, :, 32:64], X[:, :, 0:32], bc(sn_t[:, 32:64]), ALU.mult, skeys + ["sn_t"], ["swp"])
                tt("dve", R, R, S, ALU.add, [dkey, "swp"], [dkey])

            def pv_slot(hd):
                bk = hd // 7
                t = (psP[:, 0, :], psP[:, 1, :], psR)[bk]
                return t[0:DB, (hd % 7) * 65:(hd % 7) * 65 + 65], ("psA", (4, 5, 2)[bk])

            k0 = stream_mm(lambda kc: cT[:, kc, :], "cT", 33, ada_w0, 8, 3072, adat[0], "ad0", bias_off=0)
            prompt_cols(adat[0], k0, 3072, 0)
            gate_bcast(adat[0], k0, 0)

            dma("sp", gv[0][:], gvec_tm[0], (), [("gv", 0)])
            dma("sp", gv[1][:], gvec_tm[2], (), [("gv", 1)])
            rs, rk = rms_tok(xs_t, ["xs_t"])
            modulate_tok(hs, "hs", xs_t, ["xs_t"], rs, rk, gv[0], ("gv", 0), adat[0][0:DB, 1024:2048], adat[0][0:DB, 0:1024], k0)
            transpose_tok(hsT, "hsT", hs, ["hs"], 8)
            pk = stream_mm(lambda kc: hsT[:, kc, :], "hsT", DB, w_in_a, 8, 3072, proj, "proj")
            rope_tok(rop[:, 0:1536], "rop", proj[0:DB, 0:1536], pk, 24, swp[:, 0:1536])
            for g, L in enumerate((128, 512, 2048)):
                cp("pool", krow[g][:, 0:256], rop[:, 768 + g * 256:768 + (g + 1) * 256], ["rop"], [("krow", g)])
                cp("pool", krow[g][:, 256:512], proj[0:DB, 1536 + g * 256:1536 + (g + 1) * 256], pk, [("krow", g)])
                dma("sp", aos[g][:, L - 1, :], krow[g][:], [("krow", g)], ())
            KV0 = KVflat[:].rearrange("p (b t h d) -> p b t h d", b=DB, t=2, h=4)
            memset("pool", KV0[:, :, 1, :, 64:65], 1.0, ["KVones"])
            for g, (L, dil) in enumerate(((128, 1), (512, 4), (2048, 16))):
                for b in range(DB):
                    dma("sp", KV0[:, b, :, :, 0:64], ca[g][b, 0:L:dil, :].rearrange("j (t h d) -> j t h d", t=2, h=4),
                        ["KVones"], [("KV", b)])
                for b in range(DB):
                    sl = b % 2
                    mm(psS[sl][:, 0:256], identf[0:DB, b:b + 1].to_broadcast([DB, 128]), rop[0:DB, g * 256:(g + 1) * 256], True, True,
                       ["rop", "identf"], [("psA", sl)])
                    p3 = prod[:, 0:256].rearrange("p (h d) -> p h d", h=4)
                    tt("dve", p3, KV0[:, b, 0, :, 0:64], psS[sl][:, 0:256].rearrange("p (h d) -> p h d", h=4), ALU.mult,
                       [("KV", b), ("psA", sl)], ["prod"])
                    op("dve", lambda e, o=scr[:, 0:4], i_=p3: e.tensor_reduce(out=o, in_=i_, axis=AX.X, op=ALU.add), ["prod"], ["scr"])
                    act(PZ[:, b, g * 4:(g + 1) * 4, b], scr[:, 0:4], AF.Exp, ["scr", "PZ"], [("PZ", b)], scale=0.125)
                for h in range(4):
                    hd = g * 4 + h
                    o, okey = pv_slot(hd)
                    for b in range(DB):
                        mm(o, PZ[:, b, hd, :], KV0[:, b, 1, h, :], b == 0, b == DB - 1, [("PZ", b), ("KV", b), "KVones"], [okey])
            cp("dve", Oc[:, 0:7, :], psP[0:DB, 0, 0:455].rearrange("p (h d) -> p h d", d=65), [("psA", 4)], ["Oc"])
            cp("dve", Oc[:, 7:12, :], psP[0:DB, 1, 0:325].rearrange("p (h d) -> p h d", d=65), [("psA", 5)], ["Oc"])

            def finish_attn(nh, qv, kvw, vv, keys):
                q3 = qv.rearrange("p (h d) -> p h d", d=64)
                pr = prod[0:DB, 0:nh * 64].rearrange("p (h d) -> p h d", d=64)
                tt("dve", pr, q3, kvw, ALU.mult, keys, ["prod"])
                op("dve", lambda e, o=sml[:, 32:32 + nh], i_=pr: e.tensor_reduce(out=o, in_=i_, axis=AX.X, op=ALU.add), ["prod"], ["sml"])
                act(sml[:, 32:32 + nh], sml[:, 32:32 + nh], AF.Exp, ["sml"], ["sml"], scale=0.125)
                tt("dve", pr, vv, sml[:, 32:32 + nh].unsqueeze(2).to_broadcast([DB, nh, 64]), ALU.mult, keys + ["sml", "prod"], ["prod"])
                tt("dve", Ot[:, 0:nh, :], Oc[:, 0:nh, 0:64], pr, ALU.add, ["Oc", "prod"], ["Ot"])
                tt("dve", sml[:, 0:nh], Oc[:, 0:nh, 64], sml[:, 32:32 + nh], ALU.add, ["Oc", "sml"], ["sml"])

            k3 = rop[:, 768:1536].rearrange("p (h d) -> p h d", d=64)
            v3 = proj[0:DB, 1536:2304].rearrange("p (h d) -> p h d", d=64)
            finish_attn(12, rop[:, 0:768], k3, v3, ["rop"] + pk)
            tt("dve", sml[:, 64:68], sml[:, 0:4], sml[:, 4:8], ALU.add, ["sml"], ["sml"])
            tt("dve", sml[:, 64:68], sml[:, 64:68], sml[:, 8:12], ALU.add, ["sml"], ["sml"])
            op("dve", lambda e, o=sml[:, 64:68], i_=sml[:, 64:68]: e.reciprocal(out=o, in_=i_), ["sml"], ["sml"])
            og = hs2[:, 0:768]
            for g in range(3):
                tt("dve", og[:, g * 256:(g + 1) * 256].rearrange("p (h d) -> p h d", d=64), Ot[:, g * 4:(g + 1) * 4, :],
                   sml[:, 64:68].unsqueeze(2).to_broadcast([DB, 4, 64]), ALU.mult, ["Ot", "sml"], ["og"])
            act(hs[:, 0:768], proj[0:DB, 2304:3072], AF.Silu, pk + ["hs"], ["hs"])
            tt("dve", og, og, hs[:, 0:768], ALU.mult, ["og", "hs"], ["og"])
            transpose_tok(hsT, "hsT", og, ["og"], 6)
            mk = stream_mm(lambda kc: hsT[:, kc, :], "hsT", DB, w_o_a, 6, 1024, hs, "mix")
            rs, rk = rms_tok(hs, mk)
            ts("dve", hs[:], hs[:], rs, None, ALU.mult, None, mk + [rk], ["mixn"])
            tt("dve", hs[:], hs[:], gv[1][:], ALU.mult, ["mixn", ("gv", 1)], ["mixn"])
            tt("dve", hs[:], hs[:], adat[0][0:DB, 2048:3072], ALU.mult, ["mixn"] + k0, ["mixn"])
            tt("dve", x1_t[:], hs[:], xs_t[:], ALU.add, ["mixn", "xs_t"], ["x1_t"])

            k1 = stream_mm(lambda kc: cT[:, kc, :], "cT", 33, ada_w1, 8, 3072, adat[0], "ad0", bias_off=3072)
            prompt_cols(adat[0], k1, 3072, 24)
            gate_bcast(adat[0], k1, 1)
            kk = stream_mm(lambda kc: cT[:, kc, :], "cT", 33, ada_kv_w, 8, 2048, adat[1], "ad1", bias_off=6144)
            prompt_cols(adat[1], kk, 2048, 48)
            for (dst, sc, gvv) in ((amod[:, 0:8], modT[:, 8:16], vecT[:, 0:8]), (amod[:, 8:16], modT[:, 32:40], vecT[:, 8:16]),
                                   (amod[:, 16:24], modT[:, 56:64], vecT[:, 32:40])):
                ts("dve", dst, sc, 1.0, None, ALU.add, None, ["modT"], ["amod"])
                tt("dve", dst, dst, gvv, ALU.mult, ["amod", "vecT"], ["amod"])

            dma("sp", gv[0][:], gvec_tm[1], (), [("gv", 0)])
            dma("sp", gv[1][:], gvec_tm[3], (), [("gv", 1)])
            dma("sp", gv[2][:], gvec_tm[4], (), [("gv", 2)])
            rs, rk = rms_tok(x1_t, ["x1_t"])
            modulate_tok(hs, "hs", x1_t, ["x1_t"], rs, rk, gv[0], ("gv", 0), adat[0][0:DB, 1024:2048], adat[0][0:DB, 0:1024], k1)
            modulate_tok(hs2, "hs2", x1_t, ["x1_t"], rs, rk, gv[2], ("gv", 2), adat[1][0:DB, 1024:2048], adat[1][0:DB, 0:1024], kk)
            transpose_tok(hsT, "hsT", hs, ["hs"], 8)
            transpose_tok(hkTs, "hkTs", hs2, ["hs2"], 8)
            kvk = stream_mm(lambda kc: hkTs[:, kc, :], "hkTs", DB, w_kv_d, 8, 256, kvp, "kvp")
            pk = stream_mm(lambda kc: hsT[:, kc, :], "hsT", DB, w_in_b, 8, 2048, proj, "proj")
            rope_tok(rop[:, 0:1024], "rop", proj[0:DB, 0:1024], pk, 16, swp[:, 0:1024])
            rope_tok(rop[:, 1024:1152], "ropk", kvp[:, 0:128], kvk, 2, swp[:, 1024:1152])
            cp("pool", krow[3][:, 0:128], rop[:, 1024:1152], ["ropk"], [("krow", 3)])
            cp("pool", krow[3][:, 128:256], kvp[:, 128:256], kvk, [("krow", 3)])
            dma("sp", bs[:, 127, :], krow[3][:, 0:256], [("krow", 3)], ())
            KV1 = KVflat[:, 0:DB * 260].rearrange("p (b t h d) -> p b t h d", b=DB, t=2, h=2)
            allkv = [("KV", b) for b in range(DB)]
            memset("pool", KV1[:, :, 1, :, 64:65], 1.0, allkv + ["KVones"])
            for b in range(DB):
                dma("sp", KV1[:, b, :, :, 0:64], cb[b].rearrange("j (t h d) -> j t h d", t=2, h=2), ["KVones"], [("KV", b)])
            for b in range(DB):
                for kh in range(2):
                    mm(psS[kh][:, :], identf[0:DB, b:b + 1].to_broadcast([DB, 128]), rop[0:DB, kh * 512:(kh + 1) * 512], True, True,
                       ["rop", "identf"], [("psA", kh)])
                    tt("dve", prod[:, kh * 512:(kh + 1) * 512].rearrange("p (h d) -> p h d", d=64),
                       psS[kh][:, :].rearrange("p (h d) -> p h d", d=64),
                       KV1[:, b, 0, kh, 0:64].unsqueeze(1).to_broadcast([128, 8, 64]), ALU.mult, [("KV", b), ("psA", kh)], ["prod"])
                op("dve", lambda e, o=scr[:, 0:16], i_=prod[:, :].rearrange("p (h d) -> p h d", d=64):
                   e.tensor_reduce(out=o, in_=i_, axis=AX.X, op=ALU.add), ["prod"], ["scr"])
                act(PZ[:, b, :, b], scr[:, 0:16], AF.Exp, ["scr", "PZ", "rowb"], [("PZ", b)], scale=0.125, bias=rowb[:, 0:1])
            for hd in range(16):
                o, okey = pv_slot(hd)
                for b in range(DB):
                    mm(o, PZ[:, b, hd, :], KV1[:, b, 1, hd // 8, :], b == 0, b == DB - 1, [("PZ", b), ("KV", b), "KVones"], [okey])
            cp("dve", Oc[:, 0:7, :], psP[0:DB, 0, 0:455].rearrange("p (h d) -> p h d", d=65), [("psA", 4)], ["Oc"])
            cp("dve", Oc[:, 7:14, :], psP[0:DB, 1, 0:455].rearrange("p (h d) -> p h d", d=65), [("psA", 5)], ["Oc"])
            cp("dve", Oc[:, 14:16, :], psR[0:DB, 0:130].rearrange("p (h d) -> p h d", d=65), [("psA", 2)], ["Oc"])
            kb3 = rop[:, 1024:1152].rearrange("p (k d) -> p k d", d=64)
            vb3 = kvp[:, 128:256].rearrange("p (k d) -> p k d", d=64)
            q4 = rop[:, 0:1024]
            for kh in range(2):
                pass
            pr = prod[0:DB, 0:1024].rearrange("p (h d) -> p h d", d=64)
            for kh in range(2):
                tt("dve", pr[:, kh * 8:(kh + 1) * 8, :], q4.rearrange("p (h d) -> p h d", d=64)[:, kh * 8:(kh + 1) * 8, :],
                   kb3[:, kh, :].unsqueeze(1).to_broadcast([DB, 8, 64]), ALU.mult, ["rop", "ropk", "prod"], ["prod"])
            op("dve", lambda e, o=sml[:, 32:48], i_=pr: e.tensor_reduce(out=o, in_=i_, axis=AX.X, op=ALU.add), ["prod"], ["sml"])
            act(sml[:, 32:48], sml[:, 32:48], AF.Exp, ["sml"], ["sml"], scale=0.125)
            for kh in range(2):
                tt("dve", pr[:, kh * 8:(kh + 1) * 8, :], sml[:, 32 + kh * 8:32 + (kh + 1) * 8].unsqueeze(2).to_broadcast([DB, 8, 64]),
                   vb3[:, kh, :].unsqueeze(1).to_broadcast([DB, 8, 64]), ALU.mult, kvk + ["sml", "prod"], ["prod"])
            tt("dve", Ot[:, 0:16, :], Oc[:, 0:16, 0:64], pr, ALU.add, ["Oc", "prod"], ["Ot"])
            tt("dve", sml[:, 0:16], Oc[:, 0:16, 64], sml[:, 32:48], ALU.add, ["Oc", "sml"], ["sml"])
            tt("dve", sml[:, 0:16], sml[:, 0:16], stm[:, :], ALU.add, ["sml", "stm"], ["sml"])
            op("dve", lambda e, o=sml[:, 0:16], i_=sml[:, 0:16]: e.reciprocal(out=o, in_=i_), ["sml"], ["sml"])
            og = hs2[:, 0:1024]
            tt("dve", og.rearrange("p (h d) -> p h d", d=64), Ot[:, 0:16, :], sml[:, 0:16].unsqueeze(2).to_broadcast([DB, 16, 64]),
               ALU.mult, ["Ot", "sml", "hs2"], ["og"])
            act(hs[:, :], proj[0:DB, 1024:2048], AF.Silu, pk + ["hs"], ["hs"])
            tt("dve", og, og, hs[:, :], ALU.mult, ["og", "hs"], ["og"])
            transpose_tok(hsT, "hsT", og, ["og"], 8)
            mk = stream_mm(lambda kc: hsT[:, kc, :], "hsT", DB, w_o_b, 8, 1024, hs, "mix")
            rs, rk = rms_tok(hs, mk)
            ts("dve", hs[:], hs[:], rs, None, ALU.mult, None, mk + [rk], ["mixn"])
            tt("dve", hs[:], hs[:], gv[1][:], ALU.mult, ["mixn", ("gv", 1)], ["mixn"])
            tt("dve", hs[:], hs[:], adat[0][0:DB, 2048:3072], ALU.mult, ["mixn"] + k1, ["mixn"])
            tt("dve", hs[:], hs[:], x1_t[:], ALU.add, ["mixn", "x1_t"], ["ys_t"])
            dma("sp", ys[:, :], hs[:], ["ys_t"], ())
            P.flush()
        if os.environ.get("KSTOP") == "setup":
            return nc

        A0 = amod[:, 0:8]
        SH0 = modT[:, 0:8]
        A1 = amod[:, 8:16]
        SH1 = modT[:, 24:32]
        AKV = amod[:, 16:24]
        SHKV = modT[:, 48:56]

        def load_x(src, srckey, i, xin):
            sl = i % len(xin)
            dma("sp", xin[sl][:], src[i * 128:(i + 1) * 128, :], [(srckey, i)] if srckey else (), [("xin", sl)])

        def norm_tile(i, u, xin, xn, col0, outs, hslot):
            sl = i % len(xin)
            xsl = i % len(xn)
            c = col0 + i
            act(junk[:], xin[sl][:], AF.Square, [("xin", sl)], [("msq", c)], scale=1.0 / 32, accum_out=msq[:, c:c + 1])
            act(rstd[:, c:c + 1], msq[:, c:c + 1], AF.Ln, [("msq", c)], [("rstd", c)], bias=EPS)
            act(rstd[:, c:c + 1], rstd[:, c:c + 1], AF.Exp, [("rstd", c)], [("rstd", c)], scale=-0.5)
            ts("dve", xn[xsl][:], xin[sl][:], rstd[:, c:c + 1], None, ALU.mult, None, [("xin", sl), ("rstd", c)], [("xn", xsl)])
            pt = psT[i % 2]
            for kc in range(8):
                tr(pt[:, kc, :], xn[xsl][:, kc * 128:(kc + 1) * 128], identb[:], [("xn", xsl)], [("psT", i % 2)])
            for (hT, hname, Am, Sm) in outs:
                for kc in range(8):
                    o = hT[:, kc, u * 128:(u + 1) * 128]
                    if i % 2 == 0:
                        ts("dve", o, pt[:, kc, :], Am[:, kc:kc + 1], Sm[:, kc:kc + 1], ALU.mult, ALU.add,
                           [("psT", i % 2)], [(hname, hslot, u, kc)])
                    else:
                        act(o, pt[:, kc, :], AF.Identity, [("psT", i % 2)], [(hname, hslot, u, kc)],
                            scale=Am[:, kc:kc + 1], bias=Sm[:, kc:kc + 1])

        def project(hT, hname, hslot, W, cols, nk=8):
            slot = nxt("p")
            for kc in range(nk):
                mm(psP[:, slot, :], W[:, kc, cols], hT[:, kc, :], kc == 0, kc == nk - 1,
                   [(hname, hslot, u, kc) for u in range(4)], [("psP", slot)])
            return slot

        def rope(slot, ctab, stab, tkey, qraw, t12, dst, dkey, f32dst=None, fkey=None):
            r = nxt("r")
            cp("act", qraw[r][:], psP[:, slot, :], [("psP", slot)], [("qraw", r)])
            mm(psR[:], rmb[:], qraw[r][:], True, True, [("qraw", r)], ["psR"])
            tt("dve", t12[:, r, :], psP[:, slot, :], ctab, ALU.mult, [("psP", slot), tkey], [("t1", r)])
            tt("dve", t12[:, 2 + r, :], psR[:], stab, ALU.mult, ["psR", tkey], [("t2", r)])
            tt("pool", dst, t12[:, r, :], t12[:, 2 + r, :], ALU.add, [("t1", r), ("t2", r)], [dkey])
            if f32dst is not None:
                tt("pool", f32dst, t12[:, r, :], t12[:, 2 + r, :], ALU.add, [("t1", r), ("t2", r)], [fkey])

        def cache_out(kvf, kkeys, u, ncol, stage, dst_rows, skey):
            n = len(kvf)
            for k4 in range(n):
                tr(psR[:, k4 * 128:(k4 + 1) * 128], kvf[k4][:, u * 128:(u + 1) * 128], identf[:], [kkeys[k4]], ["psR"])
            sl = nxt("o")
            cp(evac_eng(), stage[sl][:, 0:ncol], psR[:, 0:ncol], ["psR"], [(skey, sl)])
            dma("sp", dst_rows, stage[sl][:, 0:ncol], [(skey, sl)], ())

        def attn_pair(l, heads, PT, rq, rk, rv):
            sl = nxt("s")
            ps = psS[sl]
            pt = PT[sl]
            has_prev = heads[0]["kp"] is not None
            for hh, h in enumerate(heads):
                if has_prev:
                    mm(ps[:, hh * 256:hh * 256 + 128], h["kp"], h["q"], True, True, rq + rk, [("psS", sl)], rg=hh)
                mm(ps[:, hh * 256 + 128:hh * 256 + 256], h["kc"], h["q"], True, True, rq + rk, [("psS", sl)], rg=hh)
            if has_prev:
                act(pt[:], ps[:], AF.Exp, [("psS", sl)], [("PT", sl)], scale=0.125)
                tt("pool", pt[:], pt[:], maskb[l][:], ALU.mult, [("PT", sl)], [("PT", sl)])
            else:
                v = lambda a: a.rearrange("p (h x) -> p h x", h=2)[:, :, 128:256]
                act(v(pt[:]), v(ps[:]), AF.Exp, [("psS", sl)], [("PT", sl)], scale=0.125)
                tt("pool", v(pt[:]), v(pt[:]), v(maskb[l][:]), ALU.mult, [("PT", sl)], [("PT", sl)])
            for hh, h in enumerate(heads):
                o = psO[:, hh * 128:(hh + 1) * 128]
                if has_prev:
                    mm(o, h["vp"], pt[:, hh * 256:hh * 256 + 128], True, False, rv + [("PT", sl)], ["psO"])
                mm(o, h["vc"], pt[:, hh * 256 + 128:hh * 256 + 256], not has_prev, True, rv + [("PT", sl)], ["psO"])
            eng = evac_eng()
            for hh, h in enumerate(heads):
                for (dst, key, lo) in ((h["od"], h["ok"], 0), (h["dd"], h["dk"], 64)):
                    src = psO[lo:lo + 64, hh * 128:(hh + 1) * 128]
                    if len(dst.shape) == 3:
                        src = src.rearrange("p (a b) -> p a b", a=dst.shape[1])
                    cp(eng, dst, src, ["psO"], [key])
            if has_prev:
                stop("p6")

        def out_proj_tile(i, u, oTf, okeys, Wo, nk, ggl, xres, col0, dst_dram, dkey, t12):
            for nb in range(2):
                for kc in range(nk):
                    mm(psP[:, nb, :], oTf[kc][:, u * 128:(u + 1) * 128], Wo[:, kc, nb * 512:(nb + 1) * 512], kc == 0, kc == nk - 1,
                       [okeys[kc]], [("psP", nb)])
            c = col0 + i
            pm = psP[:].rearrange("p a b -> p (a b)")
            act(junk[:], pm, AF.Square, [("psP", 0), ("psP", 1)], [("msq", c)], scale=1.0 / 32, accum_out=msq[:, c:c + 1])
            act(rstd[:, c:c + 1], msq[:, c:c + 1], AF.Ln, [("msq", c)], [("rstd", c)], bias=EPS)
            act(rstd[:, c:c + 1], rstd[:, c:c + 1], AF.Exp, [("rstd", c)], [("rstd", c)], scale=-0.5)
            tm = t12[:, 0:2, :].rearrange("p a b -> p (a b)")
            stt("dve", tm, pm, rstd[:, c:c + 1], ggl[:], ALU.mult, ALU.mult, [("psP", 0), ("psP", 1), ("rstd", c)],
                [("t1", 0), ("t1", 1)])
            sl = i % len(xres)
            tt("pool", xres[sl][:], tm, xres[sl][:], ALU.add, [("t1", 0), ("t1", 1), ("xres", sl)], [("xres", sl)])
            dma("sp", dst_dram[i * 128:(i + 1) * 128, :], xres[sl][:], [("xres", sl)], [(dkey, i)] if dkey else ())

        with ExitStack() as st:
            Wa2 = sbt(st, "Wa2", [128, 8, 768], BF16)
            wst = [sbt(st, "wstA%d" % i, [128, 1024], F32) for i in range(2)]
            xin = [sbt(st, "xinA%d" % i, [128, D], F32) for i in range(4)]
            xn = [sbt(st, "xnA%d" % i, [128, D], BF16) for i in range(2)]
            hT = [sbt(st, "hTA%d" % i, [128, 8, 512], BF16) for i in range(2)]
            QT2 = [sbt(st, "QT2_%d" % c, [128, T], BF16) for c in range(2)]
            KT2 = [sbt(st, "KT2_%d" % c, [128, T], BF16) for c in range(2)]
            VT2 = [sbt(st, "VT2_%d" % c, [128, T], BF16) for c in range(2)]
            ctab = [sbt(st, "ctabA%d" % i, [128, 512], F32) for i in range(2)]
            stab = [sbt(st, "stabA%d" % i, [128, 512], F32) for i in range(2)]
            qraw = [sbt(st, "qrawA%d" % i, [128, 512], BF16) for i in range(2)]
            t12 = sbt(st, "t12A", [128, 4, 512], F32)
            kvf = [sbt(st, "kvfA%d" % i, [128, 512], F32) for i in range(4)]
            stage = [sbt(st, "stageA%d" % i, [128, 512], F32) for i in range(2)]
            PT = [sbt(st, "PTA%d" % i, [128, 512], BF16) for i in range(2)]
            VA = [sbt(st, "VAA%d" % i, [128, 4, 128], BF16) for i in range(4)]
            G2st = [sbt(st, "G2st%d" % k, [128, T], BF16) for k in range(4)]
            load_w(wst, Wa2, w_in_a, 1024, 512, 256, 0, "w")
            load_w(wst, Wa2, w_in_a, 1024, 768 + 512, 256, 256, "w")
            load_w(wst, Wa2, w_in_a, 1024, 1536 + 512, 256, 512, "w")
            for v in VA:
                memset("pool", v[:, :, 64:128], 1.0, [("w", "va")])
            P.flush()
            print("[phaseA] sbuf remaining", nc.sbuf_bytes_remaining)
            stop("A0")
            for u in range(4):
                load_x(xp, None, u, xin)
            for s in range(NS):
                hs = s % 2
                tk = slice(512 * s, 512 * (s + 1))
                dma("sp", ctab[hs][:], cosT_d[:, tk], (), [("tab", hs)])
                dma("sp", stab[hs][:], sinT_d[:, tk], (), [("tab", hs)])
                for u in range(4):
                    norm_tile(4 * s + u, u, xin, xn, 0, [(hT[hs], "hT", A0, SH0)], hs)
                stop("A1")
                if s + 1 < NS:
                    for u in range(4):
                        load_x(xp, None, 4 * (s + 1) + u, xin)
                for (kind, cc, wc) in (("k", 0, 256), ("k", 1, 384), ("v", 0, 512), ("v", 1, 640), ("q", 0, 0), ("q", 1, 128)):
                    slot = project(hT[hs], "hT", hs, Wa2, slice(wc, wc + 128))
                    if kind == "k":
                        rope(slot, ctab[hs][:], stab[hs][:], ("tab", hs), qraw, t12, KT2[cc][:, tk], ("KT2", cc, s),
                             kvf[cc][:] if s >= 4 else None, ("kvf", cc))
                    elif kind == "q":
                        rope(slot, ctab[hs][:], stab[hs][:], ("tab", hs), qraw, t12, QT2[cc][:, tk], ("QT2", cc, s))
                    else:
                        cp("act", VT2[cc][:, tk], psP[:, slot, :], [("psP", slot)], [("VT2", cc, s)])
                        if s >= 4:
                            cp("dve", kvf[2 + cc][:], psP[:, slot, :], [("psP", slot)], [("kvf", 2 + cc)])
                stop("A2")
                if s >= 4:
                    for u in range(4):
                        i = 4 * s + u
                        cache_out(kvf, [("kvf", k) for k in range(4)], u, 512, stage, aop[2][(i - 16) * 128:(i - 15) * 128, :], "stage")
            stop("A3")
            allq = [[("QT2", c, s) for s in range(NS)] for c in range(2)]
            allk = [[("KT2", c, s) for s in range(NS)] for c in range(2)]
            allv = [[("VT2", c, s) for s in range(NS)] for c in range(2)]
            for r in range(16):
                for j in range(2):
                    tq = slice(2048 * j + r, 2048 * (j + 1), 16)
                    tp = slice(2048 * (j - 1) + r, 2048 * j, 16)
                    b = 2 * r + j
                    vsl = b % 4
                    ptv = psT[b % 2]
                    for c in range(2):
                        tr(ptv[:, c, :], VT2[c][:, tq], identb[:], allv[c], [("psT", b % 2)])
                    cp(evac_eng(), VA[vsl][:, :, 0:64], ptv[:, 0:2, :].rearrange("p c (h d) -> p (c h) d", h=2),
                       [("psT", b % 2)], [("VA", vsl)])
                    for c in range(2):
                        heads = []
                        for hh in range(2):
                            h = 2 * c + hh
                            rows = slice(hh * 64, (hh + 1) * 64)
                            od = G2st[c][rows, :].rearrange("p (s r m) -> p s r m", s=8, r=16)[:, 4 * j:4 * j + 4, r, :]
                            dd = G2st[2 + c][rows, :].rearrange("p (s r m) -> p s r m", s=8, r=16)[:, 4 * j:4 * j + 4, r, :]
                            heads.append(dict(q=QT2[c][rows, tq], kc=KT2[c][rows, tq], kp=(KT2[c][rows, tp] if j == 1 else None),
                                              vc=VA[vsl][:, h, :], vp=(VA[(b - 1) % 4][:, h, :] if j == 1 else None),
                                              od=od, dd=dd, ok=("G2", c, b, hh), dk=("G2", 2 + c, b, hh)))
                        attn_pair(0, heads, PT, allq[c], allk[c], [("VA", vsl), ("VA", (b - 1) % 4)])
            for k in range(4):
                dma("sp", g2s[k], G2st[k][:], [("G2", k, b, hh) for b in range(32) for hh in range(2)], [("g2s", k)])
            P.flush()
        if os.environ.get("KSTOP") == "A":
            return nc

        with ExitStack() as st:
            Wab = sbt(st, "Wab", [128, 8, 2304], BF16)
            Woa = sbt(st, "Woa", [128, 6, 1024], BF16)
            wst = [sbt(st, "wstB%d" % i, [128, 1024], F32) for i in range(2)]
            xin = [sbt(st, "xinB%d" % i, [128, D], F32) for i in range(2)]
            xres = [sbt(st, "xresB%d" % i, [128, D], F32) for i in range(2)]
            xn = [sbt(st, "xnB%d" % i, [128, D], BF16) for i in range(2)]
            hT = [sbt(st, "hTB%d" % i, [128, 8, 512], BF16) for i in range(1)]
            QT = [sbt(st, "QTB%d" % c, [128, 512], BF16) for c in range(4)]
            KT = [sbt(st, "KTB%d" % c, [128, 1024], BF16) for c in range(4)]
            VT = [sbt(st, "VTB%d" % c, [128, 1024], BF16) for c in range(4)]
            gT = [sbt(st, "gTB%d" % c, [128, 512], BF16) for c in range(6)]
            OT = [sbt(st, "OTB%d" % c, [128, 512], BF16) for c in range(4)]
            DT = [sbt(st, "DTB%d" % c, [128, 512], BF16) for c in range(4)]
            oTf = [sbt(st, "oTfB%d" % c, [128, 512], BF16) for c in range(6)]
            G2in = [sbt(st, "G2in%d" % k, [128, 512], BF16) for k in range(4)]
            dsum = sbt(st, "dsumB", [128, 512], F32)
            rinv = sbt(st, "rinvB", [128, 512], F32)
            otmp = sbt(st, "otmpB", [128, 512], F32)
            ctab = [sbt(st, "ctabB%d" % i, [128, 512], F32) for i in range(2)]
            stab = [sbt(st, "stabB%d" % i, [128, 512], F32) for i in range(2)]
            qraw = [sbt(st, "qrawB%d" % i, [128, 512], BF16) for i in range(2)]
            t12 = sbt(st, "t12B", [128, 4, 512], F32)
            kvf = [sbt(st, "kvfB%d" % i, [128, 512], F32) for i in range(4)]
            stage = [sbt(st, "stageB%d" % i, [128, 512], F32) for i in range(2)]
            PT = [sbt(st, "PTB%d" % i, [128, 512], BF16) for i in range(2)]
            VA0 = [sbt(st, "VA0_%d" % i, [128, 4, 128], BF16) for i in range(4)]
            VA1 = [sbt(st, "VA1_%d" % i, [128, 4, 128], BF16) for i in range(8)]
            load_w(wst, Wab, w_in_a, 1024, 0, 512, 0, "w")
            load_w(wst, Wab, w_in_a, 1024, 768, 512, 512, "w")
            load_w(wst, Wab, w_in_a, 1024, 1536, 512, 1024, "w")
            load_w(wst, Wab, w_in_a, 1024, 2304, 768, 1536, "w")
            load_w(wst, Woa, w_o_a, 768, 0, 1024, 0, "w")
            for v in VA0 + VA1:
                memset("pool", v[:, :, 64:128], 1.0, [("w", "va")])
            P.flush()
            load_x(xp, None, 0, xin)
            load_x(xp, None, 1, xin)
            for s in range(NS):
                hs = s % 2
                ts_ = hs * 512
                tk = slice(512 * s, 512 * (s + 1))
                dma("sp", ctab[hs][:], cosT_d[:, tk], (), [("tab", hs)])
                dma("sp", stab[hs][:], sinT_d[:, tk], (), [("tab", hs)])
                for k in range(4):
                    dma("sp", G2in[k][:], g2s[k][:, tk], [("g2s", k)], [("G2in", k)])
                for u in range(4):
                    i = 4 * s + u
                    norm_tile(i, u, xin, xn, 32, [(hT[0], "hT", A0, SH0)], 0)
                    if i + 2 < NT:
                        load_x(xp, None, i + 2, xin)
                ring = slice(ts_, ts_ + 512)
                for g in range(2):
                    for (kind, cc, wc) in (("k", 2 * g, 512 + 256 * g), ("k", 2 * g + 1, 640 + 256 * g),
                                           ("v", 2 * g, 1024 + 256 * g), ("v", 2 * g + 1, 1152 + 256 * g)):
                        slot = project(hT[0], "hT", 0, Wab, slice(wc, wc + 128))
                        need_f = (s == 7)
                        if kind == "k":
                            rope(slot, ctab[hs][:], stab[hs][:], ("tab", hs), qraw, t12, KT[cc][:, ring], ("KT", cc, hs),
                                 kvf[cc % 2][:] if need_f else None, ("kvf", cc % 2))
                        else:
                            cp("act", VT[cc][:, ring], psP[:, slot, :], [("psP", slot)], [("VT", cc, hs)])
                            if need_f:
                                cp("dve", kvf[2 + cc % 2][:], psP[:, slot, :], [("psP", slot)], [("kvf", 2 + cc % 2)])
                    if s == 7:
                        if g == 0:
                            cache_out(kvf, [("kvf", k) for k in range(4)], 3, 512, stage, aop[0][:, :], "stage")
                        else:
                            for u in range(4):
                                cache_out(kvf, [("kvf", k) for k in range(4)], u, 512, stage, aop[1][u * 128:(u + 1) * 128, :], "stage")
                for cc in range(4):
                    slot = project(hT[0], "hT", 0, Wab, slice(cc * 128, (cc + 1) * 128))
                    rope(slot, ctab[hs][:], stab[hs][:], ("tab", hs), qraw, t12, QT[cc][:], ("QT", cc))
                for cc in range(6):
                    slot = project(hT[0], "hT", 0, Wab, slice(1536 + cc * 128, 1536 + (cc + 1) * 128))
                    act(gT[cc][:], psP[:, slot, :], AF.Silu, [("psP", slot)], [("gT", cc)])
                for u in range(4):
                    i = 4 * s + u
                    vsl = i % 4
                    ptv = psT[i % 2]
                    cols = slice(ts_ + u * 128, ts_ + (u + 1) * 128)
                    if u > 0:
                        pcols = slice(ts_ + (u - 1) * 128, ts_ + u * 128)
                    else:
                        pcols = slice((1 - hs) * 512 + 384, (1 - hs) * 512 + 512)
                    for c in range(2):
                        tr(ptv[:, c, :], VT[c][:, cols], identb[:], [("VT", c, hs)], [("psT", i % 2)])
                    cp(evac_eng(), VA0[vsl][:, :, 0:64], ptv[:, 0:2, :].rearrange("p c (h d) -> p (c h) d", h=2),
                       [("psT", i % 2)], [("VA0", vsl)])
                    for c in range(2):
                        heads = []
                        for hh in range(2):
                            h = 2 * c + hh
                            rows = slice(hh * 64, (hh + 1) * 64)
                            heads.append(dict(q=QT[c][rows, u * 128:(u + 1) * 128], kc=KT[c][rows, cols],
                                              kp=(KT[c][rows, pcols] if i > 0 else None),
                                              vc=VA0[vsl][:, h, :], vp=(VA0[(i - 1) % 4][:, h, :] if i > 0 else None),
                                              od=OT[c][rows, u * 128:(u + 1) * 128], dd=DT[c][rows, u * 128:(u + 1) * 128],
                                              ok=("OT", c, u, hh), dk=("DT", c, u, hh)))
                        attn_pair(0, heads, PT, [("QT", c)], [("KT", c, 0), ("KT", c, 1)], [("VA0", vsl), ("VA0", (i - 1) % 4)])
                for r4 in range(4):
                    vsl = hs * 4 + r4
                    psl = (1 - hs) * 4 + r4
                    b = 4 * s + r4
                    ptv = psT[b % 2]
                    cols = slice(ts_ + r4, ts_ + 512, 4)
                    pcols = slice((1 - hs) * 512 + r4, (1 - hs) * 512 + 512, 4)
                    for c in range(2):
                        tr(ptv[:, c, :], VT[2 + c][:, cols], identb[:], [("VT", 2 + c, hs)], [("psT", b % 2)])
                    cp(evac_eng(), VA1[vsl][:, :, 0:64], ptv[:, 0:2, :].rearrange("p c (h d) -> p (c h) d", h=2),
                       [("psT", b % 2)], [("VA1", vsl)])
                    for c in range(2):
                        heads = []
                        for hh in range(2):
                            h = 2 * c + hh
                            rows = slice(hh * 64, (hh + 1) * 64)
                            heads.append(dict(q=QT[2 + c][rows, r4:512:4], kc=KT[2 + c][rows, cols],
                                              kp=(KT[2 + c][rows, pcols] if s > 0 else None),
                                              vc=VA1[vsl][:, h, :], vp=(VA1[psl][:, h, :] if s > 0 else None),
                                              od=OT[2 + c][rows, r4:512:4], dd=DT[2 + c][rows, r4:512:4],
                                              ok=("OT", 2 + c, r4, hh), dk=("DT", 2 + c, r4, hh)))
                        attn_pair(0, heads, PT, [("QT", 2 + c)], [("KT", 2 + c, 0), ("KT", 2 + c, 1)], [("VA1", vsl), ("VA1", psl)])
                nat = lambda a: a.rearrange("p (r m) -> p m r", r=16)
                v3 = lambda a: a.rearrange("p (m r) -> p m r", r=16)
                for cc in range(2):
                    k01 = [("DT", c, x, hh) for c in (cc, 2 + cc) for x in range(4) for hh in range(2)]
                    tt("pool", dsum[:], DT[cc][:], DT[2 + cc][:], ALU.add, k01, ["dsum"])
                    tt("pool", v3(dsum[:]), v3(dsum[:]), nat(G2in[2 + cc][:]), ALU.add, ["dsum", ("G2in", 2 + cc)], ["dsum"])
                    op("dve", lambda e, o=rinv[:], i=dsum[:]: e.reciprocal(out=o, in_=i), ["dsum"], ["rinv"])
                    for g in range(3):
                        c6 = 2 * g + cc
                        if g < 2:
                            src = OT[c6][:]
                            sk = [("OT", c6, x, hh) for x in range(4) for hh in range(2)]
                            tt("dve", otmp[:], src, rinv[:], ALU.mult, sk + ["rinv"], ["otmp"])
                        else:
                            tt("dve", v3(otmp[:]), nat(G2in[cc][:]), v3(rinv[:]), ALU.mult, [("G2in", cc), "rinv"], ["otmp"])
                        tt("pool", oTf[c6][:], otmp[:], gT[c6][:], ALU.mult, ["otmp", ("gT", c6)], [("oTf", c6)])
                for u in range(4):
                    i = 4 * s + u
                    dma("sp", xres[i % 2][:], xp[i * 128:(i + 1) * 128, :], (), [("xres", i % 2)])
                    out_proj_tile(i, u, oTf, [("oTf", c) for c in range(6)], Woa, 6, ggbc[0], xres, 64, x1s, "x1s", t12)
            P.flush()
        if os.environ.get("KSTOP") == "B":
            return nc

        with ExitStack() as st:
            Wq = sbt(st, "Wq", [128, 8, 2048], BF16)
            Wob = sbt(st, "Wob", [128, 8, 1024], BF16)
            Wkv = sbt(st, "Wkv", [128, 8, 384], BF16)
            wst = [sbt(st, "wstC%d" % i, [128, 1024], F32) for i in range(2)]
            xin = [sbt(st, "xinC%d" % i, [128, D], F32) for i in range(2)]
            xres = [sbt(st, "xresC%d" % i, [128, D], F32) for i in range(2)]
            xn = [sbt(st, "xnC%d" % i, [128, D], BF16) for i in range(2)]
            hT = sbt(st, "hTC", [128, 8, 512], BF16)
            hkT = sbt(st, "hkTC", [128, 8, 512], BF16)
            QT = [sbt(st, "QTC%d" % c, [128, 512], BF16) for c in range(8)]
            KT = [sbt(st, "KTC%d" % c, [128, 1024], BF16) for c in range(2)]
            VT = sbt(st, "VTC", [128, 1024], BF16)
            gT = [sbt(st, "gTC%d" % c, [128, 512], BF16) for c in range(8)]
            OT = [sbt(st, "OTC%d" % c, [128, 512], BF16) for c in range(8)]
            DT = [sbt(st, "DTC%d" % c, [128, 512], BF16) for c in range(8)]
            dsum = sbt(st, "dsumC", [128, 512], F32)
            rinv = sbt(st, "rinvC", [128, 512], F32)
            otmp = sbt(st, "otmpC", [128, 512], F32)
            ctab = [sbt(st, "ctabC%d" % i, [128, 512], F32) for i in range(2)]
            stab = [sbt(st, "stabC%d" % i, [128, 512], F32) for i in range(2)]
            qraw = [sbt(st, "qrawC%d" % i, [128, 512], BF16) for i in range(2)]
            t12 = sbt(st, "t12C", [128, 4, 512], F32)
            kvf = [sbt(st, "kvfC%d" % i, [128, 512], F32) for i in range(2)]
            stage = [sbt(st, "stageC%d" % i, [128, 256], F32) for i in range(2)]
            PT = [sbt(st, "PTC%d" % i, [128, 512], BF16) for i in range(2)]
            VAc = [sbt(st, "VAc%d" % i, [128, 2, 128], BF16) for i in range(4)]
            load_w(wst, Wq, w_in_b, 1024, 0, 2048, 0, "w")
            load_w(wst, Wob, w_o_b, 1024, 0, 1024, 0, "w")
            load_w(wst, Wkv, w_kvx, 1024, 0, 384, 0, "w")
            for v in VAc:
                memset("pool", v[:, :, 64:128], 1.0, [("w", "va")])
            P.flush()
            dma("sp", xin[0][:], x1s[0:128, :], (), [("xin", 0)])
            dma("sp", xin[1][:], x1s[128:256, :], (), [("xin", 1)])
            for s in range(NS):
                hs = s % 2
                ts_ = hs * 512
                tk = slice(512 * s, 512 * (s + 1))
                ring = slice(ts_, ts_ + 512)
                dma("sp", ctab[hs][:], cosT_d[:, tk], (), [("tab", hs)])
                dma("sp", stab[hs][:], sinT_d[:, tk], (), [("tab", hs)])
                for u in range(4):
                    i = 4 * s + u
                    norm_tile(i, u, xin, xn, 0, [(hT, "hT", A1, SH1), (hkT, "hkT", AKV, SHKV)], 0)
                    if i + 2 < NT:
                        dma("sp", xin[i % 2][:], x1s[(i + 2) * 128:(i + 3) * 128, :], (), [("xin", i % 2)])
                need_f = (s == 7)
                slot = project(hkT, "hkT", 0, Wkv, slice(0, 128))
                rope(slot, ctab[hs][:], stab[hs][:], ("tab", hs), qraw, t12, KT[0][:, ring], ("KT", 0, hs),
                     kvf[0][:] if need_f else None, ("kvf", 0))
                slot = project(hkT, "hkT", 0, Wkv, slice(256, 384))
                rope(slot, ctab[hs][:], stab[hs][:], ("tab", hs), qraw, t12, KT[1][:, ring], ("KT", 1, hs))
                slot = project(hkT, "hkT", 0, Wkv, slice(128, 256))
                cp("act", VT[:, ring], psP[:, slot, :], [("psP", slot)], [("VT", hs)])
                if need_f:
                    cp("dve", kvf[1][:], psP[:, slot, :], [("psP", slot)], [("kvf", 1)])
                    cache_out(kvf, [("kvf", 0), ("kvf", 1)], 3, 256, stage, bp[:, :], "stage")
                for cc in range(8):
                    slot = project(hT, "hT", 0, Wq, slice(cc * 128, (cc + 1) * 128))
                    rope(slot, ctab[hs][:], stab[hs][:], ("tab", hs), qraw, t12, QT[cc][:], ("QT", cc))
                for cc in range(8):
                    slot = project(hT, "hT", 0, Wq, slice(1024 + cc * 128, 1024 + (cc + 1) * 128))
                    act(gT[cc][:], psP[:, slot, :], AF.Silu, [("psP", slot)], [("gT", cc)])
                for u in range(4):
                    i = 4 * s + u
                    vsl = i % 4
                    ptv = psT[i % 2]
                    cols = slice(ts_ + u * 128, ts_ + (u + 1) * 128)
                    if u > 0:
                        pcols = slice(ts_ + (u - 1) * 128, ts_ + u * 128)
                    else:
                        pcols = slice((1 - hs) * 512 + 384, (1 - hs) * 512 + 512)
                    tr(ptv[:, 0, :], VT[:, cols], identb[:], [("VT", hs)], [("psT", i % 2)])
                    cp(evac_eng(), VAc[vsl][:, :, 0:64], ptv[:, 0, :].rearrange("p (h d) -> p h d", h=2),
                       [("psT", i % 2)], [("VAc", vsl)])
                    for c in range(8):
                        kvh = c // 4
                        heads = []
                        for hh in range(2):
                            rows = slice(hh * 64, (hh + 1) * 64)
                            ksrc = KT[0] if (kvh == hh) else KT[1]
                            heads.append(dict(q=QT[c][rows, u * 128:(u + 1) * 128], kc=ksrc[rows, cols],
                                              kp=(ksrc[rows, pcols] if i > 0 else None),
                                              vc=VAc[vsl][:, kvh, :], vp=(VAc[(i - 1) % 4][:, kvh, :] if i > 0 else None),
                                              od=OT[c][rows, u * 128:(u + 1) * 128], dd=DT[c][rows, u * 128:(u + 1) * 128],
                                              ok=("OT", c, u, hh), dk=("DT", c, u, hh)))
                        attn_pair(1, heads, PT, [("QT", c)], [("KT", 0, 0), ("KT", 0, 1), ("KT", 1, 0), ("KT", 1, 1)],
                                  [("VAc", vsl), ("VAc", (i - 1) % 4)])
                for c in range(8):
                    dk = [("DT", c, x, hh) for x in range(4) for hh in range(2)]
                    okk = [("OT", c, x, hh) for x in range(4) for hh in range(2)]
                    ts("pool", dsum[:], DT[c][:], esink[:, c:c + 1], None, ALU.add, None, dk, ["dsum"])
                    op("dve", lambda e, o=rinv[:], i_=dsum[:]: e.reciprocal(out=o, in_=i_), ["dsum"], ["rinv"])
                    tt("dve", otmp[:], OT[c][:], rinv[:], ALU.mult, okk + ["rinv"], ["otmp"])
                    tt("pool", OT[c][:], otmp[:], gT[c][:], ALU.mult, ["otmp", ("gT", c)] + okk, [("oTf", c)] + okk)
                for u in range(4):
                    i = 4 * s + u
                    dma("sp", xres[i % 2][:], x1s[i * 128:(i + 1) * 128, :], (), [("xres", i % 2)])
                    out_proj_tile(i, u, OT, [("oTf", c) for c in range(8)], Wob, 8, ggbc[1], xres, 32, yp, None, t12)
            P.flush()
        print("[kernel] ops=%d waits=%d sig=%s" % (P.nops, P.nwait, P.cnt))
    return nc


_NC_CACHE = {}


def kernel(x_prompt, x_sample, c_prompt, c_sample, cache_a_kv_g0, cache_a_kv_g1, cache_a_kv_g2, cache_b_kv,
           ada_w, ada_b, g_pre, g_post, w_in_a, w_o_a, w_in_b, w_o_b, sinks_b, ada_kv_w, ada_kv_b, g_kv, w_kv,
           _cores=None):
    f = lambda a: np.ascontiguousarray(np.asarray(a, dtype=np.float32))
    x_prompt, x_sample, c_prompt, c_sample = f(x_prompt), f(x_sample), f(c_prompt), f(c_sample)
    ca = [f(cache_a_kv_g0), f(cache_a_kv_g1), f(cache_a_kv_g2)]
    cbk = f(cache_b_kv)
    ada_w, ada_b, g_pre, g_post = f(ada_w), f(ada_b), f(g_pre), f(g_post)
    w_in_a, w_o_a, w_in_b, w_o_b = f(w_in_a), f(w_o_a), f(w_in_b), f(w_o_b)
    sinks_b, ada_kv_w, ada_kv_b, g_kv, w_kv = f(sinks_b), f(ada_kv_w), f(ada_kv_b), f(g_kv), f(w_kv)
    cst = _consts()
    cores = list(range(NCORES)) if _cores is None else list(_cores)
    shared = dict(
        ada_w0=f(ada_w[0]), ada_w1=f(ada_w[1]), ada_kv_w=ada_kv_w,
        brow=f(np.concatenate([ada_b[0], ada_b[1], ada_kv_b])[None, :]),
        vecs=f(np.concatenate([g_pre[0].reshape(8, 128), g_pre[1].reshape(8, 128), g_post[0].reshape(8, 128),
                               g_post[1].reshape(8, 128), g_kv.reshape(8, 128)], axis=0)),
        gpost_rows=g_post, w_in_a=f(w_in_a[0]), w_o_a=f(w_o_a[0]), w_in_b=f(w_in_b[0]), w_o_b=f(w_o_b[0]),
        w_kvx=f(np.concatenate([w_kv[:, 0:256], w_kv[:, 64:128], w_kv[:, 0:64]], axis=1)),
        sink_fm=f(sinks_b[0].reshape(8, 2).T.repeat(64, axis=0)),
        sink_tm=f(np.tile(sinks_b[0][None, :], (DB, 1))),
        ident=cst["ident"], rmat=cst["rmat"], cosT=cst["cosT"], sinT=cst["sinT"], masks=cst["masks"],
        cs_tok=cst["cs_tok"], sn_tok=cst["sn_tok"], rowb_in=cst["rowb"], w_kv=w_kv,
        gvec_tm=f(np.stack([np.tile(v[None, :], (DB, 1)) for v in (g_pre[0], g_pre[1], g_post[0], g_post[1], g_kv)])),
    )
    in_maps = []
    for b in cores:
        crows = np.zeros((33, D), np.float32)
        crows[0:DB] = c_sample[DB * b:DB * (b + 1)]
        crows[32] = c_prompt[b]
        m = dict(shared)
        m.update(xp=x_prompt[b], crows=crows, xs=f(x_sample[DB * b:DB * (b + 1), 0, :]),
                 ca0=ca[0][0, DB * b:DB * (b + 1)].reshape(DB, 128, 512),
                 ca1=ca[1][0, DB * b:DB * (b + 1)].reshape(DB, 512, 512),
                 ca2=ca[2][0, DB * b:DB * (b + 1)].reshape(DB, 2048, 512),
                 cb=cbk[DB * b:DB * (b + 1)].reshape(DB, 128, 256))
        in_maps.append(m)
    if "nc" not in _NC_CACHE:
        _NC_CACHE["nc"] = build_nc()
    nc = _NC_CACHE["nc"]
    res = run_bass_kernel_spmd(nc, in_maps, core_ids=list(range(len(cores))))
    R = res.results
    n = len(cores)
    y_prompt = np.stack([R[k]["yp"] for k in range(n)])
    y_sample = np.concatenate([R[k]["ys"] for k in range(n)])[:, None, :]
    a_p = [np.stack([R[k]["a%dp" % g] for k in range(n)]).reshape(1, n, -1, 2, 4, 64) for g in range(3)]
    b_p = np.stack([R[k]["bp"] for k in range(n)]).reshape(n, 128, 2, 2, 64)
    a_s = [np.concatenate([R[k]["a%ds" % g] for k in range(n)]).reshape(1, n * DB, -1, 2, 4, 64) for g in range(3)]
    b_s = np.concatenate([R[k]["bs"] for k in range(n)]).reshape(n * DB, 128, 2, 2, 64)
    return (y_prompt, y_sample, a_p[0], a_p[1], a_p[2], b_p, a_s[0], a_s[1], a_s[2], b_s)
```

```python
import os
import numpy as np
from contextlib import ExitStack
import concourse.bass as bass
import concourse.mybir as mybir
from concourse.bass_utils import run_bass_kernel_spmd

F32 = mybir.dt.float32
BF16 = mybir.dt.bfloat16
ALU = mybir.AluOpType
AF = mybir.ActivationFunctionType

NCORES = 8
T = 4096
D = 1024
NT = 32
NS = 8
EPS = 1e-6
PAST = 16384
DB = 16


class Prog:
    CE = ("pe", "act", "dve", "pool")
    ENG = ("pe", "act", "dve", "pool", "sp")

    def __init__(self, nc, stack, ndsem=20):
        self.nc = nc
        self.eng = dict(pe=nc.tensor, act=nc.scalar, dve=nc.vector, pool=nc.gpsimd, sp=nc.sync)
        self.ops = []
        self.last_w = {}
        self.readers = {}
        self.esem = {e: stack.enter_context(nc.semaphore("S_" + e)) for e in self.CE}
        self.cnt = {e: 0 for e in self.CE}
        self.dsem = {e: [stack.enter_context(nc.semaphore("D_%s%d" % (e, i))) for i in range(ndsem)]
                     for e in ("sp", "pool", "act")}
        self.dcum = {e: [0] * ndsem for e in self.dsem}
        self.dnext = {e: 0 for e in self.dsem}
        self.waited = {e: {} for e in self.ENG}
        self.nwait = 0
        self.nops = 0

    @staticmethod
    def _isps(k):
        n = k[0] if isinstance(k, tuple) else k
        return isinstance(n, str) and n.startswith("ps")

    def op(self, eng, fn, reads=(), writes=(), dma=False, rg=None):
        writes = list(writes) + [r for r in reads if self._isps(r)]
        reads = [r for r in reads if not self._isps(r)]
        i = len(self.ops)
        raw = set()
        oth = set()
        for r in reads:
            if r in self.last_w:
                raw.add(self.last_w[r])
        for w in writes:
            if w in self.last_w:
                oth.add(self.last_w[w])
            for j in self.readers.get(w, {}).values():
                oth.add(j)
        raw.discard(i)
        oth.discard(i)
        self.ops.append(dict(eng=eng, fn=fn, raw=raw, oth=oth - raw, dma=dma, rg=rg))
        for r in reads:
            self.readers.setdefault(r, {})[(eng if not dma else ("dma", i))] = i
        for w in writes:
            self.last_w[w] = i
            self.readers[w] = {}
        return i

    def _needs_wait(self, o, p, is_raw):
        if p["dma"] or o["dma"]:
            return True
        if p["eng"] != o["eng"]:
            return True
        if o["eng"] == "pe":
            return o["rg"] is not None and p["rg"] is not None and o["rg"] != p["rg"]
        return is_raw

    def _wait(self, e, sem, val):
        k = id(sem)
        if self.waited[e].get(k, 0) >= val:
            return
        self.eng[e].wait_ge(sem, val)
        self.waited[e][k] = val
        self.nwait += 1

    def flush(self, final=False):
        ops = self.ops
        n = len(ops)
        need = [False] * n
        for o in ops:
            for d in o["raw"]:
                if self._needs_wait(o, ops[d], True):
                    need[d] = True
            for d in o["oth"]:
                if self._needs_wait(o, ops[d], False):
                    need[d] = True
        last = {}
        for i, o in enumerate(ops):
            if not o["dma"]:
                last[o["eng"]] = i
        for i in last.values():
            need[i] = True
        sig = [None] * n
        for i, o in enumerate(ops):
            e = o["eng"]
            eo = self.eng[e]
            deps = [(d, True) for d in o["raw"]] + [(d, False) for d in o["oth"]]
            for d, is_raw in sorted(deps):
                p = ops[d]
                if not self._needs_wait(o, p, is_raw):
                    continue
                s, v = sig[d]
                self._wait(e, s, v)
            if o["dma"]:
                k = self.dnext[e]
                self.dnext[e] = (k + 1) % len(self.dsem[e])
                s = self.dsem[e][k]
                self._wait(e, s, self.dcum[e][k])
                ins = o["fn"](eo)
                self.dcum[e][k] += 16
                ins.then_inc(s, 16)
                sig[i] = (s, self.dcum[e][k])
            else:
                ins = o["fn"](eo)
                if need[i]:
                    self.cnt[e] += 1
                    ins.then_inc(self.esem[e], 1)
                    sig[i] = (self.esem[e], self.cnt[e])
        for e in self.ENG:
            for x in self.CE:
                if x != e and self.cnt[x] > 0:
                    self._wait(e, self.esem[x], self.cnt[x])
            for q in self.dsem:
                if q == "pool" and not final:
                    continue
                for k, s in enumerate(self.dsem[q]):
                    if self.dcum[q][k] > 0:
                        self._wait(e, s, self.dcum[q][k])
        self.nops += n
        print("[flush] ops", n, "cnt", self.cnt, "dcum max", {q: max(v) for q, v in self.dcum.items()})
        self.ops = []
        self.last_w = {}
        self.readers = {}


def _consts():
    half = 32
    inv = (10000.0 ** (-np.arange(half, dtype=np.float32) / half)).astype(np.float32)
    pos = np.arange(T, dtype=np.float32)
    ang = pos[:, None] * inv[None, :]
    cos = np.cos(ang).astype(np.float32)
    sin = np.sin(ang).astype(np.float32)
    fi = (np.arange(128) % 64) % 32
    cosT = np.ascontiguousarray(cos[:, fi].T)
    sinT = np.ascontiguousarray(sin[:, fi].T)
    angs = (np.float32(PAST) * inv).astype(np.float32)
    cs = np.cos(angs).astype(np.float32)
    ss = np.sin(angs).astype(np.float32)
    cs_tok = np.tile(np.concatenate([cs, cs])[None, :], (DB, 1)).astype(np.float32)
    sn_tok = np.tile(np.concatenate([-ss, ss])[None, :], (DB, 1)).astype(np.float32)
    rmat = np.zeros((128, 128), np.float32)
    for po in range(128):
        if (po % 64) < 32:
            rmat[po + 32, po] = -1.0
        else:
            rmat[po - 32, po] = 1.0
    k = np.arange(128)[:, None]
    q = np.arange(128)[None, :]
    cur = (k <= q).astype(np.float32)
    prev0 = (k >= q).astype(np.float32)
    prev1 = (k > q).astype(np.float32)
    m0 = np.concatenate([prev0, cur, prev0, cur], axis=1)
    m1 = np.concatenate([prev1, cur, prev1, cur], axis=1)
    masks = np.stack([m0, m1]).astype(np.float32)
    mbias = ((1.0 - masks[:, :, 0:256]) * -30000.0).astype(np.float32)
    rowb = np.zeros((128, 1), np.float32)
    rowb[0, 0] = -30000.0
    return dict(mbias=mbias, rowb=rowb, cosT=cosT, sinT=sinT, cs_tok=cs_tok, sn_tok=sn_tok, rmat=rmat, masks=masks,
                ident=np.eye(128, dtype=np.float32))


class _Stop(Exception):
    pass


def build_nc():
    nc = bass.Bass("TRN2", target_bir_lowering=False)
    try:
        return _build(nc)
    except _Stop:
        return nc


def _build(nc):

    def din(name, shape, dt=F32):
        return nc.dram_tensor(name, list(shape), dt, kind="ExternalInput").ap()

    def dout(name, shape, dt=F32):
        return nc.dram_tensor(name, list(shape), dt, kind="ExternalOutput").ap()

    xp = din("xp", [T, D])
    crows = din("crows", [33, D])
    xs = din("xs", [DB, D])
    ca = [din("ca0", [DB, 128, 512]), din("ca1", [DB, 512, 512]), din("ca2", [DB, 2048, 512])]
    cb = din("cb", [DB, 128, 256])
    ada_w0 = din("ada_w0", [D, 3072])
    ada_w1 = din("ada_w1", [D, 3072])
    ada_kv_w = din("ada_kv_w", [D, 2048])
    brow = din("brow", [1, 8192])
    vecs = din("vecs", [40, 128])
    gpost_rows = din("gpost_rows", [2, D])
    w_in_a = din("w_in_a", [D, 3072])
    w_o_a = din("w_o_a", [768, D])
    w_in_b = din("w_in_b", [D, 2048])
    w_o_b = din("w_o_b", [D, D])
    w_kvx = din("w_kvx", [D, 384])
    sink_fm = din("sink_fm", [128, 8])
    sink_tm = din("sink_tm", [DB, 16])
    ident_d = din("ident", [128, 128])
    rmat_d = din("rmat", [128, 128])
    cosT_d = din("cosT", [128, T])
    sinT_d = din("sinT", [128, T])
    masks_d = din("masks", [2, 128, 512])
    mbias_d = din("mbias", [2, 128, 256])
    cs_tok_d = din("cs_tok", [DB, 64])
    sn_tok_d = din("sn_tok", [DB, 64])
    rowb_d = din("rowb_in", [128, 1])
    gvec_tm = din("gvec_tm", [5, DB, D])
    w_kv_d = din("w_kv", [D, 256])

    yp = dout("yp", [T, D])
    ys = dout("ys", [DB, D])
    aop = [dout("a0p", [128, 512]), dout("a1p", [512, 512]), dout("a2p", [2048, 512])]
    bp = dout("bp", [128, 256])
    aos = [dout("a0s", [DB, 128, 512]), dout("a1s", [DB, 512, 512]), dout("a2s", [DB, 2048, 512])]
    bs = dout("bs", [DB, 128, 256])
    x1s = nc.dram_tensor("x1s", [T, D], F32, kind="Internal").ap()
    g2s = nc.dram_tensor("g2s", [4, 128, T], BF16, kind="Internal").ap()

    with ExitStack() as top:
        P = Prog(nc, top)
        op = P.op

        def stop(tag):
            if os.environ.get("KSTOP") == tag:
                P.flush()
                raise _Stop()

        def sbt(st, name, shape, dt):
            return st.enter_context(nc.sbuf_tensor(name, list(shape), dt))

        def pst(st, name, shape, dt):
            return st.enter_context(nc.psum_tensor(name, list(shape), dt))

        identf = sbt(top, "identf", [128, 128], F32)
        identb = sbt(top, "identb", [128, 128], BF16)
        rmf = sbt(top, "rmf", [128, 128], F32)
        rmb = sbt(top, "rmb", [128, 128], BF16)
        maskb = [sbt(top, "maskb%d" % l, [128, 512], BF16) for l in range(2)]
        mbb = [sbt(top, "mbb%d" % l, [128, 256], BF16) for l in range(2)]
        vecT = sbt(top, "vecT", [128, 40], F32)
        modT = sbt(top, "modT", [128, 64], F32)
        amod = sbt(top, "amod", [128, 24], F32)
        ggbc = [sbt(top, "ggbc%d" % l, [128, D], F32) for l in range(2)]
        esink = sbt(top, "esink", [128, 8], F32)
        ones_b = sbt(top, "ones_b", [128, 64], BF16)
        msq = sbt(top, "msq", [128, 4 * NT], F32)
        rstd = sbt(top, "rstd", [128, 4 * NT], F32)
        junk = sbt(top, "junk", [128, D], BF16)

        psT = [pst(top, "psT0", [128, 8, 128], BF16)]
        psP = pst(top, "psP", [128, 2, 512], F32)
        psR = pst(top, "psR", [128, 512], F32)
        psS = [pst(top, "psS%d" % i, [128, 512], F32) for i in range(2)]
        psOs = [pst(top, "psO%d" % i, [128, 512], F32) for i in range(2)]
        psOf = psOs[0]

        def dma(q, out, in_, reads=(), writes=()):
            return op(q, lambda e, o=out, i=in_: e.dma_start(out=o, in_=i), reads, writes, dma=True)

        def mm(out, lhsT, rhs, start, stop, reads, writes, rg=None):
            return op("pe", lambda e, o=out, l=lhsT, r=rhs, a=start, b=stop: e.matmul(o, lhsT=l, rhs=r, start=a, stop=b),
                      reads, writes, rg=rg)

        def tr(out, in_, ident, reads, writes):
            return op("pe", lambda e, o=out, i=in_, d=ident: e.transpose(out=o, in_=i, identity=d), reads, writes)

        def act(out, in_, func, reads, writes, **kw):
            return op("act", lambda e, o=out, i=in_, f=func, k=kw: e.activation(out=o, in_=i, func=f, **k), reads, writes)

        def cp(eng, out, in_, reads, writes):
            if eng == "act":
                return act(out, in_, AF.Copy, reads, writes)
            return op(eng, lambda e, o=out, i=in_: e.tensor_copy(out=o, in_=i), reads, writes)

        def tt(eng, out, in0, in1, alu, reads, writes):
            return op(eng, lambda e, o=out, a=in0, b=in1, f=alu: e.tensor_tensor(out=o, in0=a, in1=b, op=f), reads, writes)

        def ts(eng, out, in0, s1, s2, op0, op1, reads, writes):
            if s2 is None:
                return op(eng, lambda e, o=out, a=in0, x=s1, f=op0: e.tensor_scalar(out=o, in0=a, scalar1=x, scalar2=None, op0=f),
                          reads, writes)
            return op(eng, lambda e, o=out, a=in0, x=s1, y=s2, f=op0, g=op1:
                      e.tensor_scalar(out=o, in0=a, scalar1=x, scalar2=y, op0=f, op1=g), reads, writes)

        def stt(eng, out, in0, scalar, in1, op0, op1, reads, writes):
            return op(eng, lambda e, o=out, a=in0, s=scalar, b=in1, f=op0, g=op1:
                      e.scalar_tensor_tensor(out=o, in0=a, scalar=s, in1=b, op0=f, op1=g), reads, writes)

        def memset(eng, ap, val, writes):
            return op(eng, lambda e, a=ap, v=val: e.memset(a, v), (), writes)

        cast_rr = [0]

        def load_w(st_slots, dst, src, rows, col0, ncols, dcol0, key):
            nk = rows // 128
            for kc in range(nk):
                for c0 in range(0, ncols, 1024):
                    w = min(1024, ncols - c0)
                    sl = cast_rr[0] % len(st_slots)
                    cast_rr[0] += 1
                    stg = st_slots[sl]
                    dma("sp", stg[:, 0:w], src[kc * 128:(kc + 1) * 128, col0 + c0:col0 + c0 + w], (), [("wst", sl)])
                    eng = ("pool", "dve", "pool", "act")[cast_rr[0] % 4]
                    cp(eng, dst[:, kc, dcol0 + c0:dcol0 + c0 + w], stg[:, 0:w], [("wst", sl)], [key])

        rr = dict(p=0, r=0, s=0, o=0, e=0, w=0)

        def nxt(k, n=2):
            v = rr[k] % n
            rr[k] += 1
            return v

        def evac_eng():
            return ("act", "dve")[nxt("e")]

        AX = mybir.AxisListType

        with ExitStack() as st:
            crow_t = sbt(st, "crow_t", [33, D], F32)
            sc_t = sbt(st, "sc_t", [33, D], F32)
            cT = sbt(st, "cT", [128, 8, 33], F32)
            vec_t = sbt(st, "vec_t", [40, 128], F32)
            ones33 = sbt(st, "ones33", [33, 128], F32)
            adat = [sbt(st, "adat0", [33, 3072], F32), sbt(st, "adat1", [33, 2048], F32)]
            gprow = sbt(st, "gprow", [33, D], F32)
            wstg = [sbt(st, "adaw%d" % i, [128, 3072], F32) for i in range(2)]
            psA = [psS[0], psS[1], psR, psOf, psP[:, 0, :], psP[:, 1, :]]
            sink_t = sbt(st, "sink_t", [128, 8], F32)
            maskf = sbt(st, "maskf", [128, 512], F32)
            xs_t = sbt(st, "xs_t", [DB, D], F32)
            x1_t = sbt(st, "x1_t", [DB, D], F32)
            gv = [sbt(st, "gv%d" % i, [DB, D], F32) for i in range(3)]
            cs_t = sbt(st, "cs_t", [DB, 64], F32)
            sn_t = sbt(st, "sn_t", [DB, 64], F32)
            stm = sbt(st, "stm", [DB, 16], F32)
            rowb = sbt(st, "rowb", [128, 1], F32)
            hs = sbt(st, "hs", [DB, D], F32)
            hs2 = sbt(st, "hs2", [DB, D], F32)
            hsT = sbt(st, "hsT", [128, 8, DB], F32)
            hkTs = sbt(st, "hkTs", [128, 8, DB], F32)
            proj = sbt(st, "proj", [33, 3072], F32)
            brow_t = proj
            kvp = sbt(st, "kvp", [DB, 256], F32)
            rop = sbt(st, "rop", [DB, 1536], F32)
            swp = sbt(st, "swp", [DB, 1536], F32)
            krow = [sbt(st, "krow%d" % i, [DB, 512], F32) for i in range(4)]
            KVflat = sbt(st, "KVflat", [128, DB * 520], F32)
            prod = sbt(st, "prod", [128, 1024], F32)
            scr = sbt(st, "scr", [128, 16], F32)
            PZ = sbt(st, "PZ", [128, DB, 16, DB], F32)
            Oc = sbt(st, "Oc", [DB, 16, 65], F32)
            Ot = sbt(st, "Ot", [DB, 16, 64], F32)
            sml = sbt(st, "sml", [DB, 128], F32)

            for g, L in enumerate((128, 512, 2048)):
                for b in range(DB):
                    dma("pool", aos[g][b, 0:L - 1, :], ca[g][b, 1:L, :], (), ())
            dma("pool", bs[:, 0:127, :], cb[:, 1:128, :], (), ())

            dma("sp", identf[:], ident_d[:, :], (), ["identf"])
            dma("sp", rmf[:], rmat_d[:, :], (), ["rmf"])
            dma("sp", crow_t[:], crows[:, :], (), ["crow"])
            dma("sp", vec_t[:], vecs[:, :], (), ["vec_t"])
            dma("sp", sink_t[:], sink_fm[:, :], (), ["sink_t"])
            dma("sp", xs_t[:], xs[:, :], (), ["xs_t"])
            dma("sp", cs_t[:], cs_tok_d[:, :], (), ["cs_t"])
            dma("sp", sn_t[:], sn_tok_d[:, :], (), ["sn_t"])
            dma("sp", stm[:], sink_tm[:, :], (), ["stm"])
            dma("sp", rowb[:], rowb_d[:, :], (), ["rowb"])
            for l in range(2):
                dma("sp", maskf[:], masks_d[l], (), ["maskf"])
                cp("dve", maskb[l][:], maskf[:], ["maskf"], ["maskb%d" % l])
                dma("sp", maskf[:, 0:256], mbias_d[l], ["maskf"], ["maskf"])
                cp("dve", mbb[l][:], maskf[:, 0:256], ["maskf"], ["mbb%d" % l])
            cp("dve", identb[:], identf[:], ["identf"], ["identb"])
            cp("dve", rmb[:], rmf[:], ["rmf"], ["rmb"])
            memset("dve", ones_b[:], 1.0, ["ones_b"])
            memset("dve", ones33[:], 1.0, ["ones33"])
            memset("pool", PZ[:], 0.0, ["PZ"])
            act(esink[:], sink_t[:], AF.Exp, ["sink_t"], ["esink"])
            act(stm[:], stm[:], AF.Exp, ["stm"], ["stm"])
            tr(psOf[:, 0:40], vec_t[:, :], identf[0:40, 0:40], ["vec_t", "identf"], [("psA", 3)])
            cp("dve", vecT[:], psOf[:, 0:40], [("psA", 3)], ["vecT"])
            act(sc_t[:], crow_t[:], AF.Silu, ["crow"], ["sc_t"])
            for kc in range(8):
                tr(psOf[:, 64 + kc * 33:64 + (kc + 1) * 33], sc_t[:, kc * 128:(kc + 1) * 128], identf[0:33, 0:33],
                   ["sc_t", "identf"], [("psA", 3)])
            cp("dve", cT[:].rearrange("p k t -> p (k t)"), psOf[:, 64:64 + 8 * 33], [("psA", 3)], ["cT"])

            def stream_mm(lhsT_of, lkey, M, wd, nk, ncols, dst, dkey, bias_off=None):
                nb = (ncols + 511) // 512
                if bias_off is not None:
                    dma("sp", brow_t[32:33, 0:ncols], brow[:, bias_off:bias_off + ncols], (), ["brow_t"])
                for kc in range(nk):
                    sl = nxt("w")
                    dma("sp", wstg[sl][:, 0:ncols], wd[kc * 128:(kc + 1) * 128, 0:ncols], (), [("adaw", sl)])
                    for b in range(nb):
                        w = min(512, ncols - b * 512)
                        mm(psA[b][0:M, 0:w], lhsT_of(kc), wstg[sl][:, b * 512:b * 512 + w], kc == 0,
                           (kc == nk - 1) and bias_off is None, [lkey, ("adaw", sl)], [("psA", b)])
                for b in range(nb):
                    w = min(512, ncols - b * 512)
                    if bias_off is not None:
                        mm(psA[b][0:M, 0:w], ones33[32:33, 0:M], brow_t[32:33, b * 512:b * 512 + w], False, True,
                           ["ones33", "brow_t"], [("psA", b)])
                    cp("dve" if b % 2 else "act", dst[0:M, b * 512:b * 512 + w], psA[b][0:M, 0:w], [("psA", b)], [(dkey, b)])
                return [(dkey, b) for b in range(nb)]

            def prompt_cols(ad, akeys, ncols, mcol):
                nch = ncols // 128
                for j in range(nch):
                    mm(psS[0][:, j:j + 1], ad[32:33, j * 128:(j + 1) * 128], ones33[32:33, 0:1], True, True,
                       akeys + ["ones33"], [("psA", 0)])
                cp("dve", modT[:, mcol:mcol + nch], psS[0][:, 0:nch], [("psA", 0)], ["modT"])

            def gate_bcast(ad, akeys, li):
                dma("sp", gprow[32:33, :], gpost_rows[li:li + 1, :], (), ["gprow"])
                tt("dve", gprow[32:33, :], gprow[32:33, :], ad[32:33, 2048:3072], ALU.mult, akeys + ["gprow"], ["gprow"])
                for b in range(2):
                    mm(psS[1][:], ones33[32:33, :], gprow[32:33, b * 512:(b + 1) * 512], True, True,
                       ["ones33", "gprow"], [("psA", 1)])
                    cp("act", ggbc[li][:, b * 512:(b + 1) * 512], psS[1][:], [("psA", 1)], ["ggbc%d" % li])

            scol = [100]

            def rms_tok(x_t, xkeys):
                c = scol[0]
                scol[0] += 1
                act(junk[0:DB, :], x_t[:], AF.Square, xkeys, [("msq", c)], scale=1.0 / 32, accum_out=msq[0:DB, c:c + 1])
                act(rstd[0:DB, c:c + 1], msq[0:DB, c:c + 1], AF.Ln, [("msq", c)], [("rstd", c)], bias=EPS)
                act(rstd[0:DB, c:c + 1], rstd[0:DB, c:c + 1], AF.Exp, [("rstd", c)], [("rstd", c)], scale=-0.5)
                return rstd[0:DB, c:c + 1], ("rstd", c)

            def modulate_tok(dst, dkey, x_t, xkeys, rs, rkey, g_t, gkey, scale_ap, shift_ap, akeys):
                ts("dve", dst[:], x_t[:], rs, None, ALU.mult, None, xkeys + [rkey], [dkey])
                tt("dve", dst[:], dst[:], g_t[:], ALU.mult, [dkey, gkey], [dkey])
                stt("dve", dst[:], scale_ap, 1.0, dst[:], ALU.add, ALU.mult, akeys + [dkey], [dkey])
                tt("dve", dst[:], dst[:], shift_ap, ALU.add, akeys + [dkey], [dkey])

            def transpose_tok(dstT, dkey, src, skeys, nk):
                for kc in range(nk):
                    tr(psOf[:, kc * DB:(kc + 1) * DB], src[0:DB, kc * 128:(kc + 1) * 128], identf[0:DB, 0:DB], skeys, [("psA", 3)])
                cp("dve", dstT[:, 0:nk, :].rearrange("p k t -> p (k t)"), psOf[:, 0:nk * DB], [("psA", 3)], [dkey])

            def rope_tok(dst, dkey, src, skeys, nh, sw):
                X = src.rearrange("p (h d) -> p h d", d=64)
                R = dst.rearrange("p (h d) -> p h d", d=64)
                S = sw.rearrange("p (h d) -> p h d", d=64)
                bc = lambda a: a.unsqueeze(1).to_broadcast([DB, nh, a.shape[1]])
                tt("dve", R, X, bc(cs_t[:, :]), ALU.mult, skeys + ["cs_t"], [dkey])
                tt("pool", S[:, :, 0:32], X[:, :, 32:64], bc(sn_t[:, 0:32]), ALU.mult, skeys + ["sn_t"], ["swp"])
                tt("pool", S[:, :, 32:64], X[:, :, 0:32], bc(sn_t[:, 32:64]), ALU.mult, skeys + ["sn_t"], ["swp"])
                tt("dve", R, R, S, ALU.add, [dkey, "swp"], [dkey])

            def pv_slot(hd):
                bk = hd // 7
                t = (psP[:, 0, :], psP[:, 1, :], psR)[bk]
                return t[0:DB, (hd % 7) * 65:(hd % 7) * 65 + 65], ("psA", (4, 5, 2)[bk])

            k0 = stream_mm(lambda kc: cT[:, kc, :], "cT", 33, ada_w0, 8, 3072, adat[0], "ad0", bias_off=0)
            prompt_cols(adat[0], k0, 3072, 0)
            gate_bcast(adat[0], k0, 0)

            dma("sp", gv[0][:], gvec_tm[0], (), [("gv", 0)])
            dma("sp", gv[1][:], gvec_tm[2], (), [("gv", 1)])
            rs, rk = rms_tok(xs_t, ["xs_t"])
            modulate_tok(hs, "hs", xs_t, ["xs_t"], rs, rk, gv[0], ("gv", 0), adat[0][0:DB, 1024:2048], adat[0][0:DB, 0:1024], k0)
            transpose_tok(hsT, "hsT", hs, ["hs"], 8)
            pk = stream_mm(lambda kc: hsT[:, kc, :], "hsT", DB, w_in_a, 8, 3072, proj, "proj")
            rope_tok(rop[:, 0:1536], "rop", proj[0:DB, 0:1536], pk, 24, swp[:, 0:1536])
            for g, L in enumerate((128, 512, 2048)):
                cp("pool", krow[g][:, 0:256], rop[:, 768 + g * 256:768 + (g + 1) * 256], ["rop"], [("krow", g)])
                cp("pool", krow[g][:, 256:512], proj[0:DB, 1536 + g * 256:1536 + (g + 1) * 256], pk, [("krow", g)])
                dma("sp", aos[g][:, L - 1, :], krow[g][:], [("krow", g)], ())
            KV0 = KVflat[:].rearrange("p (b t h d) -> p b t h d", b=DB, t=2, h=4)
            memset("pool", KV0[:, :, 1, :, 64:65], 1.0, ["KVones"])
            for g, (L, dil) in enumerate(((128, 1), (512, 4), (2048, 16))):
                for b in range(DB):
                    dma("sp", KV0[:, b, :, :, 0:64], ca[g][b, 0:L:dil, :].rearrange("j (t h d) -> j t h d", t=2, h=4),
                        ["KVones"], [("KV", b)])
                for b in range(DB):
                    sl = b % 2
                    mm(psS[sl][:, 0:256], identf[0:DB, b:b + 1].to_broadcast([DB, 128]), rop[0:DB, g * 256:(g + 1) * 256], True, True,
                       ["rop", "identf"], [("psA", sl)])
                    p3 = prod[:, 0:256].rearrange("p (h d) -> p h d", h=4)
                    tt("dve", p3, KV0[:, b, 0, :, 0:64], psS[sl][:, 0:256].rearrange("p (h d) -> p h d", h=4), ALU.mult,
                       [("KV", b), ("psA", sl)], ["prod"])
                    op("dve", lambda e, o=scr[:, 0:4], i_=p3: e.tensor_reduce(out=o, in_=i_, axis=AX.X, op=ALU.add), ["prod"], ["scr"])
                    act(PZ[:, b, g * 4:(g + 1) * 4, b], scr[:, 0:4], AF.Exp, ["scr", "PZ"], [("PZ", b)], scale=0.125)
                for h in range(4):
                    hd = g * 4 + h
                    o, okey = pv_slot(hd)
                    for b in range(DB):
                        mm(o, PZ[:, b, hd, :], KV0[:, b, 1, h, :], b == 0, b == DB - 1, [("PZ", b), ("KV", b), "KVones"], [okey])
            cp("dve", Oc[:, 0:7, :], psP[0:DB, 0, 0:455].rearrange("p (h d) -> p h d", d=65), [("psA", 4)], ["Oc"])
            cp("dve", Oc[:, 7:12, :], psP[0:DB, 1, 0:325].rearrange("p (h d) -> p h d", d=65), [("psA", 5)], ["Oc"])

            def finish_attn(nh, qv, kvw, vv, keys):
                q3 = qv.rearrange("p (h d) -> p h d", d=64)
                pr = prod[0:DB, 0:nh * 64].rearrange("p (h d) -> p h d", d=64)
                tt("dve", pr, q3, kvw, ALU.mult, keys, ["prod"])
                op("dve", lambda e, o=sml[:, 32:32 + nh], i_=pr: e.tensor_reduce(out=o, in_=i_, axis=AX.X, op=ALU.add), ["prod"], ["sml"])
                act(sml[:, 32:32 + nh], sml[:, 32:32 + nh], AF.Exp, ["sml"], ["sml"], scale=0.125)
                tt("dve", pr, vv, sml[:, 32:32 + nh].unsqueeze(2).to_broadcast([DB, nh, 64]), ALU.mult, keys + ["sml", "prod"], ["prod"])
                tt("dve", Ot[:, 0:nh, :], Oc[:, 0:nh, 0:64], pr, ALU.add, ["Oc", "prod"], ["Ot"])
                tt("dve", sml[:, 0:nh], Oc[:, 0:nh, 64], sml[:, 32:32 + nh], ALU.add, ["Oc", "sml"], ["sml"])

            k3 = rop[:, 768:1536].rearrange("p (h d) -> p h d", d=64)
            v3 = proj[0:DB, 1536:2304].rearrange("p (h d) -> p h d", d=64)
            finish_attn(12, rop[:, 0:768], k3, v3, ["rop"] + pk)
            tt("dve", sml[:, 64:68], sml[:, 0:4], sml[:, 4:8], ALU.add, ["sml"], ["sml"])
            tt("dve", sml[:, 64:68], sml[:, 64:68], sml[:, 8:12], ALU.add, ["sml"], ["sml"])
            op("dve", lambda e, o=sml[:, 64:68], i_=sml[:, 64:68]: e.reciprocal(out=o, in_=i_), ["sml"], ["sml"])
            og = hs2[:, 0:768]
            for g in range(3):
                tt("dve", og[:, g * 256:(g + 1) * 256].rearrange("p (h d) -> p h d", d=64), Ot[:, g * 4:(g + 1) * 4, :],
                   sml[:, 64:68].unsqueeze(2).to_broadcast([DB, 4, 64]), ALU.mult, ["Ot", "sml"], ["og"])
            act(hs[:, 0:768], proj[0:DB, 2304:3072], AF.Silu, pk + ["hs"], ["hs"])
            tt("dve", og, og, hs[:, 0:768], ALU.mult, ["og", "hs"], ["og"])
            transpose_tok(hsT, "hsT", og, ["og"], 6)
            mk = stream_mm(lambda kc: hsT[:, kc, :], "hsT", DB, w_o_a, 6, 1024, hs, "mix")
            rs, rk = rms_tok(hs, mk)
            ts("dve", hs[:], hs[:], rs, None, ALU.mult, None, mk + [rk], ["mixn"])
            tt("dve", hs[:], hs[:], gv[1][:], ALU.mult, ["mixn", ("gv", 1)], ["mixn"])
            tt("dve", hs[:], hs[:], adat[0][0:DB, 2048:3072], ALU.mult, ["mixn"] + k0, ["mixn"])
            tt("dve", x1_t[:], hs[:], xs_t[:], ALU.add, ["mixn", "xs_t"], ["x1_t"])

            k1 = stream_mm(lambda kc: cT[:, kc, :], "cT", 33, ada_w1, 8, 3072, adat[0], "ad0", bias_off=3072)
            prompt_cols(adat[0], k1, 3072, 24)
            gate_bcast(adat[0], k1, 1)
            kk = stream_mm(lambda kc: cT[:, kc, :], "cT", 33, ada_kv_w, 8, 2048, adat[1], "ad1", bias_off=6144)
            prompt_cols(adat[1], kk, 2048, 48)
            for (dst, sc, gvv) in ((amod[:, 0:8], modT[:, 8:16], vecT[:, 0:8]), (amod[:, 8:16], modT[:, 32:40], vecT[:, 8:16]),
                                   (amod[:, 16:24], modT[:, 56:64], vecT[:, 32:40])):
                ts("dve", dst, sc, 1.0, None, ALU.add, None, ["modT"], ["amod"])
                tt("dve", dst, dst, gvv, ALU.mult, ["amod", "vecT"], ["amod"])

            dma("sp", gv[0][:], gvec_tm[1], (), [("gv", 0)])
            dma("sp", gv[1][:], gvec_tm[3], (), [("gv", 1)])
            dma("sp", gv[2][:], gvec_tm[4], (), [("gv", 2)])
            rs, rk = rms_tok(x1_t, ["x1_t"])
            modulate_tok(hs, "hs", x1_t, ["x1_t"], rs, rk, gv[0], ("gv", 0), adat[0][0:DB, 1024:2048], adat[0][0:DB, 0:1024], k1)
            modulate_tok(hs2, "hs2", x1_t, ["x1_t"], rs, rk, gv[2], ("gv", 2), adat[1][0:DB, 1024:2048], adat[1][0:DB, 0:1024], kk)
            transpose_tok(hsT, "hsT", hs, ["hs"], 8)
            transpose_tok(hkTs, "hkTs", hs2, ["hs2"], 8)
            kvk = stream_mm(lambda kc: hkTs[:, kc, :], "hkTs", DB, w_kv_d, 8, 256, kvp, "kvp")
            pk = stream_mm(lambda kc: hsT[:, kc, :], "hsT", DB, w_in_b, 8, 2048, proj, "proj")
            rope_tok(rop[:, 0:1024], "rop", proj[0:DB, 0:1024], pk, 16, swp[:, 0:1024])
            rope_tok(rop[:, 1024:1152], "ropk", kvp[:, 0:128], kvk, 2, swp[:, 1024:1152])
            cp("pool", krow[3][:, 0:128], rop[:, 1024:1152], ["ropk"], [("krow", 3)])
            cp("pool", krow[3][:, 128:256], kvp[:, 128:256], kvk, [("krow", 3)])
            dma("sp", bs[:, 127, :], krow[3][:, 0:256], [("krow", 3)], ())
            KV1 = KVflat[:, 0:DB * 260].rearrange("p (b t h d) -> p b t h d", b=DB, t=2, h=2)
            allkv = [("KV", b) for b in range(DB)]
            memset("pool", KV1[:, :, 1, :, 64:65], 1.0, allkv + ["KVones"])
            for b in range(DB):
                dma("sp", KV1[:, b, :, :, 0:64], cb[b].rearrange("j (t h d) -> j t h d", t=2, h=2), ["KVones"], [("KV", b)])
            for b in range(DB):
                for kh in range(2):
                    mm(psS[kh][:, :], identf[0:DB, b:b + 1].to_broadcast([DB, 128]), rop[0:DB, kh * 512:(kh + 1) * 512], True, True,
                       ["rop", "identf"], [("psA", kh)])
                    tt("dve", prod[:, kh * 512:(kh + 1) * 512].rearrange("p (h d) -> p h d", d=64),
                       psS[kh][:, :].rearrange("p (h d) -> p h d", d=64),
                       KV1[:, b, 0, kh, 0:64].unsqueeze(1).to_broadcast([128, 8, 64]), ALU.mult, [("KV", b), ("psA", kh)], ["prod"])
                op("dve", lambda e, o=scr[:, 0:16], i_=prod[:, :].rearrange("p (h d) -> p h d", d=64):
                   e.tensor_reduce(out=o, in_=i_, axis=AX.X, op=ALU.add), ["prod"], ["scr"])
                act(PZ[:, b, :, b], scr[:, 0:16], AF.Exp, ["scr", "PZ", "rowb"], [("PZ", b)], scale=0.125, bias=rowb[:, 0:1])
            for hd in range(16):
                o, okey = pv_slot(hd)
                for b in range(DB):
                    mm(o, PZ[:, b, hd, :], KV1[:, b, 1, hd // 8, :], b == 0, b == DB - 1, [("PZ", b), ("KV", b), "KVones"], [okey])
            cp("dve", Oc[:, 0:7, :], psP[0:DB, 0, 0:455].rearrange("p (h d) -> p h d", d=65), [("psA", 4)], ["Oc"])
            cp("dve", Oc[:, 7:14, :], psP[0:DB, 1, 0:455].rearrange("p (h d) -> p h d", d=65), [("psA", 5)], ["Oc"])
            cp("dve", Oc[:, 14:16, :], psR[0:DB, 0:130].rearrange("p (h d) -> p h d", d=65), [("psA", 2)], ["Oc"])
            kb3 = rop[:, 1024:1152].rearrange("p (k d) -> p k d", d=64)
            vb3 = kvp[:, 128:256].rearrange("p (k d) -> p k d", d=64)
            q4 = rop[:, 0:1024]
            for kh in range(2):
                pass
            pr = prod[0:DB, 0:1024].rearrange("p (h d) -> p h d", d=64)
            for kh in range(2):
                tt("dve", pr[:, kh * 8:(kh + 1) * 8, :], q4.rearrange("p (h d) -> p h d", d=64)[:, kh * 8:(kh + 1) * 8, :],
                   kb3[:, kh, :].unsqueeze(1).to_broadcast([DB, 8, 64]), ALU.mult, ["rop", "ropk", "prod"], ["prod"])
            op("dve", lambda e, o=sml[:, 32:48], i_=pr: e.tensor_reduce(out=o, in_=i_, axis=AX.X, op=ALU.add), ["prod"], ["sml"])
            act(sml[:, 32:48], sml[:, 32:48], AF.Exp, ["sml"], ["sml"], scale=0.125)
            for kh in range(2):
                tt("dve", pr[:, kh * 8:(kh + 1) * 8, :], sml[:, 32 + kh * 8:32 + (kh + 1) * 8].unsqueeze(2).to_broadcast([DB, 8, 64]),
                   vb3[:, kh, :].unsqueeze(1).to_broadcast([DB, 8, 64]), ALU.mult, kvk + ["sml", "prod"], ["prod"])
            tt("dve", Ot[:, 0:16, :], Oc[:, 0:16, 0:64], pr, ALU.add, ["Oc", "prod"], ["Ot"])
            tt("dve", sml[:, 0:16], Oc[:, 0:16, 64], sml[:, 32:48], ALU.add, ["Oc", "sml"], ["sml"])
            tt("dve", sml[:, 0:16], sml[:, 0:16], stm[:, :], ALU.add, ["sml", "stm"], ["sml"])
            op("dve", lambda e, o=sml[:, 0:16], i_=sml[:, 0:16]: e.reciprocal(out=o, in_=i_), ["sml"], ["sml"])
            og = hs2[:, 0:1024]
            tt("dve", og.rearrange("p (h d) -> p h d", d=64), Ot[:, 0:16, :], sml[:, 0:16].unsqueeze(2).to_broadcast([DB, 16, 64]),
               ALU.mult, ["Ot", "sml", "hs2"], ["og"])
            act(hs[:, :], proj[0:DB, 1024:2048], AF.Silu, pk + ["hs"], ["hs"])
            tt("dve", og, og, hs[:, :], ALU.mult, ["og", "hs"], ["og"])
            transpose_tok(hsT, "hsT", og, ["og"], 8)
            mk = stream_mm(lambda kc: hsT[:, kc, :], "hsT", DB, w_o_b, 8, 1024, hs, "mix")
            rs, rk = rms_tok(hs, mk)
            ts("dve", hs[:], hs[:], rs, None, ALU.mult, None, mk + [rk], ["mixn"])
            tt("dve", hs[:], hs[:], gv[1][:], ALU.mult, ["mixn", ("gv", 1)], ["mixn"])
            tt("dve", hs[:], hs[:], adat[0][0:DB, 2048:3072], ALU.mult, ["mixn"] + k1, ["mixn"])
            tt("dve", hs[:], hs[:], x1_t[:], ALU.add, ["mixn", "x1_t"], ["ys_t"])
            dma("sp", ys[:, :], hs[:], ["ys_t"], ())
            P.flush()
        if os.environ.get("KSTOP") == "setup":
            return nc

        A0 = amod[:, 0:8]
        SH0 = modT[:, 0:8]
        A1 = amod[:, 8:16]
        SH1 = modT[:, 24:32]
        AKV = amod[:, 16:24]
        SHKV = modT[:, 48:56]

        def load_x(src, srckey, i, xin):
            sl = i % len(xin)
            dma("sp", xin[sl][:], src[i * 128:(i + 1) * 128, :], [(srckey, i)] if srckey else (), [("xin", sl)])

        def norm_tile(i, u, xin, xn, col0, outs, hslot):
            sl = i % len(xin)
            xsl = i % len(xn)
            c = col0 + i
            act(junk[:], xin[sl][:], AF.Square, [("xin", sl)], [("msq", c)], scale=1.0 / 32, accum_out=msq[:, c:c + 1])
            act(rstd[:, c:c + 1], msq[:, c:c + 1], AF.Ln, [("msq", c)], [("rstd", c)], bias=EPS)
            act(rstd[:, c:c + 1], rstd[:, c:c + 1], AF.Exp, [("rstd", c)], [("rstd", c)], scale=-0.5)
            ts("dve", xn[xsl][:], xin[sl][:], rstd[:, c:c + 1], None, ALU.mult, None, [("xin", sl), ("rstd", c)], [("xn", xsl)])
            pt = psT[0]
            for kc in range(8):
                tr(pt[:, kc, :], xn[xsl][:, kc * 128:(kc + 1) * 128], identb[:], [("xn", xsl)], [("psT", 0)])
            for (hT, hname, Am, Sm) in outs:
                for kc in range(8):
                    o = hT[:, kc, u * 128:(u + 1) * 128]
                    if i % 2 == 0:
                        ts("dve", o, pt[:, kc, :], Am[:, kc:kc + 1], Sm[:, kc:kc + 1], ALU.mult, ALU.add,
                           [("psT", 0)], [(hname, hslot, u, kc)])
                    else:
                        act(o, pt[:, kc, :], AF.Identity, [("psT", 0)], [(hname, hslot, u, kc)],
                            scale=Am[:, kc:kc + 1], bias=Sm[:, kc:kc + 1])

        def project(hT, hname, hslot, W, cols, nk=8):
            slot = nxt("p")
            for kc in range(nk):
                mm(psP[:, slot, :], W[:, kc, cols], hT[:, kc, :], kc == 0, kc == nk - 1,
                   [(hname, hslot, u, kc) for u in range(4)], [("psP", slot)])
            return slot

        def rope(slot, ctab, stab, tkey, qraw, t12, dst, dkey, f32dst=None, fkey=None):
            r = nxt("r")
            cp("act", qraw[r][:], psP[:, slot, :], [("psP", slot)], [("qraw", r)])
            mm(psR[:], rmb[:], qraw[r][:], True, True, [("qraw", r)], ["psR"])
            tt("dve", t12[:, r, :], psP[:, slot, :], ctab, ALU.mult, [("psP", slot), tkey], [("t1", r)])
            tt("dve", t12[:, 2 + r, :], psR[:], stab, ALU.mult, ["psR", tkey], [("t2", r)])
            tt("pool", dst, t12[:, r, :], t12[:, 2 + r, :], ALU.add, [("t1", r), ("t2", r)], [dkey])
            if f32dst is not None:
                tt("pool", f32dst, t12[:, r, :], t12[:, 2 + r, :], ALU.add, [("t1", r), ("t2", r)], [fkey])

        def cache_out(kvf, kkeys, u, ncol, stage, dst_rows, skey):
            n = len(kvf)
            for k4 in range(n):
                tr(psR[:, k4 * 128:(k4 + 1) * 128], kvf[k4][:, u * 128:(u + 1) * 128], identf[:], [kkeys[k4]], ["psR"])
            sl = nxt("o")
            cp(evac_eng(), stage[sl][:, 0:ncol], psR[:, 0:ncol], ["psR"], [(skey, sl)])
            dma("sp", dst_rows, stage[sl][:, 0:ncol], [(skey, sl)], ())

        pending = []

        def attn_back(l, heads, pt, sl, has_prev, rv):
            osl = nxt("o")
            psO = psOs[osl]
            for hh, h in enumerate(heads):
                o = psO[:, hh * 128:(hh + 1) * 128]
                if has_prev:
                    mm(o, h["vp"], pt[:, hh * 256:hh * 256 + 128], True, False, rv + [("PT", sl)], [("psO", osl)])
                mm(o, h["vc"], pt[:, hh * 256 + 128:hh * 256 + 256], not has_prev, True, rv + [("PT", sl)], [("psO", osl)])
            eng = evac_eng()
            for hh, h in enumerate(heads):
                for (dst, key, lo) in ((h["od"], h["ok"], 0), (h["dd"], h["dk"], 64)):
                    src = psO[lo:lo + 64, hh * 128:(hh + 1) * 128]
                    if len(dst.shape) == 3:
                        src = src.rearrange("p (a b) -> p a b", a=dst.shape[1])
                    cp(eng, dst, src, [("psO", osl)], [key])

        def attn_flush():
            while pending:
                attn_back(*pending.pop(0))

        def attn_pair(l, heads, PT, rq, rk, rv):
            sl = nxt("s")
            ps = psS[sl]
            pt = PT[sl]
            has_prev = heads[0]["kp"] is not None
            for hh, h in enumerate(heads):
                if has_prev:
                    mm(ps[:, hh * 256:hh * 256 + 256], identb[:], mbb[l][:], True, False, (), [("psS", sl)])
                    mm(ps[:, hh * 256:hh * 256 + 128], h["kp"], h["q"], False, False, rq + rk, [("psS", sl)])
                    mm(ps[:, hh * 256 + 128:hh * 256 + 256], h["kc"], h["q"], False, True, rq + rk, [("psS", sl)])
                else:
                    mm(ps[:, hh * 256 + 128:hh * 256 + 256], identb[:], mbb[l][:, 128:256], True, False, (), [("psS", sl)])
                    mm(ps[:, hh * 256 + 128:hh * 256 + 256], h["kc"], h["q"], False, True, rq + rk, [("psS", sl)])
            if has_prev:
                act(pt[:], ps[:], AF.Exp, [("psS", sl)], [("PT", sl)], scale=0.125)
            else:
                v = lambda a: a.rearrange("p (h x) -> p h x", h=2)[:, :, 128:256]
                act(v(pt[:]), v(ps[:]), AF.Exp, [("psS", sl)], [("PT", sl)], scale=0.125)
            attn_flush()
            pending.append((l, heads, pt, sl, has_prev, rv))

        def out_proj_tile(i, u, oTf, okeys, Wo, nk, ggl, xres, col0, dst_dram, dkey, t12):
            for nb in range(2):
                for kc in range(nk):
                    mm(psP[:, nb, :], oTf[kc][:, u * 128:(u + 1) * 128], Wo[:, kc, nb * 512:(nb + 1) * 512], kc == 0, kc == nk - 1,
                       [okeys[kc]], [("psP", nb)])
            c = col0 + i
            pm = psP[:].rearrange("p a b -> p (a b)")
            act(junk[:], pm, AF.Square, [("psP", 0), ("psP", 1)], [("msq", c)], scale=1.0 / 32, accum_out=msq[:, c:c + 1])
            act(rstd[:, c:c + 1], msq[:, c:c + 1], AF.Ln, [("msq", c)], [("rstd", c)], bias=EPS)
            act(rstd[:, c:c + 1], rstd[:, c:c + 1], AF.Exp, [("rstd", c)], [("rstd", c)], scale=-0.5)
            tm = t12[:, 0:2, :].rearrange("p a b -> p (a b)")
            stt("dve", tm, pm, rstd[:, c:c + 1], ggl[:], ALU.mult, ALU.mult, [("psP", 0), ("psP", 1), ("rstd", c)],
                [("t1", 0), ("t1", 1)])
            sl = i % len(xres)
            tt("pool", xres[sl][:], tm, xres[sl][:], ALU.add, [("t1", 0), ("t1", 1), ("xres", sl)], [("xres", sl)])
            dma("sp", dst_dram[i * 128:(i + 1) * 128, :], xres[sl][:], [("xres", sl)], [(dkey, i)] if dkey else ())

        with ExitStack() as st:
            Wa2 = sbt(st, "Wa2", [128, 8, 768], BF16)
            wst = [sbt(st, "wstA%d" % i, [128, 1024], F32) for i in range(2)]
            xin = [sbt(st, "xinA%d" % i, [128, D], F32) for i in range(4)]
            xn = [sbt(st, "xnA%d" % i, [128, D], BF16) for i in range(2)]
            hT = [sbt(st, "hTA%d" % i, [128, 8, 512], BF16) for i in range(2)]
            QT2 = [sbt(st, "QT2_%d" % c, [128, T], BF16) for c in range(2)]
            KT2 = [sbt(st, "KT2_%d" % c, [128, T], BF16) for c in range(2)]
            VT2 = [sbt(st, "VT2_%d" % c, [128, T], BF16) for c in range(2)]
            ctab = [sbt(st, "ctabA%d" % i, [128, 512], F32) for i in range(2)]
            stab = [sbt(st, "stabA%d" % i, [128, 512], F32) for i in range(2)]
            qraw = [sbt(st, "qrawA%d" % i, [128, 512], BF16) for i in range(2)]
            t12 = sbt(st, "t12A", [128, 4, 512], F32)
            kvf = [sbt(st, "kvfA%d" % i, [128, 512], F32) for i in range(4)]
            stage = [sbt(st, "stageA%d" % i, [128, 512], F32) for i in range(2)]
            PT = [sbt(st, "PTA%d" % i, [128, 512], BF16) for i in range(2)]
            VA = [sbt(st, "VAA%d" % i, [128, 4, 128], BF16) for i in range(4)]
            G2st = [sbt(st, "G2st%d" % k, [128, T], BF16) for k in range(4)]
            load_w(wst, Wa2, w_in_a, 1024, 512, 256, 0, "w")
            load_w(wst, Wa2, w_in_a, 1024, 768 + 512, 256, 256, "w")
            load_w(wst, Wa2, w_in_a, 1024, 1536 + 512, 256, 512, "w")
            for v in VA:
                memset("pool", v[:, :, 64:128], 1.0, [("w", "va")])
            P.flush()
            print("[phaseA] sbuf remaining", nc.sbuf_bytes_remaining)
            stop("A0")
            for u in range(4):
                load_x(xp, None, u, xin)
            for s in range(NS):
                hs = s % 2
                tk = slice(512 * s, 512 * (s + 1))
                dma("sp", ctab[hs][:], cosT_d[:, tk], (), [("tab", hs)])
                dma("sp", stab[hs][:], sinT_d[:, tk], (), [("tab", hs)])
                for u in range(4):
                    norm_tile(4 * s + u, u, xin, xn, 0, [(hT[hs], "hT", A0, SH0)], hs)
                stop("A1")
                if s + 1 < NS:
                    for u in range(4):
                        load_x(xp, None, 4 * (s + 1) + u, xin)
                for (kind, cc, wc) in (("k", 0, 256), ("k", 1, 384), ("v", 0, 512), ("v", 1, 640), ("q", 0, 0), ("q", 1, 128)):
                    slot = project(hT[hs], "hT", hs, Wa2, slice(wc, wc + 128))
                    if kind == "k":
                        rope(slot, ctab[hs][:], stab[hs][:], ("tab", hs), qraw, t12, KT2[cc][:, tk], ("KT2", cc, s),
                             kvf[cc][:] if s >= 4 else None, ("kvf", cc))
                    elif kind == "q":
                        rope(slot, ctab[hs][:], stab[hs][:], ("tab", hs), qraw, t12, QT2[cc][:, tk], ("QT2", cc, s))
                    else:
                        cp("act", VT2[cc][:, tk], psP[:, slot, :], [("psP", slot)], [("VT2", cc, s)])
                        if s >= 4:
                            cp("dve", kvf[2 + cc][:], psP[:, slot, :], [("psP", slot)], [("kvf", 2 + cc)])
                stop("A2")
                if s >= 4:
                    for u in range(4):
                        i = 4 * s + u
                        cache_out(kvf, [("kvf", k) for k in range(4)], u, 512, stage, aop[2][(i - 16) * 128:(i - 15) * 128, :], "stage")
            stop("A3")
            allq = [[("QT2", c, s) for s in range(NS)] for c in range(2)]
            allk = [[("KT2", c, s) for s in range(NS)] for c in range(2)]
            allv = [[("VT2", c, s) for s in range(NS)] for c in range(2)]
            for r in range(16):
                for j in range(2):
                    tq = slice(2048 * j + r, 2048 * (j + 1), 16)
                    tp = slice(2048 * (j - 1) + r, 2048 * j, 16)
                    b = 2 * r + j
                    vsl = b % 4
                    ptv = psT[0]
                    for c in range(2):
                        tr(ptv[:, c, :], VT2[c][:, tq], identb[:], allv[c], [("psT", 0)])
                    cp(evac_eng(), VA[vsl][:, :, 0:64], ptv[:, 0:2, :].rearrange("p c (h d) -> p (c h) d", h=2),
                       [("psT", 0)], [("VA", vsl)])
                    for c in range(2):
                        heads = []
                        for hh in range(2):
                            h = 2 * c + hh
                            rows = slice(hh * 64, (hh + 1) * 64)
                            od = G2st[c][rows, :].rearrange("p (s r m) -> p s r m", s=8, r=16)[:, 4 * j:4 * j + 4, r, :]
                            dd = G2st[2 + c][rows, :].rearrange("p (s r m) -> p s r m", s=8, r=16)[:, 4 * j:4 * j + 4, r, :]
                            heads.append(dict(q=QT2[c][rows, tq], kc=KT2[c][rows, tq], kp=(KT2[c][rows, tp] if j == 1 else None),
                                              vc=VA[vsl][:, h, :], vp=(VA[(b - 1) % 4][:, h, :] if j == 1 else None),
                                              od=od, dd=dd, ok=("G2", c, b, hh), dk=("G2", 2 + c, b, hh)))
                        attn_pair(0, heads, PT, allq[c], allk[c], [("VA", vsl), ("VA", (b - 1) % 4)])
            attn_flush()
            for k in range(4):
                dma("sp", g2s[k], G2st[k][:], [("G2", k, b, hh) for b in range(32) for hh in range(2)], [("g2s", k)])
            P.flush()
        if os.environ.get("KSTOP") == "A":
            return nc

        with ExitStack() as st:
            Wab = sbt(st, "Wab", [128, 8, 2304], BF16)
            Woa = sbt(st, "Woa", [128, 6, 1024], BF16)
            wst = [sbt(st, "wstB%d" % i, [128, 1024], F32) for i in range(2)]
            xin = [sbt(st, "xinB%d" % i, [128, D], F32) for i in range(2)]
            xres = [sbt(st, "xresB%d" % i, [128, D], F32) for i in range(2)]
            xn = [sbt(st, "xnB%d" % i, [128, D], BF16) for i in range(2)]
            hT = [sbt(st, "hTB%d" % i, [128, 8, 512], BF16) for i in range(1)]
            QT = [sbt(st, "QTB%d" % c, [128, 512], BF16) for c in range(4)]
            KT = [sbt(st, "KTB%d" % c, [128, 1024], BF16) for c in range(4)]
            VT = [sbt(st, "VTB%d" % c, [128, 1024], BF16) for c in range(4)]
            gT = [sbt(st, "gTB%d" % c, [128, 512], BF16) for c in range(6)]
            OT = [sbt(st, "OTB%d" % c, [128, 512], BF16) for c in range(4)]
            DT = [sbt(st, "DTB%d" % c, [128, 512], BF16) for c in range(4)]
            oTf = [sbt(st, "oTfB%d" % c, [128, 512], BF16) for c in range(6)]
            G2in = [sbt(st, "G2in%d" % k, [128, 512], BF16) for k in range(4)]
            dsum = sbt(st, "dsumB", [128, 512], F32)
            rinv = sbt(st, "rinvB", [128, 512], F32)
            otmp = sbt(st, "otmpB", [128, 512], F32)
            ctab = [sbt(st, "ctabB%d" % i, [128, 512], F32) for i in range(2)]
            stab = [sbt(st, "stabB%d" % i, [128, 512], F32) for i in range(2)]
            qraw = [sbt(st, "qrawB%d" % i, [128, 512], BF16) for i in range(2)]
            t12 = sbt(st, "t12B", [128, 4, 512], F32)
            kvf = [sbt(st, "kvfB%d" % i, [128, 512], F32) for i in range(4)]
            stage = [sbt(st, "stageB%d" % i, [128, 512], F32) for i in range(2)]
            PT = [sbt(st, "PTB%d" % i, [128, 512], BF16) for i in range(2)]
            VA0 = [sbt(st, "VA0_%d" % i, [128, 4, 128], BF16) for i in range(4)]
            VA1 = [sbt(st, "VA1_%d" % i, [128, 4, 128], BF16) for i in range(8)]
            load_w(wst, Wab, w_in_a, 1024, 0, 512, 0, "w")
            load_w(wst, Wab, w_in_a, 1024, 768, 512, 512, "w")
            load_w(wst, Wab, w_in_a, 1024, 1536, 512, 1024, "w")
            load_w(wst, Wab, w_in_a, 1024, 2304, 768, 1536, "w")
            load_w(wst, Woa, w_o_a, 768, 0, 1024, 0, "w")
            for v in VA0 + VA1:
                memset("pool", v[:, :, 64:128], 1.0, [("w", "va")])
            P.flush()
            load_x(xp, None, 0, xin)
            load_x(xp, None, 1, xin)
            for s in range(NS):
                hs = s % 2
                ts_ = hs * 512
                tk = slice(512 * s, 512 * (s + 1))
                dma("sp", ctab[hs][:], cosT_d[:, tk], (), [("tab", hs)])
                dma("sp", stab[hs][:], sinT_d[:, tk], (), [("tab", hs)])
                for k in range(4):
                    dma("sp", G2in[k][:], g2s[k][:, tk], [("g2s", k)], [("G2in", k)])
                for u in range(4):
                    i = 4 * s + u
                    norm_tile(i, u, xin, xn, 32, [(hT[0], "hT", A0, SH0)], 0)
                    if i + 2 < NT:
                        load_x(xp, None, i + 2, xin)
                ring = slice(ts_, ts_ + 512)
                for g in range(2):
                    for (kind, cc, wc) in (("k", 2 * g, 512 + 256 * g), ("k", 2 * g + 1, 640 + 256 * g),
                                           ("v", 2 * g, 1024 + 256 * g), ("v", 2 * g + 1, 1152 + 256 * g)):
                        slot = project(hT[0], "hT", 0, Wab, slice(wc, wc + 128))
                        need_f = (s == 7)
                        if kind == "k":
                            rope(slot, ctab[hs][:], stab[hs][:], ("tab", hs), qraw, t12, KT[cc][:, ring], ("KT", cc, hs),
                                 kvf[cc % 2][:] if need_f else None, ("kvf", cc % 2))
                        else:
                            cp("act", VT[cc][:, ring], psP[:, slot, :], [("psP", slot)], [("VT", cc, hs)])
                            if need_f:
                                cp("dve", kvf[2 + cc % 2][:], psP[:, slot, :], [("psP", slot)], [("kvf", 2 + cc % 2)])
                    if s == 7:
                        if g == 0:
                            cache_out(kvf, [("kvf", k) for k in range(4)], 3, 512, stage, aop[0][:, :], "stage")
                        else:
                            for u in range(4):
                                cache_out(kvf, [("kvf", k) for k in range(4)], u, 512, stage, aop[1][u * 128:(u + 1) * 128, :], "stage")
                for cc in range(4):
                    slot = project(hT[0], "hT", 0, Wab, slice(cc * 128, (cc + 1) * 128))
                    rope(slot, ctab[hs][:], stab[hs][:], ("tab", hs), qraw, t12, QT[cc][:], ("QT", cc))
                for cc in range(6):
                    slot = project(hT[0], "hT", 0, Wab, slice(1536 + cc * 128, 1536 + (cc + 1) * 128))
                    act(gT[cc][:], psP[:, slot, :], AF.Silu, [("psP", slot)], [("gT", cc)])
                for u in range(4):
                    i = 4 * s + u
                    vsl = i % 4
                    ptv = psT[0]
                    cols = slice(ts_ + u * 128, ts_ + (u + 1) * 128)
                    if u > 0:
                        pcols = slice(ts_ + (u - 1) * 128, ts_ + u * 128)
                    else:
                        pcols = slice((1 - hs) * 512 + 384, (1 - hs) * 512 + 512)
                    for c in range(2):
                        tr(ptv[:, c, :], VT[c][:, cols], identb[:], [("VT", c, hs)], [("psT", 0)])
                    cp(evac_eng(), VA0[vsl][:, :, 0:64], ptv[:, 0:2, :].rearrange("p c (h d) -> p (c h) d", h=2),
                       [("psT", 0)], [("VA0", vsl)])
                    for c in range(2):
                        heads = []
                        for hh in range(2):
                            h = 2 * c + hh
                            rows = slice(hh * 64, (hh + 1) * 64)
                            heads.append(dict(q=QT[c][rows, u * 128:(u + 1) * 128], kc=KT[c][rows, cols],
                                              kp=(KT[c][rows, pcols] if i > 0 else None),
                                              vc=VA0[vsl][:, h, :], vp=(VA0[(i - 1) % 4][:, h, :] if i > 0 else None),
                                              od=OT[c][rows, u * 128:(u + 1) * 128], dd=DT[c][rows, u * 128:(u + 1) * 128],
                                              ok=("OT", c, u, hh), dk=("DT", c, u, hh)))
                        attn_pair(0, heads, PT, [("QT", c)], [("KT", c, 0), ("KT", c, 1)], [("VA0", vsl), ("VA0", (i - 1) % 4)])
                for r4 in range(4):
                    vsl = hs * 4 + r4
                    psl = (1 - hs) * 4 + r4
                    b = 4 * s + r4
                    ptv = psT[0]
                    cols = slice(ts_ + r4, ts_ + 512, 4)
                    pcols = slice((1 - hs) * 512 + r4, (1 - hs) * 512 + 512, 4)
                    for c in range(2):
                        tr(ptv[:, c, :], VT[2 + c][:, cols], identb[:], [("VT", 2 + c, hs)], [("psT", 0)])
                    cp(evac_eng(), VA1[vsl][:, :, 0:64], ptv[:, 0:2, :].rearrange("p c (h d) -> p (c h) d", h=2),
                       [("psT", 0)], [("VA1", vsl)])
                    for c in range(2):
                        heads = []
                        for hh in range(2):
                            h = 2 * c + hh
                            rows = slice(hh * 64, (hh + 1) * 64)
                            heads.append(dict(q=QT[2 + c][rows, r4:512:4], kc=KT[2 + c][rows, cols],
                                              kp=(KT[2 + c][rows, pcols] if s > 0 else None),
                                              vc=VA1[vsl][:, h, :], vp=(VA1[psl][:, h, :] if s > 0 else None),
                                              od=OT[2 + c][rows, r4:512:4], dd=DT[2 + c][rows, r4:512:4],
                                              ok=("OT", 2 + c, r4, hh), dk=("DT", 2 + c, r4, hh)))
                        attn_pair(0, heads, PT, [("QT", 2 + c)], [("KT", 2 + c, 0), ("KT", 2 + c, 1)], [("VA1", vsl), ("VA1", psl)])
                attn_flush()
                nat = lambda a: a.rearrange("p (r m) -> p m r", r=16)
                v3 = lambda a: a.rearrange("p (m r) -> p m r", r=16)
                for cc in range(2):
                    k01 = [("DT", c, x, hh) for c in (cc, 2 + cc) for x in range(4) for hh in range(2)]
                    tt("pool", dsum[:], DT[cc][:], DT[2 + cc][:], ALU.add, k01, ["dsum"])
                    tt("pool", v3(dsum[:]), v3(dsum[:]), nat(G2in[2 + cc][:]), ALU.add, ["dsum", ("G2in", 2 + cc)], ["dsum"])
                    op("dve", lambda e, o=rinv[:], i=dsum[:]: e.reciprocal(out=o, in_=i), ["dsum"], ["rinv"])
                    for g in range(3):
                        c6 = 2 * g + cc
                        if g < 2:
                            src = OT[c6][:]
                            sk = [("OT", c6, x, hh) for x in range(4) for hh in range(2)]
                            tt("dve", otmp[:], src, rinv[:], ALU.mult, sk + ["rinv"], ["otmp"])
                        else:
                            tt("dve", v3(otmp[:]), nat(G2in[cc][:]), v3(rinv[:]), ALU.mult, [("G2in", cc), "rinv"], ["otmp"])
                        tt("pool", oTf[c6][:], otmp[:], gT[c6][:], ALU.mult, ["otmp", ("gT", c6)], [("oTf", c6)])
                for u in range(4):
                    i = 4 * s + u
                    dma("sp", xres[i % 2][:], xp[i * 128:(i + 1) * 128, :], (), [("xres", i % 2)])
                    out_proj_tile(i, u, oTf, [("oTf", c) for c in range(6)], Woa, 6, ggbc[0], xres, 64, x1s, "x1s", t12)
            P.flush()
        if os.environ.get("KSTOP") == "B":
            return nc

        with ExitStack() as st:
            Wq = sbt(st, "Wq", [128, 8, 2048], BF16)
            Wob = sbt(st, "Wob", [128, 8, 1024], BF16)
            Wkv = sbt(st, "Wkv", [128, 8, 384], BF16)
            wst = [sbt(st, "wstC%d" % i, [128, 1024], F32) for i in range(2)]
            xin = [sbt(st, "xinC%d" % i, [128, D], F32) for i in range(2)]
            xres = [sbt(st, "xresC%d" % i, [128, D], F32) for i in range(2)]
            xn = [sbt(st, "xnC%d" % i, [128, D], BF16) for i in range(2)]
            hT = sbt(st, "hTC", [128, 8, 512], BF16)
            hkT = sbt(st, "hkTC", [128, 8, 512], BF16)
            QT = [sbt(st, "QTC%d" % c, [128, 512], BF16) for c in range(8)]
            KT = [sbt(st, "KTC%d" % c, [128, 1024], BF16) for c in range(2)]
            VT = sbt(st, "VTC", [128, 1024], BF16)
            gT = [sbt(st, "gTC%d" % c, [128, 512], BF16) for c in range(8)]
            OT = [sbt(st, "OTC%d" % c, [128, 512], BF16) for c in range(8)]
            DT = [sbt(st, "DTC%d" % c, [128, 512], BF16) for c in range(8)]
            dsum = sbt(st, "dsumC", [128, 512], F32)
            rinv = sbt(st, "rinvC", [128, 512], F32)
            otmp = sbt(st, "otmpC", [128, 512], F32)
            ctab = [sbt(st, "ctabC%d" % i, [128, 512], F32) for i in range(2)]
            stab = [sbt(st, "stabC%d" % i, [128, 512], F32) for i in range(2)]
            qraw = [sbt(st, "qrawC%d" % i, [128, 512], BF16) for i in range(2)]
            t12 = sbt(st, "t12C", [128, 4, 512], F32)
            kvf = [sbt(st, "kvfC%d" % i, [128, 512], F32) for i in range(2)]
            stage = [sbt(st, "stageC%d" % i, [128, 256], F32) for i in range(2)]
            PT = [sbt(st, "PTC%d" % i, [128, 512], BF16) for i in range(2)]
            VAc = [sbt(st, "VAc%d" % i, [128, 2, 128], BF16) for i in range(4)]
            load_w(wst, Wq, w_in_b, 1024, 0, 2048, 0, "w")
            load_w(wst, Wob, w_o_b, 1024, 0, 1024, 0, "w")
            load_w(wst, Wkv, w_kvx, 1024, 0, 384, 0, "w")
            for v in VAc:
                memset("pool", v[:, :, 64:128], 1.0, [("w", "va")])
            P.flush()
            dma("sp", xin[0][:], x1s[0:128, :], (), [("xin", 0)])
            dma("sp", xin[1][:], x1s[128:256, :], (), [("xin", 1)])
            for s in range(NS):
                hs = s % 2
                ts_ = hs * 512
                tk = slice(512 * s, 512 * (s + 1))
                ring = slice(ts_, ts_ + 512)
                dma("sp", ctab[hs][:], cosT_d[:, tk], (), [("tab", hs)])
                dma("sp", stab[hs][:], sinT_d[:, tk], (), [("tab", hs)])
                for u in range(4):
                    i = 4 * s + u
                    norm_tile(i, u, xin, xn, 0, [(hT, "hT", A1, SH1), (hkT, "hkT", AKV, SHKV)], 0)
                    if i + 2 < NT:
                        dma("sp", xin[i % 2][:], x1s[(i + 2) * 128:(i + 3) * 128, :], (), [("xin", i % 2)])
                need_f = (s == 7)
                slot = project(hkT, "hkT", 0, Wkv, slice(0, 128))
                rope(slot, ctab[hs][:], stab[hs][:], ("tab", hs), qraw, t12, KT[0][:, ring], ("KT", 0, hs),
                     kvf[0][:] if need_f else None, ("kvf", 0))
                slot = project(hkT, "hkT", 0, Wkv, slice(256, 384))
                rope(slot, ctab[hs][:], stab[hs][:], ("tab", hs), qraw, t12, KT[1][:, ring], ("KT", 1, hs))
                slot = project(hkT, "hkT", 0, Wkv, slice(128, 256))
                cp("act", VT[:, ring], psP[:, slot, :], [("psP", slot)], [("VT", hs)])
                if need_f:
                    cp("dve", kvf[1][:], psP[:, slot, :], [("psP", slot)], [("kvf", 1)])
                    cache_out(kvf, [("kvf", 0), ("kvf", 1)], 3, 256, stage, bp[:, :], "stage")
                for cc in range(8):
                    slot = project(hT, "hT", 0, Wq, slice(cc * 128, (cc + 1) * 128))
                    rope(slot, ctab[hs][:], stab[hs][:], ("tab", hs), qraw, t12, QT[cc][:], ("QT", cc))
                for cc in range(8):
                    slot = project(hT, "hT", 0, Wq, slice(1024 + cc * 128, 1024 + (cc + 1) * 128))
                    act(gT[cc][:], psP[:, slot, :], AF.Silu, [("psP", slot)], [("gT", cc)])
                for u in range(4):
                    i = 4 * s + u
                    vsl = i % 4
                    ptv = psT[0]
                    cols = slice(ts_ + u * 128, ts_ + (u + 1) * 128)
                    if u > 0:
                        pcols = slice(ts_ + (u - 1) * 128, ts_ + u * 128)
                    else:
                        pcols = slice((1 - hs) * 512 + 384, (1 - hs) * 512 + 512)
                    tr(ptv[:, 0, :], VT[:, cols], identb[:], [("VT", hs)], [("psT", 0)])
                    cp(evac_eng(), VAc[vsl][:, :, 0:64], ptv[:, 0, :].rearrange("p (h d) -> p h d", h=2),
                       [("psT", 0)], [("VAc", vsl)])
                    for c in range(8):
                        kvh = c // 4
                        heads = []
                        for hh in range(2):
                            rows = slice(hh * 64, (hh + 1) * 64)
                            ksrc = KT[0] if (kvh == hh) else KT[1]
                            heads.append(dict(q=QT[c][rows, u * 128:(u + 1) * 128], kc=ksrc[rows, cols],
                                              kp=(ksrc[rows, pcols] if i > 0 else None),
                                              vc=VAc[vsl][:, kvh, :], vp=(VAc[(i - 1) % 4][:, kvh, :] if i > 0 else None),
                                              od=OT[c][rows, u * 128:(u + 1) * 128], dd=DT[c][rows, u * 128:(u + 1) * 128],
                                              ok=("OT", c, u, hh), dk=("DT", c, u, hh)))
                        attn_pair(1, heads, PT, [("QT", c)], [("KT", 0, 0), ("KT", 0, 1), ("KT", 1, 0), ("KT", 1, 1)],
                                  [("VAc", vsl), ("VAc", (i - 1) % 4)])
                attn_flush()
                for c in range(8):
                    dk = [("DT", c, x, hh) for x in range(4) for hh in range(2)]
                    okk = [("OT", c, x, hh) for x in range(4) for hh in range(2)]
                    act(dsum[:], DT[c][:], AF.Identity, dk, ["dsum"], bias=esink[:, c:c + 1])
                    op("dve", lambda e, o=rinv[:], i_=dsum[:]: e.reciprocal(out=o, in_=i_), ["dsum"], ["rinv"])
                    tt("dve", otmp[:], OT[c][:], rinv[:], ALU.mult, okk + ["rinv"], ["otmp"])
                    tt("pool", OT[c][:], otmp[:], gT[c][:], ALU.mult, ["otmp", ("gT", c)] + okk, [("oTf", c)] + okk)
                for u in range(4):
                    i = 4 * s + u
                    dma("sp", xres[i % 2][:], x1s[i * 128:(i + 1) * 128, :], (), [("xres", i % 2)])
                    out_proj_tile(i, u, OT, [("oTf", c) for c in range(8)], Wob, 8, ggbc[1], xres, 32, yp, None, t12)
            P.flush(final=True)
        print("[kernel] ops=%d waits=%d sig=%s" % (P.nops, P.nwait, P.cnt))
    return nc


_NC_CACHE = {}


def kernel(x_prompt, x_sample, c_prompt, c_sample, cache_a_kv_g0, cache_a_kv_g1, cache_a_kv_g2, cache_b_kv,
           ada_w, ada_b, g_pre, g_post, w_in_a, w_o_a, w_in_b, w_o_b, sinks_b, ada_kv_w, ada_kv_b, g_kv, w_kv,
           _cores=None):
    f = lambda a: np.ascontiguousarray(np.asarray(a, dtype=np.float32))
    x_prompt, x_sample, c_prompt, c_sample = f(x_prompt), f(x_sample), f(c_prompt), f(c_sample)
    ca = [f(cache_a_kv_g0), f(cache_a_kv_g1), f(cache_a_kv_g2)]
    cbk = f(cache_b_kv)
    ada_w, ada_b, g_pre, g_post = f(ada_w), f(ada_b), f(g_pre), f(g_post)
    w_in_a, w_o_a, w_in_b, w_o_b = f(w_in_a), f(w_o_a), f(w_in_b), f(w_o_b)
    sinks_b, ada_kv_w, ada_kv_b, g_kv, w_kv = f(sinks_b), f(ada_kv_w), f(ada_kv_b), f(g_kv), f(w_kv)
    cst = _consts()
    cores = list(range(NCORES)) if _cores is None else list(_cores)
    shared = dict(
        ada_w0=f(ada_w[0]), ada_w1=f(ada_w[1]), ada_kv_w=ada_kv_w,
        brow=f(np.concatenate([ada_b[0], ada_b[1], ada_kv_b])[None, :]),
        vecs=f(np.concatenate([g_pre[0].reshape(8, 128), g_pre[1].reshape(8, 128), g_post[0].reshape(8, 128),
                               g_post[1].reshape(8, 128), g_kv.reshape(8, 128)], axis=0)),
        gpost_rows=g_post, w_in_a=f(w_in_a[0]), w_o_a=f(w_o_a[0]), w_in_b=f(w_in_b[0]), w_o_b=f(w_o_b[0]),
        w_kvx=f(np.concatenate([w_kv[:, 0:256], w_kv[:, 64:128], w_kv[:, 0:64]], axis=1)),
        sink_fm=f(sinks_b[0].reshape(8, 2).T.repeat(64, axis=0)),
        sink_tm=f(np.tile(sinks_b[0][None, :], (DB, 1))),
        ident=cst["ident"], rmat=cst["rmat"], cosT=cst["cosT"], sinT=cst["sinT"], masks=cst["masks"], mbias=cst["mbias"],
        cs_tok=cst["cs_tok"], sn_tok=cst["sn_tok"], rowb_in=cst["rowb"], w_kv=w_kv,
        gvec_tm=f(np.stack([np.tile(v[None, :], (DB, 1)) for v in (g_pre[0], g_pre[1], g_post[0], g_post[1], g_kv)])),
    )
    in_maps = []
    for b in cores:
        crows = np.zeros((33, D), np.float32)
        crows[0:DB] = c_sample[DB * b:DB * (b + 1)]
        crows[32] = c_prompt[b]
        m = dict(shared)
        m.update(xp=x_prompt[b], crows=crows, xs=f(x_sample[DB * b:DB * (b + 1), 0, :]),
                 ca0=ca[0][0, DB * b:DB * (b + 1)].reshape(DB, 128, 512),
                 ca1=ca[1][0, DB * b:DB * (b + 1)].reshape(DB, 512, 512),
                 ca2=ca[2][0, DB * b:DB * (b + 1)].reshape(DB, 2048, 512),
                 cb=cbk[DB * b:DB * (b + 1)].reshape(DB, 128, 256))
        in_maps.append(m)
    if "nc" not in _NC_CACHE:
        _NC_CACHE["nc"] = build_nc()
    nc = _NC_CACHE["nc"]
    res = run_bass_kernel_spmd(nc, in_maps, core_ids=list(range(len(cores))))
    R = res.results
    n = len(cores)
    y_prompt = np.stack([R[k]["yp"] for k in range(n)])
    y_sample = np.concatenate([R[k]["ys"] for k in range(n)])[:, None, :]
    a_p = [np.stack([R[k]["a%dp" % g] for k in range(n)]).reshape(1, n, -1, 2, 4, 64) for g in range(3)]
    b_p = np.stack([R[k]["bp"] for k in range(n)]).reshape(n, 128, 2, 2, 64)
    a_s = [np.concatenate([R[k]["a%ds" % g] for k in range(n)]).reshape(1, n * DB, -1, 2, 4, 64) for g in range(3)]
    b_s = np.concatenate([R[k]["bs"] for k in range(n)]).reshape(n * DB, 128, 2, 2, 64)
    return (y_prompt, y_sample, a_p[0], a_p[1], a_p[2], b_p, a_s[0], a_s[1], a_s[2], b_s)
```

```python
import os
import numpy as np
from contextlib import ExitStack
import concourse.bass as bass
import concourse.mybir as mybir
from concourse.bass_utils import run_bass_kernel_spmd

F32 = mybir.dt.float32
BF16 = mybir.dt.bfloat16
ALU = mybir.AluOpType
AF = mybir.ActivationFunctionType

NCORES = 8
T = 4096
D = 1024
NT = 32
NS = 8
EPS = 1e-6
PAST = 16384
DB = 16


class Prog:
    CE = ("pe", "act", "dve", "pool")
    ENG = ("pe", "act", "dve", "pool", "sp")

    def __init__(self, nc, stack, ndsem=20):
        self.nc = nc
        self.eng = dict(pe=nc.tensor, act=nc.scalar, dve=nc.vector, pool=nc.gpsimd, sp=nc.sync)
        self.ops = []
        self.last_w = {}
        self.readers = {}
        self.esem = {e: stack.enter_context(nc.semaphore("S_" + e)) for e in self.CE}
        self.cnt = {e: 0 for e in self.CE}
        self.dsem = {e: [stack.enter_context(nc.semaphore("D_%s%d" % (e, i))) for i in range(ndsem)]
                     for e in ("sp", "pool", "act")}
        self.dcum = {e: [0] * ndsem for e in self.dsem}
        self.dnext = {e: 0 for e in self.dsem}
        self.waited = {e: {} for e in self.ENG}
        self.nwait = 0
        self.nops = 0

    @staticmethod
    def _isps(k):
        n = k[0] if isinstance(k, tuple) else k
        return isinstance(n, str) and n.startswith("ps")

    def op(self, eng, fn, reads=(), writes=(), dma=False, rg=None):
        writes = list(writes) + [r for r in reads if self._isps(r)]
        reads = [r for r in reads if not self._isps(r)]
        i = len(self.ops)
        raw = set()
        oth = set()
        for r in reads:
            if r in self.last_w:
                raw.add(self.last_w[r])
        for w in writes:
            if w in self.last_w:
                oth.add(self.last_w[w])
            for j in self.readers.get(w, {}).values():
                oth.add(j)
        raw.discard(i)
        oth.discard(i)
        self.ops.append(dict(eng=eng, fn=fn, raw=raw, oth=oth - raw, dma=dma, rg=rg))
        for r in reads:
            self.readers.setdefault(r, {})[(eng if not dma else ("dma", i))] = i
        for w in writes:
            self.last_w[w] = i
            self.readers[w] = {}
        return i

    def _needs_wait(self, o, p, is_raw):
        if p["dma"] or o["dma"]:
            return True
        if p["eng"] != o["eng"]:
            return True
        if o["eng"] == "pe":
            return o["rg"] is not None and p["rg"] is not None and o["rg"] != p["rg"]
        return is_raw

    def _wait(self, e, sem, val):
        k = id(sem)
        if self.waited[e].get(k, 0) >= val:
            return
        self.eng[e].wait_ge(sem, val)
        self.waited[e][k] = val
        self.nwait += 1

    def flush(self, final=False):
        ops = self.ops
        n = len(ops)
        need = [False] * n
        for o in ops:
            for d in o["raw"]:
                if self._needs_wait(o, ops[d], True):
                    need[d] = True
            for d in o["oth"]:
                if self._needs_wait(o, ops[d], False):
                    need[d] = True
        last = {}
        for i, o in enumerate(ops):
            if not o["dma"]:
                last[o["eng"]] = i
        for i in last.values():
            need[i] = True
        sig = [None] * n
        for i, o in enumerate(ops):
            e = o["eng"]
            eo = self.eng[e]
            deps = [(d, True) for d in o["raw"]] + [(d, False) for d in o["oth"]]
            for d, is_raw in sorted(deps):
                p = ops[d]
                if not self._needs_wait(o, p, is_raw):
                    continue
                s, v = sig[d]
                self._wait(e, s, v)
            if o["dma"]:
                k = self.dnext[e]
                self.dnext[e] = (k + 1) % len(self.dsem[e])
                s = self.dsem[e][k]
                self._wait(e, s, self.dcum[e][k])
                ins = o["fn"](eo)
                self.dcum[e][k] += 16
                ins.then_inc(s, 16)
                sig[i] = (s, self.dcum[e][k])
            else:
                ins = o["fn"](eo)
                if need[i]:
                    self.cnt[e] += 1
                    ins.then_inc(self.esem[e], 1)
                    sig[i] = (self.esem[e], self.cnt[e])
        for e in self.ENG:
            for x in self.CE:
                if x != e and self.cnt[x] > 0:
                    self._wait(e, self.esem[x], self.cnt[x])
            for q in self.dsem:
                if q == "pool" and not final:
                    continue
                for k, s in enumerate(self.dsem[q]):
                    if self.dcum[q][k] > 0:
                        self._wait(e, s, self.dcum[q][k])
        self.nops += n
        print("[flush] ops", n, "cnt", self.cnt, "dcum max", {q: max(v) for q, v in self.dcum.items()})
        self.ops = []
        self.last_w = {}
        self.readers = {}


def _consts():
    half = 32
    inv = (10000.0 ** (-np.arange(half, dtype=np.float32) / half)).astype(np.float32)
    pos = np.arange(T, dtype=np.float32)
    ang = pos[:, None] * inv[None, :]
    cos = np.cos(ang).astype(np.float32)
    sin = np.sin(ang).astype(np.float32)
    fi = (np.arange(128) % 64) % 32
    cosT = np.ascontiguousarray(cos[:, fi].T)
    sinT = np.ascontiguousarray(sin[:, fi].T)
    angs = (np.float32(PAST) * inv).astype(np.float32)
    cs = np.cos(angs).astype(np.float32)
    ss = np.sin(angs).astype(np.float32)
    cs_tok = np.tile(np.concatenate([cs, cs])[None, :], (DB, 1)).astype(np.float32)
    sn_tok = np.tile(np.concatenate([-ss, ss])[None, :], (DB, 1)).astype(np.float32)
    rmat = np.zeros((128, 128), np.float32)
    for po in range(128):
        if (po % 64) < 32:
            rmat[po + 32, po] = -1.0
        else:
            rmat[po - 32, po] = 1.0
    k = np.arange(128)[:, None]
    q = np.arange(128)[None, :]
    cur = (k <= q).astype(np.float32)
    prev0 = (k >= q).astype(np.float32)
    prev1 = (k > q).astype(np.float32)
    m0 = np.concatenate([prev0, cur, prev0, cur], axis=1)
    m1 = np.concatenate([prev1, cur, prev1, cur], axis=1)
    masks = np.stack([m0, m1]).astype(np.float32)
    mbias = ((1.0 - masks[:, :, 0:256]) * -30000.0).astype(np.float32)
    rowb = np.zeros((128, 1), np.float32)
    rowb[0, 0] = -30000.0
    return dict(mbias=mbias, rowb=rowb, cosT=cosT, sinT=sinT, cs_tok=cs_tok, sn_tok=sn_tok, rmat=rmat, masks=masks,
                ident=np.eye(128, dtype=np.float32))


class _Stop(Exception):
    pass


def build_nc():
    nc = bass.Bass("TRN2", target_bir_lowering=False)
    try:
        return _build(nc)
    except _Stop:
        return nc


def _build(nc):

    def din(name, shape, dt=F32):
        return nc.dram_tensor(name, list(shape), dt, kind="ExternalInput").ap()

    def dout(name, shape, dt=F32):
        return nc.dram_tensor(name, list(shape), dt, kind="ExternalOutput").ap()

    xp = din("xp", [T, D])
    crows = din("crows", [33, D])
    xs = din("xs", [DB, D])
    ca = [din("ca0", [DB, 128, 512]), din("ca1", [DB, 512, 512]), din("ca2", [DB, 2048, 512])]
    cb = din("cb", [DB, 128, 256])
    ada_w0 = din("ada_w0", [D, 3072])
    ada_w1 = din("ada_w1", [D, 3072])
    ada_kv_w = din("ada_kv_w", [D, 2048])
    brow = din("brow", [1, 8192])
    vecs = din("vecs", [40, 128])
    gpost_rows = din("gpost_rows", [2, D])
    w_in_a = din("w_in_a", [D, 3072])
    w_o_a = din("w_o_a", [768, D])
    w_in_b = din("w_in_b", [D, 2048])
    w_o_b = din("w_o_b", [D, D])
    w_kvx = din("w_kvx", [D, 384])
    sink_fm = din("sink_fm", [128, 8])
    sink_tm = din("sink_tm", [DB, 16])
    ident_d = din("ident", [128, 128])
    rmat_d = din("rmat", [128, 128])
    cosT_d = din("cosT", [128, T])
    sinT_d = din("sinT", [128, T])
    masks_d = din("masks", [2, 128, 512])
    mbias_d = din("mbias", [2, 128, 256])
    cs_tok_d = din("cs_tok", [DB, 64])
    sn_tok_d = din("sn_tok", [DB, 64])
    rowb_d = din("rowb_in", [128, 1])
    gvec_tm = din("gvec_tm", [5, DB, D])
    w_kv_d = din("w_kv", [D, 256])

    yp = dout("yp", [T, D])
    ys = dout("ys", [DB, D])
    aop = [dout("a0p", [128, 512]), dout("a1p", [512, 512]), dout("a2p", [2048, 512])]
    bp = dout("bp", [128, 256])
    aos = [dout("a0s", [DB, 128, 512]), dout("a1s", [DB, 512, 512]), dout("a2s", [DB, 2048, 512])]
    bs = dout("bs", [DB, 128, 256])
    x1s = nc.dram_tensor("x1s", [T, D], F32, kind="Internal").ap()
    g2s = nc.dram_tensor("g2s", [4, 128, T], BF16, kind="Internal").ap()

    with ExitStack() as top:
        P = Prog(nc, top)
        op = P.op

        def stop(tag):
            if os.environ.get("KSTOP") == tag:
                P.flush()
                raise _Stop()

        def sbt(st, name, shape, dt):
            return st.enter_context(nc.sbuf_tensor(name, list(shape), dt))

        def pst(st, name, shape, dt):
            return st.enter_context(nc.psum_tensor(name, list(shape), dt))

        identf = sbt(top, "identf", [128, 128], F32)
        identb = sbt(top, "identb", [128, 128], BF16)
        rmf = sbt(top, "rmf", [128, 128], F32)
        rmb = sbt(top, "rmb", [128, 128], BF16)
        maskb = [sbt(top, "maskb%d" % l, [128, 512], BF16) for l in range(2)]
        mbb = [sbt(top, "mbb%d" % l, [128, 256], BF16) for l in range(2)]
        vecT = sbt(top, "vecT", [128, 40], F32)
        modT = sbt(top, "modT", [128, 64], F32)
        amod = sbt(top, "amod", [128, 24], F32)
        ggbc = [sbt(top, "ggbc%d" % l, [128, D], F32) for l in range(2)]
        esink = sbt(top, "esink", [128, 8], F32)
        ones_b = sbt(top, "ones_b", [128, 64], BF16)
        msq = sbt(top, "msq", [128, 4 * NT], F32)
        msq2 = sbt(top, "msq2", [128, 2 * NT], F32)
        rstd = sbt(top, "rstd", [128, 4 * NT], F32)
        junk = sbt(top, "junk", [128, D], BF16)

        psT = [pst(top, "psT0", [128, 8, 128], BF16)]
        psP = pst(top, "psP", [128, 2, 512], F32)
        psR = pst(top, "psR", [128, 512], F32)
        psS = [pst(top, "psS%d" % i, [128, 512], F32) for i in range(2)]
        psT.append(psR[:].bitcast(BF16).rearrange("p (k t) -> p k t", k=8))
        psOs = [pst(top, "psO%d" % i, [128, 512], F32) for i in range(2)]
        psOf = psOs[0]

        def dma(q, out, in_, reads=(), writes=()):
            return op(q, lambda e, o=out, i=in_: e.dma_start(out=o, in_=i), reads, writes, dma=True)

        def mm(out, lhsT, rhs, start, stop, reads, writes, rg=None):
            return op("pe", lambda e, o=out, l=lhsT, r=rhs, a=start, b=stop: e.matmul(o, lhsT=l, rhs=r, start=a, stop=b),
                      reads, writes, rg=rg)

        def tr(out, in_, ident, reads, writes):
            return op("pe", lambda e, o=out, i=in_, d=ident: e.transpose(out=o, in_=i, identity=d), reads, writes)

        def act(out, in_, func, reads, writes, **kw):
            return op("act", lambda e, o=out, i=in_, f=func, k=kw: e.activation(out=o, in_=i, func=f, **k), reads, writes)

        def cp(eng, out, in_, reads, writes):
            if eng == "act":
                return act(out, in_, AF.Copy, reads, writes)
            return op(eng, lambda e, o=out, i=in_: e.tensor_copy(out=o, in_=i), reads, writes)

        def tt(eng, out, in0, in1, alu, reads, writes):
            return op(eng, lambda e, o=out, a=in0, b=in1, f=alu: e.tensor_tensor(out=o, in0=a, in1=b, op=f), reads, writes)

        def ts(eng, out, in0, s1, s2, op0, op1, reads, writes):
            if s2 is None:
                return op(eng, lambda e, o=out, a=in0, x=s1, f=op0: e.tensor_scalar(out=o, in0=a, scalar1=x, scalar2=None, op0=f),
                          reads, writes)
            return op(eng, lambda e, o=out, a=in0, x=s1, y=s2, f=op0, g=op1:
                      e.tensor_scalar(out=o, in0=a, scalar1=x, scalar2=y, op0=f, op1=g), reads, writes)

        def stt(eng, out, in0, scalar, in1, op0, op1, reads, writes):
            return op(eng, lambda e, o=out, a=in0, s=scalar, b=in1, f=op0, g=op1:
                      e.scalar_tensor_tensor(out=o, in0=a, scalar=s, in1=b, op0=f, op1=g), reads, writes)

        def memset(eng, ap, val, writes):
            return op(eng, lambda e, a=ap, v=val: e.memset(a, v), (), writes)

        cast_rr = [0]

        def load_w(st_slots, dst, src, rows, col0, ncols, dcol0, key):
            nk = rows // 128
            for kc in range(nk):
                for c0 in range(0, ncols, 1024):
                    w = min(1024, ncols - c0)
                    sl = cast_rr[0] % len(st_slots)
                    cast_rr[0] += 1
                    stg = st_slots[sl]
                    dma("sp", stg[:, 0:w], src[kc * 128:(kc + 1) * 128, col0 + c0:col0 + c0 + w], (), [("wst", sl)])
                    eng = ("pool", "dve", "pool", "act")[cast_rr[0] % 4]
                    cp(eng, dst[:, kc, dcol0 + c0:dcol0 + c0 + w], stg[:, 0:w], [("wst", sl)], [key])

        rr = dict(p=0, r=0, s=0, o=0, e=0, w=0)

        def nxt(k, n=2):
            v = rr[k] % n
            rr[k] += 1
            return v

        def evac_eng():
            return ("act", "dve")[nxt("e")]

        AX = mybir.AxisListType

        with ExitStack() as st:
            crow_t = sbt(st, "crow_t", [33, D], F32)
            sc_t = sbt(st, "sc_t", [33, D], F32)
            cT = sbt(st, "cT", [128, 8, 33], F32)
            vec_t = sbt(st, "vec_t", [40, 128], F32)
            ones33 = sbt(st, "ones33", [33, 128], F32)
            adat = [sbt(st, "adat0", [33, 3072], F32), sbt(st, "adat1", [33, 2048], F32)]
            gprow = sbt(st, "gprow", [33, D], F32)
            wstg = [sbt(st, "adaw%d" % i, [128, 3072], F32) for i in range(2)]
            psA = [psS[0], psS[1], psR, psOf, psP[:, 0, :], psP[:, 1, :]]
            sink_t = sbt(st, "sink_t", [128, 8], F32)
            maskf = sbt(st, "maskf", [128, 512], F32)
            xs_t = sbt(st, "xs_t", [DB, D], F32)
            x1_t = sbt(st, "x1_t", [DB, D], F32)
            gv = [sbt(st, "gv%d" % i, [DB, D], F32) for i in range(3)]
            cs_t = sbt(st, "cs_t", [DB, 64], F32)
            sn_t = sbt(st, "sn_t", [DB, 64], F32)
            stm = sbt(st, "stm", [DB, 16], F32)
            rowb = sbt(st, "rowb", [128, 1], F32)
            hs = sbt(st, "hs", [DB, D], F32)
            hs2 = sbt(st, "hs2", [DB, D], F32)
            hsT = sbt(st, "hsT", [128, 8, DB], F32)
            hkTs = sbt(st, "hkTs", [128, 8, DB], F32)
            proj = sbt(st, "proj", [33, 3072], F32)
            brow_t = proj
            kvp = sbt(st, "kvp", [DB, 256], F32)
            rop = sbt(st, "rop", [DB, 1536], F32)
            swp = sbt(st, "swp", [DB, 1536], F32)
            krow = [sbt(st, "krow%d" % i, [DB, 512], F32) for i in range(4)]
            KVflat = sbt(st, "KVflat", [128, DB * 520], F32)
            prod = sbt(st, "prod", [128, 1024], F32)
            scr = sbt(st, "scr", [128, 16], F32)
            PZ = sbt(st, "PZ", [128, DB, 16, DB], F32)
            Oc = sbt(st, "Oc", [DB, 16, 65], F32)
            Ot = sbt(st, "Ot", [DB, 16, 64], F32)
            sml = sbt(st, "sml", [DB, 128], F32)

            for g, L in enumerate((128, 512, 2048)):
                for b in range(DB):
                    dma("pool", aos[g][b, 0:L - 1, :], ca[g][b, 1:L, :], (), ())
            dma("pool", bs[:, 0:127, :], cb[:, 1:128, :], (), ())

            dma("sp", identf[:], ident_d[:, :], (), ["identf"])
            dma("sp", rmf[:], rmat_d[:, :], (), ["rmf"])
            dma("sp", crow_t[:], crows[:, :], (), ["crow"])
            dma("sp", vec_t[:], vecs[:, :], (), ["vec_t"])
            dma("sp", sink_t[:], sink_fm[:, :], (), ["sink_t"])
            dma("sp", xs_t[:], xs[:, :], (), ["xs_t"])
            dma("sp", cs_t[:], cs_tok_d[:, :], (), ["cs_t"])
            dma("sp", sn_t[:], sn_tok_d[:, :], (), ["sn_t"])
            dma("sp", stm[:], sink_tm[:, :], (), ["stm"])
            dma("sp", rowb[:], rowb_d[:, :], (), ["rowb"])
            for l in range(2):
                dma("sp", maskf[:], masks_d[l], (), ["maskf"])
                cp("dve", maskb[l][:], maskf[:], ["maskf"], ["maskb%d" % l])
                dma("sp", maskf[:, 0:256], mbias_d[l], ["maskf"], ["maskf"])
                cp("dve", mbb[l][:], maskf[:, 0:256], ["maskf"], ["mbb%d" % l])
            cp("dve", identb[:], identf[:], ["identf"], ["identb"])
            cp("dve", rmb[:], rmf[:], ["rmf"], ["rmb"])
            memset("dve", ones_b[:], 1.0, ["ones_b"])
            memset("dve", ones33[:], 1.0, ["ones33"])
            memset("pool", PZ[:], 0.0, ["PZ"])
            act(esink[:], sink_t[:], AF.Exp, ["sink_t"], ["esink"])
            act(stm[:], stm[:], AF.Exp, ["stm"], ["stm"])
            tr(psOf[:, 0:40], vec_t[:, :], identf[0:40, 0:40], ["vec_t", "identf"], [("psA", 3)])
            cp("dve", vecT[:], psOf[:, 0:40], [("psA", 3)], ["vecT"])
            act(sc_t[:], crow_t[:], AF.Silu, ["crow"], ["sc_t"])
            for kc in range(8):
                tr(psOf[:, 64 + kc * 33:64 + (kc + 1) * 33], sc_t[:, kc * 128:(kc + 1) * 128], identf[0:33, 0:33],
                   ["sc_t", "identf"], [("psA", 3)])
            cp("dve", cT[:].rearrange("p k t -> p (k t)"), psOf[:, 64:64 + 8 * 33], [("psA", 3)], ["cT"])

            def stream_mm(lhsT_of, lkey, M, wd, nk, ncols, dst, dkey, bias_off=None):
                nb = (ncols + 511) // 512
                if bias_off is not None:
                    dma("sp", brow_t[32:33, 0:ncols], brow[:, bias_off:bias_off + ncols], (), ["brow_t"])
                for kc in range(nk):
                    sl = nxt("w")
                    dma("sp", wstg[sl][:, 0:ncols], wd[kc * 128:(kc + 1) * 128, 0:ncols], (), [("adaw", sl)])
                    for b in range(nb):
                        w = min(512, ncols - b * 512)
                        mm(psA[b][0:M, 0:w], lhsT_of(kc), wstg[sl][:, b * 512:b * 512 + w], kc == 0,
                           (kc == nk - 1) and bias_off is None, [lkey, ("adaw", sl)], [("psA", b)])
                for b in range(nb):
                    w = min(512, ncols - b * 512)
                    if bias_off is not None:
                        mm(psA[b][0:M, 0:w], ones33[32:33, 0:M], brow_t[32:33, b * 512:b * 512 + w], False, True,
                           ["ones33", "brow_t"], [("psA", b)])
                    cp("dve" if b % 2 else "act", dst[0:M, b * 512:b * 512 + w], psA[b][0:M, 0:w], [("psA", b)], [(dkey, b)])
                return [(dkey, b) for b in range(nb)]

            def prompt_cols(ad, akeys, ncols, mcol):
                nch = ncols // 128
                for j in range(nch):
                    mm(psS[0][:, j:j + 1], ad[32:33, j * 128:(j + 1) * 128], ones33[32:33, 0:1], True, True,
                       akeys + ["ones33"], [("psA", 0)])
                cp("dve", modT[:, mcol:mcol + nch], psS[0][:, 0:nch], [("psA", 0)], ["modT"])

            def gate_bcast(ad, akeys, li):
                dma("sp", gprow[32:33, :], gpost_rows[li:li + 1, :], (), ["gprow"])
                tt("dve", gprow[32:33, :], gprow[32:33, :], ad[32:33, 2048:3072], ALU.mult, akeys + ["gprow"], ["gprow"])
                for b in range(2):
                    mm(psS[1][:], ones33[32:33, :], gprow[32:33, b * 512:(b + 1) * 512], True, True,
                       ["ones33", "gprow"], [("psA", 1)])
                    cp("act", ggbc[li][:, b * 512:(b + 1) * 512], psS[1][:], [("psA", 1)], ["ggbc%d" % li])

            scol = [100]

            def rms_tok(x_t, xkeys):
                c = scol[0]
                scol[0] += 1
                act(junk[0:DB, :], x_t[:], AF.Square, xkeys, [("msq", c)], scale=1.0 / 32, accum_out=msq[0:DB, c:c + 1])
                act(rstd[0:DB, c:c + 1], msq[0:DB, c:c + 1], AF.Ln, [("msq", c)], [("rstd", c)], bias=EPS)
                act(rstd[0:DB, c:c + 1], rstd[0:DB, c:c + 1], AF.Exp, [("rstd", c)], [("rstd", c)], scale=-0.5)
                return rstd[0:DB, c:c + 1], ("rstd", c)

            def modulate_tok(dst, dkey, x_t, xkeys, rs, rkey, g_t, gkey, scale_ap, shift_ap, akeys):
                ts("dve", dst[:], x_t[:], rs, None, ALU.mult, None, xkeys + [rkey], [dkey])
                tt("dve", dst[:], dst[:], g_t[:], ALU.mult, [dkey, gkey], [dkey])
                stt("dve", dst[:], scale_ap, 1.0, dst[:], ALU.add, ALU.mult, akeys + [dkey], [dkey])
                tt("dve", dst[:], dst[:], shift_ap, ALU.add, akeys + [dkey], [dkey])

            def transpose_tok(dstT, dkey, src, skeys, nk):
                for kc in range(nk):
                    tr(psOf[:, kc * DB:(kc + 1) * DB], src[0:DB, kc * 128:(kc + 1) * 128], identf[0:DB, 0:DB], skeys, [("psA", 3)])
                cp("dve", dstT[:, 0:nk, :].rearrange("p k t -> p (k t)"), psOf[:, 0:nk * DB], [("psA", 3)], [dkey])

            def rope_tok(dst, dkey, src, skeys, nh, sw):
                X = src.rearrange("p (h d) -> p h d", d=64)
                R = dst.rearrange("p (h d) -> p h d", d=64)
                S = sw.rearrange("p (h d) -> p h d", d=64)
                bc = lambda a: a.unsqueeze(1).to_broadcast([DB, nh, a.shape[1]])
                tt("dve", R, X, bc(cs_t[:, :]), ALU.mult, skeys + ["cs_t"], [dkey])
                tt("pool", S[:, :, 0:32], X[:, :, 32:64], bc(sn_t[:, 0:32]), ALU.mult, skeys + ["sn_t"], ["swp"])
                tt("pool", S[:, :, 32:64], X[:, :, 0:32], bc(sn_t[:, 32:64]), ALU.mult, skeys + ["sn_t"], ["swp"])
                tt("dve", R, R, S, ALU.add, [dkey, "swp"], [dkey])

            def pv_slot(hd):
                bk = hd // 7
                t = (psP[:, 0, :], psP[:, 1, :], psR)[bk]
                return t[0:DB, (hd % 7) * 65:(hd % 7) * 65 + 65], ("psA", (4, 5, 2)[bk])

            k0 = stream_mm(lambda kc: cT[:, kc, :], "cT", 33, ada_w0, 8, 3072, adat[0], "ad0", bias_off=0)
            prompt_cols(adat[0], k0, 3072, 0)
            gate_bcast(adat[0], k0, 0)

            dma("sp", gv[0][:], gvec_tm[0], (), [("gv", 0)])
            dma("sp", gv[1][:], gvec_tm[2], (), [("gv", 1)])
            rs, rk = rms_tok(xs_t, ["xs_t"])
            modulate_tok(hs, "hs", xs_t, ["xs_t"], rs, rk, gv[0], ("gv", 0), adat[0][0:DB, 1024:2048], adat[0][0:DB, 0:1024], k0)
            transpose_tok(hsT, "hsT", hs, ["hs"], 8)
            pk = stream_mm(lambda kc: hsT[:, kc, :], "hsT", DB, w_in_a, 8, 3072, proj, "proj")
            rope_tok(rop[:, 0:1536], "rop", proj[0:DB, 0:1536], pk, 24, swp[:, 0:1536])
            for g, L in enumerate((128, 512, 2048)):
                cp("pool", krow[g][:, 0:256], rop[:, 768 + g * 256:768 + (g + 1) * 256], ["rop"], [("krow", g)])
                cp("pool", krow[g][:, 256:512], proj[0:DB, 1536 + g * 256:1536 + (g + 1) * 256], pk, [("krow", g)])
                dma("sp", aos[g][:, L - 1, :], krow[g][:], [("krow", g)], ())
            KV0 = KVflat[:].rearrange("p (b t h d) -> p b t h d", b=DB, t=2, h=4)
            memset("pool", KV0[:, :, 1, :, 64:65], 1.0, ["KVones"])
            for g, (L, dil) in enumerate(((128, 1), (512, 4), (2048, 16))):
                for b in range(DB):
                    dma("sp", KV0[:, b, :, :, 0:64], ca[g][b, 0:L:dil, :].rearrange("j (t h d) -> j t h d", t=2, h=4),
                        ["KVones"], [("KV", b)])
                for b in range(DB):
                    sl = b % 2
                    mm(psS[sl][:, 0:256], identf[0:DB, b:b + 1].to_broadcast([DB, 128]), rop[0:DB, g * 256:(g + 1) * 256], True, True,
                       ["rop", "identf"], [("psA", sl)])
                    p3 = prod[:, 0:256].rearrange("p (h d) -> p h d", h=4)
                    tt("dve", p3, KV0[:, b, 0, :, 0:64], psS[sl][:, 0:256].rearrange("p (h d) -> p h d", h=4), ALU.mult,
                       [("KV", b), ("psA", sl)], ["prod"])
                    op("dve", lambda e, o=scr[:, 0:4], i_=p3: e.tensor_reduce(out=o, in_=i_, axis=AX.X, op=ALU.add), ["prod"], ["scr"])
                    act(PZ[:, b, g * 4:(g + 1) * 4, b], scr[:, 0:4], AF.Exp, ["scr", "PZ"], [("PZ", b)], scale=0.125)
                for h in range(4):
                    hd = g * 4 + h
                    o, okey = pv_slot(hd)
                    for b in range(DB):
                        mm(o, PZ[:, b, hd, :], KV0[:, b, 1, h, :], b == 0, b == DB - 1, [("PZ", b), ("KV", b), "KVones"], [okey])
            cp("dve", Oc[:, 0:7, :], psP[0:DB, 0, 0:455].rearrange("p (h d) -> p h d", d=65), [("psA", 4)], ["Oc"])
            cp("dve", Oc[:, 7:12, :], psP[0:DB, 1, 0:325].rearrange("p (h d) -> p h d", d=65), [("psA", 5)], ["Oc"])

            def finish_attn(nh, qv, kvw, vv, keys):
                q3 = qv.rearrange("p (h d) -> p h d", d=64)
                pr = prod[0:DB, 0:nh * 64].rearrange("p (h d) -> p h d", d=64)
                tt("dve", pr, q3, kvw, ALU.mult, keys, ["prod"])
                op("dve", lambda e, o=sml[:, 32:32 + nh], i_=pr: e.tensor_reduce(out=o, in_=i_, axis=AX.X, op=ALU.add), ["prod"], ["sml"])
                act(sml[:, 32:32 + nh], sml[:, 32:32 + nh], AF.Exp, ["sml"], ["sml"], scale=0.125)
                tt("dve", pr, vv, sml[:, 32:32 + nh].unsqueeze(2).to_broadcast([DB, nh, 64]), ALU.mult, keys + ["sml", "prod"], ["prod"])
                tt("dve", Ot[:, 0:nh, :], Oc[:, 0:nh, 0:64], pr, ALU.add, ["Oc", "prod"], ["Ot"])
                tt("dve", sml[:, 0:nh], Oc[:, 0:nh, 64], sml[:, 32:32 + nh], ALU.add, ["Oc", "sml"], ["sml"])

            k3 = rop[:, 768:1536].rearrange("p (h d) -> p h d", d=64)
            v3 = proj[0:DB, 1536:2304].rearrange("p (h d) -> p h d", d=64)
            finish_attn(12, rop[:, 0:768], k3, v3, ["rop"] + pk)
            tt("dve", sml[:, 64:68], sml[:, 0:4], sml[:, 4:8], ALU.add, ["sml"], ["sml"])
            tt("dve", sml[:, 64:68], sml[:, 64:68], sml[:, 8:12], ALU.add, ["sml"], ["sml"])
            op("dve", lambda e, o=sml[:, 64:68], i_=sml[:, 64:68]: e.reciprocal(out=o, in_=i_), ["sml"], ["sml"])
            og = hs2[:, 0:768]
            for g in range(3):
                tt("dve", og[:, g * 256:(g + 1) * 256].rearrange("p (h d) -> p h d", d=64), Ot[:, g * 4:(g + 1) * 4, :],
                   sml[:, 64:68].unsqueeze(2).to_broadcast([DB, 4, 64]), ALU.mult, ["Ot", "sml"], ["og"])
            act(hs[:, 0:768], proj[0:DB, 2304:3072], AF.Silu, pk + ["hs"], ["hs"])
            tt("dve", og, og, hs[:, 0:768], ALU.mult, ["og", "hs"], ["og"])
            transpose_tok(hsT, "hsT", og, ["og"], 6)
            mk = stream_mm(lambda kc: hsT[:, kc, :], "hsT", DB, w_o_a, 6, 1024, hs, "mix")
            rs, rk = rms_tok(hs, mk)
            ts("dve", hs[:], hs[:], rs, None, ALU.mult, None, mk + [rk], ["mixn"])
            tt("dve", hs[:], hs[:], gv[1][:], ALU.mult, ["mixn", ("gv", 1)], ["mixn"])
            tt("dve", hs[:], hs[:], adat[0][0:DB, 2048:3072], ALU.mult, ["mixn"] + k0, ["mixn"])
            tt("dve", x1_t[:], hs[:], xs_t[:], ALU.add, ["mixn", "xs_t"], ["x1_t"])

            k1 = stream_mm(lambda kc: cT[:, kc, :], "cT", 33, ada_w1, 8, 3072, adat[0], "ad0", bias_off=3072)
            prompt_cols(adat[0], k1, 3072, 24)
            gate_bcast(adat[0], k1, 1)
            kk = stream_mm(lambda kc: cT[:, kc, :], "cT", 33, ada_kv_w, 8, 2048, adat[1], "ad1", bias_off=6144)
            prompt_cols(adat[1], kk, 2048, 48)
            for (dst, sc, gvv) in ((amod[:, 0:8], modT[:, 8:16], vecT[:, 0:8]), (amod[:, 8:16], modT[:, 32:40], vecT[:, 8:16]),
                                   (amod[:, 16:24], modT[:, 56:64], vecT[:, 32:40])):
                ts("dve", dst, sc, 1.0, None, ALU.add, None, ["modT"], ["amod"])
                tt("dve", dst, dst, gvv, ALU.mult, ["amod", "vecT"], ["amod"])

            dma("sp", gv[0][:], gvec_tm[1], (), [("gv", 0)])
            dma("sp", gv[1][:], gvec_tm[3], (), [("gv", 1)])
            dma("sp", gv[2][:], gvec_tm[4], (), [("gv", 2)])
            rs, rk = rms_tok(x1_t, ["x1_t"])
            modulate_tok(hs, "hs", x1_t, ["x1_t"], rs, rk, gv[0], ("gv", 0), adat[0][0:DB, 1024:2048], adat[0][0:DB, 0:1024], k1)
            modulate_tok(hs2, "hs2", x1_t, ["x1_t"], rs, rk, gv[2], ("gv", 2), adat[1][0:DB, 1024:2048], adat[1][0:DB, 0:1024], kk)
            transpose_tok(hsT, "hsT", hs, ["hs"], 8)
            transpose_tok(hkTs, "hkTs", hs2, ["hs2"], 8)
            kvk = stream_mm(lambda kc: hkTs[:, kc, :], "hkTs", DB, w_kv_d, 8, 256, kvp, "kvp")
            pk = stream_mm(lambda kc: hsT[:, kc, :], "hsT", DB, w_in_b, 8, 2048, proj, "proj")
            rope_tok(rop[:, 0:1024], "rop", proj[0:DB, 0:1024], pk, 16, swp[:, 0:1024])
            rope_tok(rop[:, 1024:1152], "ropk", kvp[:, 0:128], kvk, 2, swp[:, 1024:1152])
            cp("pool", krow[3][:, 0:128], rop[:, 1024:1152], ["ropk"], [("krow", 3)])
            cp("pool", krow[3][:, 128:256], kvp[:, 128:256], kvk, [("krow", 3)])
            dma("sp", bs[:, 127, :], krow[3][:, 0:256], [("krow", 3)], ())
            KV1 = KVflat[:, 0:DB * 260].rearrange("p (b t h d) -> p b t h d", b=DB, t=2, h=2)
            allkv = [("KV", b) for b in range(DB)]
            memset("pool", KV1[:, :, 1, :, 64:65], 1.0, allkv + ["KVones"])
            for b in range(DB):
                dma("sp", KV1[:, b, :, :, 0:64], cb[b].rearrange("j (t h d) -> j t h d", t=2, h=2), ["KVones"], [("KV", b)])
            for b in range(DB):
                for kh in range(2):
                    mm(psS[kh][:, :], identf[0:DB, b:b + 1].to_broadcast([DB, 128]), rop[0:DB, kh * 512:(kh + 1) * 512], True, True,
                       ["rop", "identf"], [("psA", kh)])
                    tt("dve", prod[:, kh * 512:(kh + 1) * 512].rearrange("p (h d) -> p h d", d=64),
                       psS[kh][:, :].rearrange("p (h d) -> p h d", d=64),
                       KV1[:, b, 0, kh, 0:64].unsqueeze(1).to_broadcast([128, 8, 64]), ALU.mult, [("KV", b), ("psA", kh)], ["prod"])
                op("dve", lambda e, o=scr[:, 0:16], i_=prod[:, :].rearrange("p (h d) -> p h d", d=64):
                   e.tensor_reduce(out=o, in_=i_, axis=AX.X, op=ALU.add), ["prod"], ["scr"])
                act(PZ[:, b, :, b], scr[:, 0:16], AF.Exp, ["scr", "PZ", "rowb"], [("PZ", b)], scale=0.125, bias=rowb[:, 0:1])
            for hd in range(16):
                o, okey = pv_slot(hd)
                for b in range(DB):
                    mm(o, PZ[:, b, hd, :], KV1[:, b, 1, hd // 8, :], b == 0, b == DB - 1, [("PZ", b), ("KV", b), "KVones"], [okey])
            cp("dve", Oc[:, 0:7, :], psP[0:DB, 0, 0:455].rearrange("p (h d) -> p h d", d=65), [("psA", 4)], ["Oc"])
            cp("dve", Oc[:, 7:14, :], psP[0:DB, 1, 0:455].rearrange("p (h d) -> p h d", d=65), [("psA", 5)], ["Oc"])
            cp("dve", Oc[:, 14:16, :], psR[0:DB, 0:130].rearrange("p (h d) -> p h d", d=65), [("psA", 2)], ["Oc"])
            kb3 = rop[:, 1024:1152].rearrange("p (k d) -> p k d", d=64)
            vb3 = kvp[:, 128:256].rearrange("p (k d) -> p k d", d=64)
            q4 = rop[:, 0:1024]
            for kh in range(2):
                pass
            pr = prod[0:DB, 0:1024].rearrange("p (h d) -> p h d", d=64)
            for kh in range(2):
                tt("dve", pr[:, kh * 8:(kh + 1) * 8, :], q4.rearrange("p (h d) -> p h d", d=64)[:, kh * 8:(kh + 1) * 8, :],
                   kb3[:, kh, :].unsqueeze(1).to_broadcast([DB, 8, 64]), ALU.mult, ["rop", "ropk", "prod"], ["prod"])
            op("dve", lambda e, o=sml[:, 32:48], i_=pr: e.tensor_reduce(out=o, in_=i_, axis=AX.X, op=ALU.add), ["prod"], ["sml"])
            act(sml[:, 32:48], sml[:, 32:48], AF.Exp, ["sml"], ["sml"], scale=0.125)
            for kh in range(2):
                tt("dve", pr[:, kh * 8:(kh + 1) * 8, :], sml[:, 32 + kh * 8:32 + (kh + 1) * 8].unsqueeze(2).to_broadcast([DB, 8, 64]),
                   vb3[:, kh, :].unsqueeze(1).to_broadcast([DB, 8, 64]), ALU.mult, kvk + ["sml", "prod"], ["prod"])
            tt("dve", Ot[:, 0:16, :], Oc[:, 0:16, 0:64], pr, ALU.add, ["Oc", "prod"], ["Ot"])
            tt("dve", sml[:, 0:16], Oc[:, 0:16, 64], sml[:, 32:48], ALU.add, ["Oc", "sml"], ["sml"])
            tt("dve", sml[:, 0:16], sml[:, 0:16], stm[:, :], ALU.add, ["sml", "stm"], ["sml"])
            op("dve", lambda e, o=sml[:, 0:16], i_=sml[:, 0:16]: e.reciprocal(out=o, in_=i_), ["sml"], ["sml"])
            og = hs2[:, 0:1024]
            tt("dve", og.rearrange("p (h d) -> p h d", d=64), Ot[:, 0:16, :], sml[:, 0:16].unsqueeze(2).to_broadcast([DB, 16, 64]),
               ALU.mult, ["Ot", "sml", "hs2"], ["og"])
            act(hs[:, :], proj[0:DB, 1024:2048], AF.Silu, pk + ["hs"], ["hs"])
            tt("dve", og, og, hs[:, :], ALU.mult, ["og", "hs"], ["og"])
            transpose_tok(hsT, "hsT", og, ["og"], 8)
            mk = stream_mm(lambda kc: hsT[:, kc, :], "hsT", DB, w_o_b, 8, 1024, hs, "mix")
            rs, rk = rms_tok(hs, mk)
            ts("dve", hs[:], hs[:], rs, None, ALU.mult, None, mk + [rk], ["mixn"])
            tt("dve", hs[:], hs[:], gv[1][:], ALU.mult, ["mixn", ("gv", 1)], ["mixn"])
            tt("dve", hs[:], hs[:], adat[0][0:DB, 2048:3072], ALU.mult, ["mixn"] + k1, ["mixn"])
            tt("dve", hs[:], hs[:], x1_t[:], ALU.add, ["mixn", "x1_t"], ["ys_t"])
            dma("sp", ys[:, :], hs[:], ["ys_t"], ())
            P.flush()
        if os.environ.get("KSTOP") == "setup":
            return nc

        A0 = amod[:, 0:8]
        SH0 = modT[:, 0:8]
        A1 = amod[:, 8:16]
        SH1 = modT[:, 24:32]
        AKV = amod[:, 16:24]
        SHKV = modT[:, 48:56]

        def load_x(src, srckey, i, xin):
            sl = i % len(xin)
            dma("sp", xin[sl][:], src[i * 128:(i + 1) * 128, :], [(srckey, i)] if srckey else (), [("xin", sl)])

        def norm_stats(tiles, xin, col0):
            for i in tiles:
                sl = i % len(xin)
                c = col0 + i
                act(junk[:], xin[sl][:], AF.Square, [("xin", sl)], [("msq", c)], scale=1.0 / 32, accum_out=msq[:, c:c + 1])
            c0 = col0 + tiles[0]
            n = len(tiles)
            mk = [("msq", col0 + i) for i in tiles]
            rk = [("rstd", col0 + i) for i in tiles]
            act(rstd[:, c0:c0 + n], msq[:, c0:c0 + n], AF.Ln, mk, rk, bias=EPS)
            act(rstd[:, c0:c0 + n], rstd[:, c0:c0 + n], AF.Exp, rk, rk, scale=-0.5)

        def norm_tile(i, u, xin, xn, col0, outs, hslot, stats=True):
            sl = i % len(xin)
            xsl = i % len(xn)
            c = col0 + i
            if stats:
                norm_stats([i], xin, col0)
            ts("dve", xn[xsl][:], xin[sl][:], rstd[:, c:c + 1], None, ALU.mult, None, [("xin", sl), ("rstd", c)], [("xn", xsl)])
            pt = psT[i % 2]
            ptk = ("psT", 0) if i % 2 == 0 else "psR"
            for kc in range(8):
                tr(pt[:, kc, :], xn[xsl][:, kc * 128:(kc + 1) * 128], identb[:], [("xn", xsl)], [ptk])
            for (hT, hname, Am, Sm) in outs:
                for kc in range(8):
                    o = hT[:, kc, u * 128:(u + 1) * 128]
                    if i % 2 == 0:
                        ts("dve", o, pt[:, kc, :], Am[:, kc:kc + 1], Sm[:, kc:kc + 1], ALU.mult, ALU.add,
                           [ptk], [(hname, hslot, u, kc)])
                    else:
                        act(o, pt[:, kc, :], AF.Identity, [ptk], [(hname, hslot, u, kc)],
                            scale=Am[:, kc:kc + 1], bias=Sm[:, kc:kc + 1])

        def project(hT, hname, hslot, W, cols, nk=8):
            slot = nxt("p")
            for kc in range(nk):
                mm(psP[:, slot, :], W[:, kc, cols], hT[:, kc, :], kc == 0, kc == nk - 1,
                   [(hname, hslot, u, kc) for u in range(4)], [("psP", slot)])
            return slot

        def rope(slot, ctab, stab, tkey, qraw, t12, dst, dkey, f32dst=None, fkey=None):
            r = nxt("r")
            cp("act", qraw[r][:], psP[:, slot, :], [("psP", slot)], [("qraw", r)])
            mm(psR[:], rmb[:], qraw[r][:], True, True, [("qraw", r)], ["psR"])
            tt("pool", t12[:, r, :], qraw[r][:], ctab, ALU.mult, [("qraw", r), tkey], [("t1", r)])
            tt("dve", t12[:, 2 + r, :], psR[:], stab, ALU.mult, ["psR", tkey], [("t2", r)])
            tt("pool", dst, t12[:, r, :], t12[:, 2 + r, :], ALU.add, [("t1", r), ("t2", r)], [dkey])
            if f32dst is not None:
                tt("pool", f32dst, t12[:, r, :], t12[:, 2 + r, :], ALU.add, [("t1", r), ("t2", r)], [fkey])

        def cache_out(kvf, kkeys, u, ncol, stage, dst_rows, skey):
            n = len(kvf)
            for k4 in range(n):
                tr(psR[:, k4 * 128:(k4 + 1) * 128], kvf[k4][:, u * 128:(u + 1) * 128], identf[:], [kkeys[k4]], ["psR"])
            sl = nxt("o")
            cp(evac_eng(), stage[sl][:, 0:ncol], psR[:, 0:ncol], ["psR"], [(skey, sl)])
            dma("sp", dst_rows, stage[sl][:, 0:ncol], [(skey, sl)], ())

        pending = []

        def attn_back(l, heads, pt, sl, has_prev, rv):
            osl = nxt("o")
            psO = psOs[osl]
            for hh, h in enumerate(heads):
                rows = slice(hh * 64, (hh + 1) * 64)
                pp = pt[:, hh * 256:hh * 256 + 128]
                pc = pt[:, hh * 256 + 128:hh * 256 + 256]
                for (o, vp, vc) in ((psO[rows, 0:128], h["vp"], h["vc"]), (psO[rows, 128:256], ones_b[:, :], ones_b[:, :])):
                    if has_prev:
                        mm(o, vp, pp, True, False, rv + [("PT", sl)], [("psO", osl)])
                    mm(o, vc, pc, not has_prev, True, rv + [("PT", sl)], [("psO", osl)])
            eng = evac_eng()
            for (dkey, okey, cols) in (("od", "ok", slice(0, 128)), ("dd", "dk", slice(128, 256))):
                for hh, h in enumerate(heads):
                    pass
                d0, d1 = heads[0][dkey], heads[1][dkey]
                src = psO[:, cols]
                dst = heads[0][dkey + "_full"]
                if len(dst.shape) == 3:
                    src = src.rearrange("p (a b) -> p a b", a=dst.shape[1])
                cp(eng, dst, src, [("psO", osl)], [heads[0][okey], heads[1][okey]])

        def attn_flush():
            while pending:
                attn_back(*pending.pop(0))

        def attn_pair(l, heads, PT, rq, rk, rv):
            sl = nxt("s")
            ps = psS[sl]
            pt = PT[sl]
            has_prev = heads[0]["kp"] is not None
            for hh, h in enumerate(heads):
                if has_prev:
                    mm(ps[:, hh * 256:hh * 256 + 256], identb[:], mbb[l][:], True, False, (), [("psS", sl)])
                    mm(ps[:, hh * 256:hh * 256 + 128], h["kp"], h["q"], False, False, rq + rk, [("psS", sl)])
                    mm(ps[:, hh * 256 + 128:hh * 256 + 256], h["kc"], h["q"], False, True, rq + rk, [("psS", sl)])
                else:
                    mm(ps[:, hh * 256 + 128:hh * 256 + 256], identb[:], mbb[l][:, 128:256], True, False, (), [("psS", sl)])
                    mm(ps[:, hh * 256 + 128:hh * 256 + 256], h["kc"], h["q"], False, True, rq + rk, [("psS", sl)])
            if has_prev:
                act(pt[:], ps[:], AF.Exp, [("psS", sl)], [("PT", sl)], scale=0.125)
            else:
                v = lambda a: a.rearrange("p (h x) -> p h x", h=2)[:, :, 128:256]
                act(v(pt[:]), v(ps[:]), AF.Exp, [("psS", sl)], [("PT", sl)], scale=0.125)
            attn_flush()
            pending.append((l, heads, pt, sl, has_prev, rv))

        T12K = [("t1", 0), ("t1", 1), ("t2", 0), ("t2", 1)]

        def out_proj_tile(i, u, oTf, okeys, Wo, nk, ggl, xres, col0, dst_dram, dkey, t12):
            if i % 2 == 0:
                banks = [(psP[:, 0, :], ("psP", 0)), (psP[:, 1, :], ("psP", 1))]
            else:
                banks = [(psS[0][:, :], ("psS", 0)), (psS[1][:, :], ("psS", 1))]
            for nb in range(2):
                for kc in range(nk):
                    mm(banks[nb][0], oTf[kc][:, u * 128:(u + 1) * 128], Wo[:, kc, nb * 512:(nb + 1) * 512], kc == 0, kc == nk - 1,
                       [okeys[kc]], [banks[nb][1]])
            c = col0 + i
            for nb in range(2):
                act(junk[:, 0:512], banks[nb][0], AF.Square, [banks[nb][1]], [("msq", c, nb)], scale=1.0 / 32,
                    accum_out=msq2[:, 2 * i + nb:2 * i + nb + 1])
            tt("dve", msq[:, c:c + 1], msq2[:, 2 * i:2 * i + 1], msq2[:, 2 * i + 1:2 * i + 2], ALU.add, [("msq", c, 0), ("msq", c, 1)], [("msq", c)])
            act(rstd[:, c:c + 1], msq[:, c:c + 1], AF.Ln, [("msq", c)], [("rstd", c)], bias=EPS)
            act(rstd[:, c:c + 1], rstd[:, c:c + 1], AF.Exp, [("rstd", c)], [("rstd", c)], scale=-0.5)
            r = i % 2
            for nb in range(2):
                stt("dve", t12[:, 2 * r + nb, :], banks[nb][0], rstd[:, c:c + 1], ggl[:, nb * 512:(nb + 1) * 512], ALU.mult, ALU.mult,
                    [banks[nb][1], ("rstd", c)], [T12K[2 * r + nb]])
            sl = i % len(xres)
            tm = t12[:, 2 * r:2 * r + 2, :].rearrange("p a b -> p (a b)")
            tt("pool", xres[sl][:], tm, xres[sl][:], ALU.add, [T12K[2 * r], T12K[2 * r + 1], ("xres", sl)], [("xres", sl)])
            dma("sp", dst_dram[i * 128:(i + 1) * 128, :], xres[sl][:], [("xres", sl)], [(dkey, i)] if dkey else ())

        with ExitStack() as st:
            Wa2 = sbt(st, "Wa2", [128, 8, 768], BF16)
            wst = [sbt(st, "wstA%d" % i, [128, 1024], F32) for i in range(2)]
            xin = [sbt(st, "xinA%d" % i, [128, D], F32) for i in range(4)]
            xn = [sbt(st, "xnA%d" % i, [128, D], BF16) for i in range(2)]
            hT = [sbt(st, "hTA%d" % i, [128, 8, 512], BF16) for i in range(2)]
            QT2 = [sbt(st, "QT2_%d" % c, [128, T], BF16) for c in range(2)]
            KT2 = [sbt(st, "KT2_%d" % c, [128, T], BF16) for c in range(2)]
            VT2 = [sbt(st, "VT2_%d" % c, [128, T], BF16) for c in range(2)]
            ctab = [sbt(st, "ctabA%d" % i, [128, 512], F32) for i in range(2)]
            stab = [sbt(st, "stabA%d" % i, [128, 512], F32) for i in range(2)]
            qraw = [sbt(st, "qrawA%d" % i, [128, 512], BF16) for i in range(2)]
            t12 = sbt(st, "t12A", [128, 4, 512], F32)
            kvf = [sbt(st, "kvfA%d" % i, [128, 512], F32) for i in range(4)]
            stage = [sbt(st, "stageA%d" % i, [128, 512], F32) for i in range(2)]
            PT = [sbt(st, "PTA%d" % i, [128, 512], BF16) for i in range(2)]
            VA = [sbt(st, "VAA%d" % i, [128, 4, 128], BF16) for i in range(4)]
            G2st = [sbt(st, "G2st%d" % k, [128, T], BF16) for k in range(4)]
            load_w(wst, Wa2, w_in_a, 1024, 512, 256, 0, "w")
            load_w(wst, Wa2, w_in_a, 1024, 768 + 512, 256, 256, "w")
            load_w(wst, Wa2, w_in_a, 1024, 1536 + 512, 256, 512, "w")
            for v in VA:
                memset("pool", v[:, :, 64:128], 1.0, [("w", "va")])
            P.flush()
            print("[phaseA] sbuf remaining", nc.sbuf_bytes_remaining)
            stop("A0")
            for u in range(4):
                load_x(xp, None, u, xin)
            for s in range(NS):
                hs = s % 2
                tk = slice(512 * s, 512 * (s + 1))
                dma("sp", ctab[hs][:], cosT_d[:, tk], (), [("tab", hs)])
                dma("sp", stab[hs][:], sinT_d[:, tk], (), [("tab", hs)])
                norm_stats([4 * s + u for u in range(4)], xin, 0)
                for u in range(4):
                    norm_tile(4 * s + u, u, xin, xn, 0, [(hT[hs], "hT", A0, SH0)], hs, stats=False)
                stop("A1")
                if s + 1 < NS:
                    for u in range(4):
                        load_x(xp, None, 4 * (s + 1) + u, xin)
                for (kind, cc, wc) in (("k", 0, 256), ("k", 1, 384), ("v", 0, 512), ("v", 1, 640), ("q", 0, 0), ("q", 1, 128)):
                    slot = project(hT[hs], "hT", hs, Wa2, slice(wc, wc + 128))
                    if kind == "k":
                        rope(slot, ctab[hs][:], stab[hs][:], ("tab", hs), qraw, t12, KT2[cc][:, tk], ("KT2", cc, s),
                             kvf[cc][:] if s >= 4 else None, ("kvf", cc))
                    elif kind == "q":
                        rope(slot, ctab[hs][:], stab[hs][:], ("tab", hs), qraw, t12, QT2[cc][:, tk], ("QT2", cc, s))
                    else:
                        cp("act", VT2[cc][:, tk], psP[:, slot, :], [("psP", slot)], [("VT2", cc, s)])
                        if s >= 4:
                            cp("dve", kvf[2 + cc][:], psP[:, slot, :], [("psP", slot)], [("kvf", 2 + cc)])
                stop("A2")
                if s >= 4:
                    for u in range(4):
                        i = 4 * s + u
                        cache_out(kvf, [("kvf", k) for k in range(4)], u, 512, stage, aop[2][(i - 16) * 128:(i - 15) * 128, :], "stage")
            stop("A3")
            allq = [[("QT2", c, s) for s in range(NS)] for c in range(2)]
            allk = [[("KT2", c, s) for s in range(NS)] for c in range(2)]
            allv = [[("VT2", c, s) for s in range(NS)] for c in range(2)]
            for r in range(16):
                for j in range(2):
                    tq = slice(2048 * j + r, 2048 * (j + 1), 16)
                    tp = slice(2048 * (j - 1) + r, 2048 * j, 16)
                    b = 2 * r + j
                    vsl = b % 4
                    ptv = psT[0]
                    for c in range(2):
                        tr(ptv[:, c, :], VT2[c][:, tq], identb[:], allv[c], [("psT", 0)])
                    cp(evac_eng(), VA[vsl][:, :, 0:64], ptv[:, 0:2, :].rearrange("p c (h d) -> p (c h) d", h=2),
                       [("psT", 0)], [("VA", vsl)])
                    for c in range(2):
                        heads = []
                        for hh in range(2):
                            h = 2 * c + hh
                            rows = slice(hh * 64, (hh + 1) * 64)
                            od = G2st[c][rows, :].rearrange("p (s r m) -> p s r m", s=8, r=16)[:, 4 * j:4 * j + 4, r, :]
                            dd = G2st[2 + c][rows, :].rearrange("p (s r m) -> p s r m", s=8, r=16)[:, 4 * j:4 * j + 4, r, :]
                            heads.append(dict(q=QT2[c][rows, tq], kc=KT2[c][rows, tq], kp=(KT2[c][rows, tp] if j == 1 else None),
                                              vc=VA[vsl][:, h, 0:64], vp=(VA[(b - 1) % 4][:, h, 0:64] if j == 1 else None),
                                              od=od, dd=dd, ok=("G2", c, b, hh), dk=("G2", 2 + c, b, hh),
                                              od_full=G2st[c][:, :].rearrange("p (s r m) -> p s r m", s=8, r=16)[:, 4 * j:4 * j + 4, r, :],
                                              dd_full=G2st[2 + c][:, :].rearrange("p (s r m) -> p s r m", s=8, r=16)[:, 4 * j:4 * j + 4, r, :]))
                        attn_pair(0, heads, PT, allq[c], allk[c], [("VA", vsl), ("VA", (b - 1) % 4)])
            attn_flush()
            for k in range(4):
                dma("sp", g2s[k], G2st[k][:], [("G2", k, b, hh) for b in range(32) for hh in range(2)], [("g2s", k)])
            P.flush()
        if os.environ.get("KSTOP") == "A":
            return nc

        with ExitStack() as st:
            Wab = sbt(st, "Wab", [128, 8, 2304], BF16)
            Woa = sbt(st, "Woa", [128, 6, 1024], BF16)
            wst = [sbt(st, "wstB%d" % i, [128, 1024], F32) for i in range(2)]
            xin = [sbt(st, "xinB%d" % i, [128, D], F32) for i in range(2)]
            xres = [sbt(st, "xresB%d" % i, [128, D], F32) for i in range(2)]
            xn = [sbt(st, "xnB%d" % i, [128, D], BF16) for i in range(2)]
            hT = [sbt(st, "hTB%d" % i, [128, 8, 512], BF16) for i in range(1)]
            QT = [sbt(st, "QTB%d" % c, [128, 512], BF16) for c in range(4)]
            KT = [sbt(st, "KTB%d" % c, [128, 1024], BF16) for c in range(4)]
            VT = [sbt(st, "VTB%d" % c, [128, 1024], BF16) for c in range(4)]
            gT = [sbt(st, "gTB%d" % c, [128, 512], BF16) for c in range(6)]
            OT = [sbt(st, "OTB%d" % c, [128, 512], BF16) for c in range(4)]
            DT = [sbt(st, "DTB%d" % c, [128, 512], BF16) for c in range(4)]
            oTf = [sbt(st, "oTfB%d" % c, [128, 512], BF16) for c in range(6)]
            G2in = [sbt(st, "G2in%d" % k, [128, 512], BF16) for k in range(4)]
            dsum = sbt(st, "dsumB", [128, 512], F32)
            rinv = sbt(st, "rinvB", [128, 512], F32)
            otmp = sbt(st, "otmpB", [128, 512], F32)
            ctab = [sbt(st, "ctabB%d" % i, [128, 512], F32) for i in range(2)]
            stab = [sbt(st, "stabB%d" % i, [128, 512], F32) for i in range(2)]
            qraw = [sbt(st, "qrawB%d" % i, [128, 512], BF16) for i in range(2)]
            t12 = sbt(st, "t12B", [128, 4, 512], F32)
            kvf = [sbt(st, "kvfB%d" % i, [128, 512], F32) for i in range(4)]
            stage = [sbt(st, "stageB%d" % i, [128, 512], F32) for i in range(2)]
            PT = [sbt(st, "PTB%d" % i, [128, 512], BF16) for i in range(2)]
            VA0 = [sbt(st, "VA0_%d" % i, [128, 4, 128], BF16) for i in range(4)]
            VA1 = [sbt(st, "VA1_%d" % i, [128, 4, 128], BF16) for i in range(8)]
            load_w(wst, Wab, w_in_a, 1024, 0, 512, 0, "w")
            load_w(wst, Wab, w_in_a, 1024, 768, 512, 512, "w")
            load_w(wst, Wab, w_in_a, 1024, 1536, 512, 1024, "w")
            load_w(wst, Wab, w_in_a, 1024, 2304, 768, 1536, "w")
            load_w(wst, Woa, w_o_a, 768, 0, 1024, 0, "w")
            for v in VA0 + VA1:
                memset("pool", v[:, :, 64:128], 1.0, [("w", "va")])
            P.flush()
            load_x(xp, None, 0, xin)
            load_x(xp, None, 1, xin)
            for s in range(NS):
                hs = s % 2
                ts_ = hs * 512
                tk = slice(512 * s, 512 * (s + 1))
                dma("sp", ctab[hs][:], cosT_d[:, tk], (), [("tab", hs)])
                dma("sp", stab[hs][:], sinT_d[:, tk], (), [("tab", hs)])
                for k in range(4):
                    dma("sp", G2in[k][:], g2s[k][:, tk], [("g2s", k)], [("G2in", k)])
                for u in range(4):
                    i = 4 * s + u
                    norm_tile(i, u, xin, xn, 32, [(hT[0], "hT", A0, SH0)], 0)
                    if i + 2 < NT:
                        load_x(xp, None, i + 2, xin)
                ring = slice(ts_, ts_ + 512)
                for g in range(2):
                    for (kind, cc, wc) in (("k", 2 * g, 512 + 256 * g), ("k", 2 * g + 1, 640 + 256 * g),
                                           ("v", 2 * g, 1024 + 256 * g), ("v", 2 * g + 1, 1152 + 256 * g)):
                        slot = project(hT[0], "hT", 0, Wab, slice(wc, wc + 128))
                        need_f = (s == 7)
                        if kind == "k":
                            rope(slot, ctab[hs][:], stab[hs][:], ("tab", hs), qraw, t12, KT[cc][:, ring], ("KT", cc, hs),
                                 kvf[cc % 2][:] if need_f else None, ("kvf", cc % 2))
                        else:
                            cp("act", VT[cc][:, ring], psP[:, slot, :], [("psP", slot)], [("VT", cc, hs)])
                            if need_f:
                                cp("dve", kvf[2 + cc % 2][:], psP[:, slot, :], [("psP", slot)], [("kvf", 2 + cc % 2)])
                    if s == 7:
                        if g == 0:
                            cache_out(kvf, [("kvf", k) for k in range(4)], 3, 512, stage, aop[0][:, :], "stage")
                        else:
                            for u in range(4):
                                cache_out(kvf, [("kvf", k) for k in range(4)], u, 512, stage, aop[1][u * 128:(u + 1) * 128, :], "stage")
                for cc in range(4):
                    slot = project(hT[0], "hT", 0, Wab, slice(cc * 128, (cc + 1) * 128))
                    rope(slot, ctab[hs][:], stab[hs][:], ("tab", hs), qraw, t12, QT[cc][:], ("QT", cc))
                for cc in range(6):
                    slot = project(hT[0], "hT", 0, Wab, slice(1536 + cc * 128, 1536 + (cc + 1) * 128))
                    act(gT[cc][:], psP[:, slot, :], AF.Silu, [("psP", slot)], [("gT", cc)])
                for u in range(4):
                    i = 4 * s + u
                    vsl = i % 4
                    ptv = psT[0]
                    cols = slice(ts_ + u * 128, ts_ + (u + 1) * 128)
                    if u > 0:
                        pcols = slice(ts_ + (u - 1) * 128, ts_ + u * 128)
                    else:
                        pcols = slice((1 - hs) * 512 + 384, (1 - hs) * 512 + 512)
                    for c in range(2):
                        tr(ptv[:, c, :], VT[c][:, cols], identb[:], [("VT", c, hs)], [("psT", 0)])
                    cp(evac_eng(), VA0[vsl][:, :, 0:64], ptv[:, 0:2, :].rearrange("p c (h d) -> p (c h) d", h=2),
                       [("psT", 0)], [("VA0", vsl)])
                    for c in range(2):
                        heads = []
                        for hh in range(2):
                            h = 2 * c + hh
                            rows = slice(hh * 64, (hh + 1) * 64)
                            heads.append(dict(q=QT[c][rows, u * 128:(u + 1) * 128], kc=KT[c][rows, cols],
                                              kp=(KT[c][rows, pcols] if i > 0 else None),
                                              vc=VA0[vsl][:, h, 0:64], vp=(VA0[(i - 1) % 4][:, h, 0:64] if i > 0 else None),
                                              od=OT[c][rows, u * 128:(u + 1) * 128], dd=DT[c][rows, u * 128:(u + 1) * 128],
                                              od_full=OT[c][:, u * 128:(u + 1) * 128], dd_full=DT[c][:, u * 128:(u + 1) * 128],
                                              ok=("OT", c, u, hh), dk=("DT", c, u, hh)))
                        attn_pair(0, heads, PT, [("QT", c)], [("KT", c, 0), ("KT", c, 1)], [("VA0", vsl), ("VA0", (i - 1) % 4)])
                for r4 in range(4):
                    vsl = hs * 4 + r4
                    psl = (1 - hs) * 4 + r4
                    b = 4 * s + r4
                    ptv = psT[0]
                    cols = slice(ts_ + r4, ts_ + 512, 4)
                    pcols = slice((1 - hs) * 512 + r4, (1 - hs) * 512 + 512, 4)
                    for c in range(2):
                        tr(ptv[:, c, :], VT[2 + c][:, cols], identb[:], [("VT", 2 + c, hs)], [("psT", 0)])
                    cp(evac_eng(), VA1[vsl][:, :, 0:64], ptv[:, 0:2, :].rearrange("p c (h d) -> p (c h) d", h=2),
                       [("psT", 0)], [("VA1", vsl)])
                    for c in range(2):
                        heads = []
                        for hh in range(2):
                            h = 2 * c + hh
                            rows = slice(hh * 64, (hh + 1) * 64)
                            heads.append(dict(q=QT[2 + c][rows, r4:512:4], kc=KT[2 + c][rows, cols],
                                              kp=(KT[2 + c][rows, pcols] if s > 0 else None),
                                              vc=VA1[vsl][:, h, 0:64], vp=(VA1[psl][:, h, 0:64] if s > 0 else None),
                                              od=OT[2 + c][rows, r4:512:4], dd=DT[2 + c][rows, r4:512:4],
                                              od_full=OT[2 + c][:, r4:512:4], dd_full=DT[2 + c][:, r4:512:4],
                                              ok=("OT", 2 + c, r4, hh), dk=("DT", 2 + c, r4, hh)))
                        attn_pair(0, heads, PT, [("QT", 2 + c)], [("KT", 2 + c, 0), ("KT", 2 + c, 1)], [("VA1", vsl), ("VA1", psl)])
                attn_flush()
                nat = lambda a: a.rearrange("p (r m) -> p m r", r=16)
                v3 = lambda a: a.rearrange("p (m r) -> p m r", r=16)
                for cc in range(2):
                    k01 = [("DT", c, x, hh) for c in (cc, 2 + cc) for x in range(4) for hh in range(2)]
                    tt("pool", dsum[:], DT[cc][:], DT[2 + cc][:], ALU.add, k01, ["dsum"])
                    tt("pool", v3(dsum[:]), v3(dsum[:]), nat(G2in[2 + cc][:]), ALU.add, ["dsum", ("G2in", 2 + cc)], ["dsum"])
                    act(rinv[:], dsum[:], AF.Ln, ["dsum"], ["rinv"])
                    act(rinv[:], rinv[:], AF.Exp, ["rinv"], ["rinv"], scale=-1.0)
                    for g in range(3):
                        c6 = 2 * g + cc
                        if g < 2:
                            src = OT[c6][:]
                            sk = [("OT", c6, x, hh) for x in range(4) for hh in range(2)]
                            tt("dve", otmp[:], src, rinv[:], ALU.mult, sk + ["rinv"], ["otmp"])
                        else:
                            tt("dve", v3(otmp[:]), nat(G2in[cc][:]), v3(rinv[:]), ALU.mult, [("G2in", cc), "rinv"], ["otmp"])
                        tt("pool", oTf[c6][:], otmp[:], gT[c6][:], ALU.mult, ["otmp", ("gT", c6)], [("oTf", c6)])
                for u in range(4):
                    i = 4 * s + u
                    dma("sp", xres[i % 2][:], xp[i * 128:(i + 1) * 128, :], (), [("xres", i % 2)])
                    out_proj_tile(i, u, oTf, [("oTf", c) for c in range(6)], Woa, 6, ggbc[0], xres, 64, x1s, "x1s", t12)
            P.flush()
        if os.environ.get("KSTOP") == "B":
            return nc

        with ExitStack() as st:
            Wq = sbt(st, "Wq", [128, 8, 2048], BF16)
            Wob = sbt(st, "Wob", [128, 8, 1024], BF16)
            Wkv = sbt(st, "Wkv", [128, 8, 384], BF16)
            wst = [sbt(st, "wstC%d" % i, [128, 1024], F32) for i in range(2)]
            xin = [sbt(st, "xinC%d" % i, [128, D], F32) for i in range(2)]
            xres = [sbt(st, "xresC%d" % i, [128, D], F32) for i in range(2)]
            xn = [sbt(st, "xnC%d" % i, [128, D], BF16) for i in range(2)]
            hT = sbt(st, "hTC", [128, 8, 512], BF16)
            hkT = sbt(st, "hkTC", [128, 8, 512], BF16)
            QT = [sbt(st, "QTC%d" % c, [128, 512], BF16) for c in range(8)]
            KT = [sbt(st, "KTC%d" % c, [128, 1024], BF16) for c in range(2)]
            VT = sbt(st, "VTC", [128, 1024], BF16)
            gT = [sbt(st, "gTC%d" % c, [128, 512], BF16) for c in range(8)]
            OT = [sbt(st, "OTC%d" % c, [128, 512], BF16) for c in range(8)]
            DT = [sbt(st, "DTC%d" % c, [128, 512], BF16) for c in range(8)]
            dsum = [sbt(st, "dsumC%d" % i, [128, 512], F32) for i in range(2)]
            otmp = [sbt(st, "otmpC%d" % i, [128, 512], BF16) for i in range(2)]
            ctab = [sbt(st, "ctabC%d" % i, [128, 512], F32) for i in range(2)]
            stab = [sbt(st, "stabC%d" % i, [128, 512], F32) for i in range(2)]
            qraw = [sbt(st, "qrawC%d" % i, [128, 512], BF16) for i in range(2)]
            t12 = sbt(st, "t12C", [128, 4, 512], F32)
            kvf = [sbt(st, "kvfC%d" % i, [128, 512], F32) for i in range(2)]
            stage = [sbt(st, "stageC%d" % i, [128, 256], F32) for i in range(2)]
            PT = [sbt(st, "PTC%d" % i, [128, 512], BF16) for i in range(2)]
            VAc = [sbt(st, "VAc%d" % i, [128, 2, 128], BF16) for i in range(4)]
            load_w(wst, Wq, w_in_b, 1024, 0, 2048, 0, "w")
            load_w(wst, Wob, w_o_b, 1024, 0, 1024, 0, "w")
            load_w(wst, Wkv, w_kvx, 1024, 0, 384, 0, "w")
            for v in VAc:
                memset("pool", v[:, :, 64:128], 1.0, [("w", "va")])
            P.flush()
            dma("sp", xin[0][:], x1s[0:128, :], (), [("xin", 0)])
            dma("sp", xin[1][:], x1s[128:256, :], (), [("xin", 1)])
            for s in range(NS):
                hs = s % 2
                ts_ = hs * 512
                tk = slice(512 * s, 512 * (s + 1))
                ring = slice(ts_, ts_ + 512)
                dma("sp", ctab[hs][:], cosT_d[:, tk], (), [("tab", hs)])
                dma("sp", stab[hs][:], sinT_d[:, tk], (), [("tab", hs)])
                for u in range(4):
                    i = 4 * s + u
                    norm_tile(i, u, xin, xn, 0, [(hT, "hT", A1, SH1), (hkT, "hkT", AKV, SHKV)], 0)
                    if i + 2 < NT:
                        dma("sp", xin[i % 2][:], x1s[(i + 2) * 128:(i + 3) * 128, :], (), [("xin", i % 2)])
                need_f = (s == 7)
                slot = project(hkT, "hkT", 0, Wkv, slice(0, 128))
                rope(slot, ctab[hs][:], stab[hs][:], ("tab", hs), qraw, t12, KT[0][:, ring], ("KT", 0, hs),
                     kvf[0][:] if need_f else None, ("kvf", 0))
                slot = project(hkT, "hkT", 0, Wkv, slice(256, 384))
                rope(slot, ctab[hs][:], stab[hs][:], ("tab", hs), qraw, t12, KT[1][:, ring], ("KT", 1, hs))
                slot = project(hkT, "hkT", 0, Wkv, slice(128, 256))
                cp("act", VT[:, ring], psP[:, slot, :], [("psP", slot)], [("VT", hs)])
                if need_f:
                    cp("dve", kvf[1][:], psP[:, slot, :], [("psP", slot)], [("kvf", 1)])
                    cache_out(kvf, [("kvf", 0), ("kvf", 1)], 3, 256, stage, bp[:, :], "stage")
                for cc in range(8):
                    slot = project(hT, "hT", 0, Wq, slice(cc * 128, (cc + 1) * 128))
                    rope(slot, ctab[hs][:], stab[hs][:], ("tab", hs), qraw, t12, QT[cc][:], ("QT", cc))
                for cc in range(8):
                    slot = project(hT, "hT", 0, Wq, slice(1024 + cc * 128, 1024 + (cc + 1) * 128))
                    act(gT[cc][:], psP[:, slot, :], AF.Silu, [("psP", slot)], [("gT", cc)])
                for u in range(4):
                    i = 4 * s + u
                    vsl = i % 4
                    ptv = psT[0]
                    cols = slice(ts_ + u * 128, ts_ + (u + 1) * 128)
                    if u > 0:
                        pcols = slice(ts_ + (u - 1) * 128, ts_ + u * 128)
                    else:
                        pcols = slice((1 - hs) * 512 + 384, (1 - hs) * 512 + 512)
                    tr(ptv[:, 0, :], VT[:, cols], identb[:], [("VT", hs)], [("psT", 0)])
                    cp(evac_eng(), VAc[vsl][:, :, 0:64], ptv[:, 0, :].rearrange("p (h d) -> p h d", h=2),
                       [("psT", 0)], [("VAc", vsl)])
                    for c in range(8):
                        kvh = c // 4
                        heads = []
                        for hh in range(2):
                            rows = slice(hh * 64, (hh + 1) * 64)
                            ksrc = KT[0] if (kvh == hh) else KT[1]
                            heads.append(dict(q=QT[c][rows, u * 128:(u + 1) * 128], kc=ksrc[rows, cols],
                                              kp=(ksrc[rows, pcols] if i > 0 else None),
                                              vc=VAc[vsl][:, kvh, 0:64], vp=(VAc[(i - 1) % 4][:, kvh, 0:64] if i > 0 else None),
                                              od=OT[c][rows, u * 128:(u + 1) * 128], dd=DT[c][rows, u * 128:(u + 1) * 128],
                                              od_full=OT[c][:, u * 128:(u + 1) * 128], dd_full=DT[c][:, u * 128:(u + 1) * 128],
                                              ok=("OT", c, u, hh), dk=("DT", c, u, hh)))
                        attn_pair(1, heads, PT, [("QT", c)], [("KT", 0, 0), ("KT", 0, 1), ("KT", 1, 0), ("KT", 1, 1)],
                                  [("VAc", vsl), ("VAc", (i - 1) % 4)])
                attn_flush()
                for c in range(8):
                    dk = [("DT", c, x, hh) for x in range(4) for hh in range(2)]
                    okk = [("OT", c, x, hh) for x in range(4) for hh in range(2)]
                    b2 = c % 2
                    act(dsum[b2][:], DT[c][:], AF.Ln, dk, [("dsum", b2)], bias=esink[:, c:c + 1])
                    act(dsum[b2][:], dsum[b2][:], AF.Exp, [("dsum", b2)], [("dsum", b2)], scale=-1.0)
                    tt("dve", otmp[b2][:], OT[c][:], dsum[b2][:], ALU.mult, okk + [("dsum", b2)], [("otmp", b2)])
                    tt("pool", OT[c][:], otmp[b2][:], gT[c][:], ALU.mult, [("otmp", b2), ("gT", c)] + okk, [("oTf", c)] + okk)
                for u in range(4):
                    i = 4 * s + u
                    dma("sp", xres[i % 2][:], x1s[i * 128:(i + 1) * 128, :], (), [("xres", i % 2)])
                    out_proj_tile(i, u, OT, [("oTf", c) for c in range(8)], Wob, 8, ggbc[1], xres, 32, yp, None, t12)
            P.flush(final=True)
        print("[kernel] ops=%d waits=%d sig=%s" % (P.nops, P.nwait, P.cnt))
    return nc


_NC_CACHE = {}


def kernel(x_prompt, x_sample, c_prompt, c_sample, cache_a_kv_g0, cache_a_kv_g1, cache_a_kv_g2, cache_b_kv,
           ada_w, ada_b, g_pre, g_post, w_in_a, w_o_a, w_in_b, w_o_b, sinks_b, ada_kv_w, ada_kv_b, g_kv, w_kv,
           _cores=None):
    f = lambda a: np.ascontiguousarray(np.asarray(a, dtype=np.float32))
    x_prompt, x_sample, c_prompt, c_sample = f(x_prompt), f(x_sample), f(c_prompt), f(c_sample)
    ca = [f(cache_a_kv_g0), f(cache_a_kv_g1), f(cache_a_kv_g2)]
    cbk = f(cache_b_kv)
    ada_w, ada_b, g_pre, g_post = f(ada_w), f(ada_b), f(g_pre), f(g_post)
    w_in_a, w_o_a, w_in_b, w_o_b = f(w_in_a), f(w_o_a), f(w_in_b), f(w_o_b)
    sinks_b, ada_kv_w, ada_kv_b, g_kv, w_kv = f(sinks_b), f(ada_kv_w), f(ada_kv_b), f(g_kv), f(w_kv)
    cst = _consts()
    cores = list(range(NCORES)) if _cores is None else list(_cores)
    shared = dict(
        ada_w0=f(ada_w[0]), ada_w1=f(ada_w[1]), ada_kv_w=ada_kv_w,
        brow=f(np.concatenate([ada_b[0], ada_b[1], ada_kv_b])[None, :]),
        vecs=f(np.concatenate([g_pre[0].reshape(8, 128), g_pre[1].reshape(8, 128), g_post[0].reshape(8, 128),
                               g_post[1].reshape(8, 128), g_kv.reshape(8, 128)], axis=0)),
        gpost_rows=g_post, w_in_a=f(w_in_a[0]), w_o_a=f(w_o_a[0]), w_in_b=f(w_in_b[0]), w_o_b=f(w_o_b[0]),
        w_kvx=f(np.concatenate([w_kv[:, 0:256], w_kv[:, 64:128], w_kv[:, 0:64]], axis=1)),
        sink_fm=f(sinks_b[0].reshape(8, 2).T.repeat(64, axis=0)),
        sink_tm=f(np.tile(sinks_b[0][None, :], (DB, 1))),
        ident=cst["ident"], rmat=cst["rmat"], cosT=cst["cosT"], sinT=cst["sinT"], masks=cst["masks"], mbias=cst["mbias"],
        cs_tok=cst["cs_tok"], sn_tok=cst["sn_tok"], rowb_in=cst["rowb"], w_kv=w_kv,
        gvec_tm=f(np.stack([np.tile(v[None, :], (DB, 1)) for v in (g_pre[0], g_pre[1], g_post[0], g_post[1], g_kv)])),
    )
    in_maps = []
    for b in cores:
        crows = np.zeros((33, D), np.float32)
        crows[0:DB] = c_sample[DB * b:DB * (b + 1)]
        crows[32] = c_prompt[b]
        m = dict(shared)
        m.update(xp=x_prompt[b], crows=crows, xs=f(x_sample[DB * b:DB * (b + 1), 0, :]),
                 ca0=ca[0][0, DB * b:DB * (b + 1)].reshape(DB, 128, 512),
                 ca1=ca[1][0, DB * b:DB * (b + 1)].reshape(DB, 512, 512),
                 ca2=ca[2][0, DB * b:DB * (b + 1)].reshape(DB, 2048, 512),
                 cb=cbk[DB * b:DB * (b + 1)].reshape(DB, 128, 256))
        in_maps.append(m)
    if "nc" not in _NC_CACHE:
        _NC_CACHE["nc"] = build_nc()
    nc = _NC_CACHE["nc"]
    res = run_bass_kernel_spmd(nc, in_maps, core_ids=list(range(len(cores))))
    R = res.results
    n = len(cores)
    y_prompt = np.stack([R[k]["yp"] for k in range(n)])
    y_sample = np.concatenate([R[k]["ys"] for k in range(n)])[:, None, :]
    a_p = [np.stack([R[k]["a%dp" % g] for k in range(n)]).reshape(1, n, -1, 2, 4, 64) for g in range(3)]
    b_p = np.stack([R[k]["bp"] for k in range(n)]).reshape(n, 128, 2, 2, 64)
    a_s = [np.concatenate([R[k]["a%ds" % g] for k in range(n)]).reshape(1, n * DB, -1, 2, 4, 64) for g in range(3)]
    b_s = np.concatenate([R[k]["bs"] for k in range(n)]).reshape(n * DB, 128, 2, 2, 64)
    return (y_prompt, y_sample, a_p[0], a_p[1], a_p[2], b_p, a_s[0], a_s[1], a_s[2], b_s)
```

```python
import os
import numpy as np
from contextlib import ExitStack
import concourse.bass as bass
import concourse.mybir as mybir
from concourse.bass_utils import run_bass_kernel_spmd

F32 = mybir.dt.float32
BF16 = mybir.dt.bfloat16
ALU = mybir.AluOpType
AF = mybir.ActivationFunctionType

NCORES = 8
T = 4096
D = 1024
NT = 32
NS = 8
EPS = 1e-6
PAST = 16384
DB = 16
STRICT_SAME_ENGINE = os.environ.get("KSTRICT", "0") == "1"


class Prog:
    CE = ("pe", "act", "dve", "pool")
    ENG = ("pe", "act", "dve", "pool", "sp")

    def __init__(self, nc, stack, ndsem=20):
        self.nc = nc
        self.eng = dict(pe=nc.tensor, act=nc.scalar, dve=nc.vector, pool=nc.gpsimd, sp=nc.sync)
        self.ops = []
        self.last_w = {}
        self.readers = {}
        self.esem = {e: stack.enter_context(nc.semaphore("S_" + e)) for e in self.CE}
        self.cnt = {e: 0 for e in self.CE}
        nq = dict(sp=ndsem, pool=2, act=72)
        self.dsem = {e: [stack.enter_context(nc.semaphore("D_%s%d" % (e, i))) for i in range(nq[e])]
                     for e in ("sp", "pool", "act")}
        self.dcum = {e: [0] * nq[e] for e in self.dsem}
        self.dnext = {e: 0 for e in self.dsem}
        self.waited = {e: {} for e in self.ENG}
        self.nwait = 0
        self.nops = 0

    @staticmethod
    def _isps(k):
        n = k[0] if isinstance(k, tuple) else k
        return isinstance(n, str) and n.startswith("ps")

    def op(self, eng, fn, reads=(), writes=(), dma=False, rg=None):
        writes = list(writes) + [r for r in reads if self._isps(r)]
        reads = [r for r in reads if not self._isps(r)]
        i = len(self.ops)
        raw = set()
        oth = set()
        for r in reads:
            if r in self.last_w:
                raw.add(self.last_w[r])
        for w in writes:
            if w in self.last_w:
                oth.add(self.last_w[w])
            for j in self.readers.get(w, {}).values():
                oth.add(j)
        raw.discard(i)
        oth.discard(i)
        self.ops.append(dict(eng=eng, fn=fn, raw=raw, oth=oth - raw, dma=dma, rg=rg))
        for r in reads:
            self.readers.setdefault(r, {})[(eng if not dma else ("dma", i))] = i
        for w in writes:
            self.last_w[w] = i
            self.readers[w] = {}
        return i

    def _needs_wait(self, o, p, is_raw):
        if p["dma"] or o["dma"]:
            return True
        if p["eng"] != o["eng"]:
            return True
        if o["eng"] == "pe":
            return o["rg"] is not None and p["rg"] is not None and o["rg"] != p["rg"]
        return is_raw or STRICT_SAME_ENGINE

    def _wait(self, e, sem, val):
        k = id(sem)
        if self.waited[e].get(k, 0) >= val:
            return
        self.eng[e].wait_ge(sem, val)
        self.waited[e][k] = val
        self.nwait += 1

    def flush(self, final=False):
        ops = self.ops
        n = len(ops)
        need = [False] * n
        for o in ops:
            for d in o["raw"]:
                if self._needs_wait(o, ops[d], True):
                    need[d] = True
            for d in o["oth"]:
                if self._needs_wait(o, ops[d], False):
                    need[d] = True
        last = {}
        for i, o in enumerate(ops):
            if not o["dma"]:
                last[o["eng"]] = i
        for i in last.values():
            need[i] = True
        sig = [None] * n
        for i, o in enumerate(ops):
            e = o["eng"]
            eo = self.eng[e]
            deps = [(d, True) for d in o["raw"]] + [(d, False) for d in o["oth"]]
            for d, is_raw in sorted(deps):
                p = ops[d]
                if not self._needs_wait(o, p, is_raw):
                    continue
                s, v = sig[d]
                self._wait(e, s, v)
            if o["dma"]:
                k = self.dnext[e]
                self.dnext[e] = (k + 1) % len(self.dsem[e])
                s = self.dsem[e][k]
                self._wait(e, s, self.dcum[e][k])
                ins = o["fn"](eo)
                self.dcum[e][k] += 16
                ins.then_inc(s, 16)
                sig[i] = (s, self.dcum[e][k])
            else:
                ins = o["fn"](eo)
                if need[i]:
                    self.cnt[e] += 1
                    ins.then_inc(self.esem[e], 1)
                    sig[i] = (self.esem[e], self.cnt[e])
        for e in self.ENG:
            for x in self.CE:
                if x != e and self.cnt[x] > 0:
                    self._wait(e, self.esem[x], self.cnt[x])
            for q in self.dsem:
                if q == "act" and not final:
                    continue
                for k, s in enumerate(self.dsem[q]):
                    if self.dcum[q][k] > 0:
                        self._wait(e, s, self.dcum[q][k])
        self.nops += n
        print("[flush] ops", n, "cnt", self.cnt, "dcum max", {q: max(v) for q, v in self.dcum.items()})
        self.ops = []
        self.last_w = {}
        self.readers = {}


def _consts():
    half = 32
    inv = (10000.0 ** (-np.arange(half, dtype=np.float32) / half)).astype(np.float32)
    pos = np.arange(T, dtype=np.float32)
    ang = pos[:, None] * inv[None, :]
    cos = np.cos(ang).astype(np.float32)
    sin = np.sin(ang).astype(np.float32)
    fi = (np.arange(128) % 64) % 32
    cosT = np.ascontiguousarray(cos[:, fi].T)
    sinT = np.ascontiguousarray(sin[:, fi].T)
    angs = (np.float32(PAST) * inv).astype(np.float32)
    cs = np.cos(angs).astype(np.float32)
    ss = np.sin(angs).astype(np.float32)
    cs_tok = np.tile(np.concatenate([cs, cs])[None, :], (DB, 1)).astype(np.float32)
    sn_tok = np.tile(np.concatenate([-ss, ss])[None, :], (DB, 1)).astype(np.float32)
    rmat = np.zeros((128, 128), np.float32)
    for po in range(128):
        if (po % 64) < 32:
            rmat[po + 32, po] = -1.0
        else:
            rmat[po - 32, po] = 1.0
    k = np.arange(128)[:, None]
    q = np.arange(128)[None, :]
    cur = (k <= q).astype(np.float32)
    prev0 = (k >= q).astype(np.float32)
    prev1 = (k > q).astype(np.float32)
    m0 = np.concatenate([prev0, cur, prev0, cur], axis=1)
    m1 = np.concatenate([prev1, cur, prev1, cur], axis=1)
    masks = np.stack([m0, m1]).astype(np.float32)
    mbias = ((1.0 - masks[:, :, 0:256]) * -30000.0).astype(np.float32)
    rowb = np.zeros((128, 1), np.float32)
    rowb[0, 0] = -30000.0
    return dict(mbias=mbias, rowb=rowb, cosT=cosT, sinT=sinT, cs_tok=cs_tok, sn_tok=sn_tok, rmat=rmat, masks=masks,
                ident=np.eye(128, dtype=np.float32))


class _Stop(Exception):
    pass


def build_nc():
    nc = bass.Bass("TRN2", target_bir_lowering=False)
    try:
        return _build(nc)
    except _Stop:
        return nc


def _build(nc):

    def din(name, shape, dt=F32):
        return nc.dram_tensor(name, list(shape), dt, kind="ExternalInput").ap()

    def dout(name, shape, dt=F32):
        return nc.dram_tensor(name, list(shape), dt, kind="ExternalOutput").ap()

    xp = din("xp", [T, D])
    crows = din("crows", [33, D])
    xs = din("xs", [DB, D])
    ca = [din("ca0", [DB, 128, 512]), din("ca1", [DB, 512, 512]), din("ca2", [DB, 2048, 512])]
    cb = din("cb", [DB, 128, 256])
    ada_w0 = din("ada_w0", [D, 3072])
    ada_w1 = din("ada_w1", [D, 3072])
    ada_kv_w = din("ada_kv_w", [D, 2048])
    brow = din("brow", [1, 8192])
    vecs = din("vecs", [40, 128])
    gpost_rows = din("gpost_rows", [2, D])
    w_in_a = din("w_in_a", [D, 3072])
    w_o_a = din("w_o_a", [768, D])
    w_in_b = din("w_in_b", [D, 2048])
    w_o_b = din("w_o_b", [D, D])
    w_kvx = din("w_kvx", [D, 384])
    sink_fm = din("sink_fm", [128, 8])
    sink_tm = din("sink_tm", [DB, 16])
    ident_d = din("ident", [128, 128])
    rmat_d = din("rmat", [128, 128])
    cosT_d = din("cosT", [128, T])
    sinT_d = din("sinT", [128, T])
    masks_d = din("masks", [2, 128, 512])
    mbias_d = din("mbias", [2, 128, 256])
    cs_tok_d = din("cs_tok", [DB, 64])
    sn_tok_d = din("sn_tok", [DB, 64])
    rowb_d = din("rowb_in", [128, 1])
    gvec_tm = din("gvec_tm", [5, DB, D])
    w_kv_d = din("w_kv", [D, 256])

    yp = dout("yp", [T, D])
    ys = dout("ys", [DB, D])
    aop = [dout("a0p", [128, 512]), dout("a1p", [512, 512]), dout("a2p", [2048, 512])]
    bp = dout("bp", [128, 256])
    aos = [dout("a0s", [DB, 128, 512]), dout("a1s", [DB, 512, 512]), dout("a2s", [DB, 2048, 512])]
    bs = dout("bs", [DB, 128, 256])
    x1s = nc.dram_tensor("x1s", [T, D], F32, kind="Internal").ap()
    g2s = nc.dram_tensor("g2s", [4, 128, T], BF16, kind="Internal").ap()

    with ExitStack() as top:
        P = Prog(nc, top)
        op = P.op

        def stop(tag):
            if os.environ.get("KSTOP") == tag:
                P.flush()
                raise _Stop()

        def sbt(st, name, shape, dt):
            return st.enter_context(nc.sbuf_tensor(name, list(shape), dt))

        def pst(st, name, shape, dt):
            return st.enter_context(nc.psum_tensor(name, list(shape), dt))

        identf = sbt(top, "identf", [128, 128], F32)
        identb = sbt(top, "identb", [128, 128], BF16)
        rmf = sbt(top, "rmf", [128, 128], F32)
        rmb = sbt(top, "rmb", [128, 128], BF16)
        maskb = [sbt(top, "maskb%d" % l, [128, 512], BF16) for l in range(2)]
        mbb = [sbt(top, "mbb%d" % l, [128, 256], BF16) for l in range(2)]
        vecT = sbt(top, "vecT", [128, 40], F32)
        modT = sbt(top, "modT", [128, 64], F32)
        amod = sbt(top, "amod", [128, 24], F32)
        ggbc = [sbt(top, "ggbc%d" % l, [128, D], F32) for l in range(2)]
        esink = sbt(top, "esink", [128, 8], F32)
        ones_b = sbt(top, "ones_b", [128, 64], BF16)
        msq = sbt(top, "msq", [128, 4 * NT], F32)
        msq2 = sbt(top, "msq2", [128, 2 * NT], F32)
        rstd = sbt(top, "rstd", [128, 4 * NT], F32)
        junk = sbt(top, "junk", [128, D], BF16)

        psT = [pst(top, "psT0", [128, 8, 128], BF16)]
        psP = pst(top, "psP", [128, 2, 512], F32)
        psR = pst(top, "psR", [128, 512], F32)
        psS = [pst(top, "psS%d" % i, [128, 512], F32) for i in range(2)]
        psT.append(psR[:].bitcast(BF16).rearrange("p (k t) -> p k t", k=8))
        psOs = [pst(top, "psO%d" % i, [128, 512], F32) for i in range(2)]
        psOf = psOs[0]

        def dma(q, out, in_, reads=(), writes=()):
            return op(q, lambda e, o=out, i=in_: e.dma_start(out=o, in_=i), reads, writes, dma=True)

        def mm(out, lhsT, rhs, start, stop, reads, writes, rg=None):
            return op("pe", lambda e, o=out, l=lhsT, r=rhs, a=start, b=stop: e.matmul(o, lhsT=l, rhs=r, start=a, stop=b),
                      reads, writes, rg=rg)

        def tr(out, in_, ident, reads, writes):
            return op("pe", lambda e, o=out, i=in_, d=ident: e.transpose(out=o, in_=i, identity=d), reads, writes)

        def act(out, in_, func, reads, writes, **kw):
            return op("act", lambda e, o=out, i=in_, f=func, k=kw: e.activation(out=o, in_=i, func=f, **k), reads, writes)

        def cp(eng, out, in_, reads, writes):
            if eng == "act":
                return act(out, in_, AF.Copy, reads, writes)
            return op(eng, lambda e, o=out, i=in_: e.tensor_copy(out=o, in_=i), reads, writes)

        def tt(eng, out, in0, in1, alu, reads, writes):
            return op(eng, lambda e, o=out, a=in0, b=in1, f=alu: e.tensor_tensor(out=o, in0=a, in1=b, op=f), reads, writes)

        def ts(eng, out, in0, s1, s2, op0, op1, reads, writes):
            if s2 is None:
                return op(eng, lambda e, o=out, a=in0, x=s1, f=op0: e.tensor_scalar(out=o, in0=a, scalar1=x, scalar2=None, op0=f),
                          reads, writes)
            return op(eng, lambda e, o=out, a=in0, x=s1, y=s2, f=op0, g=op1:
                      e.tensor_scalar(out=o, in0=a, scalar1=x, scalar2=y, op0=f, op1=g), reads, writes)

        def stt(eng, out, in0, scalar, in1, op0, op1, reads, writes):
            return op(eng, lambda e, o=out, a=in0, s=scalar, b=in1, f=op0, g=op1:
                      e.scalar_tensor_tensor(out=o, in0=a, scalar=s, in1=b, op0=f, op1=g), reads, writes)

        def memset(eng, ap, val, writes):
            return op(eng, lambda e, a=ap, v=val: e.memset(a, v), (), writes)

        cast_rr = [0]

        def load_w(st_slots, dst, src, rows, col0, ncols, dcol0, key):
            nk = rows // 128
            for kc in range(nk):
                for c0 in range(0, ncols, 1024):
                    w = min(1024, ncols - c0)
                    sl = cast_rr[0] % len(st_slots)
                    cast_rr[0] += 1
                    stg = st_slots[sl]
                    dma("sp", stg[:, 0:w], src[kc * 128:(kc + 1) * 128, col0 + c0:col0 + c0 + w], (), [("wst", sl)])
                    eng = ("pool", "dve", "pool", "act")[cast_rr[0] % 4]
                    cp(eng, dst[:, kc, dcol0 + c0:dcol0 + c0 + w], stg[:, 0:w], [("wst", sl)], [key])

        rr = dict(p=0, r=0, s=0, o=0, e=0, w=0)

        def nxt(k, n=2):
            v = rr[k] % n
            rr[k] += 1
            return v

        def evac_eng():
            return ("act", "dve")[nxt("e")]

        AX = mybir.AxisListType

        with ExitStack() as st:
            crow_t = sbt(st, "crow_t", [33, D], F32)
            sc_t = sbt(st, "sc_t", [33, D], F32)
            cT = sbt(st, "cT", [128, 8, 33], F32)
            vec_t = sbt(st, "vec_t", [40, 128], F32)
            ones33 = sbt(st, "ones33", [33, 128], F32)
            adat = [sbt(st, "adat0", [33, 3072], F32), sbt(st, "adat1", [33, 2048], F32)]
            gprow = sbt(st, "gprow", [33, D], F32)
            wstg = [sbt(st, "adaw%d" % i, [128, 3072], F32) for i in range(2)]
            psA = [psS[0], psS[1], psR, psOf, psP[:, 0, :], psP[:, 1, :]]
            sink_t = sbt(st, "sink_t", [128, 8], F32)
            maskf = sbt(st, "maskf", [128, 512], F32)
            xs_t = sbt(st, "xs_t", [DB, D], F32)
            x1_t = sbt(st, "x1_t", [DB, D], F32)
            gv = [sbt(st, "gv%d" % i, [DB, D], F32) for i in range(3)]
            cs_t = sbt(st, "cs_t", [DB, 64], F32)
            sn_t = sbt(st, "sn_t", [DB, 64], F32)
            stm = sbt(st, "stm", [DB, 16], F32)
            rowb = sbt(st, "rowb", [128, 1], F32)
            hs = sbt(st, "hs", [DB, D], F32)
            hs2 = sbt(st, "hs2", [DB, D], F32)
            hsT = sbt(st, "hsT", [128, 8, DB], F32)
            hkTs = sbt(st, "hkTs", [128, 8, DB], F32)
            proj = sbt(st, "proj", [33, 3072], F32)
            brow_t = proj
            kvp = sbt(st, "kvp", [DB, 256], F32)
            rop = sbt(st, "rop", [DB, 1536], F32)
            swp = sbt(st, "swp", [DB, 1536], F32)
            krow = [sbt(st, "krow%d" % i, [DB, 512], F32) for i in range(4)]
            KVflat = sbt(st, "KVflat", [128, DB * 520], F32)
            prod = sbt(st, "prod", [128, 1024], F32)
            scr = sbt(st, "scr", [128, 16], F32)
            PZ = sbt(st, "PZ", [128, DB, 16, DB], F32)
            Oc = sbt(st, "Oc", [DB, 16, 65], F32)
            Ot = sbt(st, "Ot", [DB, 16, 64], F32)
            sml = sbt(st, "sml", [DB, 128], F32)

            dma("sp", identf[:], ident_d[:, :], (), ["identf"])
            dma("sp", rmf[:], rmat_d[:, :], (), ["rmf"])
            dma("sp", crow_t[:], crows[:, :], (), ["crow"])
            dma("sp", vec_t[:], vecs[:, :], (), ["vec_t"])
            dma("sp", sink_t[:], sink_fm[:, :], (), ["sink_t"])
            dma("sp", xs_t[:], xs[:, :], (), ["xs_t"])
            dma("sp", cs_t[:], cs_tok_d[:, :], (), ["cs_t"])
            dma("sp", sn_t[:], sn_tok_d[:, :], (), ["sn_t"])
            dma("sp", stm[:], sink_tm[:, :], (), ["stm"])
            dma("sp", rowb[:], rowb_d[:, :], (), ["rowb"])
            for l in range(2):
                dma("sp", maskf[:], masks_d[l], (), ["maskf"])
                cp("dve", maskb[l][:], maskf[:], ["maskf"], ["maskb%d" % l])
                dma("sp", maskf[:, 0:256], mbias_d[l], ["maskf"], ["maskf"])
                cp("dve", mbb[l][:], maskf[:, 0:256], ["maskf"], ["mbb%d" % l])
            cp("dve", identb[:], identf[:], ["identf"], ["identb"])
            cp("dve", rmb[:], rmf[:], ["rmf"], ["rmb"])
            memset("dve", ones_b[:], 1.0, ["ones_b"])
            memset("dve", ones33[:], 1.0, ["ones33"])
            memset("pool", PZ[:], 0.0, ["PZ"])
            act(esink[:], sink_t[:], AF.Exp, ["sink_t"], ["esink"])
            act(stm[:], stm[:], AF.Exp, ["stm"], ["stm"])
            tr(psOf[:, 0:40], vec_t[:, :], identf[0:40, 0:40], ["vec_t", "identf"], [("psA", 3)])
            cp("dve", vecT[:], psOf[:, 0:40], [("psA", 3)], ["vecT"])
            act(sc_t[:], crow_t[:], AF.Silu, ["crow"], ["sc_t"])
            for kc in range(8):
                tr(psOf[:, 64 + kc * 33:64 + (kc + 1) * 33], sc_t[:, kc * 128:(kc + 1) * 128], identf[0:33, 0:33],
                   ["sc_t", "identf"], [("psA", 3)])
            cp("dve", cT[:].rearrange("p k t -> p (k t)"), psOf[:, 64:64 + 8 * 33], [("psA", 3)], ["cT"])

            def stream_mm(lhsT_of, lkey, M, wd, nk, ncols, dst, dkey, bias_off=None):
                nb = (ncols + 511) // 512
                if bias_off is not None:
                    dma("sp", brow_t[32:33, 0:ncols], brow[:, bias_off:bias_off + ncols], (), ["brow_t"])
                for kc in range(nk):
                    sl = nxt("w")
                    dma("sp", wstg[sl][:, 0:ncols], wd[kc * 128:(kc + 1) * 128, 0:ncols], (), [("adaw", sl)])
                    for b in range(nb):
                        w = min(512, ncols - b * 512)
                        mm(psA[b][0:M, 0:w], lhsT_of(kc), wstg[sl][:, b * 512:b * 512 + w], kc == 0,
                           (kc == nk - 1) and bias_off is None, [lkey, ("adaw", sl)], [("psA", b)])
                for b in range(nb):
                    w = min(512, ncols - b * 512)
                    if bias_off is not None:
                        mm(psA[b][0:M, 0:w], ones33[32:33, 0:M], brow_t[32:33, b * 512:b * 512 + w], False, True,
                           ["ones33", "brow_t"], [("psA", b)])
                    cp("dve" if b % 2 else "act", dst[0:M, b * 512:b * 512 + w], psA[b][0:M, 0:w], [("psA", b)], [(dkey, b)])
                return [(dkey, b) for b in range(nb)]

            def prompt_cols(ad, akeys, ncols, mcol):
                nch = ncols // 128
                for j in range(nch):
                    mm(psS[0][:, j:j + 1], ad[32:33, j * 128:(j + 1) * 128], ones33[32:33, 0:1], True, True,
                       akeys + ["ones33"], [("psA", 0)])
                cp("dve", modT[:, mcol:mcol + nch], psS[0][:, 0:nch], [("psA", 0)], ["modT"])

            def gate_bcast(ad, akeys, li):
                dma("sp", gprow[32:33, :], gpost_rows[li:li + 1, :], (), ["gprow"])
                tt("dve", gprow[32:33, :], gprow[32:33, :], ad[32:33, 2048:3072], ALU.mult, akeys + ["gprow"], ["gprow"])
                for b in range(2):
                    mm(psS[1][:], ones33[32:33, :], gprow[32:33, b * 512:(b + 1) * 512], True, True,
                       ["ones33", "gprow"], [("psA", 1)])
                    cp("act", ggbc[li][:, b * 512:(b + 1) * 512], psS[1][:], [("psA", 1)], ["ggbc%d" % li])

            scol = [100]

            def rms_tok(x_t, xkeys):
                c = scol[0]
                scol[0] += 1
                act(junk[0:DB, :], x_t[:], AF.Square, xkeys, [("msq", c)], scale=1.0 / 32, accum_out=msq[0:DB, c:c + 1])
                act(rstd[0:DB, c:c + 1], msq[0:DB, c:c + 1], AF.Ln, [("msq", c)], [("rstd", c)], bias=EPS)
                act(rstd[0:DB, c:c + 1], rstd[0:DB, c:c + 1], AF.Exp, [("rstd", c)], [("rstd", c)], scale=-0.5)
                return rstd[0:DB, c:c + 1], ("rstd", c)

            def modulate_tok(dst, dkey, x_t, xkeys, rs, rkey, g_t, gkey, scale_ap, shift_ap, akeys):
                ts("dve", dst[:], x_t[:], rs, None, ALU.mult, None, xkeys + [rkey], [dkey])
                tt("dve", dst[:], dst[:], g_t[:], ALU.mult, [dkey, gkey], [dkey])
                stt("dve", dst[:], scale_ap, 1.0, dst[:], ALU.add, ALU.mult, akeys + [dkey], [dkey])
                tt("dve", dst[:], dst[:], shift_ap, ALU.add, akeys + [dkey], [dkey])

            def transpose_tok(dstT, dkey, src, skeys, nk):
                for kc in range(nk):
                    tr(psOf[:, kc * DB:(kc + 1) * DB], src[0:DB, kc * 128:(kc + 1) * 128], identf[0:DB, 0:DB], skeys, [("psA", 3)])
                cp("dve", dstT[:, 0:nk, :].rearrange("p k t -> p (k t)"), psOf[:, 0:nk * DB], [("psA", 3)], [dkey])

            def rope_tok(dst, dkey, src, skeys, nh, sw):
                X = src.rearrange("p (h d) -> p h d", d=64)
                R = dst.rearrange("p (h d) -> p h d", d=64)
                S = sw.rearrange("p (h d) -> p h d", d=64)
                bc = lambda a: a.unsqueeze(1).to_broadcast([DB, nh, a.shape[1]])
                tt("dve", R, X, bc(cs_t[:, :]), ALU.mult, skeys + ["cs_t"], [dkey])
                tt("pool", S[:, :, 0:32], X[:, :, 32:64], bc(sn_t[:, 0:32]), ALU.mult, skeys + ["sn_t"], ["swp"])
                tt("pool", S[:, :, 32:64], X[:, :, 0:32], bc(sn_t[:, 32:64]), ALU.mult, skeys + ["sn_t"], ["swp"])
                tt("dve", R, R, S, ALU.add, [dkey, "swp"], [dkey])

            def pv_slot(hd):
                bk = hd // 7
                t = (psP[:, 0, :], psP[:, 1, :], psR)[bk]
                return t[0:DB, (hd % 7) * 65:(hd % 7) * 65 + 65], ("psA", (4, 5, 2)[bk])

            k0 = stream_mm(lambda kc: cT[:, kc, :], "cT", 33, ada_w0, 8, 3072, adat[0], "ad0", bias_off=0)
            prompt_cols(adat[0], k0, 3072, 0)
            gate_bcast(adat[0], k0, 0)

            dma("sp", gv[0][:], gvec_tm[0], (), [("gv", 0)])
            dma("sp", gv[1][:], gvec_tm[2], (), [("gv", 1)])
            rs, rk = rms_tok(xs_t, ["xs_t"])
            modulate_tok(hs, "hs", xs_t, ["xs_t"], rs, rk, gv[0], ("gv", 0), adat[0][0:DB, 1024:2048], adat[0][0:DB, 0:1024], k0)
            transpose_tok(hsT, "hsT", hs, ["hs"], 8)
            pk = stream_mm(lambda kc: hsT[:, kc, :], "hsT", DB, w_in_a, 8, 3072, proj, "proj")
            rope_tok(rop[:, 0:1536], "rop", proj[0:DB, 0:1536], pk, 24, swp[:, 0:1536])
            for g, L in enumerate((128, 512, 2048)):
                cp("pool", krow[g][:, 0:256], rop[:, 768 + g * 256:768 + (g + 1) * 256], ["rop"], [("krow", g)])
                cp("pool", krow[g][:, 256:512], proj[0:DB, 1536 + g * 256:1536 + (g + 1) * 256], pk, [("krow", g)])
                dma("sp", aos[g][:, L - 1, :], krow[g][:], [("krow", g)], ())
            KV0 = KVflat[:].rearrange("p (b t h d) -> p b t h d", b=DB, t=2, h=4)
            memset("pool", KV0[:, :, 1, :, 64:65], 1.0, ["KVones"])
            for g, (L, dil) in enumerate(((128, 1), (512, 4), (2048, 16))):
                for b in range(DB):
                    dma("sp", KV0[:, b, :, :, 0:64], ca[g][b, 0:L:dil, :].rearrange("j (t h d) -> j t h d", t=2, h=4),
                        ["KVones"], [("KV", b)])
                for b in range(DB):
                    sl = b % 2
                    mm(psS[sl][:, 0:256], identf[0:DB, b:b + 1].to_broadcast([DB, 128]), rop[0:DB, g * 256:(g + 1) * 256], True, True,
                       ["rop", "identf"], [("psA", sl)])
                    p3 = prod[:, 0:256].rearrange("p (h d) -> p h d", h=4)
                    tt("dve", p3, KV0[:, b, 0, :, 0:64], psS[sl][:, 0:256].rearrange("p (h d) -> p h d", h=4), ALU.mult,
                       [("KV", b), ("psA", sl)], ["prod"])
                    op("dve", lambda e, o=scr[:, 0:4], i_=p3: e.tensor_reduce(out=o, in_=i_, axis=AX.X, op=ALU.add), ["prod"], ["scr"])
                    act(PZ[:, b, g * 4:(g + 1) * 4, b], scr[:, 0:4], AF.Exp, ["scr", "PZ"], [("PZ", b)], scale=0.125)
                for h in range(4):
                    hd = g * 4 + h
                    o, okey = pv_slot(hd)
                    for b in range(DB):
                        mm(o, PZ[:, b, hd, :], KV0[:, b, 1, h, :], b == 0, b == DB - 1, [("PZ", b), ("KV", b), "KVones"], [okey])
            cp("dve", Oc[:, 0:7, :], psP[0:DB, 0, 0:455].rearrange("p (h d) -> p h d", d=65), [("psA", 4)], ["Oc"])
            cp("dve", Oc[:, 7:12, :], psP[0:DB, 1, 0:325].rearrange("p (h d) -> p h d", d=65), [("psA", 5)], ["Oc"])

            def finish_attn(nh, qv, kvw, vv, keys):
                q3 = qv.rearrange("p (h d) -> p h d", d=64)
                pr = prod[0:DB, 0:nh * 64].rearrange("p (h d) -> p h d", d=64)
                tt("dve", pr, q3, kvw, ALU.mult, keys, ["prod"])
                op("dve", lambda e, o=sml[:, 32:32 + nh], i_=pr: e.tensor_reduce(out=o, in_=i_, axis=AX.X, op=ALU.add), ["prod"], ["sml"])
                act(sml[:, 32:32 + nh], sml[:, 32:32 + nh], AF.Exp, ["sml"], ["sml"], scale=0.125)
                tt("dve", pr, vv, sml[:, 32:32 + nh].unsqueeze(2).to_broadcast([DB, nh, 64]), ALU.mult, keys + ["sml", "prod"], ["prod"])
                tt("dve", Ot[:, 0:nh, :], Oc[:, 0:nh, 0:64], pr, ALU.add, ["Oc", "prod"], ["Ot"])
                tt("dve", sml[:, 0:nh], Oc[:, 0:nh, 64], sml[:, 32:32 + nh], ALU.add, ["Oc", "sml"], ["sml"])

            k3 = rop[:, 768:1536].rearrange("p (h d) -> p h d", d=64)
            v3 = proj[0:DB, 1536:2304].rearrange("p (h d) -> p h d", d=64)
            finish_attn(12, rop[:, 0:768], k3, v3, ["rop"] + pk)
            tt("dve", sml[:, 64:68], sml[:, 0:4], sml[:, 4:8], ALU.add, ["sml"], ["sml"])
            tt("dve", sml[:, 64:68], sml[:, 64:68], sml[:, 8:12], ALU.add, ["sml"], ["sml"])
            op("dve", lambda e, o=sml[:, 64:68], i_=sml[:, 64:68]: e.reciprocal(out=o, in_=i_), ["sml"], ["sml"])
            og = hs2[:, 0:768]
            for g in range(3):
                tt("dve", og[:, g * 256:(g + 1) * 256].rearrange("p (h d) -> p h d", d=64), Ot[:, g * 4:(g + 1) * 4, :],
                   sml[:, 64:68].unsqueeze(2).to_broadcast([DB, 4, 64]), ALU.mult, ["Ot", "sml"], ["og"])
            act(hs[:, 0:768], proj[0:DB, 2304:3072], AF.Silu, pk + ["hs"], ["hs"])
            tt("dve", og, og, hs[:, 0:768], ALU.mult, ["og", "hs"], ["og"])
            transpose_tok(hsT, "hsT", og, ["og"], 6)
            mk = stream_mm(lambda kc: hsT[:, kc, :], "hsT", DB, w_o_a, 6, 1024, hs, "mix")
            rs, rk = rms_tok(hs, mk)
            ts("dve", hs[:], hs[:], rs, None, ALU.mult, None, mk + [rk], ["mixn"])
            tt("dve", hs[:], hs[:], gv[1][:], ALU.mult, ["mixn", ("gv", 1)], ["mixn"])
            tt("dve", hs[:], hs[:], adat[0][0:DB, 2048:3072], ALU.mult, ["mixn"] + k0, ["mixn"])
            tt("dve", x1_t[:], hs[:], xs_t[:], ALU.add, ["mixn", "xs_t"], ["x1_t"])

            k1 = stream_mm(lambda kc: cT[:, kc, :], "cT", 33, ada_w1, 8, 3072, adat[0], "ad0", bias_off=3072)
            prompt_cols(adat[0], k1, 3072, 24)
            gate_bcast(adat[0], k1, 1)
            kk = stream_mm(lambda kc: cT[:, kc, :], "cT", 33, ada_kv_w, 8, 2048, adat[1], "ad1", bias_off=6144)
            prompt_cols(adat[1], kk, 2048, 48)
            for (dst, sc, gvv) in ((amod[:, 0:8], modT[:, 8:16], vecT[:, 0:8]), (amod[:, 8:16], modT[:, 32:40], vecT[:, 8:16]),
                                   (amod[:, 16:24], modT[:, 56:64], vecT[:, 32:40])):
                ts("dve", dst, sc, 1.0, None, ALU.add, None, ["modT"], ["amod"])
                tt("dve", dst, dst, gvv, ALU.mult, ["amod", "vecT"], ["amod"])

            dma("sp", gv[0][:], gvec_tm[1], (), [("gv", 0)])
            dma("sp", gv[1][:], gvec_tm[3], (), [("gv", 1)])
            dma("sp", gv[2][:], gvec_tm[4], (), [("gv", 2)])
            rs, rk = rms_tok(x1_t, ["x1_t"])
            modulate_tok(hs, "hs", x1_t, ["x1_t"], rs, rk, gv[0], ("gv", 0), adat[0][0:DB, 1024:2048], adat[0][0:DB, 0:1024], k1)
            modulate_tok(hs2, "hs2", x1_t, ["x1_t"], rs, rk, gv[2], ("gv", 2), adat[1][0:DB, 1024:2048], adat[1][0:DB, 0:1024], kk)
            transpose_tok(hsT, "hsT", hs, ["hs"], 8)
            transpose_tok(hkTs, "hkTs", hs2, ["hs2"], 8)
            kvk = stream_mm(lambda kc: hkTs[:, kc, :], "hkTs", DB, w_kv_d, 8, 256, kvp, "kvp")
            pk = stream_mm(lambda kc: hsT[:, kc, :], "hsT", DB, w_in_b, 8, 2048, proj, "proj")
            rope_tok(rop[:, 0:1024], "rop", proj[0:DB, 0:1024], pk, 16, swp[:, 0:1024])
            rope_tok(rop[:, 1024:1152], "ropk", kvp[:, 0:128], kvk, 2, swp[:, 1024:1152])
            cp("pool", krow[3][:, 0:128], rop[:, 1024:1152], ["ropk"], [("krow", 3)])
            cp("pool", krow[3][:, 128:256], kvp[:, 128:256], kvk, [("krow", 3)])
            dma("sp", bs[:, 127, :], krow[3][:, 0:256], [("krow", 3)], ())
            KV1 = KVflat[:, 0:DB * 260].rearrange("p (b t h d) -> p b t h d", b=DB, t=2, h=2)
            allkv = [("KV", b) for b in range(DB)]
            memset("pool", KV1[:, :, 1, :, 64:65], 1.0, allkv + ["KVones"])
            for b in range(DB):
                dma("sp", KV1[:, b, :, :, 0:64], cb[b].rearrange("j (t h d) -> j t h d", t=2, h=2), ["KVones"], [("KV", b)])
            for b in range(DB):
                for kh in range(2):
                    mm(psS[kh][:, :], identf[0:DB, b:b + 1].to_broadcast([DB, 128]), rop[0:DB, kh * 512:(kh + 1) * 512], True, True,
                       ["rop", "identf"], [("psA", kh)])
                    tt("dve", prod[:, kh * 512:(kh + 1) * 512].rearrange("p (h d) -> p h d", d=64),
                       psS[kh][:, :].rearrange("p (h d) -> p h d", d=64),
                       KV1[:, b, 0, kh, 0:64].unsqueeze(1).to_broadcast([128, 8, 64]), ALU.mult, [("KV", b), ("psA", kh)], ["prod"])
                op("dve", lambda e, o=scr[:, 0:16], i_=prod[:, :].rearrange("p (h d) -> p h d", d=64):
                   e.tensor_reduce(out=o, in_=i_, axis=AX.X, op=ALU.add), ["prod"], ["scr"])
                act(PZ[:, b, :, b], scr[:, 0:16], AF.Exp, ["scr", "PZ", "rowb"], [("PZ", b)], scale=0.125, bias=rowb[:, 0:1])
            for hd in range(16):
                o, okey = pv_slot(hd)
                for b in range(DB):
                    mm(o, PZ[:, b, hd, :], KV1[:, b, 1, hd // 8, :], b == 0, b == DB - 1, [("PZ", b), ("KV", b), "KVones"], [okey])
            cp("dve", Oc[:, 0:7, :], psP[0:DB, 0, 0:455].rearrange("p (h d) -> p h d", d=65), [("psA", 4)], ["Oc"])
            cp("dve", Oc[:, 7:14, :], psP[0:DB, 1, 0:455].rearrange("p (h d) -> p h d", d=65), [("psA", 5)], ["Oc"])
            cp("dve", Oc[:, 14:16, :], psR[0:DB, 0:130].rearrange("p (h d) -> p h d", d=65), [("psA", 2)], ["Oc"])
            kb3 = rop[:, 1024:1152].rearrange("p (k d) -> p k d", d=64)
            vb3 = kvp[:, 128:256].rearrange("p (k d) -> p k d", d=64)
            q4 = rop[:, 0:1024]
            for kh in range(2):
                pass
            pr = prod[0:DB, 0:1024].rearrange("p (h d) -> p h d", d=64)
            for kh in range(2):
                tt("dve", pr[:, kh * 8:(kh + 1) * 8, :], q4.rearrange("p (h d) -> p h d", d=64)[:, kh * 8:(kh + 1) * 8, :],
                   kb3[:, kh, :].unsqueeze(1).to_broadcast([DB, 8, 64]), ALU.mult, ["rop", "ropk", "prod"], ["prod"])
            op("dve", lambda e, o=sml[:, 32:48], i_=pr: e.tensor_reduce(out=o, in_=i_, axis=AX.X, op=ALU.add), ["prod"], ["sml"])
            act(sml[:, 32:48], sml[:, 32:48], AF.Exp, ["sml"], ["sml"], scale=0.125)
            for kh in range(2):
                tt("dve", pr[:, kh * 8:(kh + 1) * 8, :], sml[:, 32 + kh * 8:32 + (kh + 1) * 8].unsqueeze(2).to_broadcast([DB, 8, 64]),
                   vb3[:, kh, :].unsqueeze(1).to_broadcast([DB, 8, 64]), ALU.mult, kvk + ["sml", "prod"], ["prod"])
            tt("dve", Ot[:, 0:16, :], Oc[:, 0:16, 0:64], pr, ALU.add, ["Oc", "prod"], ["Ot"])
            tt("dve", sml[:, 0:16], Oc[:, 0:16, 64], sml[:, 32:48], ALU.add, ["Oc", "sml"], ["sml"])
            tt("dve", sml[:, 0:16], sml[:, 0:16], stm[:, :], ALU.add, ["sml", "stm"], ["sml"])
            op("dve", lambda e, o=sml[:, 0:16], i_=sml[:, 0:16]: e.reciprocal(out=o, in_=i_), ["sml"], ["sml"])
            og = hs2[:, 0:1024]
            tt("dve", og.rearrange("p (h d) -> p h d", d=64), Ot[:, 0:16, :], sml[:, 0:16].unsqueeze(2).to_broadcast([DB, 16, 64]),
               ALU.mult, ["Ot", "sml", "hs2"], ["og"])
            act(hs[:, :], proj[0:DB, 1024:2048], AF.Silu, pk + ["hs"], ["hs"])
            tt("dve", og, og, hs[:, :], ALU.mult, ["og", "hs"], ["og"])
            transpose_tok(hsT, "hsT", og, ["og"], 8)
            mk = stream_mm(lambda kc: hsT[:, kc, :], "hsT", DB, w_o_b, 8, 1024, hs, "mix")
            rs, rk = rms_tok(hs, mk)
            ts("dve", hs[:], hs[:], rs, None, ALU.mult, None, mk + [rk], ["mixn"])
            tt("dve", hs[:], hs[:], gv[1][:], ALU.mult, ["mixn", ("gv", 1)], ["mixn"])
            tt("dve", hs[:], hs[:], adat[0][0:DB, 2048:3072], ALU.mult, ["mixn"] + k1, ["mixn"])
            tt("dve", hs[:], hs[:], x1_t[:], ALU.add, ["mixn", "x1_t"], ["ys_t"])
            dma("sp", ys[:, :], hs[:], ["ys_t"], ())
            P.flush()
        if os.environ.get("KSTOP") == "setup":
            return nc

        A0 = amod[:, 0:8]
        SH0 = modT[:, 0:8]
        A1 = amod[:, 8:16]
        SH1 = modT[:, 24:32]
        AKV = amod[:, 16:24]
        SHKV = modT[:, 48:56]

        def load_x(src, srckey, i, xin):
            sl = i % len(xin)
            dma("sp", xin[sl][:], src[i * 128:(i + 1) * 128, :], [(srckey, i)] if srckey else (), [("xin", sl)])

        def norm_stats(tiles, xin, col0):
            for i in tiles:
                sl = i % len(xin)
                c = col0 + i
                act(junk[:], xin[sl][:], AF.Square, [("xin", sl)], [("msq", c)], scale=1.0 / 32, accum_out=msq[:, c:c + 1])
            c0 = col0 + tiles[0]
            n = len(tiles)
            mk = [("msq", col0 + i) for i in tiles]
            rk = [("rstd", col0 + i) for i in tiles]
            act(rstd[:, c0:c0 + n], msq[:, c0:c0 + n], AF.Ln, mk, rk, bias=EPS)
            act(rstd[:, c0:c0 + n], rstd[:, c0:c0 + n], AF.Exp, rk, rk, scale=-0.5)

        def norm_tile(i, u, xin, xn, col0, outs, hslot, stats=True):
            sl = i % len(xin)
            xsl = i % len(xn)
            c = col0 + i
            if stats:
                norm_stats([i], xin, col0)
            ts("dve", xn[xsl][:], xin[sl][:], rstd[:, c:c + 1], None, ALU.mult, None, [("xin", sl), ("rstd", c)], [("xn", xsl)])
            pt = psT[i % 2]
            ptk = ("psT", 0) if i % 2 == 0 else "psR"
            for kc in range(8):
                tr(pt[:, kc, :], xn[xsl][:, kc * 128:(kc + 1) * 128], identb[:], [("xn", xsl)], [ptk])
            for (hT, hname, Am, Sm) in outs:
                for kc in range(8):
                    o = hT[:, kc, u * 128:(u + 1) * 128]
                    if i % 2 == 0:
                        ts("dve", o, pt[:, kc, :], Am[:, kc:kc + 1], Sm[:, kc:kc + 1], ALU.mult, ALU.add,
                           [ptk], [(hname, hslot, u, kc)])
                    else:
                        act(o, pt[:, kc, :], AF.Identity, [ptk], [(hname, hslot, u, kc)],
                            scale=Am[:, kc:kc + 1], bias=Sm[:, kc:kc + 1])

        def project(hT, hname, hslot, W, cols, nk=8):
            slot = nxt("p")
            for kc in range(nk):
                mm(psP[:, slot, :], W[:, kc, cols], hT[:, kc, :], kc == 0, kc == nk - 1,
                   [(hname, hslot, u, kc) for u in range(4)], [("psP", slot)])
            return slot

        def rope(slot, ctab, stab, tkey, qraw, t12, dst, dkey, f32dst=None, fkey=None):
            r = nxt("r")
            cp("act", qraw[r][:], psP[:, slot, :], [("psP", slot)], [("qraw", r)])
            mm(psR[:], rmb[:], qraw[r][:], True, True, [("qraw", r)], ["psR"])
            tt("pool", t12[:, r, :], qraw[r][:], ctab, ALU.mult, [("qraw", r), tkey], [("t1", r)])
            tt("dve", t12[:, 2 + r, :], psR[:], stab, ALU.mult, ["psR", tkey], [("t2", r)])
            tt("pool", dst, t12[:, r, :], t12[:, 2 + r, :], ALU.add, [("t1", r), ("t2", r)], [dkey])
            if f32dst is not None:
                tt("pool", f32dst, t12[:, r, :], t12[:, 2 + r, :], ALU.add, [("t1", r), ("t2", r)], [fkey])

        def cache_out(kvf, kkeys, u, ncol, stage, dst_rows, skey):
            n = len(kvf)
            for k4 in range(n):
                tr(psR[:, k4 * 128:(k4 + 1) * 128], kvf[k4][:, u * 128:(u + 1) * 128], identf[:], [kkeys[k4]], ["psR"])
            sl = nxt("o")
            cp(evac_eng(), stage[sl][:, 0:ncol], psR[:, 0:ncol], ["psR"], [(skey, sl)])
            dma("sp", dst_rows, stage[sl][:, 0:ncol], [(skey, sl)], ())

        pending = []

        def attn_back(l, heads, pt, sl, has_prev, rv):
            osl = nxt("o")
            psO = psOs[osl]
            for hh, h in enumerate(heads):
                rows = slice(hh * 64, (hh + 1) * 64)
                pp = pt[:, hh * 256:hh * 256 + 128]
                pc = pt[:, hh * 256 + 128:hh * 256 + 256]
                for (o, vp, vc) in ((psO[rows, 0:128], h["vp"], h["vc"]), (psO[rows, 128:256], ones_b[:, :], ones_b[:, :])):
                    if has_prev:
                        mm(o, vp, pp, True, False, rv + [("PT", sl)], [("psO", osl)])
                    mm(o, vc, pc, not has_prev, True, rv + [("PT", sl)], [("psO", osl)])
            eng = evac_eng()
            for (dkey, okey, cols) in (("od", "ok", slice(0, 128)), ("dd", "dk", slice(128, 256))):
                for hh, h in enumerate(heads):
                    pass
                d0, d1 = heads[0][dkey], heads[1][dkey]
                src = psO[:, cols]
                dst = heads[0][dkey + "_full"]
                if len(dst.shape) == 3:
                    src = src.rearrange("p (a b) -> p a b", a=dst.shape[1])
                cp(eng, dst, src, [("psO", osl)], [heads[0][okey], heads[1][okey]])

        def attn_flush():
            while pending:
                attn_back(*pending.pop(0))

        def attn_pair(l, heads, PT, rq, rk, rv):
            sl = nxt("s")
            ps = psS[sl]
            pt = PT[sl]
            has_prev = heads[0]["kp"] is not None
            for hh, h in enumerate(heads):
                if has_prev:
                    mm(ps[:, hh * 256:hh * 256 + 256], identb[:], mbb[l][:], True, False, (), [("psS", sl)])
                    mm(ps[:, hh * 256:hh * 256 + 128], h["kp"], h["q"], False, False, rq + rk, [("psS", sl)])
                    mm(ps[:, hh * 256 + 128:hh * 256 + 256], h["kc"], h["q"], False, True, rq + rk, [("psS", sl)])
                else:
                    mm(ps[:, hh * 256 + 128:hh * 256 + 256], identb[:], mbb[l][:, 128:256], True, False, (), [("psS", sl)])
                    mm(ps[:, hh * 256 + 128:hh * 256 + 256], h["kc"], h["q"], False, True, rq + rk, [("psS", sl)])
            if has_prev:
                act(pt[:], ps[:], AF.Exp, [("psS", sl)], [("PT", sl)], scale=0.125)
            else:
                v = lambda a: a.rearrange("p (h x) -> p h x", h=2)[:, :, 128:256]
                act(v(pt[:]), v(ps[:]), AF.Exp, [("psS", sl)], [("PT", sl)], scale=0.125)
            attn_flush()
            pending.append((l, heads, pt, sl, has_prev, rv))

        T12K = [("t1", 0), ("t1", 1), ("t2", 0), ("t2", 1)]

        def out_proj_tile(i, u, oTf, okeys, Wo, nk, ggl, xres, col0, dst_dram, dkey, t12):
            if i % 2 == 0:
                banks = [(psP[:, 0, :], ("psP", 0)), (psP[:, 1, :], ("psP", 1))]
            else:
                banks = [(psS[0][:, :], ("psS", 0)), (psS[1][:, :], ("psS", 1))]
            for nb in range(2):
                for kc in range(nk):
                    mm(banks[nb][0], oTf[kc][:, u * 128:(u + 1) * 128], Wo[:, kc, nb * 512:(nb + 1) * 512], kc == 0, kc == nk - 1,
                       [okeys[kc]], [banks[nb][1]])
            c = col0 + i
            for nb in range(2):
                act(junk[:, 0:512], banks[nb][0], AF.Square, [banks[nb][1]], [("msq", c, nb)], scale=1.0 / 32,
                    accum_out=msq2[:, 2 * i + nb:2 * i + nb + 1])
            tt("dve", msq[:, c:c + 1], msq2[:, 2 * i:2 * i + 1], msq2[:, 2 * i + 1:2 * i + 2], ALU.add, [("msq", c, 0), ("msq", c, 1)], [("msq", c)])
            act(rstd[:, c:c + 1], msq[:, c:c + 1], AF.Ln, [("msq", c)], [("rstd", c)], bias=EPS)
            act(rstd[:, c:c + 1], rstd[:, c:c + 1], AF.Exp, [("rstd", c)], [("rstd", c)], scale=-0.5)
            r = i % 2
            for nb in range(2):
                stt("dve", t12[:, 2 * r + nb, :], banks[nb][0], rstd[:, c:c + 1], ggl[:, nb * 512:(nb + 1) * 512], ALU.mult, ALU.mult,
                    [banks[nb][1], ("rstd", c)], [T12K[2 * r + nb]])
            sl = i % len(xres)
            tm = t12[:, 2 * r:2 * r + 2, :].rearrange("p a b -> p (a b)")
            tt("pool", xres[sl][:], tm, xres[sl][:], ALU.add, [T12K[2 * r], T12K[2 * r + 1], ("xres", sl)], [("xres", sl)])
            dma("sp", dst_dram[i * 128:(i + 1) * 128, :], xres[sl][:], [("xres", sl)], [(dkey, i)] if dkey else ())

        bg = []
        for b in range(DB):
            for q4 in range(4):
                lo, hi = 512 * q4, min(512 * (q4 + 1), 2047)
                bg.append((aos[2][b, lo:hi, :], ca[2][b, lo + 1:hi + 1, :]))
        for b in range(DB):
            bg.append((aos[1][b, 0:511, :], ca[1][b, 1:512, :]))
        bg.append((aos[0][:, 0:127, :], ca[0][:, 1:128, :]))
        bg.append((bs[:, 0:127, :], cb[:, 1:128, :]))

        def bg_issue(n):
            for _ in range(n):
                if bg:
                    d, s_ = bg.pop(0)
                    dma("act", d, s_, (), ())

        with ExitStack() as st:
            Wa2 = sbt(st, "Wa2", [128, 8, 768], BF16)
            wst = [sbt(st, "wstA%d" % i, [128, 1024], F32) for i in range(2)]
            xin = [sbt(st, "xinA%d" % i, [128, D], F32) for i in range(4)]
            xn = [sbt(st, "xnA%d" % i, [128, D], BF16) for i in range(2)]
            hT = [sbt(st, "hTA%d" % i, [128, 8, 512], BF16) for i in range(2)]
            QT2 = [sbt(st, "QT2_%d" % c, [128, T], BF16) for c in range(2)]
            KT2 = [sbt(st, "KT2_%d" % c, [128, T], BF16) for c in range(2)]
            VT2 = [sbt(st, "VT2_%d" % c, [128, T], BF16) for c in range(2)]
            ctab = [sbt(st, "ctabA%d" % i, [128, 512], F32) for i in range(2)]
            stab = [sbt(st, "stabA%d" % i, [128, 512], F32) for i in range(2)]
            qraw = [sbt(st, "qrawA%d" % i, [128, 512], BF16) for i in range(2)]
            t12 = sbt(st, "t12A", [128, 4, 512], F32)
            kvf = [sbt(st, "kvfA%d" % i, [128, 512], F32) for i in range(4)]
            stage = [sbt(st, "stageA%d" % i, [128, 512], F32) for i in range(2)]
            PT = [sbt(st, "PTA%d" % i, [128, 512], BF16) for i in range(2)]
            VA = [sbt(st, "VAA%d" % i, [128, 4, 128], BF16) for i in range(4)]
            G2st = [sbt(st, "G2st%d" % k, [128, T], BF16) for k in range(4)]
            load_w(wst, Wa2, w_in_a, 1024, 512, 256, 0, "w")
            load_w(wst, Wa2, w_in_a, 1024, 768 + 512, 256, 256, "w")
            load_w(wst, Wa2, w_in_a, 1024, 1536 + 512, 256, 512, "w")
            for v in VA:
                memset("pool", v[:, :, 64:128], 1.0, [("w", "va")])
            P.flush()
            print("[phaseA] sbuf remaining", nc.sbuf_bytes_remaining)
            stop("A0")
            for u in range(4):
                load_x(xp, None, u, xin)
            for s in range(NS):
                hs = s % 2
                tk = slice(512 * s, 512 * (s + 1))
                dma("sp", ctab[hs][:], cosT_d[:, tk], (), [("tab", hs)])
                dma("sp", stab[hs][:], sinT_d[:, tk], (), [("tab", hs)])
                norm_stats([4 * s + u for u in range(4)], xin, 0)
                for u in range(4):
                    norm_tile(4 * s + u, u, xin, xn, 0, [(hT[hs], "hT", A0, SH0)], hs, stats=False)
                stop("A1")
                if s + 1 < NS:
                    for u in range(4):
                        load_x(xp, None, 4 * (s + 1) + u, xin)
                for (kind, cc, wc) in (("k", 0, 256), ("k", 1, 384), ("v", 0, 512), ("v", 1, 640), ("q", 0, 0), ("q", 1, 128)):
                    slot = project(hT[hs], "hT", hs, Wa2, slice(wc, wc + 128))
                    if kind == "k":
                        rope(slot, ctab[hs][:], stab[hs][:], ("tab", hs), qraw, t12, KT2[cc][:, tk], ("KT2", cc, s),
                             kvf[cc][:] if s >= 4 else None, ("kvf", cc))
                    elif kind == "q":
                        rope(slot, ctab[hs][:], stab[hs][:], ("tab", hs), qraw, t12, QT2[cc][:, tk], ("QT2", cc, s))
                    else:
                        cp("act", VT2[cc][:, tk], psP[:, slot, :], [("psP", slot)], [("VT2", cc, s)])
                        if s >= 4:
                            cp("dve", kvf[2 + cc][:], psP[:, slot, :], [("psP", slot)], [("kvf", 2 + cc)])
                stop("A2")
                if s >= 4:
                    for u in range(4):
                        i = 4 * s + u
                        cache_out(kvf, [("kvf", k) for k in range(4)], u, 512, stage, aop[2][(i - 16) * 128:(i - 15) * 128, :], "stage")
            stop("A3")
            allq = [[("QT2", c, s) for s in range(NS)] for c in range(2)]
            allk = [[("KT2", c, s) for s in range(NS)] for c in range(2)]
            allv = [[("VT2", c, s) for s in range(NS)] for c in range(2)]
            for r in range(16):
                for j in range(2):
                    tq = slice(2048 * j + r, 2048 * (j + 1), 16)
                    tp = slice(2048 * (j - 1) + r, 2048 * j, 16)
                    b = 2 * r + j
                    vsl = b % 4
                    ptv = psT[0]
                    for c in range(2):
                        tr(ptv[:, c, :], VT2[c][:, tq], identb[:], allv[c], [("psT", 0)])
                    cp(evac_eng(), VA[vsl][:, :, 0:64], ptv[:, 0:2, :].rearrange("p c (h d) -> p (c h) d", h=2),
                       [("psT", 0)], [("VA", vsl)])
                    for c in range(2):
                        heads = []
                        for hh in range(2):
                            h = 2 * c + hh
                            rows = slice(hh * 64, (hh + 1) * 64)
                            od = G2st[c][rows, :].rearrange("p (s r m) -> p s r m", s=8, r=16)[:, 4 * j:4 * j + 4, r, :]
                            dd = G2st[2 + c][rows, :].rearrange("p (s r m) -> p s r m", s=8, r=16)[:, 4 * j:4 * j + 4, r, :]
                            heads.append(dict(q=QT2[c][rows, tq], kc=KT2[c][rows, tq], kp=(KT2[c][rows, tp] if j == 1 else None),
                                              vc=VA[vsl][:, h, 0:64], vp=(VA[(b - 1) % 4][:, h, 0:64] if j == 1 else None),
                                              od=od, dd=dd, ok=("G2", c, b, hh), dk=("G2", 2 + c, b, hh),
                                              od_full=G2st[c][:, :].rearrange("p (s r m) -> p s r m", s=8, r=16)[:, 4 * j:4 * j + 4, r, :],
                                              dd_full=G2st[2 + c][:, :].rearrange("p (s r m) -> p s r m", s=8, r=16)[:, 4 * j:4 * j + 4, r, :]))
                        attn_pair(0, heads, PT, allq[c], allk[c], [("VA", vsl), ("VA", (b - 1) % 4)])
            attn_flush()
            for k in range(4):
                dma("sp", g2s[k], G2st[k][:], [("G2", k, b, hh) for b in range(32) for hh in range(2)], [("g2s", k)])
            P.flush()
        if os.environ.get("KSTOP") == "A":
            return nc

        with ExitStack() as st:
            Wab = sbt(st, "Wab", [128, 8, 2304], BF16)
            Woa = sbt(st, "Woa", [128, 6, 1024], BF16)
            wst = [sbt(st, "wstB%d" % i, [128, 1024], F32) for i in range(2)]
            xin = [sbt(st, "xinB%d" % i, [128, D], F32) for i in range(2)]
            xres = [sbt(st, "xresB%d" % i, [128, D], F32) for i in range(2)]
            xn = [sbt(st, "xnB%d" % i, [128, D], BF16) for i in range(2)]
            hT = [sbt(st, "hTB%d" % i, [128, 8, 512], BF16) for i in range(1)]
            QT = [sbt(st, "QTB%d" % c, [128, 512], BF16) for c in range(4)]
            KT = [sbt(st, "KTB%d" % c, [128, 1024], BF16) for c in range(4)]
            VT = [sbt(st, "VTB%d" % c, [128, 1024], BF16) for c in range(4)]
            gT = [sbt(st, "gTB%d" % c, [128, 512], BF16) for c in range(6)]
            OT = [sbt(st, "OTB%d" % c, [128, 512], BF16) for c in range(4)]
            DT = [sbt(st, "DTB%d" % c, [128, 512], BF16) for c in range(4)]
            oTf = [sbt(st, "oTfB%d" % c, [128, 512], BF16) for c in range(6)]
            G2in = [sbt(st, "G2in%d" % k, [128, 512], BF16) for k in range(4)]
            dsum2 = [sbt(st, "dsumB%d" % i, [128, 512], F32) for i in range(2)]
            otmp2 = [sbt(st, "otmpB%d" % i, [128, 512], BF16) for i in range(2)]
            ctab = [sbt(st, "ctabB%d" % i, [128, 512], F32) for i in range(2)]
            stab = [sbt(st, "stabB%d" % i, [128, 512], F32) for i in range(2)]
            qraw = [sbt(st, "qrawB%d" % i, [128, 512], BF16) for i in range(2)]
            t12 = sbt(st, "t12B", [128, 4, 512], F32)
            kvf = [sbt(st, "kvfB%d" % i, [128, 512], F32) for i in range(4)]
            stage = [sbt(st, "stageB%d" % i, [128, 512], F32) for i in range(2)]
            PT = [sbt(st, "PTB%d" % i, [128, 512], BF16) for i in range(2)]
            VA0 = [sbt(st, "VA0_%d" % i, [128, 4, 128], BF16) for i in range(4)]
            VA1 = [sbt(st, "VA1_%d" % i, [128, 4, 128], BF16) for i in range(8)]
            print("[phaseB] sbuf remaining", nc.sbuf_bytes_remaining)
            load_w(wst, Wab, w_in_a, 1024, 0, 512, 0, "w")
            load_w(wst, Wab, w_in_a, 1024, 768, 512, 512, "w")
            load_w(wst, Wab, w_in_a, 1024, 1536, 512, 1024, "w")
            load_w(wst, Wab, w_in_a, 1024, 2304, 768, 1536, "w")
            load_w(wst, Woa, w_o_a, 768, 0, 1024, 0, "w")
            for v in VA0 + VA1:
                memset("pool", v[:, :, 64:128], 1.0, [("w", "va")])
            P.flush()
            load_x(xp, None, 0, xin)
            load_x(xp, None, 1, xin)
            for s in range(NS):
                hs = s % 2
                ts_ = hs * 512
                tk = slice(512 * s, 512 * (s + 1))
                dma("sp", ctab[hs][:], cosT_d[:, tk], (), [("tab", hs)])
                dma("sp", stab[hs][:], sinT_d[:, tk], (), [("tab", hs)])
                for k in range(4):
                    dma("sp", G2in[k][:], g2s[k][:, tk], [("g2s", k)], [("G2in", k)])
                bg_issue(6)
                for u in range(4):
                    i = 4 * s + u
                    norm_tile(i, u, xin, xn, 32, [(hT[0], "hT", A0, SH0)], 0)
                    if i + 2 < NT:
                        load_x(xp, None, i + 2, xin)
                ring = slice(ts_, ts_ + 512)
                for g in range(2):
                    for (kind, cc, wc) in (("k", 2 * g, 512 + 256 * g), ("k", 2 * g + 1, 640 + 256 * g),
                                           ("v", 2 * g, 1024 + 256 * g), ("v", 2 * g + 1, 1152 + 256 * g)):
                        slot = project(hT[0], "hT", 0, Wab, slice(wc, wc + 128))
                        need_f = (s == 7)
                        if kind == "k":
                            rope(slot, ctab[hs][:], stab[hs][:], ("tab", hs), qraw, t12, KT[cc][:, ring], ("KT", cc, hs),
                                 kvf[cc % 2][:] if need_f else None, ("kvf", cc % 2))
                        else:
                            cp("act", VT[cc][:, ring], psP[:, slot, :], [("psP", slot)], [("VT", cc, hs)])
                            if need_f:
                                cp("dve", kvf[2 + cc % 2][:], psP[:, slot, :], [("psP", slot)], [("kvf", 2 + cc % 2)])
                    if s == 7:
                        if g == 0:
                            cache_out(kvf, [("kvf", k) for k in range(4)], 3, 512, stage, aop[0][:, :], "stage")
                        else:
                            for u in range(4):
                                cache_out(kvf, [("kvf", k) for k in range(4)], u, 512, stage, aop[1][u * 128:(u + 1) * 128, :], "stage")
                for cc in range(4):
                    slot = project(hT[0], "hT", 0, Wab, slice(cc * 128, (cc + 1) * 128))
                    rope(slot, ctab[hs][:], stab[hs][:], ("tab", hs), qraw, t12, QT[cc][:], ("QT", cc))
                for cc in range(6):
                    slot = project(hT[0], "hT", 0, Wab, slice(1536 + cc * 128, 1536 + (cc + 1) * 128))
                    act(gT[cc][:], psP[:, slot, :], AF.Silu, [("psP", slot)], [("gT", cc)])
                for u in range(4):
                    i = 4 * s + u
                    vsl = i % 4
                    ptv = psT[0]
                    cols = slice(ts_ + u * 128, ts_ + (u + 1) * 128)
                    if u > 0:
                        pcols = slice(ts_ + (u - 1) * 128, ts_ + u * 128)
                    else:
                        pcols = slice((1 - hs) * 512 + 384, (1 - hs) * 512 + 512)
                    for c in range(2):
                        tr(ptv[:, c, :], VT[c][:, cols], identb[:], [("VT", c, hs)], [("psT", 0)])
                    cp(evac_eng(), VA0[vsl][:, :, 0:64], ptv[:, 0:2, :].rearrange("p c (h d) -> p (c h) d", h=2),
                       [("psT", 0)], [("VA0", vsl)])
                    for c in range(2):
                        heads = []
                        for hh in range(2):
                            h = 2 * c + hh
                            rows = slice(hh * 64, (hh + 1) * 64)
                            heads.append(dict(q=QT[c][rows, u * 128:(u + 1) * 128], kc=KT[c][rows, cols],
                                              kp=(KT[c][rows, pcols] if i > 0 else None),
                                              vc=VA0[vsl][:, h, 0:64], vp=(VA0[(i - 1) % 4][:, h, 0:64] if i > 0 else None),
                                              od=OT[c][rows, u * 128:(u + 1) * 128], dd=DT[c][rows, u * 128:(u + 1) * 128],
                                              od_full=OT[c][:, u * 128:(u + 1) * 128], dd_full=DT[c][:, u * 128:(u + 1) * 128],
                                              ok=("OT", c, u, hh), dk=("DT", c, u, hh)))
                        attn_pair(0, heads, PT, [("QT", c)], [("KT", c, 0), ("KT", c, 1)], [("VA0", vsl), ("VA0", (i - 1) % 4)])
                for r4 in range(4):
                    vsl = hs * 4 + r4
                    psl = (1 - hs) * 4 + r4
                    b = 4 * s + r4
                    ptv = psT[0]
                    cols = slice(ts_ + r4, ts_ + 512, 4)
                    pcols = slice((1 - hs) * 512 + r4, (1 - hs) * 512 + 512, 4)
                    for c in range(2):
                        tr(ptv[:, c, :], VT[2 + c][:, cols], identb[:], [("VT", 2 + c, hs)], [("psT", 0)])
                    cp(evac_eng(), VA1[vsl][:, :, 0:64], ptv[:, 0:2, :].rearrange("p c (h d) -> p (c h) d", h=2),
                       [("psT", 0)], [("VA1", vsl)])
                    for c in range(2):
                        heads = []
                        for hh in range(2):
                            h = 2 * c + hh
                            rows = slice(hh * 64, (hh + 1) * 64)
                            heads.append(dict(q=QT[2 + c][rows, r4:512:4], kc=KT[2 + c][rows, cols],
                                              kp=(KT[2 + c][rows, pcols] if s > 0 else None),
                                              vc=VA1[vsl][:, h, 0:64], vp=(VA1[psl][:, h, 0:64] if s > 0 else None),
                                              od=OT[2 + c][rows, r4:512:4], dd=DT[2 + c][rows, r4:512:4],
                                              od_full=OT[2 + c][:, r4:512:4], dd_full=DT[2 + c][:, r4:512:4],
                                              ok=("OT", 2 + c, r4, hh), dk=("DT", 2 + c, r4, hh)))
                        attn_pair(0, heads, PT, [("QT", 2 + c)], [("KT", 2 + c, 0), ("KT", 2 + c, 1)], [("VA1", vsl), ("VA1", psl)])
                attn_flush()
                nat = lambda a: a.rearrange("p (r m) -> p m r", r=16)
                v3 = lambda a: a.rearrange("p (m r) -> p m r", r=16)
                for cc in range(2):
                    k01 = [("DT", c, x, hh) for c in (cc, 2 + cc) for x in range(4) for hh in range(2)]
                    ds = dsum2[cc]
                    tt("dve", ds[:], DT[cc][:], DT[2 + cc][:], ALU.add, k01, [("dsum", cc)])
                    tt("pool", v3(ds[:]), v3(ds[:]), nat(G2in[2 + cc][:]), ALU.add, [("dsum", cc), ("G2in", 2 + cc)], [("dsum", cc)])
                    act(ds[:], ds[:], AF.Ln, [("dsum", cc)], [("dsum", cc)])
                    act(ds[:], ds[:], AF.Exp, [("dsum", cc)], [("dsum", cc)], scale=-1.0)
                    for g in range(3):
                        c6 = 2 * g + cc
                        ot = otmp2[g % 2]
                        ok_ = ("otmp", g % 2)
                        if g < 2:
                            sk = [("OT", c6, x, hh) for x in range(4) for hh in range(2)]
                            tt("dve", ot[:], OT[c6][:], ds[:], ALU.mult, sk + [("dsum", cc)], [ok_])
                        else:
                            tt("pool", v3(ot[:]), nat(G2in[cc][:]), v3(ds[:]), ALU.mult, [("G2in", cc), ("dsum", cc)], [ok_])
                        tt("dve" if g == 2 else "pool", oTf[c6][:], ot[:], gT[c6][:], ALU.mult, [ok_, ("gT", c6)], [("oTf", c6)])
                for u in range(4):
                    i = 4 * s + u
                    dma("sp", xres[i % 2][:], xp[i * 128:(i + 1) * 128, :], (), [("xres", i % 2)])
                    out_proj_tile(i, u, oTf, [("oTf", c) for c in range(6)], Woa, 6, ggbc[0], xres, 64, x1s, "x1s", t12)
            P.flush()
        if os.environ.get("KSTOP") == "B":
            return nc

        with ExitStack() as st:
            Wq = sbt(st, "Wq", [128, 8, 2048], BF16)
            Wob = sbt(st, "Wob", [128, 8, 1024], BF16)
            Wkv = sbt(st, "Wkv", [128, 8, 384], BF16)
            wst = [sbt(st, "wstC%d" % i, [128, 1024], F32) for i in range(2)]
            xin = [sbt(st, "xinC%d" % i, [128, D], F32) for i in range(2)]
            xres = [sbt(st, "xresC%d" % i, [128, D], F32) for i in range(2)]
            xn = [sbt(st, "xnC%d" % i, [128, D], BF16) for i in range(2)]
            hT = sbt(st, "hTC", [128, 8, 512], BF16)
            hkT = sbt(st, "hkTC", [128, 8, 512], BF16)
            QT = [sbt(st, "QTC%d" % c, [128, 512], BF16) for c in range(8)]
            KT = [sbt(st, "KTC%d" % c, [128, 1024], BF16) for c in range(2)]
            VT = sbt(st, "VTC", [128, 1024], BF16)
            gT = [sbt(st, "gTC%d" % c, [128, 512], BF16) for c in range(8)]
            OT = [sbt(st, "OTC%d" % c, [128, 512], BF16) for c in range(8)]
            DT = [sbt(st, "DTC%d" % c, [128, 512], BF16) for c in range(8)]
            dsum = [sbt(st, "dsumC%d" % i, [128, 512], F32) for i in range(2)]
            otmp = [sbt(st, "otmpC%d" % i, [128, 512], BF16) for i in range(2)]
            ctab = [sbt(st, "ctabC%d" % i, [128, 512], F32) for i in range(2)]
            stab = [sbt(st, "stabC%d" % i, [128, 512], F32) for i in range(2)]
            qraw = [sbt(st, "qrawC%d" % i, [128, 512], BF16) for i in range(2)]
            t12 = sbt(st, "t12C", [128, 4, 512], F32)
            kvf = [sbt(st, "kvfC%d" % i, [128, 512], F32) for i in range(2)]
            stage = [sbt(st, "stageC%d" % i, [128, 256], F32) for i in range(2)]
            PT = [sbt(st, "PTC%d" % i, [128, 512], BF16) for i in range(2)]
            VAc = [sbt(st, "VAc%d" % i, [128, 2, 128], BF16) for i in range(4)]
            print("[phaseC] sbuf remaining", nc.sbuf_bytes_remaining)
            load_w(wst, Wq, w_in_b, 1024, 0, 2048, 0, "w")
            load_w(wst, Wob, w_o_b, 1024, 0, 1024, 0, "w")
            load_w(wst, Wkv, w_kvx, 1024, 0, 384, 0, "w")
            for v in VAc:
                memset("pool", v[:, :, 64:128], 1.0, [("w", "va")])
            P.flush()
            dma("sp", xin[0][:], x1s[0:128, :], (), [("xin", 0)])
            dma("sp", xin[1][:], x1s[128:256, :], (), [("xin", 1)])
            for s in range(NS):
                hs = s % 2
                ts_ = hs * 512
                tk = slice(512 * s, 512 * (s + 1))
                ring = slice(ts_, ts_ + 512)
                dma("sp", ctab[hs][:], cosT_d[:, tk], (), [("tab", hs)])
                dma("sp", stab[hs][:], sinT_d[:, tk], (), [("tab", hs)])
                bg_issue(6 if s < 6 else 40)
                for u in range(4):
                    i = 4 * s + u
                    norm_tile(i, u, xin, xn, 0, [(hT, "hT", A1, SH1), (hkT, "hkT", AKV, SHKV)], 0)
                    if i + 2 < NT:
                        dma("sp", xin[i % 2][:], x1s[(i + 2) * 128:(i + 3) * 128, :], (), [("xin", i % 2)])
                need_f = (s == 7)
                slot = project(hkT, "hkT", 0, Wkv, slice(0, 128))
                rope(slot, ctab[hs][:], stab[hs][:], ("tab", hs), qraw, t12, KT[0][:, ring], ("KT", 0, hs),
                     kvf[0][:] if need_f else None, ("kvf", 0))
                slot = project(hkT, "hkT", 0, Wkv, slice(256, 384))
                rope(slot, ctab[hs][:], stab[hs][:], ("tab", hs), qraw, t12, KT[1][:, ring], ("KT", 1, hs))
                slot = project(hkT, "hkT", 0, Wkv, slice(128, 256))
                cp("act", VT[:, ring], psP[:, slot, :], [("psP", slot)], [("VT", hs)])
                if need_f:
                    cp("dve", kvf[1][:], psP[:, slot, :], [("psP", slot)], [("kvf", 1)])
                    cache_out(kvf, [("kvf", 0), ("kvf", 1)], 3, 256, stage, bp[:, :], "stage")
                for cc in range(8):
                    slot = project(hT, "hT", 0, Wq, slice(cc * 128, (cc + 1) * 128))
                    rope(slot, ctab[hs][:], stab[hs][:], ("tab", hs), qraw, t12, QT[cc][:], ("QT", cc))
                for cc in range(8):
                    slot = project(hT, "hT", 0, Wq, slice(1024 + cc * 128, 1024 + (cc + 1) * 128))
                    act(gT[cc][:], psP[:, slot, :], AF.Silu, [("psP", slot)], [("gT", cc)])
                for u in range(4):
                    i = 4 * s + u
                    vsl = i % 4
                    ptv = psT[0]
                    cols = slice(ts_ + u * 128, ts_ + (u + 1) * 128)
                    if u > 0:
                        pcols = slice(ts_ + (u - 1) * 128, ts_ + u * 128)
                    else:
                        pcols = slice((1 - hs) * 512 + 384, (1 - hs) * 512 + 512)
                    tr(ptv[:, 0, :], VT[:, cols], identb[:], [("VT", hs)], [("psT", 0)])
                    cp(evac_eng(), VAc[vsl][:, :, 0:64], ptv[:, 0, :].rearrange("p (h d) -> p h d", h=2),
                       [("psT", 0)], [("VAc", vsl)])
                    for c in range(8):
                        kvh = c // 4
                        heads = []
                        for hh in range(2):
                            rows = slice(hh * 64, (hh + 1) * 64)
                            ksrc = KT[0] if (kvh == hh) else KT[1]
                            heads.append(dict(q=QT[c][rows, u * 128:(u + 1) * 128], kc=ksrc[rows, cols],
                                              kp=(ksrc[rows, pcols] if i > 0 else None),
                                              vc=VAc[vsl][:, kvh, 0:64], vp=(VAc[(i - 1) % 4][:, kvh, 0:64] if i > 0 else None),
                                              od=OT[c][rows, u * 128:(u + 1) * 128], dd=DT[c][rows, u * 128:(u + 1) * 128],
                                              od_full=OT[c][:, u * 128:(u + 1) * 128], dd_full=DT[c][:, u * 128:(u + 1) * 128],
                                              ok=("OT", c, u, hh), dk=("DT", c, u, hh)))
                        attn_pair(1, heads, PT, [("QT", c)], [("KT", 0, 0), ("KT", 0, 1), ("KT", 1, 0), ("KT", 1, 1)],
                                  [("VAc", vsl), ("VAc", (i - 1) % 4)])
                attn_flush()
                for c in range(8):
                    dk = [("DT", c, x, hh) for x in range(4) for hh in range(2)]
                    okk = [("OT", c, x, hh) for x in range(4) for hh in range(2)]
                    b2 = c % 2
                    act(dsum[b2][:], DT[c][:], AF.Ln, dk, [("dsum", b2)], bias=esink[:, c:c + 1])
                    act(dsum[b2][:], dsum[b2][:], AF.Exp, [("dsum", b2)], [("dsum", b2)], scale=-1.0)
                    tt("dve", otmp[b2][:], OT[c][:], dsum[b2][:], ALU.mult, okk + [("dsum", b2)], [("otmp", b2)])
                    tt("pool" if c % 2 else "dve", OT[c][:], otmp[b2][:], gT[c][:], ALU.mult, [("otmp", b2), ("gT", c)] + okk, [("oTf", c)] + okk)
                for u in range(4):
                    i = 4 * s + u
                    dma("sp", xres[i % 2][:], x1s[i * 128:(i + 1) * 128, :], (), [("xres", i % 2)])
                    out_proj_tile(i, u, OT, [("oTf", c) for c in range(8)], Wob, 8, ggbc[1], xres, 32, yp, None, t12)
            P.flush(final=True)
        print("[kernel] ops=%d waits=%d sig=%s" % (P.nops, P.nwait, P.cnt))
    return nc


_NC_CACHE = {}


def kernel(x_prompt, x_sample, c_prompt, c_sample, cache_a_kv_g0, cache_a_kv_g1, cache_a_kv_g2, cache_b_kv,
           ada_w, ada_b, g_pre, g_post, w_in_a, w_o_a, w_in_b, w_o_b, sinks_b, ada_kv_w, ada_kv_b, g_kv, w_kv,
           _cores=None):
    f = lambda a: np.ascontiguousarray(np.asarray(a, dtype=np.float32))
    x_prompt, x_sample, c_prompt, c_sample = f(x_prompt), f(x_sample), f(c_prompt), f(c_sample)
    ca = [f(cache_a_kv_g0), f(cache_a_kv_g1), f(cache_a_kv_g2)]
    cbk = f(cache_b_kv)
    ada_w, ada_b, g_pre, g_post = f(ada_w), f(ada_b), f(g_pre), f(g_post)
    w_in_a, w_o_a, w_in_b, w_o_b = f(w_in_a), f(w_o_a), f(w_in_b), f(w_o_b)
    sinks_b, ada_kv_w, ada_kv_b, g_kv, w_kv = f(sinks_b), f(ada_kv_w), f(ada_kv_b), f(g_kv), f(w_kv)
    cst = _consts()
    cores = list(range(NCORES)) if _cores is None else list(_cores)
    shared = dict(
        ada_w0=f(ada_w[0]), ada_w1=f(ada_w[1]), ada_kv_w=ada_kv_w,
        brow=f(np.concatenate([ada_b[0], ada_b[1], ada_kv_b])[None, :]),
        vecs=f(np.concatenate([g_pre[0].reshape(8, 128), g_pre[1].reshape(8, 128), g_post[0].reshape(8, 128),
                               g_post[1].reshape(8, 128), g_kv.reshape(8, 128)], axis=0)),
        gpost_rows=g_post, w_in_a=f(w_in_a[0]), w_o_a=f(w_o_a[0]), w_in_b=f(w_in_b[0]), w_o_b=f(w_o_b[0]),
        w_kvx=f(np.concatenate([w_kv[:, 0:256], w_kv[:, 64:128], w_kv[:, 0:64]], axis=1)),
        sink_fm=f(sinks_b[0].reshape(8, 2).T.repeat(64, axis=0)),
        sink_tm=f(np.tile(sinks_b[0][None, :], (DB, 1))),
        ident=cst["ident"], rmat=cst["rmat"], cosT=cst["cosT"], sinT=cst["sinT"], masks=cst["masks"], mbias=cst["mbias"],
        cs_tok=cst["cs_tok"], sn_tok=cst["sn_tok"], rowb_in=cst["rowb"], w_kv=w_kv,
        gvec_tm=f(np.stack([np.tile(v[None, :], (DB, 1)) for v in (g_pre[0], g_pre[1], g_post[0], g_post[1], g_kv)])),
    )
    in_maps = []
    for b in cores:
        crows = np.zeros((33, D), np.float32)
        crows[0:DB] = c_sample[DB * b:DB * (b + 1)]
        crows[32] = c_prompt[b]
        m = dict(shared)
        m.update(xp=x_prompt[b], crows=crows, xs=f(x_sample[DB * b:DB * (b + 1), 0, :]),
                 ca0=ca[0][0, DB * b:DB * (b + 1)].reshape(DB, 128, 512),
                 ca1=ca[1][0, DB * b:DB * (b + 1)].reshape(DB, 512, 512),
                 ca2=ca[2][0, DB * b:DB * (b + 1)].reshape(DB, 2048, 512),
                 cb=cbk[DB * b:DB * (b + 1)].reshape(DB, 128, 256))
        in_maps.append(m)
    if "nc" not in _NC_CACHE:
        _NC_CACHE["nc"] = build_nc()
    nc = _NC_CACHE["nc"]
    res = run_bass_kernel_spmd(nc, in_maps, core_ids=list(range(len(cores))))
    R = res.results
    n = len(cores)
    y_prompt = np.stack([R[k]["yp"] for k in range(n)])
    y_sample = np.concatenate([R[k]["ys"] for k in range(n)])[:, None, :]
    a_p = [np.stack([R[k]["a%dp" % g] for k in range(n)]).reshape(1, n, -1, 2, 4, 64) for g in range(3)]
    b_p = np.stack([R[k]["bp"] for k in range(n)]).reshape(n, 128, 2, 2, 64)
    a_s = [np.concatenate([R[k]["a%ds" % g] for k in range(n)]).reshape(1, n * DB, -1, 2, 4, 64) for g in range(3)]
    b_s = np.concatenate([R[k]["bs"] for k in range(n)]).reshape(n * DB, 128, 2, 2, 64)
    return (y_prompt, y_sample, a_p[0], a_p[1], a_p[2], b_p, a_s[0], a_s[1], a_s[2], b_s)
```

```python
import os
import numpy as np
from contextlib import ExitStack
import concourse.bass as bass
import concourse.mybir as mybir
from concourse.bass_utils import run_bass_kernel_spmd

F32 = mybir.dt.float32
BF16 = mybir.dt.bfloat16
ALU = mybir.AluOpType
AF = mybir.ActivationFunctionType

NCORES = 8
T = 4096
D = 1024
NT = 32
NS = 8
EPS = 1e-6
PAST = 16384
DB = 16
STRICT_SAME_ENGINE = os.environ.get("KSTRICT", "0") == "1"


class Prog:
    CE = ("pe", "act", "dve", "pool")
    ENG = ("pe", "act", "dve", "pool", "sp")

    def __init__(self, nc, stack, ndsem=20):
        self.nc = nc
        self.eng = dict(pe=nc.tensor, act=nc.scalar, dve=nc.vector, pool=nc.gpsimd, sp=nc.sync)
        self.ops = []
        self.last_w = {}
        self.readers = {}
        self.esem = {e: stack.enter_context(nc.semaphore("S_" + e)) for e in self.CE}
        self.cnt = {e: 0 for e in self.CE}
        nq = dict(sp=ndsem, pool=2, act=72)
        self.dsem = {e: [stack.enter_context(nc.semaphore("D_%s%d" % (e, i))) for i in range(nq[e])]
                     for e in ("sp", "pool", "act")}
        self.dcum = {e: [0] * nq[e] for e in self.dsem}
        self.dnext = {e: 0 for e in self.dsem}
        self.waited = {e: {} for e in self.ENG}
        self.nwait = 0
        self.nops = 0

    @staticmethod
    def _isps(k):
        n = k[0] if isinstance(k, tuple) else k
        return isinstance(n, str) and n.startswith("ps")

    def op(self, eng, fn, reads=(), writes=(), dma=False, rg=None):
        writes = list(writes) + [r for r in reads if self._isps(r)]
        reads = [r for r in reads if not self._isps(r)]
        i = len(self.ops)
        raw = set()
        oth = set()
        for r in reads:
            if r in self.last_w:
                raw.add(self.last_w[r])
        for w in writes:
            if w in self.last_w:
                oth.add(self.last_w[w])
            for j in self.readers.get(w, {}).values():
                oth.add(j)
        raw.discard(i)
        oth.discard(i)
        self.ops.append(dict(eng=eng, fn=fn, raw=raw, oth=oth - raw, dma=dma, rg=rg))
        for r in reads:
            self.readers.setdefault(r, {})[(eng if not dma else ("dma", i))] = i
        for w in writes:
            self.last_w[w] = i
            self.readers[w] = {}
        return i

    def _needs_wait(self, o, p, is_raw):
        if p["dma"] or o["dma"]:
            return True
        if p["eng"] != o["eng"]:
            return True
        if o["eng"] == "pe":
            return o["rg"] is not None and p["rg"] is not None and o["rg"] != p["rg"]
        return is_raw or STRICT_SAME_ENGINE

    def _wait(self, e, sem, val):
        k = id(sem)
        if self.waited[e].get(k, 0) >= val:
            return
        self.eng[e].wait_ge(sem, val)
        self.waited[e][k] = val
        self.nwait += 1

    def flush(self, final=False):
        ops = self.ops
        n = len(ops)
        need = [False] * n
        for o in ops:
            for d in o["raw"]:
                if self._needs_wait(o, ops[d], True):
                    need[d] = True
            for d in o["oth"]:
                if self._needs_wait(o, ops[d], False):
                    need[d] = True
        last = {}
        for i, o in enumerate(ops):
            if not o["dma"]:
                last[o["eng"]] = i
        for i in last.values():
            need[i] = True
        sig = [None] * n
        for i, o in enumerate(ops):
            e = o["eng"]
            eo = self.eng[e]
            deps = [(d, True) for d in o["raw"]] + [(d, False) for d in o["oth"]]
            for d, is_raw in sorted(deps):
                p = ops[d]
                if not self._needs_wait(o, p, is_raw):
                    continue
                s, v = sig[d]
                self._wait(e, s, v)
            if o["dma"]:
                k = self.dnext[e]
                self.dnext[e] = (k + 1) % len(self.dsem[e])
                s = self.dsem[e][k]
                self._wait(e, s, self.dcum[e][k])
                ins = o["fn"](eo)
                self.dcum[e][k] += 16
                ins.then_inc(s, 16)
                sig[i] = (s, self.dcum[e][k])
            else:
                ins = o["fn"](eo)
                if need[i]:
                    self.cnt[e] += 1
                    ins.then_inc(self.esem[e], 1)
                    sig[i] = (self.esem[e], self.cnt[e])
        for e in self.ENG:
            for x in self.CE:
                if x != e and self.cnt[x] > 0:
                    self._wait(e, self.esem[x], self.cnt[x])
            for q in self.dsem:
                if q == "act" and not final:
                    continue
                for k, s in enumerate(self.dsem[q]):
                    if self.dcum[q][k] > 0:
                        self._wait(e, s, self.dcum[q][k])
        self.nops += n
        print("[flush] ops", n, "cnt", self.cnt, "dcum max", {q: max(v) for q, v in self.dcum.items()})
        self.ops = []
        self.last_w = {}
        self.readers = {}


def _consts():
    half = 32
    inv = (10000.0 ** (-np.arange(half, dtype=np.float32) / half)).astype(np.float32)
    pos = np.arange(T, dtype=np.float32)
    ang = pos[:, None] * inv[None, :]
    cos = np.cos(ang).astype(np.float32)
    sin = np.sin(ang).astype(np.float32)
    fi = (np.arange(128) % 64) % 32
    cosT = np.ascontiguousarray(cos[:, fi].T)
    sinT = np.ascontiguousarray(sin[:, fi].T)
    angs = (np.float32(PAST) * inv).astype(np.float32)
    cs = np.cos(angs).astype(np.float32)
    ss = np.sin(angs).astype(np.float32)
    cs_tok = np.tile(np.concatenate([cs, cs])[None, :], (DB, 1)).astype(np.float32)
    sn_tok = np.tile(np.concatenate([-ss, ss])[None, :], (DB, 1)).astype(np.float32)
    rmat = np.zeros((128, 128), np.float32)
    for po in range(128):
        if (po % 64) < 32:
            rmat[po + 32, po] = -1.0
        else:
            rmat[po - 32, po] = 1.0
    k = np.arange(128)[:, None]
    q = np.arange(128)[None, :]
    cur = (k <= q).astype(np.float32)
    prev0 = (k >= q).astype(np.float32)
    prev1 = (k > q).astype(np.float32)
    m0 = np.concatenate([prev0, cur, prev0, cur], axis=1)
    m1 = np.concatenate([prev1, cur, prev1, cur], axis=1)
    masks = np.stack([m0, m1]).astype(np.float32)
    mbias = ((1.0 - masks[:, :, 0:256]) * -30000.0).astype(np.float32)
    rowb = np.zeros((128, 1), np.float32)
    rowb[0, 0] = -30000.0
    return dict(mbias=mbias, rowb=rowb, cosT=cosT, sinT=sinT, cs_tok=cs_tok, sn_tok=sn_tok, rmat=rmat, masks=masks,
                ident=np.eye(128, dtype=np.float32))


class _Stop(Exception):
    pass


def build_nc():
    nc = bass.Bass("TRN2", target_bir_lowering=False)
    try:
        return _build(nc)
    except _Stop:
        return nc


def _build(nc):

    def din(name, shape, dt=F32):
        return nc.dram_tensor(name, list(shape), dt, kind="ExternalInput").ap()

    def dout(name, shape, dt=F32):
        return nc.dram_tensor(name, list(shape), dt, kind="ExternalOutput").ap()

    xp = din("xp", [T, D])
    crows = din("crows", [33, D])
    xs = din("xs", [DB, D])
    ca = [din("ca0", [DB, 128, 512]), din("ca1", [DB, 512, 512]), din("ca2", [DB, 2048, 512])]
    cb = din("cb", [DB, 128, 256])
    ada_w0 = din("ada_w0", [D, 3072])
    ada_w1 = din("ada_w1", [D, 3072])
    ada_kv_w = din("ada_kv_w", [D, 2048])
    brow = din("brow", [1, 8192])
    vecs = din("vecs", [40, 128])
    gpost_rows = din("gpost_rows", [2, D])
    w_in_a = din("w_in_a", [D, 3072])
    w_o_a = din("w_o_a", [768, D])
    w_in_b = din("w_in_b", [D, 2048])
    w_o_b = din("w_o_b", [D, D])
    w_kvx = din("w_kvx", [D, 384])
    sink_fm = din("sink_fm", [128, 8])
    sink_tm = din("sink_tm", [DB, 16])
    ident_d = din("ident", [128, 128])
    rmat_d = din("rmat", [128, 128])
    cosT_d = din("cosT", [128, T])
    sinT_d = din("sinT", [128, T])
    masks_d = din("masks", [2, 128, 512])
    mbias_d = din("mbias", [2, 128, 256])
    cs_tok_d = din("cs_tok", [DB, 64])
    sn_tok_d = din("sn_tok", [DB, 64])
    rowb_d = din("rowb_in", [128, 1])
    gvec_tm = din("gvec_tm", [5, DB, D])
    w_kv_d = din("w_kv", [D, 256])

    yp = dout("yp", [T, D])
    ys = dout("ys", [DB, D])
    aop = [dout("a0p", [128, 512]), dout("a1p", [512, 512]), dout("a2p", [2048, 512])]
    bp = dout("bp", [128, 256])
    aos = [dout("a0s", [DB, 128, 512]), dout("a1s", [DB, 512, 512]), dout("a2s", [DB, 2048, 512])]
    bs = dout("bs", [DB, 128, 256])
    x1s = nc.dram_tensor("x1s", [T, D], F32, kind="Internal").ap()
    g2s = nc.dram_tensor("g2s", [4, 128, T], BF16, kind="Internal").ap()

    with ExitStack() as top:
        P = Prog(nc, top)
        op = P.op

        def stop(tag):
            if os.environ.get("KSTOP") == tag:
                P.flush()
                raise _Stop()

        def sbt(st, name, shape, dt):
            return st.enter_context(nc.sbuf_tensor(name, list(shape), dt))

        def pst(st, name, shape, dt):
            return st.enter_context(nc.psum_tensor(name, list(shape), dt))

        identf = sbt(top, "identf", [128, 128], F32)
        identb = sbt(top, "identb", [128, 128], BF16)
        rmf = sbt(top, "rmf", [128, 128], F32)
        rmb = sbt(top, "rmb", [128, 128], BF16)
        maskb = [sbt(top, "maskb%d" % l, [128, 512], BF16) for l in range(2)]
        mbb = [sbt(top, "mbb%d" % l, [128, 256], BF16) for l in range(2)]
        vecT = sbt(top, "vecT", [128, 40], F32)
        modT = sbt(top, "modT", [128, 64], F32)
        amod = sbt(top, "amod", [128, 24], F32)
        ggbc = [sbt(top, "ggbc%d" % l, [128, D], F32) for l in range(2)]
        esink = sbt(top, "esink", [128, 8], F32)
        ones_b = sbt(top, "ones_b", [128, 64], BF16)
        msq = sbt(top, "msq", [128, 4 * NT], F32)
        msq2 = sbt(top, "msq2", [128, 2 * NT], F32)
        rstd = sbt(top, "rstd", [128, 4 * NT], F32)
        junk = sbt(top, "junk", [128, D], BF16)

        psT = [pst(top, "psT0", [128, 8, 128], BF16)]
        psP = pst(top, "psP", [128, 2, 512], F32)
        psR = pst(top, "psR", [128, 512], F32)
        psS = [pst(top, "psS%d" % i, [128, 512], F32) for i in range(2)]
        psT.append(psR[:].bitcast(BF16).rearrange("p (k t) -> p k t", k=8))
        psOs = [pst(top, "psO%d" % i, [128, 512], F32) for i in range(2)]
        psOf = psOs[0]

        def dma(q, out, in_, reads=(), writes=()):
            return op(q, lambda e, o=out, i=in_: e.dma_start(out=o, in_=i), reads, writes, dma=True)

        def mm(out, lhsT, rhs, start, stop, reads, writes, rg=None):
            return op("pe", lambda e, o=out, l=lhsT, r=rhs, a=start, b=stop: e.matmul(o, lhsT=l, rhs=r, start=a, stop=b),
                      reads, writes, rg=rg)

        def tr(out, in_, ident, reads, writes):
            return op("pe", lambda e, o=out, i=in_, d=ident: e.transpose(out=o, in_=i, identity=d), reads, writes)

        def act(out, in_, func, reads, writes, **kw):
            return op("act", lambda e, o=out, i=in_, f=func, k=kw: e.activation(out=o, in_=i, func=f, **k), reads, writes)

        def cp(eng, out, in_, reads, writes):
            if eng == "act":
                return act(out, in_, AF.Copy, reads, writes)
            return op(eng, lambda e, o=out, i=in_: e.tensor_copy(out=o, in_=i), reads, writes)

        def tt(eng, out, in0, in1, alu, reads, writes):
            return op(eng, lambda e, o=out, a=in0, b=in1, f=alu: e.tensor_tensor(out=o, in0=a, in1=b, op=f), reads, writes)

        def ts(eng, out, in0, s1, s2, op0, op1, reads, writes):
            if s2 is None:
                return op(eng, lambda e, o=out, a=in0, x=s1, f=op0: e.tensor_scalar(out=o, in0=a, scalar1=x, scalar2=None, op0=f),
                          reads, writes)
            return op(eng, lambda e, o=out, a=in0, x=s1, y=s2, f=op0, g=op1:
                      e.tensor_scalar(out=o, in0=a, scalar1=x, scalar2=y, op0=f, op1=g), reads, writes)

        def stt(eng, out, in0, scalar, in1, op0, op1, reads, writes):
            return op(eng, lambda e, o=out, a=in0, s=scalar, b=in1, f=op0, g=op1:
                      e.scalar_tensor_tensor(out=o, in0=a, scalar=s, in1=b, op0=f, op1=g), reads, writes)

        def memset(eng, ap, val, writes):
            return op(eng, lambda e, a=ap, v=val: e.memset(a, v), (), writes)

        cast_rr = [0]

        def load_w(st_slots, dst, src, rows, col0, ncols, dcol0, key):
            nk = rows // 128
            for kc in range(nk):
                for c0 in range(0, ncols, 1024):
                    w = min(1024, ncols - c0)
                    sl = cast_rr[0] % len(st_slots)
                    cast_rr[0] += 1
                    stg = st_slots[sl]
                    dma("sp", stg[:, 0:w], src[kc * 128:(kc + 1) * 128, col0 + c0:col0 + c0 + w], (), [("wst", sl)])
                    eng = ("pool", "dve", "pool", "act")[cast_rr[0] % 4]
                    cp(eng, dst[:, kc, dcol0 + c0:dcol0 + c0 + w], stg[:, 0:w], [("wst", sl)], [key])

        rr = dict(p=0, r=0, s=0, o=0, e=0, w=0)

        def nxt(k, n=2):
            v = rr[k] % n
            rr[k] += 1
            return v

        def evac_eng():
            return ("act", "dve")[nxt("e")]

        AX = mybir.AxisListType

        with ExitStack() as st:
            crow_t = sbt(st, "crow_t", [33, D], F32)
            sc_t = sbt(st, "sc_t", [33, D], F32)
            cT = sbt(st, "cT", [128, 8, 33], F32)
            vec_t = sbt(st, "vec_t", [40, 128], F32)
            ones33 = sbt(st, "ones33", [33, 128], F32)
            adat = [sbt(st, "adat0", [33, 3072], F32), sbt(st, "adat1", [33, 2048], F32)]
            gprow = sbt(st, "gprow", [33, D], F32)
            wstg = [sbt(st, "adaw%d" % i, [128, 3072], F32) for i in range(2)]
            psA = [psS[0], psS[1], psR, psOf, psP[:, 0, :], psP[:, 1, :]]
            sink_t = sbt(st, "sink_t", [128, 8], F32)
            maskf = sbt(st, "maskf", [128, 512], F32)
            xs_t = sbt(st, "xs_t", [DB, D], F32)
            x1_t = sbt(st, "x1_t", [DB, D], F32)
            gv = [sbt(st, "gv%d" % i, [DB, D], F32) for i in range(3)]
            cs_t = sbt(st, "cs_t", [DB, 64], F32)
            sn_t = sbt(st, "sn_t", [DB, 64], F32)
            stm = sbt(st, "stm", [DB, 16], F32)
            rowb = sbt(st, "rowb", [128, 1], F32)
            hs = sbt(st, "hs", [DB, D], F32)
            hs2 = sbt(st, "hs2", [DB, D], F32)
            hsT = sbt(st, "hsT", [128, 8, DB], F32)
            hkTs = sbt(st, "hkTs", [128, 8, DB], F32)
            proj = sbt(st, "proj", [33, 3072], F32)
            brow_t = proj
            kvp = sbt(st, "kvp", [DB, 256], F32)
            rop = sbt(st, "rop", [DB, 1536], F32)
            swp = sbt(st, "swp", [DB, 1536], F32)
            krow = [sbt(st, "krow%d" % i, [DB, 512], F32) for i in range(4)]
            KVflat = sbt(st, "KVflat", [128, DB * 520], F32)
            prod = sbt(st, "prod", [128, 1024], F32)
            scr = sbt(st, "scr", [128, 16], F32)
            PZ = sbt(st, "PZ", [128, DB, 16, DB], F32)
            Oc = sbt(st, "Oc", [DB, 16, 65], F32)
            Ot = sbt(st, "Ot", [DB, 16, 64], F32)
            sml = sbt(st, "sml", [DB, 128], F32)

            dma("sp", identf[:], ident_d[:, :], (), ["identf"])
            dma("sp", rmf[:], rmat_d[:, :], (), ["rmf"])
            dma("sp", crow_t[:], crows[:, :], (), ["crow"])
            dma("sp", vec_t[:], vecs[:, :], (), ["vec_t"])
            dma("sp", sink_t[:], sink_fm[:, :], (), ["sink_t"])
            dma("sp", xs_t[:], xs[:, :], (), ["xs_t"])
            dma("sp", cs_t[:], cs_tok_d[:, :], (), ["cs_t"])
            dma("sp", sn_t[:], sn_tok_d[:, :], (), ["sn_t"])
            dma("sp", stm[:], sink_tm[:, :], (), ["stm"])
            dma("sp", rowb[:], rowb_d[:, :], (), ["rowb"])
            for l in range(2):
                dma("sp", maskf[:], masks_d[l], (), ["maskf"])
                cp("dve", maskb[l][:], maskf[:], ["maskf"], ["maskb%d" % l])
                dma("sp", maskf[:, 0:256], mbias_d[l], ["maskf"], ["maskf"])
                cp("dve", mbb[l][:], maskf[:, 0:256], ["maskf"], ["mbb%d" % l])
            cp("dve", identb[:], identf[:], ["identf"], ["identb"])
            cp("dve", rmb[:], rmf[:], ["rmf"], ["rmb"])
            memset("dve", ones_b[:], 1.0, ["ones_b"])
            memset("dve", ones33[:], 1.0, ["ones33"])
            memset("pool", PZ[:], 0.0, ["PZ"])
            act(esink[:], sink_t[:], AF.Exp, ["sink_t"], ["esink"])
            act(stm[:], stm[:], AF.Exp, ["stm"], ["stm"])
            tr(psOf[:, 0:40], vec_t[:, :], identf[0:40, 0:40], ["vec_t", "identf"], [("psA", 3)])
            cp("dve", vecT[:], psOf[:, 0:40], [("psA", 3)], ["vecT"])
            act(sc_t[:], crow_t[:], AF.Silu, ["crow"], ["sc_t"])
            for kc in range(8):
                tr(psOf[:, 64 + kc * 33:64 + (kc + 1) * 33], sc_t[:, kc * 128:(kc + 1) * 128], identf[0:33, 0:33],
                   ["sc_t", "identf"], [("psA", 3)])
            cp("dve", cT[:].rearrange("p k t -> p (k t)"), psOf[:, 64:64 + 8 * 33], [("psA", 3)], ["cT"])

            def stream_mm(lhsT_of, lkey, M, wd, nk, ncols, dst, dkey, bias_off=None):
                nb = (ncols + 511) // 512
                if bias_off is not None:
                    dma("sp", brow_t[32:33, 0:ncols], brow[:, bias_off:bias_off + ncols], (), ["brow_t"])
                for kc in range(nk):
                    sl = nxt("w")
                    dma("sp", wstg[sl][:, 0:ncols], wd[kc * 128:(kc + 1) * 128, 0:ncols], (), [("adaw", sl)])
                    for b in range(nb):
                        w = min(512, ncols - b * 512)
                        mm(psA[b][0:M, 0:w], lhsT_of(kc), wstg[sl][:, b * 512:b * 512 + w], kc == 0,
                           (kc == nk - 1) and bias_off is None, [lkey, ("adaw", sl)], [("psA", b)])
                for b in range(nb):
                    w = min(512, ncols - b * 512)
                    if bias_off is not None:
                        mm(psA[b][0:M, 0:w], ones33[32:33, 0:M], brow_t[32:33, b * 512:b * 512 + w], False, True,
                           ["ones33", "brow_t"], [("psA", b)])
                    cp("dve" if b % 2 else "act", dst[0:M, b * 512:b * 512 + w], psA[b][0:M, 0:w], [("psA", b)], [(dkey, b)])
                return [(dkey, b) for b in range(nb)]

            def prompt_cols(ad, akeys, ncols, mcol):
                nch = ncols // 128
                for j in range(nch):
                    mm(psS[0][:, j:j + 1], ad[32:33, j * 128:(j + 1) * 128], ones33[32:33, 0:1], True, True,
                       akeys + ["ones33"], [("psA", 0)])
                cp("dve", modT[:, mcol:mcol + nch], psS[0][:, 0:nch], [("psA", 0)], ["modT"])

            def gate_bcast(ad, akeys, li):
                dma("sp", gprow[32:33, :], gpost_rows[li:li + 1, :], (), ["gprow"])
                tt("dve", gprow[32:33, :], gprow[32:33, :], ad[32:33, 2048:3072], ALU.mult, akeys + ["gprow"], ["gprow"])
                for b in range(2):
                    mm(psS[1][:], ones33[32:33, :], gprow[32:33, b * 512:(b + 1) * 512], True, True,
                       ["ones33", "gprow"], [("psA", 1)])
                    cp("act", ggbc[li][:, b * 512:(b + 1) * 512], psS[1][:], [("psA", 1)], ["ggbc%d" % li])

            scol = [100]

            def rms_tok(x_t, xkeys):
                c = scol[0]
                scol[0] += 1
                act(junk[0:DB, :], x_t[:], AF.Square, xkeys, [("msq", c)], scale=1.0 / 32, accum_out=msq[0:DB, c:c + 1])
                act(rstd[0:DB, c:c + 1], msq[0:DB, c:c + 1], AF.Ln, [("msq", c)], [("rstd", c)], bias=EPS)
                act(rstd[0:DB, c:c + 1], rstd[0:DB, c:c + 1], AF.Exp, [("rstd", c)], [("rstd", c)], scale=-0.5)
                return rstd[0:DB, c:c + 1], ("rstd", c)

            def modulate_tok(dst, dkey, x_t, xkeys, rs, rkey, g_t, gkey, scale_ap, shift_ap, akeys):
                ts("dve", dst[:], x_t[:], rs, None, ALU.mult, None, xkeys + [rkey], [dkey])
                tt("dve", dst[:], dst[:], g_t[:], ALU.mult, [dkey, gkey], [dkey])
                stt("dve", dst[:], scale_ap, 1.0, dst[:], ALU.add, ALU.mult, akeys + [dkey], [dkey])
                tt("dve", dst[:], dst[:], shift_ap, ALU.add, akeys + [dkey], [dkey])

            def transpose_tok(dstT, dkey, src, skeys, nk):
                for kc in range(nk):
                    tr(psOf[:, kc * DB:(kc + 1) * DB], src[0:DB, kc * 128:(kc + 1) * 128], identf[0:DB, 0:DB], skeys, [("psA", 3)])
                cp("dve", dstT[:, 0:nk, :].rearrange("p k t -> p (k t)"), psOf[:, 0:nk * DB], [("psA", 3)], [dkey])

            def rope_tok(dst, dkey, src, skeys, nh, sw):
                X = src.rearrange("p (h d) -> p h d", d=64)
                R = dst.rearrange("p (h d) -> p h d", d=64)
                S = sw.rearrange("p (h d) -> p h d", d=64)
                bc = lambda a: a.unsqueeze(1).to_broadcast([DB, nh, a.shape[1]])
                tt("dve", R, X, bc(cs_t[:, :]), ALU.mult, skeys + ["cs_t"], [dkey])
                tt("pool", S[:, :, 0:32], X[:, :, 32:64], bc(sn_t[:, 0:32]), ALU.mult, skeys + ["sn_t"], ["swp"])
                tt("pool", S[:, :, 32:64], X[:, :, 0:32], bc(sn_t[:, 32:64]), ALU.mult, skeys + ["sn_t"], ["swp"])
                tt("dve", R, R, S, ALU.add, [dkey, "swp"], [dkey])

            def pv_slot(hd):
                bk = hd // 7
                t = (psP[:, 0, :], psP[:, 1, :], psR)[bk]
                return t[0:DB, (hd % 7) * 65:(hd % 7) * 65 + 65], ("psA", (4, 5, 2)[bk])

            k0 = stream_mm(lambda kc: cT[:, kc, :], "cT", 33, ada_w0, 8, 3072, adat[0], "ad0", bias_off=0)
            prompt_cols(adat[0], k0, 3072, 0)
            gate_bcast(adat[0], k0, 0)

            dma("sp", gv[0][:], gvec_tm[0], (), [("gv", 0)])
            dma("sp", gv[1][:], gvec_tm[2], (), [("gv", 1)])
            rs, rk = rms_tok(xs_t, ["xs_t"])
            modulate_tok(hs, "hs", xs_t, ["xs_t"], rs, rk, gv[0], ("gv", 0), adat[0][0:DB, 1024:2048], adat[0][0:DB, 0:1024], k0)
            transpose_tok(hsT, "hsT", hs, ["hs"], 8)
            pk = stream_mm(lambda kc: hsT[:, kc, :], "hsT", DB, w_in_a, 8, 3072, proj, "proj")
            rope_tok(rop[:, 0:1536], "rop", proj[0:DB, 0:1536], pk, 24, swp[:, 0:1536])
            for g, L in enumerate((128, 512, 2048)):
                cp("pool", krow[g][:, 0:256], rop[:, 768 + g * 256:768 + (g + 1) * 256], ["rop"], [("krow", g)])
                cp("pool", krow[g][:, 256:512], proj[0:DB, 1536 + g * 256:1536 + (g + 1) * 256], pk, [("krow", g)])
                dma("sp", aos[g][:, L - 1, :], krow[g][:], [("krow", g)], ())
            KV0 = KVflat[:].rearrange("p (b t h d) -> p b t h d", b=DB, t=2, h=4)
            memset("pool", KV0[:, :, 1, :, 64:65], 1.0, ["KVones"])
            for g, (L, dil) in enumerate(((128, 1), (512, 4), (2048, 16))):
                for b in range(DB):
                    dma("sp", KV0[:, b, :, :, 0:64], ca[g][b, 0:L:dil, :].rearrange("j (t h d) -> j t h d", t=2, h=4),
                        ["KVones"], [("KV", b)])
                for b in range(DB):
                    sl = b % 2
                    mm(psS[sl][:, 0:256], identf[0:DB, b:b + 1].to_broadcast([DB, 128]), rop[0:DB, g * 256:(g + 1) * 256], True, True,
                       ["rop", "identf"], [("psA", sl)])
                    p3 = prod[:, 0:256].rearrange("p (h d) -> p h d", h=4)
                    tt("dve", p3, KV0[:, b, 0, :, 0:64], psS[sl][:, 0:256].rearrange("p (h d) -> p h d", h=4), ALU.mult,
                       [("KV", b), ("psA", sl)], ["prod"])
                    op("dve", lambda e, o=scr[:, 0:4], i_=p3: e.tensor_reduce(out=o, in_=i_, axis=AX.X, op=ALU.add), ["prod"], ["scr"])
                    act(PZ[:, b, g * 4:(g + 1) * 4, b], scr[:, 0:4], AF.Exp, ["scr", "PZ"], [("PZ", b)], scale=0.125)
                for h in range(4):
                    hd = g * 4 + h
                    o, okey = pv_slot(hd)
                    for b in range(DB):
                        mm(o, PZ[:, b, hd, :], KV0[:, b, 1, h, :], b == 0, b == DB - 1, [("PZ", b), ("KV", b), "KVones"], [okey])
            cp("dve", Oc[:, 0:7, :], psP[0:DB, 0, 0:455].rearrange("p (h d) -> p h d", d=65), [("psA", 4)], ["Oc"])
            cp("dve", Oc[:, 7:12, :], psP[0:DB, 1, 0:325].rearrange("p (h d) -> p h d", d=65), [("psA", 5)], ["Oc"])

            def finish_attn(nh, qv, kvw, vv, keys):
                q3 = qv.rearrange("p (h d) -> p h d", d=64)
                pr = prod[0:DB, 0:nh * 64].rearrange("p (h d) -> p h d", d=64)
                tt("dve", pr, q3, kvw, ALU.mult, keys, ["prod"])
                op("dve", lambda e, o=sml[:, 32:32 + nh], i_=pr: e.tensor_reduce(out=o, in_=i_, axis=AX.X, op=ALU.add), ["prod"], ["sml"])
                act(sml[:, 32:32 + nh], sml[:, 32:32 + nh], AF.Exp, ["sml"], ["sml"], scale=0.125)
                tt("dve", pr, vv, sml[:, 32:32 + nh].unsqueeze(2).to_broadcast([DB, nh, 64]), ALU.mult, keys + ["sml", "prod"], ["prod"])
                tt("dve", Ot[:, 0:nh, :], Oc[:, 0:nh, 0:64], pr, ALU.add, ["Oc", "prod"], ["Ot"])
                tt("dve", sml[:, 0:nh], Oc[:, 0:nh, 64], sml[:, 32:32 + nh], ALU.add, ["Oc", "sml"], ["sml"])

            k3 = rop[:, 768:1536].rearrange("p (h d) -> p h d", d=64)
            v3 = proj[0:DB, 1536:2304].rearrange("p (h d) -> p h d", d=64)
            finish_attn(12, rop[:, 0:768], k3, v3, ["rop"] + pk)
            tt("dve", sml[:, 64:68], sml[:, 0:4], sml[:, 4:8], ALU.add, ["sml"], ["sml"])
            tt("dve", sml[:, 64:68], sml[:, 64:68], sml[:, 8:12], ALU.add, ["sml"], ["sml"])
            op("dve", lambda e, o=sml[:, 64:68], i_=sml[:, 64:68]: e.reciprocal(out=o, in_=i_), ["sml"], ["sml"])
            og = hs2[:, 0:768]
            for g in range(3):
                tt("dve", og[:, g * 256:(g + 1) * 256].rearrange("p (h d) -> p h d", d=64), Ot[:, g * 4:(g + 1) * 4, :],
                   sml[:, 64:68].unsqueeze(2).to_broadcast([DB, 4, 64]), ALU.mult, ["Ot", "sml"], ["og"])
            act(hs[:, 0:768], proj[0:DB, 2304:3072], AF.Silu, pk + ["hs"], ["hs"])
            tt("dve", og, og, hs[:, 0:768], ALU.mult, ["og", "hs"], ["og"])
            transpose_tok(hsT, "hsT", og, ["og"], 6)
            mk = stream_mm(lambda kc: hsT[:, kc, :], "hsT", DB, w_o_a, 6, 1024, hs, "mix")
            rs, rk = rms_tok(hs, mk)
            ts("dve", hs[:], hs[:], rs, None, ALU.mult, None, mk + [rk], ["mixn"])
            tt("dve", hs[:], hs[:], gv[1][:], ALU.mult, ["mixn", ("gv", 1)], ["mixn"])
            tt("dve", hs[:], hs[:], adat[0][0:DB, 2048:3072], ALU.mult, ["mixn"] + k0, ["mixn"])
            tt("dve", x1_t[:], hs[:], xs_t[:], ALU.add, ["mixn", "xs_t"], ["x1_t"])

            k1 = stream_mm(lambda kc: cT[:, kc, :], "cT", 33, ada_w1, 8, 3072, adat[0], "ad0", bias_off=3072)
            prompt_cols(adat[0], k1, 3072, 24)
            gate_bcast(adat[0], k1, 1)
            kk = stream_mm(lambda kc: cT[:, kc, :], "cT", 33, ada_kv_w, 8, 2048, adat[1], "ad1", bias_off=6144)
            prompt_cols(adat[1], kk, 2048, 48)
            for (dst, sc, gvv) in ((amod[:, 0:8], modT[:, 8:16], vecT[:, 0:8]), (amod[:, 8:16], modT[:, 32:40], vecT[:, 8:16]),
                                   (amod[:, 16:24], modT[:, 56:64], vecT[:, 32:40])):
                ts("dve", dst, sc, 1.0, None, ALU.add, None, ["modT"], ["amod"])
                tt("dve", dst, dst, gvv, ALU.mult, ["amod", "vecT"], ["amod"])

            dma("sp", gv[0][:], gvec_tm[1], (), [("gv", 0)])
            dma("sp", gv[1][:], gvec_tm[3], (), [("gv", 1)])
            dma("sp", gv[2][:], gvec_tm[4], (), [("gv", 2)])
            rs, rk = rms_tok(x1_t, ["x1_t"])
            modulate_tok(hs, "hs", x1_t, ["x1_t"], rs, rk, gv[0], ("gv", 0), adat[0][0:DB, 1024:2048], adat[0][0:DB, 0:1024], k1)
            modulate_tok(hs2, "hs2", x1_t, ["x1_t"], rs, rk, gv[2], ("gv", 2), adat[1][0:DB, 1024:2048], adat[1][0:DB, 0:1024], kk)
            transpose_tok(hsT, "hsT", hs, ["hs"], 8)
            transpose_tok(hkTs, "hkTs", hs2, ["hs2"], 8)
            kvk = stream_mm(lambda kc: hkTs[:, kc, :], "hkTs", DB, w_kv_d, 8, 256, kvp, "kvp")
            pk = stream_mm(lambda kc: hsT[:, kc, :], "hsT", DB, w_in_b, 8, 2048, proj, "proj")
            rope_tok(rop[:, 0:1024], "rop", proj[0:DB, 0:1024], pk, 16, swp[:, 0:1024])
            rope_tok(rop[:, 1024:1152], "ropk", kvp[:, 0:128], kvk, 2, swp[:, 1024:1152])
            cp("pool", krow[3][:, 0:128], rop[:, 1024:1152], ["ropk"], [("krow", 3)])
            cp("pool", krow[3][:, 128:256], kvp[:, 128:256], kvk, [("krow", 3)])
            dma("sp", bs[:, 127, :], krow[3][:, 0:256], [("krow", 3)], ())
            KV1 = KVflat[:, 0:DB * 260].rearrange("p (b t h d) -> p b t h d", b=DB, t=2, h=2)
            allkv = [("KV", b) for b in range(DB)]
            memset("pool", KV1[:, :, 1, :, 64:65], 1.0, allkv + ["KVones"])
            for b in range(DB):
                dma("sp", KV1[:, b, :, :, 0:64], cb[b].rearrange("j (t h d) -> j t h d", t=2, h=2), ["KVones"], [("KV", b)])
            for b in range(DB):
                for kh in range(2):
                    mm(psS[kh][:, :], identf[0:DB, b:b + 1].to_broadcast([DB, 128]), rop[0:DB, kh * 512:(kh + 1) * 512], True, True,
                       ["rop", "identf"], [("psA", kh)])
                    tt("dve", prod[:, kh * 512:(kh + 1) * 512].rearrange("p (h d) -> p h d", d=64),
                       psS[kh][:, :].rearrange("p (h d) -> p h d", d=64),
                       KV1[:, b, 0, kh, 0:64].unsqueeze(1).to_broadcast([128, 8, 64]), ALU.mult, [("KV", b), ("psA", kh)], ["prod"])
                op("dve", lambda e, o=scr[:, 0:16], i_=prod[:, :].rearrange("p (h d) -> p h d", d=64):
                   e.tensor_reduce(out=o, in_=i_, axis=AX.X, op=ALU.add), ["prod"], ["scr"])
                act(PZ[:, b, :, b], scr[:, 0:16], AF.Exp, ["scr", "PZ", "rowb"], [("PZ", b)], scale=0.125, bias=rowb[:, 0:1])
            for hd in range(16):
                o, okey = pv_slot(hd)
                for b in range(DB):
                    mm(o, PZ[:, b, hd, :], KV1[:, b, 1, hd // 8, :], b == 0, b == DB - 1, [("PZ", b), ("KV", b), "KVones"], [okey])
            cp("dve", Oc[:, 0:7, :], psP[0:DB, 0, 0:455].rearrange("p (h d) -> p h d", d=65), [("psA", 4)], ["Oc"])
            cp("dve", Oc[:, 7:14, :], psP[0:DB, 1, 0:455].rearrange("p (h d) -> p h d", d=65), [("psA", 5)], ["Oc"])
            cp("dve", Oc[:, 14:16, :], psR[0:DB, 0:130].rearrange("p (h d) -> p h d", d=65), [("psA", 2)], ["Oc"])
            kb3 = rop[:, 1024:1152].rearrange("p (k d) -> p k d", d=64)
            vb3 = kvp[:, 128:256].rearrange("p (k d) -> p k d", d=64)
            q4 = rop[:, 0:1024]
            for kh in range(2):
                pass
            pr = prod[0:DB, 0:1024].rearrange("p (h d) -> p h d", d=64)
            for kh in range(2):
                tt("dve", pr[:, kh * 8:(kh + 1) * 8, :], q4.rearrange("p (h d) -> p h d", d=64)[:, kh * 8:(kh + 1) * 8, :],
                   kb3[:, kh, :].unsqueeze(1).to_broadcast([DB, 8, 64]), ALU.mult, ["rop", "ropk", "prod"], ["prod"])
            op("dve", lambda e, o=sml[:, 32:48], i_=pr: e.tensor_reduce(out=o, in_=i_, axis=AX.X, op=ALU.add), ["prod"], ["sml"])
            act(sml[:, 32:48], sml[:, 32:48], AF.Exp, ["sml"], ["sml"], scale=0.125)
            for kh in range(2):
                tt("dve", pr[:, kh * 8:(kh + 1) * 8, :], sml[:, 32 + kh * 8:32 + (kh + 1) * 8].unsqueeze(2).to_broadcast([DB, 8, 64]),
                   vb3[:, kh, :].unsqueeze(1).to_broadcast([DB, 8, 64]), ALU.mult, kvk + ["sml", "prod"], ["prod"])
            tt("dve", Ot[:, 0:16, :], Oc[:, 0:16, 0:64], pr, ALU.add, ["Oc", "prod"], ["Ot"])
            tt("dve", sml[:, 0:16], Oc[:, 0:16, 64], sml[:, 32:48], ALU.add, ["Oc", "sml"], ["sml"])
            tt("dve", sml[:, 0:16], sml[:, 0:16], stm[:, :], ALU.add, ["sml", "stm"], ["sml"])
            op("dve", lambda e, o=sml[:, 0:16], i_=sml[:, 0:16]: e.reciprocal(out=o, in_=i_), ["sml"], ["sml"])
            og = hs2[:, 0:1024]
            tt("dve", og.rearrange("p (h d) -> p h d", d=64), Ot[:, 0:16, :], sml[:, 0:16].unsqueeze(2).to_broadcast([DB, 16, 64]),
               ALU.mult, ["Ot", "sml", "hs2"], ["og"])
            act(hs[:, :], proj[0:DB, 1024:2048], AF.Silu, pk + ["hs"], ["hs"])
            tt("dve", og, og, hs[:, :], ALU.mult, ["og", "hs"], ["og"])
            transpose_tok(hsT, "hsT", og, ["og"], 8)
            mk = stream_mm(lambda kc: hsT[:, kc, :], "hsT", DB, w_o_b, 8, 1024, hs, "mix")
            rs, rk = rms_tok(hs, mk)
            ts("dve", hs[:], hs[:], rs, None, ALU.mult, None, mk + [rk], ["mixn"])
            tt("dve", hs[:], hs[:], gv[1][:], ALU.mult, ["mixn", ("gv", 1)], ["mixn"])
            tt("dve", hs[:], hs[:], adat[0][0:DB, 2048:3072], ALU.mult, ["mixn"] + k1, ["mixn"])
            tt("dve", hs[:], hs[:], x1_t[:], ALU.add, ["mixn", "x1_t"], ["ys_t"])
            dma("sp", ys[:, :], hs[:], ["ys_t"], ())
            P.flush()
        if os.environ.get("KSTOP") == "setup":
            return nc

        A0 = amod[:, 0:8]
        SH0 = modT[:, 0:8]
        A1 = amod[:, 8:16]
        SH1 = modT[:, 24:32]
        AKV = amod[:, 16:24]
        SHKV = modT[:, 48:56]

        def load_x(src, srckey, i, xin):
            sl = i % len(xin)
            dma("sp", xin[sl][:], src[i * 128:(i + 1) * 128, :], [(srckey, i)] if srckey else (), [("xin", sl)])

        def norm_stats(tiles, xin, col0):
            for i in tiles:
                sl = i % len(xin)
                c = col0 + i
                act(junk[:], xin[sl][:], AF.Square, [("xin", sl)], [("msq", c)], scale=1.0 / 32, accum_out=msq[:, c:c + 1])
            c0 = col0 + tiles[0]
            n = len(tiles)
            mk = [("msq", col0 + i) for i in tiles]
            rk = [("rstd", col0 + i) for i in tiles]
            act(rstd[:, c0:c0 + n], msq[:, c0:c0 + n], AF.Ln, mk, rk, bias=EPS)
            act(rstd[:, c0:c0 + n], rstd[:, c0:c0 + n], AF.Exp, rk, rk, scale=-0.5)

        def norm_tile(i, u, xin, xn, col0, outs, hslot, stats=True):
            sl = i % len(xin)
            xsl = i % len(xn)
            c = col0 + i
            if stats:
                norm_stats([i], xin, col0)
            ts("dve", xn[xsl][:], xin[sl][:], rstd[:, c:c + 1], None, ALU.mult, None, [("xin", sl), ("rstd", c)], [("xn", xsl)])
            pt = psT[i % 2]
            ptk = ("psT", 0) if i % 2 == 0 else "psR"
            for kc in range(8):
                tr(pt[:, kc, :], xn[xsl][:, kc * 128:(kc + 1) * 128], identb[:], [("xn", xsl)], [ptk])
            for (hT, hname, Am, Sm) in outs:
                for kc in range(8):
                    o = hT[:, kc, u * 128:(u + 1) * 128]
                    if i % 2 == 0:
                        ts("dve", o, pt[:, kc, :], Am[:, kc:kc + 1], Sm[:, kc:kc + 1], ALU.mult, ALU.add,
                           [ptk], [(hname, hslot, u, kc)])
                    else:
                        act(o, pt[:, kc, :], AF.Identity, [ptk], [(hname, hslot, u, kc)],
                            scale=Am[:, kc:kc + 1], bias=Sm[:, kc:kc + 1])

        def project(hT, hname, hslot, W, cols, nk=8):
            slot = nxt("p")
            for kc in range(nk):
                mm(psP[:, slot, :], W[:, kc, cols], hT[:, kc, :], kc == 0, kc == nk - 1,
                   [(hname, hslot, u, kc) for u in range(4)], [("psP", slot)])
            return slot

        def rope(slot, ctab, stab, tkey, qraw, t12, dst, dkey, f32dst=None, fkey=None):
            r = nxt("r")
            cp("act", qraw[r][:], psP[:, slot, :], [("psP", slot)], [("qraw", r)])
            mm(psR[:], rmb[:], qraw[r][:], True, True, [("qraw", r)], ["psR"])
            tt("pool", t12[:, r, :], qraw[r][:], ctab, ALU.mult, [("qraw", r), tkey], [("t1", r)])
            tt("dve", t12[:, 2 + r, :], psR[:], stab, ALU.mult, ["psR", tkey], [("t2", r)])
            tt("pool", dst, t12[:, r, :], t12[:, 2 + r, :], ALU.add, [("t1", r), ("t2", r)], [dkey])
            if f32dst is not None:
                tt("pool", f32dst, t12[:, r, :], t12[:, 2 + r, :], ALU.add, [("t1", r), ("t2", r)], [fkey])

        def cache_out(kvf, kkeys, u, ncol, stage, dst_rows, skey):
            n = len(kvf)
            for k4 in range(n):
                tr(psR[:, k4 * 128:(k4 + 1) * 128], kvf[k4][:, u * 128:(u + 1) * 128], identf[:], [kkeys[k4]], ["psR"])
            sl = nxt("o")
            cp(evac_eng(), stage[sl][:, 0:ncol], psR[:, 0:ncol], ["psR"], [(skey, sl)])
            dma("sp", dst_rows, stage[sl][:, 0:ncol], [(skey, sl)], ())

        pending = []

        def attn_back(l, heads, pt, sl, has_prev, rv):
            osl = nxt("o")
            psO = psOs[osl]
            for hh, h in enumerate(heads):
                rows = slice(hh * 64, (hh + 1) * 64)
                pp = pt[:, hh * 256:hh * 256 + 128]
                pc = pt[:, hh * 256 + 128:hh * 256 + 256]
                for (o, vp, vc) in ((psO[rows, 0:128], h["vp"], h["vc"]), (psO[rows, 128:256], ones_b[:, :], ones_b[:, :])):
                    if has_prev:
                        mm(o, vp, pp, True, False, rv + [("PT", sl)], [("psO", osl)])
                    mm(o, vc, pc, not has_prev, True, rv + [("PT", sl)], [("psO", osl)])
            eng = evac_eng()
            for (dkey, okey, cols) in (("od", "ok", slice(0, 128)), ("dd", "dk", slice(128, 256))):
                for hh, h in enumerate(heads):
                    pass
                d0, d1 = heads[0][dkey], heads[1][dkey]
                src = psO[:, cols]
                dst = heads[0][dkey + "_full"]
                if len(dst.shape) == 3:
                    src = src.rearrange("p (a b) -> p a b", a=dst.shape[1])
                cp(eng, dst, src, [("psO", osl)], [heads[0][okey], heads[1][okey]])

        def attn_flush():
            while pending:
                attn_back(*pending.pop(0))

        def attn_pair(l, heads, PT, rq, rk, rv):
            sl = nxt("s")
            ps = psS[sl]
            pt = PT[sl]
            has_prev = heads[0]["kp"] is not None
            for hh, h in enumerate(heads):
                if has_prev:
                    mm(ps[:, hh * 256:hh * 256 + 256], identb[:], mbb[l][:], True, False, (), [("psS", sl)])
                    mm(ps[:, hh * 256:hh * 256 + 128], h["kp"], h["q"], False, False, rq + rk, [("psS", sl)])
                    mm(ps[:, hh * 256 + 128:hh * 256 + 256], h["kc"], h["q"], False, True, rq + rk, [("psS", sl)])
                else:
                    mm(ps[:, hh * 256 + 128:hh * 256 + 256], identb[:], mbb[l][:, 128:256], True, False, (), [("psS", sl)])
                    mm(ps[:, hh * 256 + 128:hh * 256 + 256], h["kc"], h["q"], False, True, rq + rk, [("psS", sl)])
            if has_prev:
                act(pt[:], ps[:], AF.Exp, [("psS", sl)], [("PT", sl)], scale=0.125)
            else:
                v = lambda a: a.rearrange("p (h x) -> p h x", h=2)[:, :, 128:256]
                act(v(pt[:]), v(ps[:]), AF.Exp, [("psS", sl)], [("PT", sl)], scale=0.125)
            attn_flush()
            pending.append((l, heads, pt, sl, has_prev, rv))

        T12K = [("t1", 0), ("t1", 1), ("t2", 0), ("t2", 1)]

        def out_proj_tile(i, u, oTf, okeys, Wo, nk, ggl, xres, col0, dst_dram, dkey, t12):
            if i % 2 == 0:
                banks = [(psP[:, 0, :], ("psP", 0)), (psP[:, 1, :], ("psP", 1))]
            else:
                banks = [(psS[0][:, :], ("psS", 0)), (psS[1][:, :], ("psS", 1))]
            for nb in range(2):
                for kc in range(nk):
                    mm(banks[nb][0], oTf[kc][:, u * 128:(u + 1) * 128], Wo[:, kc, nb * 512:(nb + 1) * 512], kc == 0, kc == nk - 1,
                       [okeys[kc]], [banks[nb][1]])
            c = col0 + i
            for nb in range(2):
                act(junk[:, 0:512], banks[nb][0], AF.Square, [banks[nb][1]], [("msq", c, nb)], scale=1.0 / 32,
                    accum_out=msq2[:, 2 * i + nb:2 * i + nb + 1])
            tt("dve", msq[:, c:c + 1], msq2[:, 2 * i:2 * i + 1], msq2[:, 2 * i + 1:2 * i + 2], ALU.add, [("msq", c, 0), ("msq", c, 1)], [("msq", c)])
            act(rstd[:, c:c + 1], msq[:, c:c + 1], AF.Ln, [("msq", c)], [("rstd", c)], bias=EPS)
            act(rstd[:, c:c + 1], rstd[:, c:c + 1], AF.Exp, [("rstd", c)], [("rstd", c)], scale=-0.5)
            r = i % 2
            for nb in range(2):
                stt("dve", t12[:, 2 * r + nb, :], banks[nb][0], rstd[:, c:c + 1], ggl[:, nb * 512:(nb + 1) * 512], ALU.mult, ALU.mult,
                    [banks[nb][1], ("rstd", c)], [T12K[2 * r + nb]])
            sl = i % len(xres)
            tm = t12[:, 2 * r:2 * r + 2, :].rearrange("p a b -> p (a b)")
            tt("pool", xres[sl][:], tm, xres[sl][:], ALU.add, [T12K[2 * r], T12K[2 * r + 1], ("xres", sl)], [("xres", sl)])
            dma("sp", dst_dram[i * 128:(i + 1) * 128, :], xres[sl][:], [("xres", sl)], [(dkey, i)] if dkey else ())

        bg = []
        for b in range(DB):
            for q8 in range(8):
                lo, hi = 256 * q8, min(256 * (q8 + 1), 2047)
                bg.append((aos[2][b, lo:hi, :], ca[2][b, lo + 1:hi + 1, :]))
            if b % 2 == 1:
                for bb in (b - 1, b):
                    bg.append((aos[1][bb, 0:256, :], ca[1][bb, 1:257, :]))
                    bg.append((aos[1][bb, 256:511, :], ca[1][bb, 257:512, :]))
        bg.append((aos[0][:, 0:127, :], ca[0][:, 1:128, :]))
        bg.append((bs[:, 0:127, :], cb[:, 1:128, :]))

        def bg_issue(n):
            for _ in range(n):
                if bg:
                    d, s_ = bg.pop(0)
                    dma("act", d, s_, (), ())

        with ExitStack() as st:
            Wa2 = sbt(st, "Wa2", [128, 8, 768], BF16)
            wst = [sbt(st, "wstA%d" % i, [128, 1024], F32) for i in range(2)]
            xin = [sbt(st, "xinA%d" % i, [128, D], F32) for i in range(4)]
            xn = [sbt(st, "xnA%d" % i, [128, D], BF16) for i in range(2)]
            hT = [sbt(st, "hTA%d" % i, [128, 8, 512], BF16) for i in range(2)]
            QT2 = [sbt(st, "QT2_%d" % c, [128, T], BF16) for c in range(2)]
            KT2 = [sbt(st, "KT2_%d" % c, [128, T], BF16) for c in range(2)]
            VT2 = [sbt(st, "VT2_%d" % c, [128, T], BF16) for c in range(2)]
            ctab = [sbt(st, "ctabA%d" % i, [128, 512], F32) for i in range(2)]
            stab = [sbt(st, "stabA%d" % i, [128, 512], F32) for i in range(2)]
            qraw = [sbt(st, "qrawA%d" % i, [128, 512], BF16) for i in range(2)]
            t12 = sbt(st, "t12A", [128, 4, 512], F32)
            kvf = [sbt(st, "kvfA%d" % i, [128, 512], F32) for i in range(4)]
            stage = [sbt(st, "stageA%d" % i, [128, 512], F32) for i in range(2)]
            PT = [sbt(st, "PTA%d" % i, [128, 512], BF16) for i in range(2)]
            VA = [sbt(st, "VAA%d" % i, [128, 4, 128], BF16) for i in range(4)]
            G2st = [sbt(st, "G2st%d" % k, [128, T], BF16) for k in range(4)]
            load_w(wst, Wa2, w_in_a, 1024, 512, 256, 0, "w")
            load_w(wst, Wa2, w_in_a, 1024, 768 + 512, 256, 256, "w")
            load_w(wst, Wa2, w_in_a, 1024, 1536 + 512, 256, 512, "w")
            for v in VA:
                memset("pool", v[:, :, 64:128], 1.0, [("w", "va")])
            P.flush()
            print("[phaseA] sbuf remaining", nc.sbuf_bytes_remaining)
            stop("A0")
            for u in range(4):
                load_x(xp, None, u, xin)
            for s in range(NS):
                hs = s % 2
                tk = slice(512 * s, 512 * (s + 1))
                dma("sp", ctab[hs][:], cosT_d[:, tk], (), [("tab", hs)])
                dma("sp", stab[hs][:], sinT_d[:, tk], (), [("tab", hs)])
                norm_stats([4 * s + u for u in range(4)], xin, 0)
                for u in range(4):
                    norm_tile(4 * s + u, u, xin, xn, 0, [(hT[hs], "hT", A0, SH0)], hs, stats=False)
                stop("A1")
                if s + 1 < NS:
                    for u in range(4):
                        load_x(xp, None, 4 * (s + 1) + u, xin)
                bg_issue(2)
                for (kind, cc, wc) in (("k", 0, 256), ("k", 1, 384), ("v", 0, 512), ("v", 1, 640), ("q", 0, 0), ("q", 1, 128)):
                    slot = project(hT[hs], "hT", hs, Wa2, slice(wc, wc + 128))
                    if kind == "k":
                        rope(slot, ctab[hs][:], stab[hs][:], ("tab", hs), qraw, t12, KT2[cc][:, tk], ("KT2", cc, s),
                             kvf[cc][:] if s >= 4 else None, ("kvf", cc))
                    elif kind == "q":
                        rope(slot, ctab[hs][:], stab[hs][:], ("tab", hs), qraw, t12, QT2[cc][:, tk], ("QT2", cc, s))
                    else:
                        cp("act", VT2[cc][:, tk], psP[:, slot, :], [("psP", slot)], [("VT2", cc, s)])
                        if s >= 4:
                            cp("dve", kvf[2 + cc][:], psP[:, slot, :], [("psP", slot)], [("kvf", 2 + cc)])
                stop("A2")
                bg_issue(2)
                if s >= 4:
                    for u in range(4):
                        i = 4 * s + u
                        cache_out(kvf, [("kvf", k) for k in range(4)], u, 512, stage, aop[2][(i - 16) * 128:(i - 15) * 128, :], "stage")
            stop("A3")
            allq = [[("QT2", c, s) for s in range(NS)] for c in range(2)]
            allk = [[("KT2", c, s) for s in range(NS)] for c in range(2)]
            allv = [[("VT2", c, s) for s in range(NS)] for c in range(2)]
            for r in range(16):
                for j in range(2):
                    tq = slice(2048 * j + r, 2048 * (j + 1), 16)
                    tp = slice(2048 * (j - 1) + r, 2048 * j, 16)
                    b = 2 * r + j
                    vsl = b % 4
                    ptv = psT[0]
                    for c in range(2):
                        tr(ptv[:, c, :], VT2[c][:, tq], identb[:], allv[c], [("psT", 0)])
                    cp(evac_eng(), VA[vsl][:, :, 0:64], ptv[:, 0:2, :].rearrange("p c (h d) -> p (c h) d", h=2),
                       [("psT", 0)], [("VA", vsl)])
                    for c in range(2):
                        heads = []
                        for hh in range(2):
                            h = 2 * c + hh
                            rows = slice(hh * 64, (hh + 1) * 64)
                            od = G2st[c][rows, :].rearrange("p (s r m) -> p s r m", s=8, r=16)[:, 4 * j:4 * j + 4, r, :]
                            dd = G2st[2 + c][rows, :].rearrange("p (s r m) -> p s r m", s=8, r=16)[:, 4 * j:4 * j + 4, r, :]
                            heads.append(dict(q=QT2[c][rows, tq], kc=KT2[c][rows, tq], kp=(KT2[c][rows, tp] if j == 1 else None),
                                              vc=VA[vsl][:, h, 0:64], vp=(VA[(b - 1) % 4][:, h, 0:64] if j == 1 else None),
                                              od=od, dd=dd, ok=("G2", c, b, hh), dk=("G2", 2 + c, b, hh),
                                              od_full=G2st[c][:, :].rearrange("p (s r m) -> p s r m", s=8, r=16)[:, 4 * j:4 * j + 4, r, :],
                                              dd_full=G2st[2 + c][:, :].rearrange("p (s r m) -> p s r m", s=8, r=16)[:, 4 * j:4 * j + 4, r, :]))
                        attn_pair(0, heads, PT, allq[c], allk[c], [("VA", vsl), ("VA", (b - 1) % 4)])
            attn_flush()
            for k in range(4):
                dma("sp", g2s[k], G2st[k][:], [("G2", k, b, hh) for b in range(32) for hh in range(2)], [("g2s", k)])
            P.flush()
        if os.environ.get("KSTOP") == "A":
            return nc

        with ExitStack() as st:
            Wab = sbt(st, "Wab", [128, 8, 2304], BF16)
            Woa = sbt(st, "Woa", [128, 6, 1024], BF16)
            wst = [sbt(st, "wstB%d" % i, [128, 1024], F32) for i in range(2)]
            xin = [sbt(st, "xinB%d" % i, [128, D], F32) for i in range(2)]
            xres = [sbt(st, "xresB%d" % i, [128, D], F32) for i in range(2)]
            xn = [sbt(st, "xnB%d" % i, [128, D], BF16) for i in range(2)]
            hT = [sbt(st, "hTB%d" % i, [128, 8, 512], BF16) for i in range(1)]
            QT = [sbt(st, "QTB%d" % c, [128, 512], BF16) for c in range(4)]
            KT = [sbt(st, "KTB%d" % c, [128, 1024], BF16) for c in range(4)]
            VT = [sbt(st, "VTB%d" % c, [128, 1024], BF16) for c in range(4)]
            gT = [sbt(st, "gTB%d" % c, [128, 512], BF16) for c in range(6)]
            OT = [sbt(st, "OTB%d" % c, [128, 512], BF16) for c in range(4)]
            DT = [sbt(st, "DTB%d" % c, [128, 512], BF16) for c in range(4)]
            oTf = [sbt(st, "oTfB%d" % c, [128, 512], BF16) for c in range(6)]
            G2in = [sbt(st, "G2in%d" % k, [128, 512], BF16) for k in range(4)]
            dsum2 = [sbt(st, "dsumB%d" % i, [128, 512], F32) for i in range(2)]
            otmp2 = [sbt(st, "otmpB%d" % i, [128, 512], BF16) for i in range(2)]
            ctab = [sbt(st, "ctabB%d" % i, [128, 512], F32) for i in range(2)]
            stab = [sbt(st, "stabB%d" % i, [128, 512], F32) for i in range(2)]
            qraw = [sbt(st, "qrawB%d" % i, [128, 512], BF16) for i in range(2)]
            t12 = sbt(st, "t12B", [128, 4, 512], F32)
            kvf = [sbt(st, "kvfB%d" % i, [128, 512], F32) for i in range(4)]
            stage = [sbt(st, "stageB%d" % i, [128, 512], F32) for i in range(2)]
            PT = [sbt(st, "PTB%d" % i, [128, 512], BF16) for i in range(2)]
            VA0 = [sbt(st, "VA0_%d" % i, [128, 4, 128], BF16) for i in range(4)]
            VA1 = [sbt(st, "VA1_%d" % i, [128, 4, 128], BF16) for i in range(8)]
            print("[phaseB] sbuf remaining", nc.sbuf_bytes_remaining)
            load_w(wst, Wab, w_in_a, 1024, 0, 512, 0, "w")
            load_w(wst, Wab, w_in_a, 1024, 768, 512, 512, "w")
            load_w(wst, Wab, w_in_a, 1024, 1536, 512, 1024, "w")
            load_w(wst, Wab, w_in_a, 1024, 2304, 768, 1536, "w")
            load_w(wst, Woa, w_o_a, 768, 0, 1024, 0, "w")
            for v in VA0 + VA1:
                memset("pool", v[:, :, 64:128], 1.0, [("w", "va")])
            P.flush()
            load_x(xp, None, 0, xin)
            load_x(xp, None, 1, xin)
            for s in range(NS):
                hs = s % 2
                ts_ = hs * 512
                tk = slice(512 * s, 512 * (s + 1))
                dma("sp", ctab[hs][:], cosT_d[:, tk], (), [("tab", hs)])
                dma("sp", stab[hs][:], sinT_d[:, tk], (), [("tab", hs)])
                for k in range(4):
                    dma("sp", G2in[k][:], g2s[k][:, tk], [("g2s", k)], [("G2in", k)])
                bg_issue(2)
                for u in range(4):
                    i = 4 * s + u
                    norm_tile(i, u, xin, xn, 32, [(hT[0], "hT", A0, SH0)], 0)
                    if i + 2 < NT:
                        load_x(xp, None, i + 2, xin)
                ring = slice(ts_, ts_ + 512)
                for g in range(2):
                    for (kind, cc, wc) in (("k", 2 * g, 512 + 256 * g), ("k", 2 * g + 1, 640 + 256 * g),
                                           ("v", 2 * g, 1024 + 256 * g), ("v", 2 * g + 1, 1152 + 256 * g)):
                        slot = project(hT[0], "hT", 0, Wab, slice(wc, wc + 128))
                        need_f = (s == 7)
                        if kind == "k":
                            rope(slot, ctab[hs][:], stab[hs][:], ("tab", hs), qraw, t12, KT[cc][:, ring], ("KT", cc, hs),
                                 kvf[cc % 2][:] if need_f else None, ("kvf", cc % 2))
                        else:
                            cp("act", VT[cc][:, ring], psP[:, slot, :], [("psP", slot)], [("VT", cc, hs)])
                            if need_f:
                                cp("dve", kvf[2 + cc % 2][:], psP[:, slot, :], [("psP", slot)], [("kvf", 2 + cc % 2)])
                    if s == 7:
                        if g == 0:
                            cache_out(kvf, [("kvf", k) for k in range(4)], 3, 512, stage, aop[0][:, :], "stage")
                        else:
                            for u in range(4):
                                cache_out(kvf, [("kvf", k) for k in range(4)], u, 512, stage, aop[1][u * 128:(u + 1) * 128, :], "stage")
                for cc in range(4):
                    slot = project(hT[0], "hT", 0, Wab, slice(cc * 128, (cc + 1) * 128))
                    rope(slot, ctab[hs][:], stab[hs][:], ("tab", hs), qraw, t12, QT[cc][:], ("QT", cc))
                for cc in range(6):
                    slot = project(hT[0], "hT", 0, Wab, slice(1536 + cc * 128, 1536 + (cc + 1) * 128))
                    act(gT[cc][:], psP[:, slot, :], AF.Silu, [("psP", slot)], [("gT", cc)])
                bg_issue(2)
                for u in range(4):
                    i = 4 * s + u
                    vsl = i % 4
                    ptv = psT[0]
                    cols = slice(ts_ + u * 128, ts_ + (u + 1) * 128)
                    if u > 0:
                        pcols = slice(ts_ + (u - 1) * 128, ts_ + u * 128)
                    else:
                        pcols = slice((1 - hs) * 512 + 384, (1 - hs) * 512 + 512)
                    for c in range(2):
                        tr(ptv[:, c, :], VT[c][:, cols], identb[:], [("VT", c, hs)], [("psT", 0)])
                    cp(evac_eng(), VA0[vsl][:, :, 0:64], ptv[:, 0:2, :].rearrange("p c (h d) -> p (c h) d", h=2),
                       [("psT", 0)], [("VA0", vsl)])
                    for c in range(2):
                        heads = []
                        for hh in range(2):
                            h = 2 * c + hh
                            rows = slice(hh * 64, (hh + 1) * 64)
                            heads.append(dict(q=QT[c][rows, u * 128:(u + 1) * 128], kc=KT[c][rows, cols],
                                              kp=(KT[c][rows, pcols] if i > 0 else None),
                                              vc=VA0[vsl][:, h, 0:64], vp=(VA0[(i - 1) % 4][:, h, 0:64] if i > 0 else None),
                                              od=OT[c][rows, u * 128:(u + 1) * 128], dd=DT[c][rows, u * 128:(u + 1) * 128],
                                              od_full=OT[c][:, u * 128:(u + 1) * 128], dd_full=DT[c][:, u * 128:(u + 1) * 128],
                                              ok=("OT", c, u, hh), dk=("DT", c, u, hh)))
                        attn_pair(0, heads, PT, [("QT", c)], [("KT", c, 0), ("KT", c, 1)], [("VA0", vsl), ("VA0", (i - 1) % 4)])
                bg_issue(2)
                for r4 in range(4):
                    vsl = hs * 4 + r4
                    psl = (1 - hs) * 4 + r4
                    b = 4 * s + r4
                    ptv = psT[0]
                    cols = slice(ts_ + r4, ts_ + 512, 4)
                    pcols = slice((1 - hs) * 512 + r4, (1 - hs) * 512 + 512, 4)
                    for c in range(2):
                        tr(ptv[:, c, :], VT[2 + c][:, cols], identb[:], [("VT", 2 + c, hs)], [("psT", 0)])
                    cp(evac_eng(), VA1[vsl][:, :, 0:64], ptv[:, 0:2, :].rearrange("p c (h d) -> p (c h) d", h=2),
                       [("psT", 0)], [("VA1", vsl)])
                    for c in range(2):
                        heads = []
                        for hh in range(2):
                            h = 2 * c + hh
                            rows = slice(hh * 64, (hh + 1) * 64)
                            heads.append(dict(q=QT[2 + c][rows, r4:512:4], kc=KT[2 + c][rows, cols],
                                              kp=(KT[2 + c][rows, pcols] if s > 0 else None),
                                              vc=VA1[vsl][:, h, 0:64], vp=(VA1[psl][:, h, 0:64] if s > 0 else None),
                                              od=OT[2 + c][rows, r4:512:4], dd=DT[2 + c][rows, r4:512:4],
                                              od_full=OT[2 + c][:, r4:512:4], dd_full=DT[2 + c][:, r4:512:4],
                                              ok=("OT", 2 + c, r4, hh), dk=("DT", 2 + c, r4, hh)))
                        attn_pair(0, heads, PT, [("QT", 2 + c)], [("KT", 2 + c, 0), ("KT", 2 + c, 1)], [("VA1", vsl), ("VA1", psl)])
                attn_flush()
                bg_issue(2)
                for u in range(2):
                    dma("sp", xres[u][:], xp[(4 * s + u) * 128:(4 * s + u + 1) * 128, :], (), [("xres", u)])
                nat = lambda a: a.rearrange("p (r m) -> p m r", r=16)
                v3 = lambda a: a.rearrange("p (m r) -> p m r", r=16)
                for cc in range(2):
                    k01 = [("DT", c, x, hh) for c in (cc, 2 + cc) for x in range(4) for hh in range(2)]
                    ds = dsum2[cc]
                    tt("dve", ds[:], DT[cc][:], DT[2 + cc][:], ALU.add, k01, [("dsum", cc)])
                    tt("pool", v3(ds[:]), v3(ds[:]), nat(G2in[2 + cc][:]), ALU.add, [("dsum", cc), ("G2in", 2 + cc)], [("dsum", cc)])
                    act(ds[:], ds[:], AF.Ln, [("dsum", cc)], [("dsum", cc)])
                    act(ds[:], ds[:], AF.Exp, [("dsum", cc)], [("dsum", cc)], scale=-1.0)
                    for g in range(3):
                        c6 = 2 * g + cc
                        ot = otmp2[g % 2]
                        ok_ = ("otmp", g % 2)
                        if g < 2:
                            sk = [("OT", c6, x, hh) for x in range(4) for hh in range(2)]
                            tt("dve", ot[:], OT[c6][:], ds[:], ALU.mult, sk + [("dsum", cc)], [ok_])
                        else:
                            tt("pool", v3(ot[:]), nat(G2in[cc][:]), v3(ds[:]), ALU.mult, [("G2in", cc), ("dsum", cc)], [ok_])
                        tt("dve" if g == 2 else "pool", oTf[c6][:], ot[:], gT[c6][:], ALU.mult, [ok_, ("gT", c6)], [("oTf", c6)])
                for u in range(4):
                    i = 4 * s + u
                    out_proj_tile(i, u, oTf, [("oTf", c) for c in range(6)], Woa, 6, ggbc[0], xres, 64, x1s, "x1s", t12)
                    if u + 2 < 4:
                        dma("sp", xres[i % 2][:], xp[(i + 2) * 128:(i + 3) * 128, :], (), [("xres", i % 2)])
            P.flush()
        if os.environ.get("KSTOP") == "B":
            return nc

        with ExitStack() as st:
            Wq = sbt(st, "Wq", [128, 8, 2048], BF16)
            Wob = sbt(st, "Wob", [128, 8, 1024], BF16)
            Wkv = sbt(st, "Wkv", [128, 8, 384], BF16)
            wst = [sbt(st, "wstC%d" % i, [128, 1024], F32) for i in range(2)]
            xin = [sbt(st, "xinC%d" % i, [128, D], F32) for i in range(2)]
            xres = [sbt(st, "xresC%d" % i, [128, D], F32) for i in range(2)]
            xn = [sbt(st, "xnC%d" % i, [128, D], BF16) for i in range(2)]
            hT = sbt(st, "hTC", [128, 8, 512], BF16)
            hkT = sbt(st, "hkTC", [128, 8, 512], BF16)
            QT = [sbt(st, "QTC%d" % c, [128, 512], BF16) for c in range(8)]
            KT = [sbt(st, "KTC%d" % c, [128, 1024], BF16) for c in range(2)]
            VT = sbt(st, "VTC", [128, 1024], BF16)
            gT = [sbt(st, "gTC%d" % c, [128, 512], BF16) for c in range(8)]
            OT = [sbt(st, "OTC%d" % c, [128, 512], BF16) for c in range(8)]
            DT = [sbt(st, "DTC%d" % c, [128, 512], BF16) for c in range(8)]
            dsum = [sbt(st, "dsumC%d" % i, [128, 512], F32) for i in range(2)]
            otmp = [sbt(st, "otmpC%d" % i, [128, 512], BF16) for i in range(2)]
            ctab = [sbt(st, "ctabC%d" % i, [128, 512], F32) for i in range(2)]
            stab = [sbt(st, "stabC%d" % i, [128, 512], F32) for i in range(2)]
            qraw = [sbt(st, "qrawC%d" % i, [128, 512], BF16) for i in range(2)]
            t12 = sbt(st, "t12C", [128, 4, 512], F32)
            kvf = [sbt(st, "kvfC%d" % i, [128, 512], F32) for i in range(2)]
            stage = [sbt(st, "stageC%d" % i, [128, 256], F32) for i in range(2)]
            PT = [sbt(st, "PTC%d" % i, [128, 512], BF16) for i in range(2)]
            VAc = [sbt(st, "VAc%d" % i, [128, 2, 128], BF16) for i in range(4)]
            print("[phaseC] sbuf remaining", nc.sbuf_bytes_remaining)
            load_w(wst, Wq, w_in_b, 1024, 0, 2048, 0, "w")
            load_w(wst, Wob, w_o_b, 1024, 0, 1024, 0, "w")
            load_w(wst, Wkv, w_kvx, 1024, 0, 384, 0, "w")
            for v in VAc:
                memset("pool", v[:, :, 64:128], 1.0, [("w", "va")])
            P.flush()
            dma("sp", xin[0][:], x1s[0:128, :], (), [("xin", 0)])
            dma("sp", xin[1][:], x1s[128:256, :], (), [("xin", 1)])
            for s in range(NS):
                hs = s % 2
                ts_ = hs * 512
                tk = slice(512 * s, 512 * (s + 1))
                ring = slice(ts_, ts_ + 512)
                dma("sp", ctab[hs][:], cosT_d[:, tk], (), [("tab", hs)])
                dma("sp", stab[hs][:], sinT_d[:, tk], (), [("tab", hs)])
                bg_issue(2)
                for u in range(4):
                    i = 4 * s + u
                    norm_tile(i, u, xin, xn, 0, [(hT, "hT", A1, SH1), (hkT, "hkT", AKV, SHKV)], 0)
                    if i + 2 < NT:
                        dma("sp", xin[i % 2][:], x1s[(i + 2) * 128:(i + 3) * 128, :], (), [("xin", i % 2)])
                need_f = (s == 7)
                slot = project(hkT, "hkT", 0, Wkv, slice(0, 128))
                rope(slot, ctab[hs][:], stab[hs][:], ("tab", hs), qraw, t12, KT[0][:, ring], ("KT", 0, hs),
                     kvf[0][:] if need_f else None, ("kvf", 0))
                slot = project(hkT, "hkT", 0, Wkv, slice(256, 384))
                rope(slot, ctab[hs][:], stab[hs][:], ("tab", hs), qraw, t12, KT[1][:, ring], ("KT", 1, hs))
                slot = project(hkT, "hkT", 0, Wkv, slice(128, 256))
                cp("act", VT[:, ring], psP[:, slot, :], [("psP", slot)], [("VT", hs)])
                if need_f:
                    cp("dve", kvf[1][:], psP[:, slot, :], [("psP", slot)], [("kvf", 1)])
                    cache_out(kvf, [("kvf", 0), ("kvf", 1)], 3, 256, stage, bp[:, :], "stage")
                for cc in range(8):
                    slot = project(hT, "hT", 0, Wq, slice(cc * 128, (cc + 1) * 128))
                    rope(slot, ctab[hs][:], stab[hs][:], ("tab", hs), qraw, t12, QT[cc][:], ("QT", cc))
                bg_issue(2)
                for cc in range(8):
                    slot = project(hT, "hT", 0, Wq, slice(1024 + cc * 128, 1024 + (cc + 1) * 128))
                    act(gT[cc][:], psP[:, slot, :], AF.Silu, [("psP", slot)], [("gT", cc)])
                for u in range(4):
                    i = 4 * s + u
                    vsl = i % 4
                    ptv = psT[0]
                    cols = slice(ts_ + u * 128, ts_ + (u + 1) * 128)
                    if u > 0:
                        pcols = slice(ts_ + (u - 1) * 128, ts_ + u * 128)
                    else:
                        pcols = slice((1 - hs) * 512 + 384, (1 - hs) * 512 + 512)
                    tr(ptv[:, 0, :], VT[:, cols], identb[:], [("VT", hs)], [("psT", 0)])
                    cp(evac_eng(), VAc[vsl][:, :, 0:64], ptv[:, 0, :].rearrange("p (h d) -> p h d", h=2),
                       [("psT", 0)], [("VAc", vsl)])
                    for c in range(8):
                        kvh = c // 4
                        heads = []
                        for hh in range(2):
                            rows = slice(hh * 64, (hh + 1) * 64)
                            ksrc = KT[0] if (kvh == hh) else KT[1]
                            heads.append(dict(q=QT[c][rows, u * 128:(u + 1) * 128], kc=ksrc[rows, cols],
                                              kp=(ksrc[rows, pcols] if i > 0 else None),
                                              vc=VAc[vsl][:, kvh, 0:64], vp=(VAc[(i - 1) % 4][:, kvh, 0:64] if i > 0 else None),
                                              od=OT[c][rows, u * 128:(u + 1) * 128], dd=DT[c][rows, u * 128:(u + 1) * 128],
                                              od_full=OT[c][:, u * 128:(u + 1) * 128], dd_full=DT[c][:, u * 128:(u + 1) * 128],
                                              ok=("OT", c, u, hh), dk=("DT", c, u, hh)))
                        attn_pair(1, heads, PT, [("QT", c)], [("KT", 0, 0), ("KT", 0, 1), ("KT", 1, 0), ("KT", 1, 1)],
                                  [("VAc", vsl), ("VAc", (i - 1) % 4)])
                attn_flush()
                bg_issue(2)
                for u in range(2):
                    dma("sp", xres[u][:], x1s[(4 * s + u) * 128:(4 * s + u + 1) * 128, :], (), [("xres", u)])
                for c in range(8):
                    dk = [("DT", c, x, hh) for x in range(4) for hh in range(2)]
                    okk = [("OT", c, x, hh) for x in range(4) for hh in range(2)]
                    b2 = c % 2
                    act(dsum[b2][:], DT[c][:], AF.Ln, dk, [("dsum", b2)], bias=esink[:, c:c + 1])
                    act(dsum[b2][:], dsum[b2][:], AF.Exp, [("dsum", b2)], [("dsum", b2)], scale=-1.0)
                    tt("dve", otmp[b2][:], OT[c][:], dsum[b2][:], ALU.mult, okk + [("dsum", b2)], [("otmp", b2)])
                    tt("pool" if c % 2 else "dve", OT[c][:], otmp[b2][:], gT[c][:], ALU.mult, [("otmp", b2), ("gT", c)] + okk, [("oTf", c)] + okk)
                bg_issue(2 if s < NS - 1 else 200)
                for u in range(4):
                    i = 4 * s + u
                    out_proj_tile(i, u, OT, [("oTf", c) for c in range(8)], Wob, 8, ggbc[1], xres, 32, yp, None, t12)
                    if u + 2 < 4:
                        dma("sp", xres[i % 2][:], x1s[(i + 2) * 128:(i + 3) * 128, :], (), [("xres", i % 2)])
            P.flush(final=True)
        print("[kernel] ops=%d waits=%d sig=%s" % (P.nops, P.nwait, P.cnt))
    return nc


_NC_CACHE = {}


def kernel(x_prompt, x_sample, c_prompt, c_sample, cache_a_kv_g0, cache_a_kv_g1, cache_a_kv_g2, cache_b_kv,
           ada_w, ada_b, g_pre, g_post, w_in_a, w_o_a, w_in_b, w_o_b, sinks_b, ada_kv_w, ada_kv_b, g_kv, w_kv,
           _cores=None):
    f = lambda a: np.ascontiguousarray(np.asarray(a, dtype=np.float32))
    x_prompt, x_sample, c_prompt, c_sample = f(x_prompt), f(x_sample), f(c_prompt), f(c_sample)
    ca = [f(cache_a_kv_g0), f(cache_a_kv_g1), f(cache_a_kv_g2)]
    cbk = f(cache_b_kv)
    ada_w, ada_b, g_pre, g_post = f(ada_w), f(ada_b), f(g_pre), f(g_post)
    w_in_a, w_o_a, w_in_b, w_o_b = f(w_in_a), f(w_o_a), f(w_in_b), f(w_o_b)
    sinks_b, ada_kv_w, ada_kv_b, g_kv, w_kv = f(sinks_b), f(ada_kv_w), f(ada_kv_b), f(g_kv), f(w_kv)
    cst = _consts()
    cores = list(range(NCORES)) if _cores is None else list(_cores)
    shared = dict(
        ada_w0=f(ada_w[0]), ada_w1=f(ada_w[1]), ada_kv_w=ada_kv_w,
        brow=f(np.concatenate([ada_b[0], ada_b[1], ada_kv_b])[None, :]),
        vecs=f(np.concatenate([g_pre[0].reshape(8, 128), g_pre[1].reshape(8, 128), g_post[0].reshape(8, 128),
                               g_post[1].reshape(8, 128), g_kv.reshape(8, 128)], axis=0)),
        gpost_rows=g_post, w_in_a=f(w_in_a[0]), w_o_a=f(w_o_a[0]), w_in_b=f(w_in_b[0]), w_o_b=f(w_o_b[0]),
        w_kvx=f(np.concatenate([w_kv[:, 0:256], w_kv[:, 64:128], w_kv[:, 0:64]], axis=1)),
        sink_fm=f(sinks_b[0].reshape(8, 2).T.repeat(64, axis=0)),
        sink_tm=f(np.tile(sinks_b[0][None, :], (DB, 1))),
        ident=cst["ident"], rmat=cst["rmat"], cosT=cst["cosT"], sinT=cst["sinT"], masks=cst["masks"], mbias=cst["mbias"],
        cs_tok=cst["cs_tok"], sn_tok=cst["sn_tok"], rowb_in=cst["rowb"], w_kv=w_kv,
        gvec_tm=f(np.stack([np.tile(v[None, :], (DB, 1)) for v in (g_pre[0], g_pre[1], g_post[0], g_post[1], g_kv)])),
    )
    in_maps = []
    for b in cores:
        crows = np.zeros((33, D), np.float32)
        crows[0:DB] = c_sample[DB * b:DB * (b + 1)]
        crows[32] = c_prompt[b]
        m = dict(shared)
        m.update(xp=x_prompt[b], crows=crows, xs=f(x_sample[DB * b:DB * (b + 1), 0, :]),
                 ca0=ca[0][0, DB * b:DB * (b + 1)].reshape(DB, 128, 512),
                 ca1=ca[1][0, DB * b:DB * (b + 1)].reshape(DB, 512, 512),
                 ca2=ca[2][0, DB * b:DB * (b + 1)].reshape(DB, 2048, 512),
                 cb=cbk[DB * b:DB * (b + 1)].reshape(DB, 128, 256))
        in_maps.append(m)
    if "nc" not in _NC_CACHE:
        _NC_CACHE["nc"] = build_nc()
    nc = _NC_CACHE["nc"]
    res = run_bass_kernel_spmd(nc, in_maps, core_ids=list(range(len(cores))))
    R = res.results
    n = len(cores)
    y_prompt = np.stack([R[k]["yp"] for k in range(n)])
    y_sample = np.concatenate([R[k]["ys"] for k in range(n)])[:, None, :]
    a_p = [np.stack([R[k]["a%dp" % g] for k in range(n)]).reshape(1, n, -1, 2, 4, 64) for g in range(3)]
    b_p = np.stack([R[k]["bp"] for k in range(n)]).reshape(n, 128, 2, 2, 64)
    a_s = [np.concatenate([R[k]["a%ds" % g] for k in range(n)]).reshape(1, n * DB, -1, 2, 4, 64) for g in range(3)]
    b_s = np.concatenate([R[k]["bs"] for k in range(n)]).reshape(n * DB, 128, 2, 2, 64)
    return (y_prompt, y_sample, a_p[0], a_p[1], a_p[2], b_p, a_s[0], a_s[1], a_s[2], b_s)
```

```python
import os
import numpy as np
from contextlib import ExitStack
import concourse.bass as bass
import concourse.mybir as mybir
from concourse.bass_utils import run_bass_kernel_spmd

F32 = mybir.dt.float32
BF16 = mybir.dt.bfloat16
ALU = mybir.AluOpType
AF = mybir.ActivationFunctionType

NCORES = 8
T = 4096
D = 1024
NT = 32
NS = 8
EPS = 1e-6
PAST = 16384
DB = 16
STRICT_SAME_ENGINE = os.environ.get("KSTRICT", "0") == "1"


class Prog:
    CE = ("pe", "act", "dve", "pool")
    ENG = ("pe", "act", "dve", "pool", "sp")

    def __init__(self, nc, stack, ndsem=20):
        self.nc = nc
        self.eng = dict(pe=nc.tensor, act=nc.scalar, dve=nc.vector, pool=nc.gpsimd, sp=nc.sync)
        self.ops = []
        self.last_w = {}
        self.readers = {}
        self.esem = {e: stack.enter_context(nc.semaphore("S_" + e)) for e in self.CE}
        self.cnt = {e: 0 for e in self.CE}
        nq = dict(sp=ndsem, pool=2, act=72)
        self.dsem = {e: [stack.enter_context(nc.semaphore("D_%s%d" % (e, i))) for i in range(nq[e])]
                     for e in ("sp", "pool", "act")}
        self.dcum = {e: [0] * nq[e] for e in self.dsem}
        self.dnext = {e: 0 for e in self.dsem}
        self.waited = {e: {} for e in self.ENG}
        self.nwait = 0
        self.nops = 0

    @staticmethod
    def _isps(k):
        n = k[0] if isinstance(k, tuple) else k
        return isinstance(n, str) and n.startswith("ps")

    def op(self, eng, fn, reads=(), writes=(), dma=False, rg=None):
        writes = list(writes) + [r for r in reads if self._isps(r)]
        reads = [r for r in reads if not self._isps(r)]
        i = len(self.ops)
        raw = set()
        oth = set()
        for r in reads:
            if r in self.last_w:
                raw.add(self.last_w[r])
        for w in writes:
            if w in self.last_w:
                oth.add(self.last_w[w])
            for j in self.readers.get(w, {}).values():
                oth.add(j)
        raw.discard(i)
        oth.discard(i)
        self.ops.append(dict(eng=eng, fn=fn, raw=raw, oth=oth - raw, dma=dma, rg=rg))
        for r in reads:
            self.readers.setdefault(r, {})[(eng if not dma else ("dma", i))] = i
        for w in writes:
            self.last_w[w] = i
            self.readers[w] = {}
        return i

    def _needs_wait(self, o, p, is_raw):
        if p["dma"] or o["dma"]:
            return True
        if p["eng"] != o["eng"]:
            return True
        if o["eng"] == "pe":
            return o["rg"] is not None and p["rg"] is not None and o["rg"] != p["rg"]
        return is_raw or STRICT_SAME_ENGINE

    def _wait(self, e, sem, val):
        k = id(sem)
        if self.waited[e].get(k, 0) >= val:
            return
        self.eng[e].wait_ge(sem, val)
        self.waited[e][k] = val
        self.nwait += 1

    def flush(self, final=False):
        ops = self.ops
        n = len(ops)
        need = [False] * n
        for o in ops:
            for d in o["raw"]:
                if self._needs_wait(o, ops[d], True):
                    need[d] = True
            for d in o["oth"]:
                if self._needs_wait(o, ops[d], False):
                    need[d] = True
        last = {}
        for i, o in enumerate(ops):
            if not o["dma"]:
                last[o["eng"]] = i
        for i in last.values():
            need[i] = True
        sig = [None] * n
        for i, o in enumerate(ops):
            e = o["eng"]
            eo = self.eng[e]
            deps = [(d, True) for d in o["raw"]] + [(d, False) for d in o["oth"]]
            for d, is_raw in sorted(deps):
                p = ops[d]
                if not self._needs_wait(o, p, is_raw):
                    continue
                s, v = sig[d]
                self._wait(e, s, v)
            if o["dma"]:
                k = self.dnext[e]
                self.dnext[e] = (k + 1) % len(self.dsem[e])
                s = self.dsem[e][k]
                self._wait(e, s, self.dcum[e][k])
                ins = o["fn"](eo)
                self.dcum[e][k] += 16
                ins.then_inc(s, 16)
                sig[i] = (s, self.dcum[e][k])
            else:
                ins = o["fn"](eo)
                if need[i]:
                    self.cnt[e] += 1
                    ins.then_inc(self.esem[e], 1)
                    sig[i] = (self.esem[e], self.cnt[e])
        for e in self.ENG:
            for x in self.CE:
                if x != e and self.cnt[x] > 0:
                    self._wait(e, self.esem[x], self.cnt[x])
            for q in self.dsem:
                if q == "act" and not final:
                    continue
                for k, s in enumerate(self.dsem[q]):
                    if self.dcum[q][k] > 0:
                        self._wait(e, s, self.dcum[q][k])
        self.nops += n
        print("[flush] ops", n, "cnt", self.cnt, "dcum max", {q: max(v) for q, v in self.dcum.items()})
        self.ops = []
        self.last_w = {}
        self.readers = {}


def _consts():
    half = 32
    inv = (10000.0 ** (-np.arange(half, dtype=np.float32) / half)).astype(np.float32)
    pos = np.arange(T, dtype=np.float32)
    ang = pos[:, None] * inv[None, :]
    cos = np.cos(ang).astype(np.float32)
    sin = np.sin(ang).astype(np.float32)
    fi = (np.arange(128) % 64) % 32
    cosT = np.ascontiguousarray(cos[:, fi].T)
    sinT = np.ascontiguousarray(sin[:, fi].T)
    angs = (np.float32(PAST) * inv).astype(np.float32)
    cs = np.cos(angs).astype(np.float32)
    ss = np.sin(angs).astype(np.float32)
    cs_tok = np.tile(np.concatenate([cs, cs])[None, :], (DB, 1)).astype(np.float32)
    sn_tok = np.tile(np.concatenate([-ss, ss])[None, :], (DB, 1)).astype(np.float32)
    rmat = np.zeros((128, 128), np.float32)
    for po in range(128):
        if (po % 64) < 32:
            rmat[po + 32, po] = -1.0
        else:
            rmat[po - 32, po] = 1.0
    k = np.arange(128)[:, None]
    q = np.arange(128)[None, :]
    cur = (k <= q).astype(np.float32)
    prev0 = (k >= q).astype(np.float32)
    prev1 = (k > q).astype(np.float32)
    m0 = np.concatenate([prev0, cur, prev0, cur], axis=1)
    m1 = np.concatenate([prev1, cur, prev1, cur], axis=1)
    masks = np.stack([m0, m1]).astype(np.float32)
    mbias = ((1.0 - masks[:, :, 0:256]) * -30000.0).astype(np.float32)
    rowb = np.zeros((128, 1), np.float32)
    rowb[0, 0] = -30000.0
    return dict(mbias=mbias, rowb=rowb, cosT=cosT, sinT=sinT, cs_tok=cs_tok, sn_tok=sn_tok, rmat=rmat, masks=masks,
                ident=np.eye(128, dtype=np.float32))


class _Stop(Exception):
    pass


def build_nc():
    nc = bass.Bass("TRN2", target_bir_lowering=False)
    try:
        return _build(nc)
    except _Stop:
        return nc


def _build(nc):

    def din(name, shape, dt=F32):
        return nc.dram_tensor(name, list(shape), dt, kind="ExternalInput").ap()

    def dout(name, shape, dt=F32):
        return nc.dram_tensor(name, list(shape), dt, kind="ExternalOutput").ap()

    xp = din("xp", [T, D])
    crows = din("crows", [33, D])
    xs = din("xs", [DB, D])
    ca = [din("ca0", [DB, 128, 512]), din("ca1", [DB, 512, 512]), din("ca2", [DB, 2048, 512])]
    cb = din("cb", [DB, 128, 256])
    ada_w0 = din("ada_w0", [D, 3072])
    ada_w1 = din("ada_w1", [D, 3072])
    ada_kv_w = din("ada_kv_w", [D, 2048])
    brow = din("brow", [1, 8192])
    vecs = din("vecs", [40, 128])
    gpost_rows = din("gpost_rows", [2, D])
    w_in_a = din("w_in_a", [D, 3072])
    w_o_a = din("w_o_a", [768, D])
    w_in_b = din("w_in_b", [D, 2048])
    w_o_b = din("w_o_b", [D, D])
    w_kvx = din("w_kvx", [D, 384])
    sink_fm = din("sink_fm", [128, 8])
    sink_tm = din("sink_tm", [DB, 16])
    ident_d = din("ident", [128, 128])
    rmat_d = din("rmat", [128, 128])
    cosT_d = din("cosT", [128, T])
    sinT_d = din("sinT", [128, T])
    masks_d = din("masks", [2, 128, 512])
    mbias_d = din("mbias", [2, 128, 256])
    cs_tok_d = din("cs_tok", [DB, 64])
    sn_tok_d = din("sn_tok", [DB, 64])
    rowb_d = din("rowb_in", [128, 1])
    gvec_tm = din("gvec_tm", [5, DB, D])
    w_kv_d = din("w_kv", [D, 256])

    yp = dout("yp", [T, D])
    ys = dout("ys", [DB, D])
    aop = [dout("a0p", [128, 512]), dout("a1p", [512, 512]), dout("a2p", [2048, 512])]
    bp = dout("bp", [128, 256])
    aos = [dout("a0s", [DB, 128, 512]), dout("a1s", [DB, 512, 512]), dout("a2s", [DB, 2048, 512])]
    bs = dout("bs", [DB, 128, 256])
    x1s = nc.dram_tensor("x1s", [T, D], F32, kind="Internal").ap()
    g2s = nc.dram_tensor("g2s", [4, 128, T], BF16, kind="Internal").ap()

    with ExitStack() as top:
        P = Prog(nc, top)
        op = P.op

        def stop(tag):
            if os.environ.get("KSTOP") == tag:
                P.flush()
                raise _Stop()

        def sbt(st, name, shape, dt):
            return st.enter_context(nc.sbuf_tensor(name, list(shape), dt))

        def pst(st, name, shape, dt):
            return st.enter_context(nc.psum_tensor(name, list(shape), dt))

        identf = sbt(top, "identf", [128, 128], F32)
        identb = sbt(top, "identb", [128, 128], BF16)
        rmf = sbt(top, "rmf", [128, 128], F32)
        rmb = sbt(top, "rmb", [128, 128], BF16)
        maskb = [sbt(top, "maskb%d" % l, [128, 512], BF16) for l in range(2)]
        mbb = [sbt(top, "mbb%d" % l, [128, 256], BF16) for l in range(2)]
        vecT = sbt(top, "vecT", [128, 40], F32)
        modT = sbt(top, "modT", [128, 64], F32)
        amod = sbt(top, "amod", [128, 24], F32)
        ggbc = [sbt(top, "ggbc%d" % l, [128, D], F32) for l in range(2)]
        esink = sbt(top, "esink", [128, 8], F32)
        ones_b = sbt(top, "ones_b", [128, 64], BF16)
        msq = sbt(top, "msq", [128, 4 * NT], F32)
        msq2 = sbt(top, "msq2", [128, 2 * NT], F32)
        rstd = sbt(top, "rstd", [128, 4 * NT], F32)
        junk = sbt(top, "junk", [128, D], BF16)

        psT = [pst(top, "psT0", [128, 8, 128], BF16)]
        psP = pst(top, "psP", [128, 2, 512], F32)
        psR = pst(top, "psR", [128, 512], F32)
        psS = [pst(top, "psS%d" % i, [128, 512], F32) for i in range(2)]
        psT.append(psR[:].bitcast(BF16).rearrange("p (k t) -> p k t", k=8))
        psOs = [pst(top, "psO%d" % i, [128, 512], F32) for i in range(2)]
        psOf = psOs[0]

        def dma(q, out, in_, reads=(), writes=()):
            return op(q, lambda e, o=out, i=in_: e.dma_start(out=o, in_=i), reads, writes, dma=True)

        def mm(out, lhsT, rhs, start, stop, reads, writes, rg=None):
            return op("pe", lambda e, o=out, l=lhsT, r=rhs, a=start, b=stop: e.matmul(o, lhsT=l, rhs=r, start=a, stop=b),
                      reads, writes, rg=rg)

        def tr(out, in_, ident, reads, writes):
            return op("pe", lambda e, o=out, i=in_, d=ident: e.transpose(out=o, in_=i, identity=d), reads, writes)

        def act(out, in_, func, reads, writes, **kw):
            return op("act", lambda e, o=out, i=in_, f=func, k=kw: e.activation(out=o, in_=i, func=f, **k), reads, writes)

        def cp(eng, out, in_, reads, writes):
            if eng == "act":
                return act(out, in_, AF.Copy, reads, writes)
            return op(eng, lambda e, o=out, i=in_: e.tensor_copy(out=o, in_=i), reads, writes)

        def tt(eng, out, in0, in1, alu, reads, writes):
            return op(eng, lambda e, o=out, a=in0, b=in1, f=alu: e.tensor_tensor(out=o, in0=a, in1=b, op=f), reads, writes)

        def ts(eng, out, in0, s1, s2, op0, op1, reads, writes):
            if s2 is None:
                return op(eng, lambda e, o=out, a=in0, x=s1, f=op0: e.tensor_scalar(out=o, in0=a, scalar1=x, scalar2=None, op0=f),
                          reads, writes)
            return op(eng, lambda e, o=out, a=in0, x=s1, y=s2, f=op0, g=op1:
                      e.tensor_scalar(out=o, in0=a, scalar1=x, scalar2=y, op0=f, op1=g), reads, writes)

        def stt(eng, out, in0, scalar, in1, op0, op1, reads, writes):
            return op(eng, lambda e, o=out, a=in0, s=scalar, b=in1, f=op0, g=op1:
                      e.scalar_tensor_tensor(out=o, in0=a, scalar=s, in1=b, op0=f, op1=g), reads, writes)

        def memset(eng, ap, val, writes):
            return op(eng, lambda e, a=ap, v=val: e.memset(a, v), (), writes)

        cast_rr = [0]

        def load_w(st_slots, dst, src, rows, col0, ncols, dcol0, key):
            nk = rows // 128
            for kc in range(nk):
                for c0 in range(0, ncols, 1024):
                    w = min(1024, ncols - c0)
                    sl = cast_rr[0] % len(st_slots)
                    cast_rr[0] += 1
                    stg = st_slots[sl]
                    dma("sp", stg[:, 0:w], src[kc * 128:(kc + 1) * 128, col0 + c0:col0 + c0 + w], (), [("wst", sl)])
                    eng = ("pool", "dve", "pool", "act")[cast_rr[0] % 4]
                    cp(eng, dst[:, kc, dcol0 + c0:dcol0 + c0 + w], stg[:, 0:w], [("wst", sl)], [key])

        rr = dict(p=0, r=0, s=0, o=0, e=0, w=0)

        def nxt(k, n=2):
            v = rr[k] % n
            rr[k] += 1
            return v

        def evac_eng():
            return ("act", "dve")[nxt("e")]

        AX = mybir.AxisListType

        with ExitStack() as st:
            crow_t = sbt(st, "crow_t", [33, D], F32)
            sc_t = sbt(st, "sc_t", [33, D], F32)
            cT = sbt(st, "cT", [128, 8, 33], F32)
            vec_t = sbt(st, "vec_t", [40, 128], F32)
            ones33 = sbt(st, "ones33", [33, 128], F32)
            adat = [sbt(st, "adat0", [33, 3072], F32), sbt(st, "adat1", [33, 2048], F32)]
            gprow = sbt(st, "gprow", [33, D], F32)
            wstg = [sbt(st, "adaw%d" % i, [128, 3072], F32) for i in range(2)]
            psA = [psS[0], psS[1], psR, psOf, psP[:, 0, :], psP[:, 1, :]]
            sink_t = sbt(st, "sink_t", [128, 8], F32)
            maskf = sbt(st, "maskf", [128, 512], F32)
            xs_t = sbt(st, "xs_t", [DB, D], F32)
            x1_t = sbt(st, "x1_t", [DB, D], F32)
            gv = [sbt(st, "gv%d" % i, [DB, D], F32) for i in range(3)]
            cs_t = sbt(st, "cs_t", [DB, 64], F32)
            sn_t = sbt(st, "sn_t", [DB, 64], F32)
            stm = sbt(st, "stm", [DB, 16], F32)
            rowb = sbt(st, "rowb", [128, 1], F32)
            hs = sbt(st, "hs", [DB, D], F32)
            hs2 = sbt(st, "hs2", [DB, D], F32)
            hsT = sbt(st, "hsT", [128, 8, DB], F32)
            hkTs = sbt(st, "hkTs", [128, 8, DB], F32)
            proj = sbt(st, "proj", [33, 3072], F32)
            brow_t = proj
            kvp = sbt(st, "kvp", [DB, 256], F32)
            rop = sbt(st, "rop", [DB, 1536], F32)
            swp = sbt(st, "swp", [DB, 1536], F32)
            krow = [sbt(st, "krow%d" % i, [DB, 512], F32) for i in range(4)]
            KVflat = sbt(st, "KVflat", [128, DB * 520], F32)
            prod = sbt(st, "prod", [128, 1024], F32)
            scr = sbt(st, "scr", [128, 16], F32)
            PZ = sbt(st, "PZ", [128, DB, 16, DB], F32)
            Oc = sbt(st, "Oc", [DB, 16, 65], F32)
            Ot = sbt(st, "Ot", [DB, 16, 64], F32)
            sml = sbt(st, "sml", [DB, 128], F32)

            dma("sp", identf[:], ident_d[:, :], (), ["identf"])
            dma("sp", rmf[:], rmat_d[:, :], (), ["rmf"])
            dma("sp", crow_t[:], crows[:, :], (), ["crow"])
            dma("sp", vec_t[:], vecs[:, :], (), ["vec_t"])
            dma("sp", sink_t[:], sink_fm[:, :], (), ["sink_t"])
            dma("sp", xs_t[:], xs[:, :], (), ["xs_t"])
            dma("sp", cs_t[:], cs_tok_d[:, :], (), ["cs_t"])
            dma("sp", sn_t[:], sn_tok_d[:, :], (), ["sn_t"])
            dma("sp", stm[:], sink_tm[:, :], (), ["stm"])
            dma("sp", rowb[:], rowb_d[:, :], (), ["rowb"])
            for l in range(2):
                dma("sp", maskf[:], masks_d[l], (), ["maskf"])
                cp("dve", maskb[l][:], maskf[:], ["maskf"], ["maskb%d" % l])
                dma("sp", maskf[:, 0:256], mbias_d[l], ["maskf"], ["maskf"])
                cp("dve", mbb[l][:], maskf[:, 0:256], ["maskf"], ["mbb%d" % l])
            cp("dve", identb[:], identf[:], ["identf"], ["identb"])
            cp("dve", rmb[:], rmf[:], ["rmf"], ["rmb"])
            memset("dve", ones_b[:], 1.0, ["ones_b"])
            memset("dve", ones33[:], 1.0, ["ones33"])
            memset("pool", PZ[:], 0.0, ["PZ"])
            act(esink[:], sink_t[:], AF.Exp, ["sink_t"], ["esink"])
            act(stm[:], stm[:], AF.Exp, ["stm"], ["stm"])
            tr(psOf[:, 0:40], vec_t[:, :], identf[0:40, 0:40], ["vec_t", "identf"], [("psA", 3)])
            cp("dve", vecT[:], psOf[:, 0:40], [("psA", 3)], ["vecT"])
            act(sc_t[:], crow_t[:], AF.Silu, ["crow"], ["sc_t"])
            for kc in range(8):
                tr(psOf[:, 64 + kc * 33:64 + (kc + 1) * 33], sc_t[:, kc * 128:(kc + 1) * 128], identf[0:33, 0:33],
                   ["sc_t", "identf"], [("psA", 3)])
            cp("dve", cT[:].rearrange("p k t -> p (k t)"), psOf[:, 64:64 + 8 * 33], [("psA", 3)], ["cT"])

            def stream_mm(lhsT_of, lkey, M, wd, nk, ncols, dst, dkey, bias_off=None):
                nb = (ncols + 511) // 512
                if bias_off is not None:
                    dma("sp", brow_t[32:33, 0:ncols], brow[:, bias_off:bias_off + ncols], (), ["brow_t"])
                for kc in range(nk):
                    sl = nxt("w")
                    dma("sp", wstg[sl][:, 0:ncols], wd[kc * 128:(kc + 1) * 128, 0:ncols], (), [("adaw", sl)])
                    for b in range(nb):
                        w = min(512, ncols - b * 512)
                        mm(psA[b][0:M, 0:w], lhsT_of(kc), wstg[sl][:, b * 512:b * 512 + w], kc == 0,
                           (kc == nk - 1) and bias_off is None, [lkey, ("adaw", sl)], [("psA", b)])
                for b in range(nb):
                    w = min(512, ncols - b * 512)
                    if bias_off is not None:
                        mm(psA[b][0:M, 0:w], ones33[32:33, 0:M], brow_t[32:33, b * 512:b * 512 + w], False, True,
                           ["ones33", "brow_t"], [("psA", b)])
                    cp("dve" if b % 2 else "act", dst[0:M, b * 512:b * 512 + w], psA[b][0:M, 0:w], [("psA", b)], [(dkey, b)])
                return [(dkey, b) for b in range(nb)]

            def prompt_cols(ad, akeys, ncols, mcol):
                nch = ncols // 128
                for j in range(nch):
                    mm(psS[0][:, j:j + 1], ad[32:33, j * 128:(j + 1) * 128], ones33[32:33, 0:1], True, True,
                       akeys + ["ones33"], [("psA", 0)])
                cp("dve", modT[:, mcol:mcol + nch], psS[0][:, 0:nch], [("psA", 0)], ["modT"])

            def gate_bcast(ad, akeys, li):
                dma("sp", gprow[32:33, :], gpost_rows[li:li + 1, :], (), ["gprow"])
                tt("dve", gprow[32:33, :], gprow[32:33, :], ad[32:33, 2048:3072], ALU.mult, akeys + ["gprow"], ["gprow"])
                for b in range(2):
                    mm(psS[1][:], ones33[32:33, :], gprow[32:33, b * 512:(b + 1) * 512], True, True,
                       ["ones33", "gprow"], [("psA", 1)])
                    cp("act", ggbc[li][:, b * 512:(b + 1) * 512], psS[1][:], [("psA", 1)], ["ggbc%d" % li])

            scol = [100]

            def rms_tok(x_t, xkeys):
                c = scol[0]
                scol[0] += 1
                act(junk[0:DB, :], x_t[:], AF.Square, xkeys, [("msq", c)], scale=1.0 / 32, accum_out=msq[0:DB, c:c + 1])
                act(rstd[0:DB, c:c + 1], msq[0:DB, c:c + 1], AF.Ln, [("msq", c)], [("rstd", c)], bias=EPS)
                act(rstd[0:DB, c:c + 1], rstd[0:DB, c:c + 1], AF.Exp, [("rstd", c)], [("rstd", c)], scale=-0.5)
                return rstd[0:DB, c:c + 1], ("rstd", c)

            def modulate_tok(dst, dkey, x_t, xkeys, rs, rkey, g_t, gkey, scale_ap, shift_ap, akeys):
                ts("dve", dst[:], x_t[:], rs, None, ALU.mult, None, xkeys + [rkey], [dkey])
                tt("dve", dst[:], dst[:], g_t[:], ALU.mult, [dkey, gkey], [dkey])
                stt("dve", dst[:], scale_ap, 1.0, dst[:], ALU.add, ALU.mult, akeys + [dkey], [dkey])
                tt("dve", dst[:], dst[:], shift_ap, ALU.add, akeys + [dkey], [dkey])

            def transpose_tok(dstT, dkey, src, skeys, nk):
                for kc in range(nk):
                    tr(psOf[:, kc * DB:(kc + 1) * DB], src[0:DB, kc * 128:(kc + 1) * 128], identf[0:DB, 0:DB], skeys, [("psA", 3)])
                cp("dve", dstT[:, 0:nk, :].rearrange("p k t -> p (k t)"), psOf[:, 0:nk * DB], [("psA", 3)], [dkey])

            def rope_tok(dst, dkey, src, skeys, nh, sw):
                X = src.rearrange("p (h d) -> p h d", d=64)
                R = dst.rearrange("p (h d) -> p h d", d=64)
                S = sw.rearrange("p (h d) -> p h d", d=64)
                bc = lambda a: a.unsqueeze(1).to_broadcast([DB, nh, a.shape[1]])
                tt("dve", R, X, bc(cs_t[:, :]), ALU.mult, skeys + ["cs_t"], [dkey])
                tt("pool", S[:, :, 0:32], X[:, :, 32:64], bc(sn_t[:, 0:32]), ALU.mult, skeys + ["sn_t"], ["swp"])
                tt("pool", S[:, :, 32:64], X[:, :, 0:32], bc(sn_t[:, 32:64]), ALU.mult, skeys + ["sn_t"], ["swp"])
                tt("dve", R, R, S, ALU.add, [dkey, "swp"], [dkey])

            def pv_slot(hd):
                bk = hd // 7
                t = (psP[:, 0, :], psP[:, 1, :], psR)[bk]
                return t[0:DB, (hd % 7) * 65:(hd % 7) * 65 + 65], ("psA", (4, 5, 2)[bk])

            k0 = stream_mm(lambda kc: cT[:, kc, :], "cT", 33, ada_w0, 8, 3072, adat[0], "ad0", bias_off=0)
            prompt_cols(adat[0], k0, 3072, 0)
            gate_bcast(adat[0], k0, 0)

            dma("sp", gv[0][:], gvec_tm[0], (), [("gv", 0)])
            dma("sp", gv[1][:], gvec_tm[2], (), [("gv", 1)])
            rs, rk = rms_tok(xs_t, ["xs_t"])
            modulate_tok(hs, "hs", xs_t, ["xs_t"], rs, rk, gv[0], ("gv", 0), adat[0][0:DB, 1024:2048], adat[0][0:DB, 0:1024], k0)
            transpose_tok(hsT, "hsT", hs, ["hs"], 8)
            pk = stream_mm(lambda kc: hsT[:, kc, :], "hsT", DB, w_in_a, 8, 3072, proj, "proj")
            rope_tok(rop[:, 0:1536], "rop", proj[0:DB, 0:1536], pk, 24, swp[:, 0:1536])
            for g, L in enumerate((128, 512, 2048)):
                cp("pool", krow[g][:, 0:256], rop[:, 768 + g * 256:768 + (g + 1) * 256], ["rop"], [("krow", g)])
                cp("pool", krow[g][:, 256:512], proj[0:DB, 1536 + g * 256:1536 + (g + 1) * 256], pk, [("krow", g)])
                dma("sp", aos[g][:, L - 1, :], krow[g][:], [("krow", g)], ())
            KV0 = KVflat[:].rearrange("p (b t h d) -> p b t h d", b=DB, t=2, h=4)
            memset("pool", KV0[:, :, 1, :, 64:65], 1.0, ["KVones"])
            for g, (L, dil) in enumerate(((128, 1), (512, 4), (2048, 16))):
                for b in range(DB):
                    dma("sp", KV0[:, b, :, :, 0:64], ca[g][b, 0:L:dil, :].rearrange("j (t h d) -> j t h d", t=2, h=4),
                        ["KVones"], [("KV", b)])
                for b in range(DB):
                    sl = b % 2
                    mm(psS[sl][:, 0:256], identf[0:DB, b:b + 1].to_broadcast([DB, 128]), rop[0:DB, g * 256:(g + 1) * 256], True, True,
                       ["rop", "identf"], [("psA", sl)])
                    p3 = prod[:, 0:256].rearrange("p (h d) -> p h d", h=4)
                    tt("dve", p3, KV0[:, b, 0, :, 0:64], psS[sl][:, 0:256].rearrange("p (h d) -> p h d", h=4), ALU.mult,
                       [("KV", b), ("psA", sl)], ["prod"])
                    op("dve", lambda e, o=scr[:, 0:4], i_=p3: e.tensor_reduce(out=o, in_=i_, axis=AX.X, op=ALU.add), ["prod"], ["scr"])
                    act(PZ[:, b, g * 4:(g + 1) * 4, b], scr[:, 0:4], AF.Exp, ["scr", "PZ"], [("PZ", b)], scale=0.125)
                for h in range(4):
                    hd = g * 4 + h
                    o, okey = pv_slot(hd)
                    for b in range(DB):
                        mm(o, PZ[:, b, hd, :], KV0[:, b, 1, h, :], b == 0, b == DB - 1, [("PZ", b), ("KV", b), "KVones"], [okey])
            cp("dve", Oc[:, 0:7, :], psP[0:DB, 0, 0:455].rearrange("p (h d) -> p h d", d=65), [("psA", 4)], ["Oc"])
            cp("dve", Oc[:, 7:12, :], psP[0:DB, 1, 0:325].rearrange("p (h d) -> p h d", d=65), [("psA", 5)], ["Oc"])

            def finish_attn(nh, qv, kvw, vv, keys):
                q3 = qv.rearrange("p (h d) -> p h d", d=64)
                pr = prod[0:DB, 0:nh * 64].rearrange("p (h d) -> p h d", d=64)
                tt("dve", pr, q3, kvw, ALU.mult, keys, ["prod"])
                op("dve", lambda e, o=sml[:, 32:32 + nh], i_=pr: e.tensor_reduce(out=o, in_=i_, axis=AX.X, op=ALU.add), ["prod"], ["sml"])
                act(sml[:, 32:32 + nh], sml[:, 32:32 + nh], AF.Exp, ["sml"], ["sml"], scale=0.125)
                tt("dve", pr, vv, sml[:, 32:32 + nh].unsqueeze(2).to_broadcast([DB, nh, 64]), ALU.mult, keys + ["sml", "prod"], ["prod"])
                tt("dve", Ot[:, 0:nh, :], Oc[:, 0:nh, 0:64], pr, ALU.add, ["Oc", "prod"], ["Ot"])
                tt("dve", sml[:, 0:nh], Oc[:, 0:nh, 64], sml[:, 32:32 + nh], ALU.add, ["Oc", "sml"], ["sml"])

            k3 = rop[:, 768:1536].rearrange("p (h d) -> p h d", d=64)
            v3 = proj[0:DB, 1536:2304].rearrange("p (h d) -> p h d", d=64)
            finish_attn(12, rop[:, 0:768], k3, v3, ["rop"] + pk)
            tt("dve", sml[:, 64:68], sml[:, 0:4], sml[:, 4:8], ALU.add, ["sml"], ["sml"])
            tt("dve", sml[:, 64:68], sml[:, 64:68], sml[:, 8:12], ALU.add, ["sml"], ["sml"])
            op("dve", lambda e, o=sml[:, 64:68], i_=sml[:, 64:68]: e.reciprocal(out=o, in_=i_), ["sml"], ["sml"])
            og = hs2[:, 0:768]
            for g in range(3):
                tt("dve", og[:, g * 256:(g + 1) * 256].rearrange("p (h d) -> p h d", d=64), Ot[:, g * 4:(g + 1) * 4, :],
                   sml[:, 64:68].unsqueeze(2).to_broadcast([DB, 4, 64]), ALU.mult, ["Ot", "sml"], ["og"])
            act(hs[:, 0:768], proj[0:DB, 2304:3072], AF.Silu, pk + ["hs"], ["hs"])
            tt("dve", og, og, hs[:, 0:768], ALU.mult, ["og", "hs"], ["og"])
            transpose_tok(hsT, "hsT", og, ["og"], 6)
            mk = stream_mm(lambda kc: hsT[:, kc, :], "hsT", DB, w_o_a, 6, 1024, hs, "mix")
            rs, rk = rms_tok(hs, mk)
            ts("dve", hs[:], hs[:], rs, None, ALU.mult, None, mk + [rk], ["mixn"])
            tt("dve", hs[:], hs[:], gv[1][:], ALU.mult, ["mixn", ("gv", 1)], ["mixn"])
            tt("dve", hs[:], hs[:], adat[0][0:DB, 2048:3072], ALU.mult, ["mixn"] + k0, ["mixn"])
            tt("dve", x1_t[:], hs[:], xs_t[:], ALU.add, ["mixn", "xs_t"], ["x1_t"])

            k1 = stream_mm(lambda kc: cT[:, kc, :], "cT", 33, ada_w1, 8, 3072, adat[0], "ad0", bias_off=3072)
            prompt_cols(adat[0], k1, 3072, 24)
            gate_bcast(adat[0], k1, 1)
            kk = stream_mm(lambda kc: cT[:, kc, :], "cT", 33, ada_kv_w, 8, 2048, adat[1], "ad1", bias_off=6144)
            prompt_cols(adat[1], kk, 2048, 48)
            for (dst, sc, gvv) in ((amod[:, 0:8], modT[:, 8:16], vecT[:, 0:8]), (amod[:, 8:16], modT[:, 32:40], vecT[:, 8:16]),
                                   (amod[:, 16:24], modT[:, 56:64], vecT[:, 32:40])):
                ts("dve", dst, sc, 1.0, None, ALU.add, None, ["modT"], ["amod"])
                tt("dve", dst, dst, gvv, ALU.mult, ["amod", "vecT"], ["amod"])

            dma("sp", gv[0][:], gvec_tm[1], (), [("gv", 0)])
            dma("sp", gv[1][:], gvec_tm[3], (), [("gv", 1)])
            dma("sp", gv[2][:], gvec_tm[4], (), [("gv", 2)])
            rs, rk = rms_tok(x1_t, ["x1_t"])
            modulate_tok(hs, "hs", x1_t, ["x1_t"], rs, rk, gv[0], ("gv", 0), adat[0][0:DB, 1024:2048], adat[0][0:DB, 0:1024], k1)
            modulate_tok(hs2, "hs2", x1_t, ["x1_t"], rs, rk, gv[2], ("gv", 2), adat[1][0:DB, 1024:2048], adat[1][0:DB, 0:1024], kk)
            transpose_tok(hsT, "hsT", hs, ["hs"], 8)
            transpose_tok(hkTs, "hkTs", hs2, ["hs2"], 8)
            kvk = stream_mm(lambda kc: hkTs[:, kc, :], "hkTs", DB, w_kv_d, 8, 256, kvp, "kvp")
            pk = stream_mm(lambda kc: hsT[:, kc, :], "hsT", DB, w_in_b, 8, 2048, proj, "proj")
            rope_tok(rop[:, 0:1024], "rop", proj[0:DB, 0:1024], pk, 16, swp[:, 0:1024])
            rope_tok(rop[:, 1024:1152], "ropk", kvp[:, 0:128], kvk, 2, swp[:, 1024:1152])
            cp("pool", krow[3][:, 0:128], rop[:, 1024:1152], ["ropk"], [("krow", 3)])
            cp("pool", krow[3][:, 128:256], kvp[:, 128:256], kvk, [("krow", 3)])
            dma("sp", bs[:, 127, :], krow[3][:, 0:256], [("krow", 3)], ())
            KV1 = KVflat[:, 0:DB * 260].rearrange("p (b t h d) -> p b t h d", b=DB, t=2, h=2)
            allkv = [("KV", b) for b in range(DB)]
            memset("pool", KV1[:, :, 1, :, 64:65], 1.0, allkv + ["KVones"])
            for b in range(DB):
                dma("sp", KV1[:, b, :, :, 0:64], cb[b].rearrange("j (t h d) -> j t h d", t=2, h=2), ["KVones"], [("KV", b)])
            for b in range(DB):
                for kh in range(2):
                    mm(psS[kh][:, :], identf[0:DB, b:b + 1].to_broadcast([DB, 128]), rop[0:DB, kh * 512:(kh + 1) * 512], True, True,
                       ["rop", "identf"], [("psA", kh)])
                    tt("dve", prod[:, kh * 512:(kh + 1) * 512].rearrange("p (h d) -> p h d", d=64),
                       psS[kh][:, :].rearrange("p (h d) -> p h d", d=64),
                       KV1[:, b, 0, kh, 0:64].unsqueeze(1).to_broadcast([128, 8, 64]), ALU.mult, [("KV", b), ("psA", kh)], ["prod"])
                op("dve", lambda e, o=scr[:, 0:16], i_=prod[:, :].rearrange("p (h d) -> p h d", d=64):
                   e.tensor_reduce(out=o, in_=i_, axis=AX.X, op=ALU.add), ["prod"], ["scr"])
                act(PZ[:, b, :, b], scr[:, 0:16], AF.Exp, ["scr", "PZ", "rowb"], [("PZ", b)], scale=0.125, bias=rowb[:, 0:1])
            for hd in range(16):
                o, okey = pv_slot(hd)
                for b in range(DB):
                    mm(o, PZ[:, b, hd, :], KV1[:, b, 1, hd // 8, :], b == 0, b == DB - 1, [("PZ", b), ("KV", b), "KVones"], [okey])
            cp("dve", Oc[:, 0:7, :], psP[0:DB, 0, 0:455].rearrange("p (h d) -> p h d", d=65), [("psA", 4)], ["Oc"])
            cp("dve", Oc[:, 7:14, :], psP[0:DB, 1, 0:455].rearrange("p (h d) -> p h d", d=65), [("psA", 5)], ["Oc"])
            cp("dve", Oc[:, 14:16, :], psR[0:DB, 0:130].rearrange("p (h d) -> p h d", d=65), [("psA", 2)], ["Oc"])
            kb3 = rop[:, 1024:1152].rearrange("p (k d) -> p k d", d=64)
            vb3 = kvp[:, 128:256].rearrange("p (k d) -> p k d", d=64)
            q4 = rop[:, 0:1024]
            for kh in range(2):
                pass
            pr = prod[0:DB, 0:1024].rearrange("p (h d) -> p h d", d=64)
            for kh in range(2):
                tt("dve", pr[:, kh * 8:(kh + 1) * 8, :], q4.rearrange("p (h d) -> p h d", d=64)[:, kh * 8:(kh + 1) * 8, :],
                   kb3[:, kh, :].unsqueeze(1).to_broadcast([DB, 8, 64]), ALU.mult, ["rop", "ropk", "prod"], ["prod"])
            op("dve", lambda e, o=sml[:, 32:48], i_=pr: e.tensor_reduce(out=o, in_=i_, axis=AX.X, op=ALU.add), ["prod"], ["sml"])
            act(sml[:, 32:48], sml[:, 32:48], AF.Exp, ["sml"], ["sml"], scale=0.125)
            for kh in range(2):
                tt("dve", pr[:, kh * 8:(kh + 1) * 8, :], sml[:, 32 + kh * 8:32 + (kh + 1) * 8].unsqueeze(2).to_broadcast([DB, 8, 64]),
                   vb3[:, kh, :].unsqueeze(1).to_broadcast([DB, 8, 64]), ALU.mult, kvk + ["sml", "prod"], ["prod"])
            tt("dve", Ot[:, 0:16, :], Oc[:, 0:16, 0:64], pr, ALU.add, ["Oc", "prod"], ["Ot"])
            tt("dve", sml[:, 0:16], Oc[:, 0:16, 64], sml[:, 32:48], ALU.add, ["Oc", "sml"], ["sml"])
            tt("dve", sml[:, 0:16], sml[:, 0:16], stm[:, :], ALU.add, ["sml", "stm"], ["sml"])
            op("dve", lambda e, o=sml[:, 0:16], i_=sml[:, 0:16]: e.reciprocal(out=o, in_=i_), ["sml"], ["sml"])
            og = hs2[:, 0:1024]
            tt("dve", og.rearrange("p (h d) -> p h d", d=64), Ot[:, 0:16, :], sml[:, 0:16].unsqueeze(2).to_broadcast([DB, 16, 64]),
               ALU.mult, ["Ot", "sml", "hs2"], ["og"])
            act(hs[:, :], proj[0:DB, 1024:2048], AF.Silu, pk + ["hs"], ["hs"])
            tt("dve", og, og, hs[:, :], ALU.mult, ["og", "hs"], ["og"])
            transpose_tok(hsT, "hsT", og, ["og"], 8)
            mk = stream_mm(lambda kc: hsT[:, kc, :], "hsT", DB, w_o_b, 8, 1024, hs, "mix")
            rs, rk = rms_tok(hs, mk)
            ts("dve", hs[:], hs[:], rs, None, ALU.mult, None, mk + [rk], ["mixn"])
            tt("dve", hs[:], hs[:], gv[1][:], ALU.mult, ["mixn", ("gv", 1)], ["mixn"])
            tt("dve", hs[:], hs[:], adat[0][0:DB, 2048:3072], ALU.mult, ["mixn"] + k1, ["mixn"])
            tt("dve", hs[:], hs[:], x1_t[:], ALU.add, ["mixn", "x1_t"], ["ys_t"])
            dma("sp", ys[:, :], hs[:], ["ys_t"], ())
            P.flush()
        if os.environ.get("KSTOP") == "setup":
            return nc

        A0 = amod[:, 0:8]
        SH0 = modT[:, 0:8]
        A1 = amod[:, 8:16]
        SH1 = modT[:, 24:32]
        AKV = amod[:, 16:24]
        SHKV = modT[:, 48:56]

        def load_x(src, srckey, i, xin):
            sl = i % len(xin)
            dma("sp", xin[sl][:], src[i * 128:(i + 1) * 128, :], [(srckey, i)] if srckey else (), [("xin", sl)])

        def norm_stats(tiles, xin, col0):
            for i in tiles:
                sl = i % len(xin)
                c = col0 + i
                act(junk[:], xin[sl][:], AF.Square, [("xin", sl)], [("msq", c)], scale=1.0 / 32, accum_out=msq[:, c:c + 1])
            c0 = col0 + tiles[0]
            n = len(tiles)
            mk = [("msq", col0 + i) for i in tiles]
            rk = [("rstd", col0 + i) for i in tiles]
            act(rstd[:, c0:c0 + n], msq[:, c0:c0 + n], AF.Ln, mk, rk, bias=EPS)
            act(rstd[:, c0:c0 + n], rstd[:, c0:c0 + n], AF.Exp, rk, rk, scale=-0.5)

        def norm_tile(i, u, xin, xn, col0, outs, hslot, stats=True):
            sl = i % len(xin)
            xsl = i % len(xn)
            c = col0 + i
            if stats:
                norm_stats([i], xin, col0)
            ts("dve", xn[xsl][:], xin[sl][:], rstd[:, c:c + 1], None, ALU.mult, None, [("xin", sl), ("rstd", c)], [("xn", xsl)])
            pt = psT[i % 2]
            ptk = ("psT", 0) if i % 2 == 0 else "psR"
            for kc in range(8):
                tr(pt[:, kc, :], xn[xsl][:, kc * 128:(kc + 1) * 128], identb[:], [("xn", xsl)], [ptk])
            for (hT, hname, Am, Sm) in outs:
                for kc in range(8):
                    o = hT[:, kc, u * 128:(u + 1) * 128]
                    if i % 2 == 0:
                        ts("dve", o, pt[:, kc, :], Am[:, kc:kc + 1], Sm[:, kc:kc + 1], ALU.mult, ALU.add,
                           [ptk], [(hname, hslot, u, kc)])
                    else:
                        act(o, pt[:, kc, :], AF.Identity, [ptk], [(hname, hslot, u, kc)],
                            scale=Am[:, kc:kc + 1], bias=Sm[:, kc:kc + 1])

        def project(hT, hname, hslot, W, cols, nk=8):
            slot = nxt("p")
            for kc in range(nk):
                mm(psP[:, slot, :], W[:, kc, cols], hT[:, kc, :], kc == 0, kc == nk - 1,
                   [(hname, hslot, u, kc) for u in range(4)], [("psP", slot)])
            return slot

        def rope(slot, ctab, stab, tkey, qraw, t12, dst, dkey, f32dst=None, fkey=None):
            r = nxt("r")
            cp("act", qraw[r][:], psP[:, slot, :], [("psP", slot)], [("qraw", r)])
            mm(psR[:], rmb[:], qraw[r][:], True, True, [("qraw", r)], ["psR"])
            tt("pool", t12[:, r, :], qraw[r][:], ctab, ALU.mult, [("qraw", r), tkey], [("t1", r)])
            tt("dve", t12[:, 2 + r, :], psR[:], stab, ALU.mult, ["psR", tkey], [("t2", r)])
            tt("pool", dst, t12[:, r, :], t12[:, 2 + r, :], ALU.add, [("t1", r), ("t2", r)], [dkey])
            if f32dst is not None:
                tt("pool", f32dst, t12[:, r, :], t12[:, 2 + r, :], ALU.add, [("t1", r), ("t2", r)], [fkey])

        def cache_out(kvf, kkeys, u, ncol, stage, dst_rows, skey):
            n = len(kvf)
            for k4 in range(n):
                tr(psR[:, k4 * 128:(k4 + 1) * 128], kvf[k4][:, u * 128:(u + 1) * 128], identf[:], [kkeys[k4]], ["psR"])
            sl = nxt("o")
            cp(evac_eng(), stage[sl][:, 0:ncol], psR[:, 0:ncol], ["psR"], [(skey, sl)])
            dma("sp", dst_rows, stage[sl][:, 0:ncol], [(skey, sl)], ())

        pending = []

        def attn_back(l, heads, pt, sl, has_prev, rv):
            osl = nxt("o")
            psO = psOs[osl]
            for hh, h in enumerate(heads):
                rows = slice(hh * 64, (hh + 1) * 64)
                pp = pt[:, hh * 256:hh * 256 + 128]
                pc = pt[:, hh * 256 + 128:hh * 256 + 256]
                for (o, vp, vc) in ((psO[rows, 0:128], h["vp"], h["vc"]), (psO[rows, 128:256], ones_b[:, :], ones_b[:, :])):
                    if has_prev:
                        mm(o, vp, pp, True, False, rv + [("PT", sl)], [("psO", osl)])
                    mm(o, vc, pc, not has_prev, True, rv + [("PT", sl)], [("psO", osl)])
            eng = evac_eng()
            for (dkey, okey, cols) in (("od", "ok", slice(0, 128)), ("dd", "dk", slice(128, 256))):
                for hh, h in enumerate(heads):
                    pass
                d0, d1 = heads[0][dkey], heads[1][dkey]
                src = psO[:, cols]
                dst = heads[0][dkey + "_full"]
                if len(dst.shape) == 3:
                    src = src.rearrange("p (a b) -> p a b", a=dst.shape[1])
                cp(eng, dst, src, [("psO", osl)], [heads[0][okey], heads[1][okey]])

        def attn_flush():
            while pending:
                attn_back(*pending.pop(0))

        def attn_pair(l, heads, PT, rq, rk, rv):
            sl = nxt("s")
            ps = psS[sl]
            pt = PT[sl]
            has_prev = heads[0]["kp"] is not None
            for hh, h in enumerate(heads):
                if has_prev:
                    mm(ps[:, hh * 256:hh * 256 + 256], identb[:], mbb[l][:], True, False, (), [("psS", sl)])
                    mm(ps[:, hh * 256:hh * 256 + 128], h["kp"], h["q"], False, False, rq + rk, [("psS", sl)])
                    mm(ps[:, hh * 256 + 128:hh * 256 + 256], h["kc"], h["q"], False, True, rq + rk, [("psS", sl)])
                else:
                    mm(ps[:, hh * 256 + 128:hh * 256 + 256], identb[:], mbb[l][:, 128:256], True, False, (), [("psS", sl)])
                    mm(ps[:, hh * 256 + 128:hh * 256 + 256], h["kc"], h["q"], False, True, rq + rk, [("psS", sl)])
            if has_prev:
                act(pt[:], ps[:], AF.Exp, [("psS", sl)], [("PT", sl)], scale=0.125)
            else:
                v = lambda a: a.rearrange("p (h x) -> p h x", h=2)[:, :, 128:256]
                act(v(pt[:]), v(ps[:]), AF.Exp, [("psS", sl)], [("PT", sl)], scale=0.125)
            attn_flush()
            pending.append((l, heads, pt, sl, has_prev, rv))

        T12K = [("t1", 0), ("t1", 1), ("t2", 0), ("t2", 1)]

        def out_proj_tile(i, u, oTf, okeys, Wo, nk, ggl, xres, col0, dst_dram, dkey, t12):
            if i % 2 == 0:
                banks = [(psP[:, 0, :], ("psP", 0)), (psP[:, 1, :], ("psP", 1))]
            else:
                banks = [(psS[0][:, :], ("psS", 0)), (psS[1][:, :], ("psS", 1))]
            for nb in range(2):
                for kc in range(nk):
                    mm(banks[nb][0], oTf[kc][:, u * 128:(u + 1) * 128], Wo[:, kc, nb * 512:(nb + 1) * 512], kc == 0, kc == nk - 1,
                       [okeys[kc]], [banks[nb][1]])
            c = col0 + i
            for nb in range(2):
                act(junk[:, 0:512], banks[nb][0], AF.Square, [banks[nb][1]], [("msq", c, nb)], scale=1.0 / 32,
                    accum_out=msq2[:, 2 * i + nb:2 * i + nb + 1])
            tt("dve", msq[:, c:c + 1], msq2[:, 2 * i:2 * i + 1], msq2[:, 2 * i + 1:2 * i + 2], ALU.add, [("msq", c, 0), ("msq", c, 1)], [("msq", c)])
            act(rstd[:, c:c + 1], msq[:, c:c + 1], AF.Ln, [("msq", c)], [("rstd", c)], bias=EPS)
            act(rstd[:, c:c + 1], rstd[:, c:c + 1], AF.Exp, [("rstd", c)], [("rstd", c)], scale=-0.5)
            r = i % 2
            for nb in range(2):
                stt("dve", t12[:, 2 * r + nb, :], banks[nb][0], rstd[:, c:c + 1], ggl[:, nb * 512:(nb + 1) * 512], ALU.mult, ALU.mult,
                    [banks[nb][1], ("rstd", c)], [T12K[2 * r + nb]])
            sl = i % len(xres)
            tm = t12[:, 2 * r:2 * r + 2, :].rearrange("p a b -> p (a b)")
            tt("pool", xres[sl][:], tm, xres[sl][:], ALU.add, [T12K[2 * r], T12K[2 * r + 1], ("xres", sl)], [("xres", sl)])
            dma("sp", dst_dram[i * 128:(i + 1) * 128, :], xres[sl][:], [("xres", sl)], [(dkey, i)] if dkey else ())

        bg = []
        for b in range(DB):
            for q8 in range(8):
                lo, hi = 256 * q8, min(256 * (q8 + 1), 2047)
                bg.append((aos[2][b, lo:hi, :], ca[2][b, lo + 1:hi + 1, :]))
            if b % 2 == 1:
                for bb in (b - 1, b):
                    bg.append((aos[1][bb, 0:256, :], ca[1][bb, 1:257, :]))
                    bg.append((aos[1][bb, 256:511, :], ca[1][bb, 257:512, :]))
        bg.append((aos[0][:, 0:127, :], ca[0][:, 1:128, :]))
        bg.append((bs[:, 0:127, :], cb[:, 1:128, :]))

        def bg_issue(n):
            for _ in range(n):
                if bg:
                    d, s_ = bg.pop(0)
                    dma("act", d, s_, (), ())

        with ExitStack() as st:
            Wa2 = sbt(st, "Wa2", [128, 8, 768], BF16)
            wst = [sbt(st, "wstA%d" % i, [128, 1024], F32) for i in range(2)]
            xin = [sbt(st, "xinA%d" % i, [128, D], F32) for i in range(4)]
            xn = [sbt(st, "xnA%d" % i, [128, D], BF16) for i in range(2)]
            hT = [sbt(st, "hTA%d" % i, [128, 8, 512], BF16) for i in range(2)]
            QT2 = [sbt(st, "QT2_%d" % c, [128, T], BF16) for c in range(2)]
            KT2 = [sbt(st, "KT2_%d" % c, [128, T], BF16) for c in range(2)]
            VT2 = [sbt(st, "VT2_%d" % c, [128, T], BF16) for c in range(2)]
            ctab = [sbt(st, "ctabA%d" % i, [128, 512], F32) for i in range(2)]
            stab = [sbt(st, "stabA%d" % i, [128, 512], F32) for i in range(2)]
            qraw = [sbt(st, "qrawA%d" % i, [128, 512], BF16) for i in range(2)]
            t12 = sbt(st, "t12A", [128, 4, 512], F32)
            kvf = [sbt(st, "kvfA%d" % i, [128, 512], F32) for i in range(4)]
            stage = [sbt(st, "stageA%d" % i, [128, 512], F32) for i in range(2)]
            PT = [sbt(st, "PTA%d" % i, [128, 512], BF16) for i in range(2)]
            VA = [sbt(st, "VAA%d" % i, [128, 4, 128], BF16) for i in range(4)]
            G2st = [sbt(st, "G2st%d" % k, [128, T], BF16) for k in range(4)]
            load_w(wst, Wa2, w_in_a, 1024, 512, 256, 0, "w")
            load_w(wst, Wa2, w_in_a, 1024, 768 + 512, 256, 256, "w")
            load_w(wst, Wa2, w_in_a, 1024, 1536 + 512, 256, 512, "w")
            for v in VA:
                memset("pool", v[:, :, 64:128], 1.0, [("w", "va")])
            P.flush()
            print("[phaseA] sbuf remaining", nc.sbuf_bytes_remaining)
            stop("A0")
            for u in range(4):
                load_x(xp, None, u, xin)
            for s in range(NS):
                hs = s % 2
                tk = slice(512 * s, 512 * (s + 1))
                dma("sp", ctab[hs][:], cosT_d[:, tk], (), [("tab", hs)])
                dma("sp", stab[hs][:], sinT_d[:, tk], (), [("tab", hs)])
                norm_stats([4 * s + u for u in range(4)], xin, 0)
                for u in range(4):
                    norm_tile(4 * s + u, u, xin, xn, 0, [(hT[hs], "hT", A0, SH0)], hs, stats=False)
                stop("A1")
                if s + 1 < NS:
                    for u in range(4):
                        load_x(xp, None, 4 * (s + 1) + u, xin)
                bg_issue(2)
                for (kind, cc, wc) in (("k", 0, 256), ("k", 1, 384), ("v", 0, 512), ("v", 1, 640), ("q", 0, 0), ("q", 1, 128)):
                    slot = project(hT[hs], "hT", hs, Wa2, slice(wc, wc + 128))
                    if kind == "k":
                        rope(slot, ctab[hs][:], stab[hs][:], ("tab", hs), qraw, t12, KT2[cc][:, tk], ("KT2", cc, s),
                             kvf[cc][:] if s >= 4 else None, ("kvf", cc))
                    elif kind == "q":
                        rope(slot, ctab[hs][:], stab[hs][:], ("tab", hs), qraw, t12, QT2[cc][:, tk], ("QT2", cc, s))
                    else:
                        cp("act", VT2[cc][:, tk], psP[:, slot, :], [("psP", slot)], [("VT2", cc, s)])
                        if s >= 4:
                            cp("dve", kvf[2 + cc][:], psP[:, slot, :], [("psP", slot)], [("kvf", 2 + cc)])
                stop("A2")
                bg_issue(2)
                if s >= 4:
                    for u in range(4):
                        i = 4 * s + u
                        cache_out(kvf, [("kvf", k) for k in range(4)], u, 512, stage, aop[2][(i - 16) * 128:(i - 15) * 128, :], "stage")
            stop("A3")
            allq = [[("QT2", c, s) for s in range(NS)] for c in range(2)]
            allk = [[("KT2", c, s) for s in range(NS)] for c in range(2)]
            allv = [[("VT2", c, s) for s in range(NS)] for c in range(2)]
            for r in range(16):
                for j in range(2):
                    tq = slice(2048 * j + r, 2048 * (j + 1), 16)
                    tp = slice(2048 * (j - 1) + r, 2048 * j, 16)
                    b = 2 * r + j
                    vsl = b % 4
                    ptv = psT[0]
                    for c in range(2):
                        tr(ptv[:, c, :], VT2[c][:, tq], identb[:], allv[c], [("psT", 0)])
                    cp(evac_eng(), VA[vsl][:, :, 0:64], ptv[:, 0:2, :].rearrange("p c (h d) -> p (c h) d", h=2),
                       [("psT", 0)], [("VA", vsl)])
                    for c in range(2):
                        heads = []
                        for hh in range(2):
                            h = 2 * c + hh
                            rows = slice(hh * 64, (hh + 1) * 64)
                            od = G2st[c][rows, :].rearrange("p (s r m) -> p s r m", s=8, r=16)[:, 4 * j:4 * j + 4, r, :]
                            dd = G2st[2 + c][rows, :].rearrange("p (s r m) -> p s r m", s=8, r=16)[:, 4 * j:4 * j + 4, r, :]
                            heads.append(dict(q=QT2[c][rows, tq], kc=KT2[c][rows, tq], kp=(KT2[c][rows, tp] if j == 1 else None),
                                              vc=VA[vsl][:, h, 0:64], vp=(VA[(b - 1) % 4][:, h, 0:64] if j == 1 else None),
                                              od=od, dd=dd, ok=("G2", c, b, hh), dk=("G2", 2 + c, b, hh),
                                              od_full=G2st[c][:, :].rearrange("p (s r m) -> p s r m", s=8, r=16)[:, 4 * j:4 * j + 4, r, :],
                                              dd_full=G2st[2 + c][:, :].rearrange("p (s r m) -> p s r m", s=8, r=16)[:, 4 * j:4 * j + 4, r, :]))
                        attn_pair(0, heads, PT, allq[c], allk[c], [("VA", vsl), ("VA", (b - 1) % 4)])
            attn_flush()
            for k in range(4):
                dma("sp", g2s[k], G2st[k][:], [("G2", k, b, hh) for b in range(32) for hh in range(2)], [("g2s", k)])
            P.flush()
        if os.environ.get("KSTOP") == "A":
            return nc

        with ExitStack() as st:
            Wab = sbt(st, "Wab", [128, 8, 2304], BF16)
            Woa = sbt(st, "Woa", [128, 6, 1024], BF16)
            wst = [sbt(st, "wstB%d" % i, [128, 1024], F32) for i in range(2)]
            xin = [sbt(st, "xinB%d" % i, [128, D], F32) for i in range(4)]
            xres = [sbt(st, "xresB%d" % i, [128, D], F32) for i in range(2)]
            xn = [sbt(st, "xnB%d" % i, [128, D], BF16) for i in range(2)]
            hT = [sbt(st, "hTB%d" % i, [128, 8, 512], BF16) for i in range(1)]
            QT = [sbt(st, "QTB%d" % c, [128, 512], BF16) for c in range(4)]
            KT = [sbt(st, "KTB%d" % c, [128, 1024], BF16) for c in range(4)]
            VT = [sbt(st, "VTB%d" % c, [128, 1024], BF16) for c in range(4)]
            gT = [sbt(st, "gTB%d" % c, [128, 512], BF16) for c in range(6)]
            OT = [sbt(st, "OTB%d" % c, [128, 512], BF16) for c in range(4)]
            DT = [sbt(st, "DTB%d" % c, [128, 512], BF16) for c in range(4)]
            oTf = [sbt(st, "oTfB%d" % c, [128, 512], BF16) for c in range(6)]
            G2in = [sbt(st, "G2in%d" % k, [128, 512], BF16) for k in range(4)]
            dsum2 = [sbt(st, "dsumB%d" % i, [128, 512], F32) for i in range(2)]
            otmp2 = [sbt(st, "otmpB%d" % i, [128, 512], BF16) for i in range(2)]
            ctab = [sbt(st, "ctabB%d" % i, [128, 512], F32) for i in range(2)]
            stab = [sbt(st, "stabB%d" % i, [128, 512], F32) for i in range(2)]
            qraw = [sbt(st, "qrawB%d" % i, [128, 512], BF16) for i in range(2)]
            t12 = sbt(st, "t12B", [128, 4, 512], F32)
            kvf = [sbt(st, "kvfB%d" % i, [128, 512], F32) for i in range(4)]
            stage = [sbt(st, "stageB%d" % i, [128, 512], F32) for i in range(2)]
            PT = [sbt(st, "PTB%d" % i, [128, 512], BF16) for i in range(2)]
            VA0 = [sbt(st, "VA0_%d" % i, [128, 4, 128], BF16) for i in range(4)]
            VA1 = [sbt(st, "VA1_%d" % i, [128, 4, 128], BF16) for i in range(8)]
            print("[phaseB] sbuf remaining", nc.sbuf_bytes_remaining)
            load_w(wst, Wab, w_in_a, 1024, 0, 512, 0, "w")
            load_w(wst, Wab, w_in_a, 1024, 768, 512, 512, "w")
            load_w(wst, Wab, w_in_a, 1024, 1536, 512, 1024, "w")
            load_w(wst, Wab, w_in_a, 1024, 2304, 768, 1536, "w")
            load_w(wst, Woa, w_o_a, 768, 0, 1024, 0, "w")
            for v in VA0 + VA1:
                memset("pool", v[:, :, 64:128], 1.0, [("w", "va")])
            P.flush()
            for u in range(4):
                load_x(xp, None, u, xin)
            for s in range(NS):
                hs = s % 2
                ts_ = hs * 512
                tk = slice(512 * s, 512 * (s + 1))
                dma("sp", ctab[hs][:], cosT_d[:, tk], (), [("tab", hs)])
                dma("sp", stab[hs][:], sinT_d[:, tk], (), [("tab", hs)])
                for k in range(4):
                    dma("sp", G2in[k][:], g2s[k][:, tk], [("g2s", k)], [("G2in", k)])
                bg_issue(2)
                for u in range(4):
                    i = 4 * s + u
                    norm_tile(i, u, xin, xn, 32, [(hT[0], "hT", A0, SH0)], 0)
                    if i + 4 < NT:
                        load_x(xp, None, i + 4, xin)
                ring = slice(ts_, ts_ + 512)
                for g in range(2):
                    for (kind, cc, wc) in (("k", 2 * g, 512 + 256 * g), ("k", 2 * g + 1, 640 + 256 * g),
                                           ("v", 2 * g, 1024 + 256 * g), ("v", 2 * g + 1, 1152 + 256 * g)):
                        slot = project(hT[0], "hT", 0, Wab, slice(wc, wc + 128))
                        need_f = (s == 7)
                        if kind == "k":
                            rope(slot, ctab[hs][:], stab[hs][:], ("tab", hs), qraw, t12, KT[cc][:, ring], ("KT", cc, hs),
                                 kvf[cc % 2][:] if need_f else None, ("kvf", cc % 2))
                        else:
                            cp("act", VT[cc][:, ring], psP[:, slot, :], [("psP", slot)], [("VT", cc, hs)])
                            if need_f:
                                cp("dve", kvf[2 + cc % 2][:], psP[:, slot, :], [("psP", slot)], [("kvf", 2 + cc % 2)])
                    if s == 7:
                        if g == 0:
                            cache_out(kvf, [("kvf", k) for k in range(4)], 3, 512, stage, aop[0][:, :], "stage")
                        else:
                            for u in range(4):
                                cache_out(kvf, [("kvf", k) for k in range(4)], u, 512, stage, aop[1][u * 128:(u + 1) * 128, :], "stage")
                for cc in range(4):
                    slot = project(hT[0], "hT", 0, Wab, slice(cc * 128, (cc + 1) * 128))
                    rope(slot, ctab[hs][:], stab[hs][:], ("tab", hs), qraw, t12, QT[cc][:], ("QT", cc))
                for cc in range(6):
                    slot = project(hT[0], "hT", 0, Wab, slice(1536 + cc * 128, 1536 + (cc + 1) * 128))
                    act(gT[cc][:], psP[:, slot, :], AF.Silu, [("psP", slot)], [("gT", cc)])
                bg_issue(2)
                for u in range(4):
                    i = 4 * s + u
                    vsl = i % 4
                    ptv = psT[0]
                    cols = slice(ts_ + u * 128, ts_ + (u + 1) * 128)
                    if u > 0:
                        pcols = slice(ts_ + (u - 1) * 128, ts_ + u * 128)
                    else:
                        pcols = slice((1 - hs) * 512 + 384, (1 - hs) * 512 + 512)
                    for c in range(2):
                        tr(ptv[:, c, :], VT[c][:, cols], identb[:], [("VT", c, hs)], [("psT", 0)])
                    cp(evac_eng(), VA0[vsl][:, :, 0:64], ptv[:, 0:2, :].rearrange("p c (h d) -> p (c h) d", h=2),
                       [("psT", 0)], [("VA0", vsl)])
                    for c in range(2):
                        heads = []
                        for hh in range(2):
                            h = 2 * c + hh
                            rows = slice(hh * 64, (hh + 1) * 64)
                            heads.append(dict(q=QT[c][rows, u * 128:(u + 1) * 128], kc=KT[c][rows, cols],
                                              kp=(KT[c][rows, pcols] if i > 0 else None),
                                              vc=VA0[vsl][:, h, 0:64], vp=(VA0[(i - 1) % 4][:, h, 0:64] if i > 0 else None),
                                              od=OT[c][rows, u * 128:(u + 1) * 128], dd=DT[c][rows, u * 128:(u + 1) * 128],
                                              od_full=OT[c][:, u * 128:(u + 1) * 128], dd_full=DT[c][:, u * 128:(u + 1) * 128],
                                              ok=("OT", c, u, hh), dk=("DT", c, u, hh)))
                        attn_pair(0, heads, PT, [("QT", c)], [("KT", c, 0), ("KT", c, 1)], [("VA0", vsl), ("VA0", (i - 1) % 4)])
                bg_issue(2)
                for r4 in range(4):
                    vsl = hs * 4 + r4
                    psl = (1 - hs) * 4 + r4
                    b = 4 * s + r4
                    ptv = psT[0]
                    cols = slice(ts_ + r4, ts_ + 512, 4)
                    pcols = slice((1 - hs) * 512 + r4, (1 - hs) * 512 + 512, 4)
                    for c in range(2):
                        tr(ptv[:, c, :], VT[2 + c][:, cols], identb[:], [("VT", 2 + c, hs)], [("psT", 0)])
                    cp(evac_eng(), VA1[vsl][:, :, 0:64], ptv[:, 0:2, :].rearrange("p c (h d) -> p (c h) d", h=2),
                       [("psT", 0)], [("VA1", vsl)])
                    for c in range(2):
                        heads = []
                        for hh in range(2):
                            h = 2 * c + hh
                            rows = slice(hh * 64, (hh + 1) * 64)
                            heads.append(dict(q=QT[2 + c][rows, r4:512:4], kc=KT[2 + c][rows, cols],
                                              kp=(KT[2 + c][rows, pcols] if s > 0 else None),
                                              vc=VA1[vsl][:, h, 0:64], vp=(VA1[psl][:, h, 0:64] if s > 0 else None),
                                              od=OT[2 + c][rows, r4:512:4], dd=DT[2 + c][rows, r4:512:4],
                                              od_full=OT[2 + c][:, r4:512:4], dd_full=DT[2 + c][:, r4:512:4],
                                              ok=("OT", 2 + c, r4, hh), dk=("DT", 2 + c, r4, hh)))
                        attn_pair(0, heads, PT, [("QT", 2 + c)], [("KT", 2 + c, 0), ("KT", 2 + c, 1)], [("VA1", vsl), ("VA1", psl)])
                attn_flush()
                bg_issue(2)
                for u in range(2):
                    dma("sp", xres[u][:], xp[(4 * s + u) * 128:(4 * s + u + 1) * 128, :], (), [("xres", u)])
                nat = lambda a: a.rearrange("p (r m) -> p m r", r=16)
                v3 = lambda a: a.rearrange("p (m r) -> p m r", r=16)
                for cc in range(2):
                    k01 = [("DT", c, x, hh) for c in (cc, 2 + cc) for x in range(4) for hh in range(2)]
                    ds = dsum2[cc]
                    tt("dve", ds[:], DT[cc][:], DT[2 + cc][:], ALU.add, k01, [("dsum", cc)])
                    tt("pool", v3(ds[:]), v3(ds[:]), nat(G2in[2 + cc][:]), ALU.add, [("dsum", cc), ("G2in", 2 + cc)], [("dsum", cc)])
                    act(ds[:], ds[:], AF.Ln, [("dsum", cc)], [("dsum", cc)])
                    act(ds[:], ds[:], AF.Exp, [("dsum", cc)], [("dsum", cc)], scale=-1.0)
                    for g in range(3):
                        c6 = 2 * g + cc
                        ot = otmp2[g % 2]
                        ok_ = ("otmp", g % 2)
                        if g < 2:
                            sk = [("OT", c6, x, hh) for x in range(4) for hh in range(2)]
                            tt("dve", ot[:], OT[c6][:], ds[:], ALU.mult, sk + [("dsum", cc)], [ok_])
                        else:
                            tt("pool", v3(ot[:]), nat(G2in[cc][:]), v3(ds[:]), ALU.mult, [("G2in", cc), ("dsum", cc)], [ok_])
                        tt("dve" if g == 2 else "pool", oTf[c6][:], ot[:], gT[c6][:], ALU.mult, [ok_, ("gT", c6)], [("oTf", c6)])
                for u in range(4):
                    i = 4 * s + u
                    out_proj_tile(i, u, oTf, [("oTf", c) for c in range(6)], Woa, 6, ggbc[0], xres, 64, x1s, "x1s", t12)
                    if u + 2 < 4:
                        dma("sp", xres[i % 2][:], xp[(i + 2) * 128:(i + 3) * 128, :], (), [("xres", i % 2)])
            P.flush()
        if os.environ.get("KSTOP") == "B":
            return nc

        with ExitStack() as st:
            Wq = sbt(st, "Wq", [128, 8, 2048], BF16)
            Wob = sbt(st, "Wob", [128, 8, 1024], BF16)
            Wkv = sbt(st, "Wkv", [128, 8, 384], BF16)
            wst = [sbt(st, "wstC%d" % i, [128, 1024], F32) for i in range(2)]
            xin = [sbt(st, "xinC%d" % i, [128, D], F32) for i in range(4)]
            xres = [sbt(st, "xresC%d" % i, [128, D], F32) for i in range(2)]
            xn = [sbt(st, "xnC%d" % i, [128, D], BF16) for i in range(2)]
            hT = sbt(st, "hTC", [128, 8, 512], BF16)
            hkT = sbt(st, "hkTC", [128, 8, 512], BF16)
            QT = [sbt(st, "QTC%d" % c, [128, 512], BF16) for c in range(8)]
            KT = [sbt(st, "KTC%d" % c, [128, 1024], BF16) for c in range(2)]
            VT = sbt(st, "VTC", [128, 1024], BF16)
            gT = [sbt(st, "gTC%d" % c, [128, 512], BF16) for c in range(8)]
            OT = [sbt(st, "OTC%d" % c, [128, 512], BF16) for c in range(8)]
            DT = [sbt(st, "DTC%d" % c, [128, 512], BF16) for c in range(8)]
            dsum = [sbt(st, "dsumC%d" % i, [128, 512], F32) for i in range(2)]
            otmp = [sbt(st, "otmpC%d" % i, [128, 512], BF16) for i in range(2)]
            ctab = [sbt(st, "ctabC%d" % i, [128, 512], F32) for i in range(2)]
            stab = [sbt(st, "stabC%d" % i, [128, 512], F32) for i in range(2)]
            qraw = [sbt(st, "qrawC%d" % i, [128, 512], BF16) for i in range(2)]
            t12 = sbt(st, "t12C", [128, 4, 512], F32)
            kvf = [sbt(st, "kvfC%d" % i, [128, 512], F32) for i in range(2)]
            stage = [sbt(st, "stageC%d" % i, [128, 256], F32) for i in range(2)]
            PT = [sbt(st, "PTC%d" % i, [128, 512], BF16) for i in range(2)]
            VAc = [sbt(st, "VAc%d" % i, [128, 2, 128], BF16) for i in range(4)]
            print("[phaseC] sbuf remaining", nc.sbuf_bytes_remaining)
            load_w(wst, Wq, w_in_b, 1024, 0, 2048, 0, "w")
            load_w(wst, Wob, w_o_b, 1024, 0, 1024, 0, "w")
            load_w(wst, Wkv, w_kvx, 1024, 0, 384, 0, "w")
            for v in VAc:
                memset("pool", v[:, :, 64:128], 1.0, [("w", "va")])
            P.flush()
            for u in range(4):
                dma("sp", xin[u][:], x1s[u * 128:(u + 1) * 128, :], (), [("xin", u)])
            for s in range(NS):
                hs = s % 2
                ts_ = hs * 512
                tk = slice(512 * s, 512 * (s + 1))
                ring = slice(ts_, ts_ + 512)
                dma("sp", ctab[hs][:], cosT_d[:, tk], (), [("tab", hs)])
                dma("sp", stab[hs][:], sinT_d[:, tk], (), [("tab", hs)])
                bg_issue(2)
                for u in range(4):
                    i = 4 * s + u
                    norm_tile(i, u, xin, xn, 0, [(hT, "hT", A1, SH1), (hkT, "hkT", AKV, SHKV)], 0)
                    if i + 4 < NT:
                        dma("sp", xin[i % 4][:], x1s[(i + 4) * 128:(i + 5) * 128, :], (), [("xin", i % 4)])
                need_f = (s == 7)
                slot = project(hkT, "hkT", 0, Wkv, slice(0, 128))
                rope(slot, ctab[hs][:], stab[hs][:], ("tab", hs), qraw, t12, KT[0][:, ring], ("KT", 0, hs),
                     kvf[0][:] if need_f else None, ("kvf", 0))
                slot = project(hkT, "hkT", 0, Wkv, slice(256, 384))
                rope(slot, ctab[hs][:], stab[hs][:], ("tab", hs), qraw, t12, KT[1][:, ring], ("KT", 1, hs))
                slot = project(hkT, "hkT", 0, Wkv, slice(128, 256))
                cp("act", VT[:, ring], psP[:, slot, :], [("psP", slot)], [("VT", hs)])
                if need_f:
                    cp("dve", kvf[1][:], psP[:, slot, :], [("psP", slot)], [("kvf", 1)])
                    cache_out(kvf, [("kvf", 0), ("kvf", 1)], 3, 256, stage, bp[:, :], "stage")
                for cc in range(8):
                    slot = project(hT, "hT", 0, Wq, slice(cc * 128, (cc + 1) * 128))
                    rope(slot, ctab[hs][:], stab[hs][:], ("tab", hs), qraw, t12, QT[cc][:], ("QT", cc))
                bg_issue(2)
                for cc in range(8):
                    slot = project(hT, "hT", 0, Wq, slice(1024 + cc * 128, 1024 + (cc + 1) * 128))
                    act(gT[cc][:], psP[:, slot, :], AF.Silu, [("psP", slot)], [("gT", cc)])
                for u in range(4):
                    i = 4 * s + u
                    vsl = i % 4
                    ptv = psT[0]
                    cols = slice(ts_ + u * 128, ts_ + (u + 1) * 128)
                    if u > 0:
                        pcols = slice(ts_ + (u - 1) * 128, ts_ + u * 128)
                    else:
                        pcols = slice((1 - hs) * 512 + 384, (1 - hs) * 512 + 512)
                    tr(ptv[:, 0, :], VT[:, cols], identb[:], [("VT", hs)], [("psT", 0)])
                    cp(evac_eng(), VAc[vsl][:, :, 0:64], ptv[:, 0, :].rearrange("p (h d) -> p h d", h=2),
                       [("psT", 0)], [("VAc", vsl)])
                    for c in range(8):
                        kvh = c // 4
                        heads = []
                        for hh in range(2):
                            rows = slice(hh * 64, (hh + 1) * 64)
                            ksrc = KT[0] if (kvh == hh) else KT[1]
                            heads.append(dict(q=QT[c][rows, u * 128:(u + 1) * 128], kc=ksrc[rows, cols],
                                              kp=(ksrc[rows, pcols] if i > 0 else None),
                                              vc=VAc[vsl][:, kvh, 0:64], vp=(VAc[(i - 1) % 4][:, kvh, 0:64] if i > 0 else None),
                                              od=OT[c][rows, u * 128:(u + 1) * 128], dd=DT[c][rows, u * 128:(u + 1) * 128],
                                              od_full=OT[c][:, u * 128:(u + 1) * 128], dd_full=DT[c][:, u * 128:(u + 1) * 128],
                                              ok=("OT", c, u, hh), dk=("DT", c, u, hh)))
                        attn_pair(1, heads, PT, [("QT", c)], [("KT", 0, 0), ("KT", 0, 1), ("KT", 1, 0), ("KT", 1, 1)],
                                  [("VAc", vsl), ("VAc", (i - 1) % 4)])
                attn_flush()
                bg_issue(2)
                for u in range(2):
                    dma("sp", xres[u][:], x1s[(4 * s + u) * 128:(4 * s + u + 1) * 128, :], (), [("xres", u)])
                for c in range(8):
                    dk = [("DT", c, x, hh) for x in range(4) for hh in range(2)]
                    okk = [("OT", c, x, hh) for x in range(4) for hh in range(2)]
                    b2 = c % 2
                    act(dsum[b2][:], DT[c][:], AF.Ln, dk, [("dsum", b2)], bias=esink[:, c:c + 1])
                    act(dsum[b2][:], dsum[b2][:], AF.Exp, [("dsum", b2)], [("dsum", b2)], scale=-1.0)
                    tt("dve", otmp[b2][:], OT[c][:], dsum[b2][:], ALU.mult, okk + [("dsum", b2)], [("otmp", b2)])
                    tt("pool" if c % 2 else "dve", OT[c][:], otmp[b2][:], gT[c][:], ALU.mult, [("otmp", b2), ("gT", c)] + okk, [("oTf", c)] + okk)
                bg_issue(2 if s < NS - 1 else 200)
                for u in range(4):
                    i = 4 * s + u
                    out_proj_tile(i, u, OT, [("oTf", c) for c in range(8)], Wob, 8, ggbc[1], xres, 32, yp, None, t12)
                    if u + 2 < 4:
                        dma("sp", xres[i % 2][:], x1s[(i + 2) * 128:(i + 3) * 128, :], (), [("xres", i % 2)])
            P.flush(final=True)
        print("[kernel] ops=%d waits=%d sig=%s" % (P.nops, P.nwait, P.cnt))
    return nc


_NC_CACHE = {}


def kernel(x_prompt, x_sample, c_prompt, c_sample, cache_a_kv_g0, cache_a_kv_g1, cache_a_kv_g2, cache_b_kv,
           ada_w, ada_b, g_pre, g_post, w_in_a, w_o_a, w_in_b, w_o_b, sinks_b, ada_kv_w, ada_kv_b, g_kv, w_kv,
           _cores=None):
    f = lambda a: np.ascontiguousarray(np.asarray(a, dtype=np.float32))
    x_prompt, x_sample, c_prompt, c_sample = f(x_prompt), f(x_sample), f(c_prompt), f(c_sample)
    ca = [f(cache_a_kv_g0), f(cache_a_kv_g1), f(cache_a_kv_g2)]
    cbk = f(cache_b_kv)
    ada_w, ada_b, g_pre, g_post = f(ada_w), f(ada_b), f(g_pre), f(g_post)
    w_in_a, w_o_a, w_in_b, w_o_b = f(w_in_a), f(w_o_a), f(w_in_b), f(w_o_b)
    sinks_b, ada_kv_w, ada_kv_b, g_kv, w_kv = f(sinks_b), f(ada_kv_w), f(ada_kv_b), f(g_kv), f(w_kv)
    cst = _consts()
    cores = list(range(NCORES)) if _cores is None else list(_cores)
    shared = dict(
        ada_w0=f(ada_w[0]), ada_w1=f(ada_w[1]), ada_kv_w=ada_kv_w,
        brow=f(np.concatenate([ada_b[0], ada_b[1], ada_kv_b])[None, :]),
        vecs=f(np.concatenate([g_pre[0].reshape(8, 128), g_pre[1].reshape(8, 128), g_post[0].reshape(8, 128),
                               g_post[1].reshape(8, 128), g_kv.reshape(8, 128)], axis=0)),
        gpost_rows=g_post, w_in_a=f(w_in_a[0]), w_o_a=f(w_o_a[0]), w_in_b=f(w_in_b[0]), w_o_b=f(w_o_b[0]),
        w_kvx=f(np.concatenate([w_kv[:, 0:256], w_kv[:, 64:128], w_kv[:, 0:64]], axis=1)),
        sink_fm=f(sinks_b[0].reshape(8, 2).T.repeat(64, axis=0)),
        sink_tm=f(np.tile(sinks_b[0][None, :], (DB, 1))),
        ident=cst["ident"], rmat=cst["rmat"], cosT=cst["cosT"], sinT=cst["sinT"], masks=cst["masks"], mbias=cst["mbias"],
        cs_tok=cst["cs_tok"], sn_tok=cst["sn_tok"], rowb_in=cst["rowb"], w_kv=w_kv,
        gvec_tm=f(np.stack([np.tile(v[None, :], (DB, 1)) for v in (g_pre[0], g_pre[1], g_post[0], g_post[1], g_kv)])),
    )
    in_maps = []
    for b in cores:
        crows = np.zeros((33, D), np.float32)
        crows[0:DB] = c_sample[DB * b:DB * (b + 1)]
        crows[32] = c_prompt[b]
        m = dict(shared)
        m.update(xp=x_prompt[b], crows=crows, xs=f(x_sample[DB * b:DB * (b + 1), 0, :]),
                 ca0=ca[0][0, DB * b:DB * (b + 1)].reshape(DB, 128, 512),
                 ca1=ca[1][0, DB * b:DB * (b + 1)].reshape(DB, 512, 512),
                 ca2=ca[2][0, DB * b:DB * (b + 1)].reshape(DB, 2048, 512),
                 cb=cbk[DB * b:DB * (b + 1)].reshape(DB, 128, 256))
        in_maps.append(m)
    if "nc" not in _NC_CACHE:
        _NC_CACHE["nc"] = build_nc()
    nc = _NC_CACHE["nc"]
    res = run_bass_kernel_spmd(nc, in_maps, core_ids=list(range(len(cores))))
    R = res.results
    n = len(cores)
    y_prompt = np.stack([R[k]["yp"] for k in range(n)])
    y_sample = np.concatenate([R[k]["ys"] for k in range(n)])[:, None, :]
    a_p = [np.stack([R[k]["a%dp" % g] for k in range(n)]).reshape(1, n, -1, 2, 4, 64) for g in range(3)]
    b_p = np.stack([R[k]["bp"] for k in range(n)]).reshape(n, 128, 2, 2, 64)
    a_s = [np.concatenate([R[k]["a%ds" % g] for k in range(n)]).reshape(1, n * DB, -1, 2, 4, 64) for g in range(3)]
    b_s = np.concatenate([R[k]["bs"] for k in range(n)]).reshape(n * DB, 128, 2, 2, 64)
    return (y_prompt, y_sample, a_p[0], a_p[1], a_p[2], b_p, a_s[0], a_s[1], a_s[2], b_s)
```

```python
import os
import numpy as np
from contextlib import ExitStack
import concourse.bass as bass
import concourse.mybir as mybir
from concourse.bass_utils import run_bass_kernel_spmd

F32 = mybir.dt.float32
BF16 = mybir.dt.bfloat16
ALU = mybir.AluOpType
AF = mybir.ActivationFunctionType

NCORES = 8
T = 4096
D = 1024
NT = 32
NS = 8
EPS = 1e-6
PAST = 16384
DB = 16
STRICT_SAME_ENGINE = os.environ.get("KSTRICT", "0") == "1"


class Prog:
    CE = ("pe", "act", "dve", "pool")
    ENG = ("pe", "act", "dve", "pool", "sp")

    def __init__(self, nc, stack, ndsem=20):
        self.nc = nc
        self.eng = dict(pe=nc.tensor, act=nc.scalar, dve=nc.vector, pool=nc.gpsimd, sp=nc.sync)
        self.ops = []
        self.last_w = {}
        self.readers = {}
        self.esem = {e: stack.enter_context(nc.semaphore("S_" + e)) for e in self.CE}
        self.cnt = {e: 0 for e in self.CE}
        nq = dict(sp=ndsem, pool=2, act=72)
        self.dsem = {e: [stack.enter_context(nc.semaphore("D_%s%d" % (e, i))) for i in range(nq[e])]
                     for e in ("sp", "pool", "act")}
        self.dcum = {e: [0] * nq[e] for e in self.dsem}
        self.dnext = {e: 0 for e in self.dsem}
        self.waited = {e: {} for e in self.ENG}
        self.nwait = 0
        self.nops = 0

    @staticmethod
    def _isps(k):
        n = k[0] if isinstance(k, tuple) else k
        return isinstance(n, str) and n.startswith("ps")

    def op(self, eng, fn, reads=(), writes=(), dma=False, rg=None):
        writes = list(writes) + [r for r in reads if self._isps(r)]
        reads = [r for r in reads if not self._isps(r)]
        i = len(self.ops)
        raw = set()
        oth = set()
        for r in reads:
            if r in self.last_w:
                raw.add(self.last_w[r])
        for w in writes:
            if w in self.last_w:
                oth.add(self.last_w[w])
            for j in self.readers.get(w, {}).values():
                oth.add(j)
        raw.discard(i)
        oth.discard(i)
        self.ops.append(dict(eng=eng, fn=fn, raw=raw, oth=oth - raw, dma=dma, rg=rg))
        for r in reads:
            self.readers.setdefault(r, {})[(eng if not dma else ("dma", i))] = i
        for w in writes:
            self.last_w[w] = i
            self.readers[w] = {}
        return i

    def _needs_wait(self, o, p, is_raw):
        if p["dma"] or o["dma"]:
            return True
        if p["eng"] != o["eng"]:
            return True
        if o["eng"] == "pe":
            return o["rg"] is not None and p["rg"] is not None and o["rg"] != p["rg"]
        return is_raw or STRICT_SAME_ENGINE

    def _wait(self, e, sem, val):
        k = id(sem)
        if self.waited[e].get(k, 0) >= val:
            return
        self.eng[e].wait_ge(sem, val)
        self.waited[e][k] = val
        self.nwait += 1

    def flush(self, final=False):
        ops = self.ops
        n = len(ops)
        need = [False] * n
        for o in ops:
            for d in o["raw"]:
                if self._needs_wait(o, ops[d], True):
                    need[d] = True
            for d in o["oth"]:
                if self._needs_wait(o, ops[d], False):
                    need[d] = True
        last = {}
        for i, o in enumerate(ops):
            if not o["dma"]:
                last[o["eng"]] = i
        for i in last.values():
            need[i] = True
        sig = [None] * n
        for i, o in enumerate(ops):
            e = o["eng"]
            eo = self.eng[e]
            deps = [(d, True) for d in o["raw"]] + [(d, False) for d in o["oth"]]
            for d, is_raw in sorted(deps):
                p = ops[d]
                if not self._needs_wait(o, p, is_raw):
                    continue
                s, v = sig[d]
                self._wait(e, s, v)
            if o["dma"]:
                k = self.dnext[e]
                self.dnext[e] = (k + 1) % len(self.dsem[e])
                s = self.dsem[e][k]
                self._wait(e, s, self.dcum[e][k])
                ins = o["fn"](eo)
                self.dcum[e][k] += 16
                ins.then_inc(s, 16)
                sig[i] = (s, self.dcum[e][k])
            else:
                ins = o["fn"](eo)
                if need[i]:
                    self.cnt[e] += 1
                    ins.then_inc(self.esem[e], 1)
                    sig[i] = (self.esem[e], self.cnt[e])
        for e in self.ENG:
            for x in self.CE:
                if x != e and self.cnt[x] > 0:
                    self._wait(e, self.esem[x], self.cnt[x])
            for q in self.dsem:
                if q == "act" and not final:
                    continue
                for k, s in enumerate(self.dsem[q]):
                    if self.dcum[q][k] > 0:
                        self._wait(e, s, self.dcum[q][k])
        self.nops += n
        print("[flush] ops", n, "cnt", self.cnt, "dcum max", {q: max(v) for q, v in self.dcum.items()})
        self.ops = []
        self.last_w = {}
        self.readers = {}


def _consts():
    half = 32
    inv = (10000.0 ** (-np.arange(half, dtype=np.float32) / half)).astype(np.float32)
    pos = np.arange(T, dtype=np.float32)
    ang = pos[:, None] * inv[None, :]
    cos = np.cos(ang).astype(np.float32)
    sin = np.sin(ang).astype(np.float32)
    fi = (np.arange(128) % 64) % 32
    cosT = np.ascontiguousarray(cos[:, fi].T)
    sinT = np.ascontiguousarray(sin[:, fi].T)
    angs = (np.float32(PAST) * inv).astype(np.float32)
    cs = np.cos(angs).astype(np.float32)
    ss = np.sin(angs).astype(np.float32)
    cs_tok = np.tile(np.concatenate([cs, cs])[None, :], (DB, 1)).astype(np.float32)
    sn_tok = np.tile(np.concatenate([-ss, ss])[None, :], (DB, 1)).astype(np.float32)
    rmat = np.zeros((128, 128), np.float32)
    for po in range(128):
        if (po % 64) < 32:
            rmat[po + 32, po] = -1.0
        else:
            rmat[po - 32, po] = 1.0
    k = np.arange(128)[:, None]
    q = np.arange(128)[None, :]
    cur = (k <= q).astype(np.float32)
    prev0 = (k >= q).astype(np.float32)
    prev1 = (k > q).astype(np.float32)
    m0 = np.concatenate([prev0, cur, prev0, cur], axis=1)
    m1 = np.concatenate([prev1, cur, prev1, cur], axis=1)
    masks = np.stack([m0, m1]).astype(np.float32)
    mbias = ((1.0 - masks[:, :, 0:256]) * -30000.0).astype(np.float32)
    rowb = np.zeros((128, 1), np.float32)
    rowb[0, 0] = -30000.0
    return dict(mbias=mbias, rowb=rowb, cosT=cosT, sinT=sinT, cs_tok=cs_tok, sn_tok=sn_tok, rmat=rmat, masks=masks,
                ident=np.eye(128, dtype=np.float32))


class _Stop(Exception):
    pass


def build_nc():
    nc = bass.Bass("TRN2", target_bir_lowering=False)
    try:
        return _build(nc)
    except _Stop:
        return nc


def _build(nc):

    def din(name, shape, dt=F32):
        return nc.dram_tensor(name, list(shape), dt, kind="ExternalInput").ap()

    def dout(name, shape, dt=F32):
        return nc.dram_tensor(name, list(shape), dt, kind="ExternalOutput").ap()

    xp = din("xp", [T, D])
    crows = din("crows", [33, D])
    xs = din("xs", [DB, D])
    ca = [din("ca0", [DB, 128, 512]), din("ca1", [DB, 512, 512]), din("ca2", [DB, 2048, 512])]
    cb = din("cb", [DB, 128, 256])
    ada_w0 = din("ada_w0", [D, 3072])
    ada_w1 = din("ada_w1", [D, 3072])
    ada_kv_w = din("ada_kv_w", [D, 2048])
    brow = din("brow", [1, 8192])
    vecs = din("vecs", [40, 128])
    gpost_rows = din("gpost_rows", [2, D])
    w_in_a = din("w_in_a", [D, 3072])
    w_o_a = din("w_o_a", [768, D])
    w_in_b = din("w_in_b", [D, 2048])
    w_o_b = din("w_o_b", [D, D])
    w_kvx = din("w_kvx", [D, 384])
    sink_fm = din("sink_fm", [128, 8])
    sink_tm = din("sink_tm", [DB, 16])
    ident_d = din("ident", [128, 128])
    rmat_d = din("rmat", [128, 128])
    cosT_d = din("cosT", [128, T])
    sinT_d = din("sinT", [128, T])
    masks_d = din("masks", [2, 128, 512])
    mbias_d = din("mbias", [2, 128, 256])
    cs_tok_d = din("cs_tok", [DB, 64])
    sn_tok_d = din("sn_tok", [DB, 64])
    rowb_d = din("rowb_in", [128, 1])
    gvec_tm = din("gvec_tm", [5, DB, D])
    w_kv_d = din("w_kv", [D, 256])

    yp = dout("yp", [T, D])
    ys = dout("ys", [DB, D])
    aop = [dout("a0p", [128, 512]), dout("a1p", [512, 512]), dout("a2p", [2048, 512])]
    bp = dout("bp", [128, 256])
    aos = [dout("a0s", [DB, 128, 512]), dout("a1s", [DB, 512, 512]), dout("a2s", [DB, 2048, 512])]
    bs = dout("bs", [DB, 128, 256])
    x1s = nc.dram_tensor("x1s", [T, D], F32, kind="Internal").ap()
    g2s = nc.dram_tensor("g2s", [4, 128, T], BF16, kind="Internal").ap()

    with ExitStack() as top:
        P = Prog(nc, top)
        op = P.op

        def stop(tag):
            if os.environ.get("KSTOP") == tag:
                P.flush()
                raise _Stop()

        def sbt(st, name, shape, dt):
            return st.enter_context(nc.sbuf_tensor(name, list(shape), dt))

        def pst(st, name, shape, dt):
            return st.enter_context(nc.psum_tensor(name, list(shape), dt))

        identf = sbt(top, "identf", [128, 128], F32)
        identb = sbt(top, "identb", [128, 128], BF16)
        rmf = sbt(top, "rmf", [128, 128], F32)
        rmb = sbt(top, "rmb", [128, 128], BF16)
        maskb = [sbt(top, "maskb%d" % l, [128, 512], BF16) for l in range(2)]
        mbb = [sbt(top, "mbb%d" % l, [128, 256], BF16) for l in range(2)]
        vecT = sbt(top, "vecT", [128, 40], F32)
        modT = sbt(top, "modT", [128, 64], F32)
        amod = sbt(top, "amod", [128, 24], F32)
        ggbc = [sbt(top, "ggbc%d" % l, [128, D], F32) for l in range(2)]
        esink = sbt(top, "esink", [128, 8], F32)
        ones_b = sbt(top, "ones_b", [128, 64], BF16)
        msq = sbt(top, "msq", [128, 4 * NT], F32)
        msq2 = sbt(top, "msq2", [128, 2 * NT], F32)
        rstd = sbt(top, "rstd", [128, 4 * NT], F32)
        junk = sbt(top, "junk", [128, D], BF16)

        psT = [pst(top, "psT0", [128, 8, 128], BF16)]
        psP = pst(top, "psP", [128, 2, 512], F32)
        psR = pst(top, "psR", [128, 512], F32)
        psS = [pst(top, "psS%d" % i, [128, 512], F32) for i in range(2)]
        psT.append(psR[:].bitcast(BF16).rearrange("p (k t) -> p k t", k=8))
        psOs = [pst(top, "psO%d" % i, [128, 512], F32) for i in range(2)]
        psOf = psOs[0]

        def dma(q, out, in_, reads=(), writes=()):
            return op(q, lambda e, o=out, i=in_: e.dma_start(out=o, in_=i), reads, writes, dma=True)

        def mm(out, lhsT, rhs, start, stop, reads, writes, rg=None):
            return op("pe", lambda e, o=out, l=lhsT, r=rhs, a=start, b=stop: e.matmul(o, lhsT=l, rhs=r, start=a, stop=b),
                      reads, writes, rg=rg)

        def tr(out, in_, ident, reads, writes):
            return op("pe", lambda e, o=out, i=in_, d=ident: e.transpose(out=o, in_=i, identity=d), reads, writes)

        def act(out, in_, func, reads, writes, **kw):
            return op("act", lambda e, o=out, i=in_, f=func, k=kw: e.activation(out=o, in_=i, func=f, **k), reads, writes)

        def cp(eng, out, in_, reads, writes):
            if eng == "act":
                return act(out, in_, AF.Copy, reads, writes)
            return op(eng, lambda e, o=out, i=in_: e.tensor_copy(out=o, in_=i), reads, writes)

        def tt(eng, out, in0, in1, alu, reads, writes):
            return op(eng, lambda e, o=out, a=in0, b=in1, f=alu: e.tensor_tensor(out=o, in0=a, in1=b, op=f), reads, writes)

        def ts(eng, out, in0, s1, s2, op0, op1, reads, writes):
            if s2 is None:
                return op(eng, lambda e, o=out, a=in0, x=s1, f=op0: e.tensor_scalar(out=o, in0=a, scalar1=x, scalar2=None, op0=f),
                          reads, writes)
            return op(eng, lambda e, o=out, a=in0, x=s1, y=s2, f=op0, g=op1:
                      e.tensor_scalar(out=o, in0=a, scalar1=x, scalar2=y, op0=f, op1=g), reads, writes)

        def stt(eng, out, in0, scalar, in1, op0, op1, reads, writes):
            return op(eng, lambda e, o=out, a=in0, s=scalar, b=in1, f=op0, g=op1:
                      e.scalar_tensor_tensor(out=o, in0=a, scalar=s, in1=b, op0=f, op1=g), reads, writes)

        def memset(eng, ap, val, writes):
            return op(eng, lambda e, a=ap, v=val: e.memset(a, v), (), writes)

        cast_rr = [0]

        def load_w(st_slots, dst, src, rows, col0, ncols, dcol0, key):
            nk = rows // 128
            for kc in range(nk):
                for c0 in range(0, ncols, 1024):
                    w = min(1024, ncols - c0)
                    sl = cast_rr[0] % len(st_slots)
                    cast_rr[0] += 1
                    stg = st_slots[sl]
                    dma("sp", stg[:, 0:w], src[kc * 128:(kc + 1) * 128, col0 + c0:col0 + c0 + w], (), [("wst", sl)])
                    eng = ("pool", "dve", "pool", "act")[cast_rr[0] % 4]
                    cp(eng, dst[:, kc, dcol0 + c0:dcol0 + c0 + w], stg[:, 0:w], [("wst", sl)], [key])

        rr = dict(p=0, r=0, s=0, o=0, e=0, w=0)

        def nxt(k, n=2):
            v = rr[k] % n
            rr[k] += 1
            return v

        def evac_eng():
            return ("act", "dve")[nxt("e")]

        AX = mybir.AxisListType

        with ExitStack() as st:
            crow_t = sbt(st, "crow_t", [33, D], F32)
            sc_t = sbt(st, "sc_t", [33, D], F32)
            cT = sbt(st, "cT", [128, 8, 33], F32)
            vec_t = sbt(st, "vec_t", [40, 128], F32)
            ones33 = sbt(st, "ones33", [33, 128], F32)
            adat = [sbt(st, "adat0", [33, 3072], F32), sbt(st, "adat1", [33, 2048], F32)]
            gprow = sbt(st, "gprow", [33, D], F32)
            wstg = [sbt(st, "adaw%d" % i, [128, 3072], F32) for i in range(2)]
            psA = [psS[0], psS[1], psR, psOf, psP[:, 0, :], psP[:, 1, :]]
            sink_t = sbt(st, "sink_t", [128, 8], F32)
            maskf = sbt(st, "maskf", [128, 512], F32)
            xs_t = sbt(st, "xs_t", [DB, D], F32)
            x1_t = sbt(st, "x1_t", [DB, D], F32)
            gv = [sbt(st, "gv%d" % i, [DB, D], F32) for i in range(3)]
            cs_t = sbt(st, "cs_t", [DB, 64], F32)
            sn_t = sbt(st, "sn_t", [DB, 64], F32)
            stm = sbt(st, "stm", [DB, 16], F32)
            rowb = sbt(st, "rowb", [128, 1], F32)
            hs = sbt(st, "hs", [DB, D], F32)
            hs2 = sbt(st, "hs2", [DB, D], F32)
            hsT = sbt(st, "hsT", [128, 8, DB], F32)
            hkTs = sbt(st, "hkTs", [128, 8, DB], F32)
            proj = sbt(st, "proj", [33, 3072], F32)
            brow_t = proj
            kvp = sbt(st, "kvp", [DB, 256], F32)
            rop = sbt(st, "rop", [DB, 1536], F32)
            swp = sbt(st, "swp", [DB, 1536], F32)
            krow = [sbt(st, "krow%d" % i, [DB, 512], F32) for i in range(4)]
            KVflat = sbt(st, "KVflat", [128, DB * 520], F32)
            prod = sbt(st, "prod", [128, 1024], F32)
            scr = sbt(st, "scr", [128, 16], F32)
            PZ = sbt(st, "PZ", [128, DB, 16, DB], F32)
            Oc = sbt(st, "Oc", [DB, 16, 65], F32)
            Ot = sbt(st, "Ot", [DB, 16, 64], F32)
            sml = sbt(st, "sml", [DB, 128], F32)

            dma("sp", identf[:], ident_d[:, :], (), ["identf"])
            dma("sp", rmf[:], rmat_d[:, :], (), ["rmf"])
            dma("sp", crow_t[:], crows[:, :], (), ["crow"])
            dma("sp", vec_t[:], vecs[:, :], (), ["vec_t"])
            dma("sp", sink_t[:], sink_fm[:, :], (), ["sink_t"])
            dma("sp", xs_t[:], xs[:, :], (), ["xs_t"])
            dma("sp", cs_t[:], cs_tok_d[:, :], (), ["cs_t"])
            dma("sp", sn_t[:], sn_tok_d[:, :], (), ["sn_t"])
            dma("sp", stm[:], sink_tm[:, :], (), ["stm"])
            dma("sp", rowb[:], rowb_d[:, :], (), ["rowb"])
            for l in range(2):
                dma("sp", maskf[:], masks_d[l], (), ["maskf"])
                cp("dve", maskb[l][:], maskf[:], ["maskf"], ["maskb%d" % l])
                dma("sp", maskf[:, 0:256], mbias_d[l], ["maskf"], ["maskf"])
                cp("dve", mbb[l][:], maskf[:, 0:256], ["maskf"], ["mbb%d" % l])
            cp("dve", identb[:], identf[:], ["identf"], ["identb"])
            cp("dve", rmb[:], rmf[:], ["rmf"], ["rmb"])
            memset("dve", ones_b[:], 1.0, ["ones_b"])
            memset("dve", ones33[:], 1.0, ["ones33"])
            memset("pool", PZ[:], 0.0, ["PZ"])
            act(esink[:], sink_t[:], AF.Exp, ["sink_t"], ["esink"])
            act(stm[:], stm[:], AF.Exp, ["stm"], ["stm"])
            tr(psOf[:, 0:40], vec_t[:, :], identf[0:40, 0:40], ["vec_t", "identf"], [("psA", 3)])
            cp("dve", vecT[:], psOf[:, 0:40], [("psA", 3)], ["vecT"])
            act(sc_t[:], crow_t[:], AF.Silu, ["crow"], ["sc_t"])
            for kc in range(8):
                tr(psOf[:, 64 + kc * 33:64 + (kc + 1) * 33], sc_t[:, kc * 128:(kc + 1) * 128], identf[0:33, 0:33],
                   ["sc_t", "identf"], [("psA", 3)])
            cp("dve", cT[:].rearrange("p k t -> p (k t)"), psOf[:, 64:64 + 8 * 33], [("psA", 3)], ["cT"])

            def stream_mm(lhsT_of, lkey, M, wd, nk, ncols, dst, dkey, bias_off=None):
                nb = (ncols + 511) // 512
                if bias_off is not None:
                    dma("sp", brow_t[32:33, 0:ncols], brow[:, bias_off:bias_off + ncols], (), ["brow_t"])
                for kc in range(nk):
                    sl = nxt("w")
                    dma("sp", wstg[sl][:, 0:ncols], wd[kc * 128:(kc + 1) * 128, 0:ncols], (), [("adaw", sl)])
                    for b in range(nb):
                        w = min(512, ncols - b * 512)
                        mm(psA[b][0:M, 0:w], lhsT_of(kc), wstg[sl][:, b * 512:b * 512 + w], kc == 0,
                           (kc == nk - 1) and bias_off is None, [lkey, ("adaw", sl)], [("psA", b)])
                for b in range(nb):
                    w = min(512, ncols - b * 512)
                    if bias_off is not None:
                        mm(psA[b][0:M, 0:w], ones33[32:33, 0:M], brow_t[32:33, b * 512:b * 512 + w], False, True,
                           ["ones33", "brow_t"], [("psA", b)])
                    cp("dve" if b % 2 else "act", dst[0:M, b * 512:b * 512 + w], psA[b][0:M, 0:w], [("psA", b)], [(dkey, b)])
                return [(dkey, b) for b in range(nb)]

            def prompt_cols(ad, akeys, ncols, mcol):
                nch = ncols // 128
                for j in range(nch):
                    mm(psS[0][:, j:j + 1], ad[32:33, j * 128:(j + 1) * 128], ones33[32:33, 0:1], True, True,
                       akeys + ["ones33"], [("psA", 0)])
                cp("dve", modT[:, mcol:mcol + nch], psS[0][:, 0:nch], [("psA", 0)], ["modT"])

            def gate_bcast(ad, akeys, li):
                dma("sp", gprow[32:33, :], gpost_rows[li:li + 1, :], (), ["gprow"])
                tt("dve", gprow[32:33, :], gprow[32:33, :], ad[32:33, 2048:3072], ALU.mult, akeys + ["gprow"], ["gprow"])
                for b in range(2):
                    mm(psS[1][:], ones33[32:33, :], gprow[32:33, b * 512:(b + 1) * 512], True, True,
                       ["ones33", "gprow"], [("psA", 1)])
                    cp("act", ggbc[li][:, b * 512:(b + 1) * 512], psS[1][:], [("psA", 1)], ["ggbc%d" % li])

            scol = [100]

            def rms_tok(x_t, xkeys):
                c = scol[0]
                scol[0] += 1
                act(junk[0:DB, :], x_t[:], AF.Square, xkeys, [("msq", c)], scale=1.0 / 32, accum_out=msq[0:DB, c:c + 1])
                act(rstd[0:DB, c:c + 1], msq[0:DB, c:c + 1], AF.Ln, [("msq", c)], [("rstd", c)], bias=EPS)
                act(rstd[0:DB, c:c + 1], rstd[0:DB, c:c + 1], AF.Exp, [("rstd", c)], [("rstd", c)], scale=-0.5)
                return rstd[0:DB, c:c + 1], ("rstd", c)

            def modulate_tok(dst, dkey, x_t, xkeys, rs, rkey, g_t, gkey, scale_ap, shift_ap, akeys):
                ts("dve", dst[:], x_t[:], rs, None, ALU.mult, None, xkeys + [rkey], [dkey])
                tt("dve", dst[:], dst[:], g_t[:], ALU.mult, [dkey, gkey], [dkey])
                stt("dve", dst[:], scale_ap, 1.0, dst[:], ALU.add, ALU.mult, akeys + [dkey], [dkey])
                tt("dve", dst[:], dst[:], shift_ap, ALU.add, akeys + [dkey], [dkey])

            def transpose_tok(dstT, dkey, src, skeys, nk):
                for kc in range(nk):
                    tr(psOf[:, kc * DB:(kc + 1) * DB], src[0:DB, kc * 128:(kc + 1) * 128], identf[0:DB, 0:DB], skeys, [("psA", 3)])
                cp("dve", dstT[:, 0:nk, :].rearrange("p k t -> p (k t)"), psOf[:, 0:nk * DB], [("psA", 3)], [dkey])

            def rope_tok(dst, dkey, src, skeys, nh, sw):
                X = src.rearrange("p (h d) -> p h d", d=64)
                R = dst.rearrange("p (h d) -> p h d", d=64)
                S = sw.rearrange("p (h d) -> p h d", d=64)
                bc = lambda a: a.unsqueeze(1).to_broadcast([DB, nh, a.shape[1]])
                tt("dve", R, X, bc(cs_t[:, :]), ALU.mult, skeys + ["cs_t"], [dkey])
                tt("pool", S[:, :, 0:32], X[:, :, 32:64], bc(sn_t[:, 0:32]), ALU.mult, skeys + ["sn_t"], ["swp"])
                tt("pool", S[:, :, 32:64], X[:, :, 0:32], bc(sn_t[:, 32:64]), ALU.mult, skeys + ["sn_t"], ["swp"])
                tt("dve", R, R, S, ALU.add, [dkey, "swp"], [dkey])

            def pv_slot(hd):
                bk = hd // 7
                t = (psP[:, 0, :], psP[:, 1, :], psR)[bk]
                return t[0:DB, (hd % 7) * 65:(hd % 7) * 65 + 65], ("psA", (4, 5, 2)[bk])

            k0 = stream_mm(lambda kc: cT[:, kc, :], "cT", 33, ada_w0, 8, 3072, adat[0], "ad0", bias_off=0)
            prompt_cols(adat[0], k0, 3072, 0)
            gate_bcast(adat[0], k0, 0)

            dma("sp", gv[0][:], gvec_tm[0], (), [("gv", 0)])
            dma("sp", gv[1][:], gvec_tm[2], (), [("gv", 1)])
            rs, rk = rms_tok(xs_t, ["xs_t"])
            modulate_tok(hs, "hs", xs_t, ["xs_t"], rs, rk, gv[0], ("gv", 0), adat[0][0:DB, 1024:2048], adat[0][0:DB, 0:1024], k0)
            transpose_tok(hsT, "hsT", hs, ["hs"], 8)
            pk = stream_mm(lambda kc: hsT[:, kc, :], "hsT", DB, w_in_a, 8, 3072, proj, "proj")
            rope_tok(rop[:, 0:1536], "rop", proj[0:DB, 0:1536], pk, 24, swp[:, 0:1536])
            for g, L in enumerate((128, 512, 2048)):
                cp("pool", krow[g][:, 0:256], rop[:, 768 + g * 256:768 + (g + 1) * 256], ["rop"], [("krow", g)])
                cp("pool", krow[g][:, 256:512], proj[0:DB, 1536 + g * 256:1536 + (g + 1) * 256], pk, [("krow", g)])
                dma("sp", aos[g][:, L - 1, :], krow[g][:], [("krow", g)], ())
            KV0 = KVflat[:].rearrange("p (b t h d) -> p b t h d", b=DB, t=2, h=4)
            memset("pool", KV0[:, :, 1, :, 64:65], 1.0, ["KVones"])
            for g, (L, dil) in enumerate(((128, 1), (512, 4), (2048, 16))):
                for b in range(DB):
                    dma("sp", KV0[:, b, :, :, 0:64], ca[g][b, 0:L:dil, :].rearrange("j (t h d) -> j t h d", t=2, h=4),
                        ["KVones"], [("KV", b)])
                for b in range(DB):
                    sl = b % 2
                    mm(psS[sl][:, 0:256], identf[0:DB, b:b + 1].to_broadcast([DB, 128]), rop[0:DB, g * 256:(g + 1) * 256], True, True,
                       ["rop", "identf"], [("psA", sl)])
                    p3 = prod[:, 0:256].rearrange("p (h d) -> p h d", h=4)
                    tt("dve", p3, KV0[:, b, 0, :, 0:64], psS[sl][:, 0:256].rearrange("p (h d) -> p h d", h=4), ALU.mult,
                       [("KV", b), ("psA", sl)], ["prod"])
                    op("dve", lambda e, o=scr[:, 0:4], i_=p3: e.tensor_reduce(out=o, in_=i_, axis=AX.X, op=ALU.add), ["prod"], ["scr"])
                    act(PZ[:, b, g * 4:(g + 1) * 4, b], scr[:, 0:4], AF.Exp, ["scr", "PZ"], [("PZ", b)], scale=0.125)
                for h in range(4):
                    hd = g * 4 + h
                    o, okey = pv_slot(hd)
                    for b in range(DB):
                        mm(o, PZ[:, b, hd, :], KV0[:, b, 1, h, :], b == 0, b == DB - 1, [("PZ", b), ("KV", b), "KVones"], [okey])
            cp("dve", Oc[:, 0:7, :], psP[0:DB, 0, 0:455].rearrange("p (h d) -> p h d", d=65), [("psA", 4)], ["Oc"])
            cp("dve", Oc[:, 7:12, :], psP[0:DB, 1, 0:325].rearrange("p (h d) -> p h d", d=65), [("psA", 5)], ["Oc"])

            def finish_attn(nh, qv, kvw, vv, keys):
                q3 = qv.rearrange("p (h d) -> p h d", d=64)
                pr = prod[0:DB, 0:nh * 64].rearrange("p (h d) -> p h d", d=64)
                tt("dve", pr, q3, kvw, ALU.mult, keys, ["prod"])
                op("dve", lambda e, o=sml[:, 32:32 + nh], i_=pr: e.tensor_reduce(out=o, in_=i_, axis=AX.X, op=ALU.add), ["prod"], ["sml"])
                act(sml[:, 32:32 + nh], sml[:, 32:32 + nh], AF.Exp, ["sml"], ["sml"], scale=0.125)
                tt("dve", pr, vv, sml[:, 32:32 + nh].unsqueeze(2).to_broadcast([DB, nh, 64]), ALU.mult, keys + ["sml", "prod"], ["prod"])
                tt("dve", Ot[:, 0:nh, :], Oc[:, 0:nh, 0:64], pr, ALU.add, ["Oc", "prod"], ["Ot"])
                tt("dve", sml[:, 0:nh], Oc[:, 0:nh, 64], sml[:, 32:32 + nh], ALU.add, ["Oc", "sml"], ["sml"])

            k3 = rop[:, 768:1536].rearrange("p (h d) -> p h d", d=64)
            v3 = proj[0:DB, 1536:2304].rearrange("p (h d) -> p h d", d=64)
            finish_attn(12, rop[:, 0:768], k3, v3, ["rop"] + pk)
            tt("dve", sml[:, 64:68], sml[:, 0:4], sml[:, 4:8], ALU.add, ["sml"], ["sml"])
            tt("dve", sml[:, 64:68], sml[:, 64:68], sml[:, 8:12], ALU.add, ["sml"], ["sml"])
            op("dve", lambda e, o=sml[:, 64:68], i_=sml[:, 64:68]: e.reciprocal(out=o, in_=i_), ["sml"], ["sml"])
            og = hs2[:, 0:768]
            for g in range(3):
                tt("dve", og[:, g * 256:(g + 1) * 256].rearrange("p (h d) -> p h d", d=64), Ot[:, g * 4:(g + 1) * 4, :],
                   sml[:, 64:68].unsqueeze(2).to_broadcast([DB, 4, 64]), ALU.mult, ["Ot", "sml"], ["og"])
            act(hs[:, 0:768], proj[0:DB, 2304:3072], AF.Silu, pk + ["hs"], ["hs"])
            tt("dve", og, og, hs[:, 0:768], ALU.mult, ["og", "hs"], ["og"])
            transpose_tok(hsT, "hsT", og, ["og"], 6)
            mk = stream_mm(lambda kc: hsT[:, kc, :], "hsT", DB, w_o_a, 6, 1024, hs, "mix")
            rs, rk = rms_tok(hs, mk)
            ts("dve", hs[:], hs[:], rs, None, ALU.mult, None, mk + [rk], ["mixn"])
            tt("dve", hs[:], hs[:], gv[1][:], ALU.mult, ["mixn", ("gv", 1)], ["mixn"])
            tt("dve", hs[:], hs[:], adat[0][0:DB, 2048:3072], ALU.mult, ["mixn"] + k0, ["mixn"])
            tt("dve", x1_t[:], hs[:], xs_t[:], ALU.add, ["mixn", "xs_t"], ["x1_t"])

            k1 = stream_mm(lambda kc: cT[:, kc, :], "cT", 33, ada_w1, 8, 3072, adat[0], "ad0", bias_off=3072)
            prompt_cols(adat[0], k1, 3072, 24)
            gate_bcast(adat[0], k1, 1)
            kk = stream_mm(lambda kc: cT[:, kc, :], "cT", 33, ada_kv_w, 8, 2048, adat[1], "ad1", bias_off=6144)
            prompt_cols(adat[1], kk, 2048, 48)
            for (dst, sc, gvv) in ((amod[:, 0:8], modT[:, 8:16], vecT[:, 0:8]), (amod[:, 8:16], modT[:, 32:40], vecT[:, 8:16]),
                                   (amod[:, 16:24], modT[:, 56:64], vecT[:, 32:40])):
                ts("dve", dst, sc, 1.0, None, ALU.add, None, ["modT"], ["amod"])
                tt("dve", dst, dst, gvv, ALU.mult, ["amod", "vecT"], ["amod"])

            dma("sp", gv[0][:], gvec_tm[1], (), [("gv", 0)])
            dma("sp", gv[1][:], gvec_tm[3], (), [("gv", 1)])
            dma("sp", gv[2][:], gvec_tm[4], (), [("gv", 2)])
            rs, rk = rms_tok(x1_t, ["x1_t"])
            modulate_tok(hs, "hs", x1_t, ["x1_t"], rs, rk, gv[0], ("gv", 0), adat[0][0:DB, 1024:2048], adat[0][0:DB, 0:1024], k1)
            modulate_tok(hs2, "hs2", x1_t, ["x1_t"], rs, rk, gv[2], ("gv", 2), adat[1][0:DB, 1024:2048], adat[1][0:DB, 0:1024], kk)
            transpose_tok(hsT, "hsT", hs, ["hs"], 8)
            transpose_tok(hkTs, "hkTs", hs2, ["hs2"], 8)
            kvk = stream_mm(lambda kc: hkTs[:, kc, :], "hkTs", DB, w_kv_d, 8, 256, kvp, "kvp")
            pk = stream_mm(lambda kc: hsT[:, kc, :], "hsT", DB, w_in_b, 8, 2048, proj, "proj")
            rope_tok(rop[:, 0:1024], "rop", proj[0:DB, 0:1024], pk, 16, swp[:, 0:1024])
            rope_tok(rop[:, 1024:1152], "ropk", kvp[:, 0:128], kvk, 2, swp[:, 1024:1152])
            cp("pool", krow[3][:, 0:128], rop[:, 1024:1152], ["ropk"], [("krow", 3)])
            cp("pool", krow[3][:, 128:256], kvp[:, 128:256], kvk, [("krow", 3)])
            dma("sp", bs[:, 127, :], krow[3][:, 0:256], [("krow", 3)], ())
            KV1 = KVflat[:, 0:DB * 260].rearrange("p (b t h d) -> p b t h d", b=DB, t=2, h=2)
            allkv = [("KV", b) for b in range(DB)]
            memset("pool", KV1[:, :, 1, :, 64:65], 1.0, allkv + ["KVones"])
            for b in range(DB):
                dma("sp", KV1[:, b, :, :, 0:64], cb[b].rearrange("j (t h d) -> j t h d", t=2, h=2), ["KVones"], [("KV", b)])
            for b in range(DB):
                for kh in range(2):
                    mm(psS[kh][:, :], identf[0:DB, b:b + 1].to_broadcast([DB, 128]), rop[0:DB, kh * 512:(kh + 1) * 512], True, True,
                       ["rop", "identf"], [("psA", kh)])
                    tt("dve", prod[:, kh * 512:(kh + 1) * 512].rearrange("p (h d) -> p h d", d=64),
                       psS[kh][:, :].rearrange("p (h d) -> p h d", d=64),
                       KV1[:, b, 0, kh, 0:64].unsqueeze(1).to_broadcast([128, 8, 64]), ALU.mult, [("KV", b), ("psA", kh)], ["prod"])
                op("dve", lambda e, o=scr[:, 0:16], i_=prod[:, :].rearrange("p (h d) -> p h d", d=64):
                   e.tensor_reduce(out=o, in_=i_, axis=AX.X, op=ALU.add), ["prod"], ["scr"])
                act(PZ[:, b, :, b], scr[:, 0:16], AF.Exp, ["scr", "PZ", "rowb"], [("PZ", b)], scale=0.125, bias=rowb[:, 0:1])
            for hd in range(16):
                o, okey = pv_slot(hd)
                for b in range(DB):
                    mm(o, PZ[:, b, hd, :], KV1[:, b, 1, hd // 8, :], b == 0, b == DB - 1, [("PZ", b), ("KV", b), "KVones"], [okey])
            cp("dve", Oc[:, 0:7, :], psP[0:DB, 0, 0:455].rearrange("p (h d) -> p h d", d=65), [("psA", 4)], ["Oc"])
            cp("dve", Oc[:, 7:14, :], psP[0:DB, 1, 0:455].rearrange("p (h d) -> p h d", d=65), [("psA", 5)], ["Oc"])
            cp("dve", Oc[:, 14:16, :], psR[0:DB, 0:130].rearrange("p (h d) -> p h d", d=65), [("psA", 2)], ["Oc"])
            kb3 = rop[:, 1024:1152].rearrange("p (k d) -> p k d", d=64)
            vb3 = kvp[:, 128:256].rearrange("p (k d) -> p k d", d=64)
            q4 = rop[:, 0:1024]
            for kh in range(2):
                pass
            pr = prod[0:DB, 0:1024].rearrange("p (h d) -> p h d", d=64)
            for kh in range(2):
                tt("dve", pr[:, kh * 8:(kh + 1) * 8, :], q4.rearrange("p (h d) -> p h d", d=64)[:, kh * 8:(kh + 1) * 8, :],
                   kb3[:, kh, :].unsqueeze(1).to_broadcast([DB, 8, 64]), ALU.mult, ["rop", "ropk", "prod"], ["prod"])
            op("dve", lambda e, o=sml[:, 32:48], i_=pr: e.tensor_reduce(out=o, in_=i_, axis=AX.X, op=ALU.add), ["prod"], ["sml"])
            act(sml[:, 32:48], sml[:, 32:48], AF.Exp, ["sml"], ["sml"], scale=0.125)
            for kh in range(2):
                tt("dve", pr[:, kh * 8:(kh + 1) * 8, :], sml[:, 32 + kh * 8:32 + (kh + 1) * 8].unsqueeze(2).to_broadcast([DB, 8, 64]),
                   vb3[:, kh, :].unsqueeze(1).to_broadcast([DB, 8, 64]), ALU.mult, kvk + ["sml", "prod"], ["prod"])
            tt("dve", Ot[:, 0:16, :], Oc[:, 0:16, 0:64], pr, ALU.add, ["Oc", "prod"], ["Ot"])
            tt("dve", sml[:, 0:16], Oc[:, 0:16, 64], sml[:, 32:48], ALU.add, ["Oc", "sml"], ["sml"])
            tt("dve", sml[:, 0:16], sml[:, 0:16], stm[:, :], ALU.add, ["sml", "stm"], ["sml"])
            op("dve", lambda e, o=sml[:, 0:16], i_=sml[:, 0:16]: e.reciprocal(out=o, in_=i_), ["sml"], ["sml"])
            og = hs2[:, 0:1024]
            tt("dve", og.rearrange("p (h d) -> p h d", d=64), Ot[:, 0:16, :], sml[:, 0:16].unsqueeze(2).to_broadcast([DB, 16, 64]),
               ALU.mult, ["Ot", "sml", "hs2"], ["og"])
            act(hs[:, :], proj[0:DB, 1024:2048], AF.Silu, pk + ["hs"], ["hs"])
            tt("dve", og, og, hs[:, :], ALU.mult, ["og", "hs"], ["og"])
            transpose_tok(hsT, "hsT", og, ["og"], 8)
            mk = stream_mm(lambda kc: hsT[:, kc, :], "hsT", DB, w_o_b, 8, 1024, hs, "mix")
            rs, rk = rms_tok(hs, mk)
            ts("dve", hs[:], hs[:], rs, None, ALU.mult, None, mk + [rk], ["mixn"])
            tt("dve", hs[:], hs[:], gv[1][:], ALU.mult, ["mixn", ("gv", 1)], ["mixn"])
            tt("dve", hs[:], hs[:], adat[0][0:DB, 2048:3072], ALU.mult, ["mixn"] + k1, ["mixn"])
            tt("dve", hs[:], hs[:], x1_t[:], ALU.add, ["mixn", "x1_t"], ["ys_t"])
            dma("sp", ys[:, :], hs[:], ["ys_t"], ())
            P.flush()
        if os.environ.get("KSTOP") == "setup":
            return nc

        A0 = amod[:, 0:8]
        SH0 = modT[:, 0:8]
        A1 = amod[:, 8:16]
        SH1 = modT[:, 24:32]
        AKV = amod[:, 16:24]
        SHKV = modT[:, 48:56]

        def load_x(src, srckey, i, xin):
            sl = i % len(xin)
            dma("sp", xin[sl][:], src[i * 128:(i + 1) * 128, :], [(srckey, i)] if srckey else (), [("xin", sl)])

        def norm_stats(tiles, xin, col0):
            for i in tiles:
                sl = i % len(xin)
                c = col0 + i
                act(junk[:], xin[sl][:], AF.Square, [("xin", sl)], [("msq", c)], scale=1.0 / 32, accum_out=msq[:, c:c + 1])
            c0 = col0 + tiles[0]
            n = len(tiles)
            mk = [("msq", col0 + i) for i in tiles]
            rk = [("rstd", col0 + i) for i in tiles]
            act(rstd[:, c0:c0 + n], msq[:, c0:c0 + n], AF.Ln, mk, rk, bias=EPS)
            act(rstd[:, c0:c0 + n], rstd[:, c0:c0 + n], AF.Exp, rk, rk, scale=-0.5)

        def norm_tile(i, u, xin, xn, col0, outs, hslot, stats=True):
            sl = i % len(xin)
            xsl = i % len(xn)
            c = col0 + i
            if stats:
                norm_stats([i], xin, col0)
            ts("dve", xn[xsl][:], xin[sl][:], rstd[:, c:c + 1], None, ALU.mult, None, [("xin", sl), ("rstd", c)], [("xn", xsl)])
            pt = psT[i % 2]
            ptk = ("psT", 0) if i % 2 == 0 else "psR"
            for kc in range(8):
                tr(pt[:, kc, :], xn[xsl][:, kc * 128:(kc + 1) * 128], identb[:], [("xn", xsl)], [ptk])
            for (hT, hname, Am, Sm) in outs:
                for kc in range(8):
                    o = hT[:, kc, u * 128:(u + 1) * 128]
                    if i % 2 == 0:
                        ts("dve", o, pt[:, kc, :], Am[:, kc:kc + 1], Sm[:, kc:kc + 1], ALU.mult, ALU.add,
                           [ptk], [(hname, hslot, u, kc)])
                    else:
                        act(o, pt[:, kc, :], AF.Identity, [ptk], [(hname, hslot, u, kc)],
                            scale=Am[:, kc:kc + 1], bias=Sm[:, kc:kc + 1])

        def project(hT, hname, hslot, W, cols, nk=8):
            slot = nxt("p")
            for kc in range(nk):
                mm(psP[:, slot, :], W[:, kc, cols], hT[:, kc, :], kc == 0, kc == nk - 1,
                   [(hname, hslot, u, kc) for u in range(4)], [("psP", slot)])
            return slot

        def rope(slot, ctab, stab, tkey, qraw, t12, dst, dkey, f32dst=None, fkey=None):
            r = nxt("r")
            cp("act", qraw[r][:], psP[:, slot, :], [("psP", slot)], [("qraw", r)])
            mm(psR[:], rmb[:], qraw[r][:], True, True, [("qraw", r)], ["psR"])
            tt("pool", t12[:, r, :], qraw[r][:], ctab, ALU.mult, [("qraw", r), tkey], [("t1", r)])
            tt("dve", t12[:, 2 + r, :], psR[:], stab, ALU.mult, ["psR", tkey], [("t2", r)])
            tt("pool", dst, t12[:, r, :], t12[:, 2 + r, :], ALU.add, [("t1", r), ("t2", r)], [dkey])
            if f32dst is not None:
                tt("pool", f32dst, t12[:, r, :], t12[:, 2 + r, :], ALU.add, [("t1", r), ("t2", r)], [fkey])

        def cache_out(kvf, kkeys, u, ncol, stage, dst_rows, skey):
            n = len(kvf)
            for k4 in range(n):
                tr(psR[:, k4 * 128:(k4 + 1) * 128], kvf[k4][:, u * 128:(u + 1) * 128], identf[:], [kkeys[k4]], ["psR"])
            sl = nxt("o")
            cp(evac_eng(), stage[sl][:, 0:ncol], psR[:, 0:ncol], ["psR"], [(skey, sl)])
            dma("sp", dst_rows, stage[sl][:, 0:ncol], [(skey, sl)], ())

        pending = []

        def attn_back(l, heads, pt, sl, has_prev, rv):
            osl = nxt("o")
            psO = psOs[osl]
            for hh, h in enumerate(heads):
                rows = slice(hh * 64, (hh + 1) * 64)
                pp = pt[:, hh * 256:hh * 256 + 128]
                pc = pt[:, hh * 256 + 128:hh * 256 + 256]
                for (o, vp, vc) in ((psO[rows, 0:128], h["vp"], h["vc"]), (psO[rows, 128:256], ones_b[:, :], ones_b[:, :])):
                    if has_prev:
                        mm(o, vp, pp, True, False, rv + [("PT", sl)], [("psO", osl)])
                    mm(o, vc, pc, not has_prev, True, rv + [("PT", sl)], [("psO", osl)])
            eng = evac_eng()
            for (dkey, okey, cols) in (("od", "ok", slice(0, 128)), ("dd", "dk", slice(128, 256))):
                for hh, h in enumerate(heads):
                    pass
                d0, d1 = heads[0][dkey], heads[1][dkey]
                src = psO[:, cols]
                dst = heads[0][dkey + "_full"]
                if len(dst.shape) == 3:
                    src = src.rearrange("p (a b) -> p a b", a=dst.shape[1])
                cp(eng, dst, src, [("psO", osl)], [heads[0][okey], heads[1][okey]])

        def attn_flush():
            while pending:
                attn_back(*pending.pop(0))

        def attn_pair(l, heads, PT, rq, rk, rv):
            sl = nxt("s")
            ps = psS[sl]
            pt = PT[sl]
            has_prev = heads[0]["kp"] is not None
            for hh, h in enumerate(heads):
                if has_prev:
                    mm(ps[:, hh * 256:hh * 256 + 256], identb[:], mbb[l][:], True, False, (), [("psS", sl)])
                    mm(ps[:, hh * 256:hh * 256 + 128], h["kp"], h["q"], False, False, rq + rk, [("psS", sl)])
                    mm(ps[:, hh * 256 + 128:hh * 256 + 256], h["kc"], h["q"], False, True, rq + rk, [("psS", sl)])
                else:
                    mm(ps[:, hh * 256 + 128:hh * 256 + 256], identb[:], mbb[l][:, 128:256], True, False, (), [("psS", sl)])
                    mm(ps[:, hh * 256 + 128:hh * 256 + 256], h["kc"], h["q"], False, True, rq + rk, [("psS", sl)])
            if has_prev:
                act(pt[:], ps[:], AF.Exp, [("psS", sl)], [("PT", sl)], scale=0.125)
            else:
                v = lambda a: a.rearrange("p (h x) -> p h x", h=2)[:, :, 128:256]
                act(v(pt[:]), v(ps[:]), AF.Exp, [("psS", sl)], [("PT", sl)], scale=0.125)
            attn_flush()
            pending.append((l, heads, pt, sl, has_prev, rv))

        T12K = [("t1", 0), ("t1", 1), ("t2", 0), ("t2", 1)]

        def out_proj_tile(i, u, oTf, okeys, Wo, nk, ggl, xres, col0, dst_dram, dkey, t12):
            if i % 2 == 0:
                banks = [(psP[:, 0, :], ("psP", 0)), (psP[:, 1, :], ("psP", 1))]
            else:
                banks = [(psS[0][:, :], ("psS", 0)), (psS[1][:, :], ("psS", 1))]
            for nb in range(2):
                for kc in range(nk):
                    mm(banks[nb][0], oTf[kc][:, u * 128:(u + 1) * 128], Wo[:, kc, nb * 512:(nb + 1) * 512], kc == 0, kc == nk - 1,
                       [okeys[kc]], [banks[nb][1]])
            c = col0 + i
            for nb in range(2):
                act(junk[:, 0:512], banks[nb][0], AF.Square, [banks[nb][1]], [("msq", c, nb)], scale=1.0 / 32,
                    accum_out=msq2[:, 2 * i + nb:2 * i + nb + 1])
            tt("dve", msq[:, c:c + 1], msq2[:, 2 * i:2 * i + 1], msq2[:, 2 * i + 1:2 * i + 2], ALU.add, [("msq", c, 0), ("msq", c, 1)], [("msq", c)])
            act(rstd[:, c:c + 1], msq[:, c:c + 1], AF.Ln, [("msq", c)], [("rstd", c)], bias=EPS)
            act(rstd[:, c:c + 1], rstd[:, c:c + 1], AF.Exp, [("rstd", c)], [("rstd", c)], scale=-0.5)
            r = i % 2
            for nb in range(2):
                stt("dve", t12[:, 2 * r + nb, :], banks[nb][0], rstd[:, c:c + 1], ggl[:, nb * 512:(nb + 1) * 512], ALU.mult, ALU.mult,
                    [banks[nb][1], ("rstd", c)], [T12K[2 * r + nb]])
            sl = i % len(xres)
            tm = t12[:, 2 * r:2 * r + 2, :].rearrange("p a b -> p (a b)")
            tt("pool", xres[sl][:], tm, xres[sl][:], ALU.add, [T12K[2 * r], T12K[2 * r + 1], ("xres", sl)], [("xres", sl)])
            dma("sp", dst_dram[i * 128:(i + 1) * 128, :], xres[sl][:], [("xres", sl)], [(dkey, i)] if dkey else ())

        bg = []
        for b in range(DB):
            for q8 in range(8):
                lo, hi = 256 * q8, min(256 * (q8 + 1), 2047)
                bg.append((aos[2][b, lo:hi, :], ca[2][b, lo + 1:hi + 1, :]))
            if b % 2 == 1:
                for bb in (b - 1, b):
                    bg.append((aos[1][bb, 0:256, :], ca[1][bb, 1:257, :]))
                    bg.append((aos[1][bb, 256:511, :], ca[1][bb, 257:512, :]))
        bg.append((aos[0][:, 0:127, :], ca[0][:, 1:128, :]))
        bg.append((bs[:, 0:127, :], cb[:, 1:128, :]))

        def bg_issue(n):
            for _ in range(n):
                if bg:
                    d, s_ = bg.pop(0)
                    dma("act", d, s_, (), ())

        with ExitStack() as st:
            Wa2 = sbt(st, "Wa2", [128, 8, 768], BF16)
            wst = [sbt(st, "wstA%d" % i, [128, 1024], F32) for i in range(2)]
            xin = [sbt(st, "xinA%d" % i, [128, D], F32) for i in range(4)]
            xn = [sbt(st, "xnA%d" % i, [128, D], BF16) for i in range(2)]
            hT = [sbt(st, "hTA%d" % i, [128, 8, 512], BF16) for i in range(2)]
            QT2 = [sbt(st, "QT2_%d" % c, [128, T], BF16) for c in range(2)]
            KT2 = [sbt(st, "KT2_%d" % c, [128, T], BF16) for c in range(2)]
            VT2 = [sbt(st, "VT2_%d" % c, [128, T], BF16) for c in range(2)]
            ctab = [sbt(st, "ctabA%d" % i, [128, 512], F32) for i in range(2)]
            stab = [sbt(st, "stabA%d" % i, [128, 512], F32) for i in range(2)]
            qraw = [sbt(st, "qrawA%d" % i, [128, 512], BF16) for i in range(2)]
            t12 = sbt(st, "t12A", [128, 4, 512], F32)
            kvf = [sbt(st, "kvfA%d" % i, [128, 512], F32) for i in range(4)]
            stage = [sbt(st, "stageA%d" % i, [128, 512], F32) for i in range(2)]
            PT = [sbt(st, "PTA%d" % i, [128, 512], BF16) for i in range(2)]
            VA = [sbt(st, "VAA%d" % i, [128, 4, 128], BF16) for i in range(4)]
            G2st = [sbt(st, "G2st%d" % k, [128, T], BF16) for k in range(4)]
            load_w(wst, Wa2, w_in_a, 1024, 512, 256, 0, "w")
            load_w(wst, Wa2, w_in_a, 1024, 768 + 512, 256, 256, "w")
            load_w(wst, Wa2, w_in_a, 1024, 1536 + 512, 256, 512, "w")
            for v in VA:
                memset("pool", v[:, :, 64:128], 1.0, [("w", "va")])
            P.flush()
            print("[phaseA] sbuf remaining", nc.sbuf_bytes_remaining)
            stop("A0")
            for u in range(4):
                load_x(xp, None, u, xin)
            for s in range(NS):
                hs = s % 2
                tk = slice(512 * s, 512 * (s + 1))
                dma("sp", ctab[hs][:], cosT_d[:, tk], (), [("tab", hs)])
                dma("sp", stab[hs][:], sinT_d[:, tk], (), [("tab", hs)])
                norm_stats([4 * s + u for u in range(4)], xin, 0)
                for u in range(4):
                    norm_tile(4 * s + u, u, xin, xn, 0, [(hT[hs], "hT", A0, SH0)], hs, stats=False)
                stop("A1")
                if s + 1 < NS:
                    for u in range(4):
                        load_x(xp, None, 4 * (s + 1) + u, xin)
                bg_issue(2)
                for (kind, cc, wc) in (("k", 0, 256), ("k", 1, 384), ("v", 0, 512), ("v", 1, 640), ("q", 0, 0), ("q", 1, 128)):
                    slot = project(hT[hs], "hT", hs, Wa2, slice(wc, wc + 128))
                    if kind == "k":
                        rope(slot, ctab[hs][:], stab[hs][:], ("tab", hs), qraw, t12, KT2[cc][:, tk], ("KT2", cc, s),
                             kvf[cc][:] if s >= 4 else None, ("kvf", cc))
                    elif kind == "q":
                        rope(slot, ctab[hs][:], stab[hs][:], ("tab", hs), qraw, t12, QT2[cc][:, tk], ("QT2", cc, s))
                    else:
                        cp("act", VT2[cc][:, tk], psP[:, slot, :], [("psP", slot)], [("VT2", cc, s)])
                        if s >= 4:
                            cp("dve", kvf[2 + cc][:], psP[:, slot, :], [("psP", slot)], [("kvf", 2 + cc)])
                stop("A2")
                bg_issue(2)
                if s >= 4:
                    for u in range(4):
                        i = 4 * s + u
                        cache_out(kvf, [("kvf", k) for k in range(4)], u, 512, stage, aop[2][(i - 16) * 128:(i - 15) * 128, :], "stage")
            stop("A3")
            allq = [[("QT2", c, s) for s in range(NS)] for c in range(2)]
            allk = [[("KT2", c, s) for s in range(NS)] for c in range(2)]
            allv = [[("VT2", c, s) for s in range(NS)] for c in range(2)]
            for r in range(16):
                for j in range(2):
                    tq = slice(2048 * j + r, 2048 * (j + 1), 16)
                    tp = slice(2048 * (j - 1) + r, 2048 * j, 16)
                    b = 2 * r + j
                    vsl = b % 4
                    ptv = psT[0]
                    for c in range(2):
                        tr(ptv[:, c, :], VT2[c][:, tq], identb[:], allv[c], [("psT", 0)])
                    cp(evac_eng(), VA[vsl][:, :, 0:64], ptv[:, 0:2, :].rearrange("p c (h d) -> p (c h) d", h=2),
                       [("psT", 0)], [("VA", vsl)])
                    for c in range(2):
                        heads = []
                        for hh in range(2):
                            h = 2 * c + hh
                            rows = slice(hh * 64, (hh + 1) * 64)
                            od = G2st[c][rows, :].rearrange("p (s r m) -> p s r m", s=8, r=16)[:, 4 * j:4 * j + 4, r, :]
                            dd = G2st[2 + c][rows, :].rearrange("p (s r m) -> p s r m", s=8, r=16)[:, 4 * j:4 * j + 4, r, :]
                            heads.append(dict(q=QT2[c][rows, tq], kc=KT2[c][rows, tq], kp=(KT2[c][rows, tp] if j == 1 else None),
                                              vc=VA[vsl][:, h, 0:64], vp=(VA[(b - 1) % 4][:, h, 0:64] if j == 1 else None),
                                              od=od, dd=dd, ok=("G2", c, b, hh), dk=("G2", 2 + c, b, hh),
                                              od_full=G2st[c][:, :].rearrange("p (s r m) -> p s r m", s=8, r=16)[:, 4 * j:4 * j + 4, r, :],
                                              dd_full=G2st[2 + c][:, :].rearrange("p (s r m) -> p s r m", s=8, r=16)[:, 4 * j:4 * j + 4, r, :]))
                        attn_pair(0, heads, PT, allq[c], allk[c], [("VA", vsl), ("VA", (b - 1) % 4)])
            attn_flush()
            for k in range(4):
                dma("sp", g2s[k], G2st[k][:], [("G2", k, b, hh) for b in range(32) for hh in range(2)], [("g2s", k)])
            P.flush()
        if os.environ.get("KSTOP") == "A":
            return nc

        with ExitStack() as st:
            Wab = sbt(st, "Wab", [128, 8, 2304], BF16)
            Woa = sbt(st, "Woa", [128, 6, 1024], BF16)
            wst = [sbt(st, "wstB%d" % i, [128, 1024], F32) for i in range(2)]
            xin = [sbt(st, "xinB%d" % i, [128, D], F32) for i in range(4)]
            xres = [sbt(st, "xresB%d" % i, [128, D], F32) for i in range(2)]
            xn = [sbt(st, "xnB%d" % i, [128, D], BF16) for i in range(2)]
            hT = [sbt(st, "hTB%d" % i, [128, 8, 512], BF16) for i in range(1)]
            QT = [sbt(st, "QTB%d" % c, [128, 512], BF16) for c in range(4)]
            KT = [sbt(st, "KTB%d" % c, [128, 1024], BF16) for c in range(4)]
            VT = [sbt(st, "VTB%d" % c, [128, 1024], BF16) for c in range(4)]
            gT = [sbt(st, "gTB%d" % c, [128, 512], BF16) for c in range(6)]
            OT = [sbt(st, "OTB%d" % c, [128, 512], BF16) for c in range(4)]
            DT = [sbt(st, "DTB%d" % c, [128, 512], BF16) for c in range(4)]
            oTf = [sbt(st, "oTfB%d" % c, [128, 512], BF16) for c in range(6)]
            G2in = [sbt(st, "G2in%d" % k, [128, 512], BF16) for k in range(4)]
            dsum2 = [sbt(st, "dsumB%d" % i, [128, 512], F32) for i in range(2)]
            otmp2 = [sbt(st, "otmpB%d" % i, [128, 512], BF16) for i in range(2)]
            ctab = [sbt(st, "ctabB%d" % i, [128, 512], F32) for i in range(2)]
            stab = [sbt(st, "stabB%d" % i, [128, 512], F32) for i in range(2)]
            qraw = [sbt(st, "qrawB%d" % i, [128, 512], BF16) for i in range(2)]
            t12 = sbt(st, "t12B", [128, 4, 512], F32)
            kvf = [sbt(st, "kvfB%d" % i, [128, 512], F32) for i in range(4)]
            stage = [sbt(st, "stageB%d" % i, [128, 512], F32) for i in range(2)]
            PT = [sbt(st, "PTB%d" % i, [128, 512], BF16) for i in range(2)]
            VA0 = [sbt(st, "VA0_%d" % i, [128, 4, 128], BF16) for i in range(4)]
            VA1 = [sbt(st, "VA1_%d" % i, [128, 4, 128], BF16) for i in range(8)]
            print("[phaseB] sbuf remaining", nc.sbuf_bytes_remaining)
            load_w(wst, Wab, w_in_a, 1024, 0, 512, 0, "w")
            load_w(wst, Wab, w_in_a, 1024, 768, 512, 512, "w")
            load_w(wst, Wab, w_in_a, 1024, 1536, 512, 1024, "w")
            load_w(wst, Wab, w_in_a, 1024, 2304, 768, 1536, "w")
            load_w(wst, Woa, w_o_a, 768, 0, 1024, 0, "w")
            for v in VA0 + VA1:
                memset("pool", v[:, :, 64:128], 1.0, [("w", "va")])
            P.flush()
            for u in range(4):
                load_x(xp, None, u, xin)
            for s in range(NS):
                hs = s % 2
                ts_ = hs * 512
                tk = slice(512 * s, 512 * (s + 1))
                dma("sp", ctab[hs][:], cosT_d[:, tk], (), [("tab", hs)])
                dma("sp", stab[hs][:], sinT_d[:, tk], (), [("tab", hs)])
                for k in range(4):
                    dma("sp", G2in[k][:], g2s[k][:, tk], [("g2s", k)], [("G2in", k)])
                bg_issue(2)
                norm_stats([4 * s + u for u in range(4)], xin, 32)
                for u in range(4):
                    i = 4 * s + u
                    norm_tile(i, u, xin, xn, 32, [(hT[0], "hT", A0, SH0)], 0, stats=False)
                    if i + 4 < NT:
                        load_x(xp, None, i + 4, xin)
                ring = slice(ts_, ts_ + 512)
                for g in range(2):
                    for (kind, cc, wc) in (("k", 2 * g, 512 + 256 * g), ("k", 2 * g + 1, 640 + 256 * g),
                                           ("v", 2 * g, 1024 + 256 * g), ("v", 2 * g + 1, 1152 + 256 * g)):
                        slot = project(hT[0], "hT", 0, Wab, slice(wc, wc + 128))
                        need_f = (s == 7)
                        if kind == "k":
                            rope(slot, ctab[hs][:], stab[hs][:], ("tab", hs), qraw, t12, KT[cc][:, ring], ("KT", cc, hs),
                                 kvf[cc % 2][:] if need_f else None, ("kvf", cc % 2))
                        else:
                            cp("act", VT[cc][:, ring], psP[:, slot, :], [("psP", slot)], [("VT", cc, hs)])
                            if need_f:
                                cp("dve", kvf[2 + cc % 2][:], psP[:, slot, :], [("psP", slot)], [("kvf", 2 + cc % 2)])
                    if s == 7:
                        if g == 0:
                            cache_out(kvf, [("kvf", k) for k in range(4)], 3, 512, stage, aop[0][:, :], "stage")
                        else:
                            for u in range(4):
                                cache_out(kvf, [("kvf", k) for k in range(4)], u, 512, stage, aop[1][u * 128:(u + 1) * 128, :], "stage")
                for cc in range(4):
                    slot = project(hT[0], "hT", 0, Wab, slice(cc * 128, (cc + 1) * 128))
                    rope(slot, ctab[hs][:], stab[hs][:], ("tab", hs), qraw, t12, QT[cc][:], ("QT", cc))
                for cc in range(6):
                    slot = project(hT[0], "hT", 0, Wab, slice(1536 + cc * 128, 1536 + (cc + 1) * 128))
                    act(gT[cc][:], psP[:, slot, :], AF.Silu, [("psP", slot)], [("gT", cc)])
                bg_issue(2)
                for u in range(4):
                    i = 4 * s + u
                    vsl = i % 4
                    ptv = psT[0]
                    cols = slice(ts_ + u * 128, ts_ + (u + 1) * 128)
                    if u > 0:
                        pcols = slice(ts_ + (u - 1) * 128, ts_ + u * 128)
                    else:
                        pcols = slice((1 - hs) * 512 + 384, (1 - hs) * 512 + 512)
                    for c in range(2):
                        tr(ptv[:, c, :], VT[c][:, cols], identb[:], [("VT", c, hs)], [("psT", 0)])
                    cp(evac_eng(), VA0[vsl][:, :, 0:64], ptv[:, 0:2, :].rearrange("p c (h d) -> p (c h) d", h=2),
                       [("psT", 0)], [("VA0", vsl)])
                    for c in range(2):
                        heads = []
                        for hh in range(2):
                            h = 2 * c + hh
                            rows = slice(hh * 64, (hh + 1) * 64)
                            heads.append(dict(q=QT[c][rows, u * 128:(u + 1) * 128], kc=KT[c][rows, cols],
                                              kp=(KT[c][rows, pcols] if i > 0 else None),
                                              vc=VA0[vsl][:, h, 0:64], vp=(VA0[(i - 1) % 4][:, h, 0:64] if i > 0 else None),
                                              od=OT[c][rows, u * 128:(u + 1) * 128], dd=DT[c][rows, u * 128:(u + 1) * 128],
                                              od_full=OT[c][:, u * 128:(u + 1) * 128], dd_full=DT[c][:, u * 128:(u + 1) * 128],
                                              ok=("OT", c, u, hh), dk=("DT", c, u, hh)))
                        attn_pair(0, heads, PT, [("QT", c)], [("KT", c, 0), ("KT", c, 1)], [("VA0", vsl), ("VA0", (i - 1) % 4)])
                bg_issue(2)
                for r4 in range(4):
                    vsl = hs * 4 + r4
                    psl = (1 - hs) * 4 + r4
                    b = 4 * s + r4
                    ptv = psT[0]
                    cols = slice(ts_ + r4, ts_ + 512, 4)
                    pcols = slice((1 - hs) * 512 + r4, (1 - hs) * 512 + 512, 4)
                    for c in range(2):
                        tr(ptv[:, c, :], VT[2 + c][:, cols], identb[:], [("VT", 2 + c, hs)], [("psT", 0)])
                    cp(evac_eng(), VA1[vsl][:, :, 0:64], ptv[:, 0:2, :].rearrange("p c (h d) -> p (c h) d", h=2),
                       [("psT", 0)], [("VA1", vsl)])
                    for c in range(2):
                        heads = []
                        for hh in range(2):
                            h = 2 * c + hh
                            rows = slice(hh * 64, (hh + 1) * 64)
                            heads.append(dict(q=QT[2 + c][rows, r4:512:4], kc=KT[2 + c][rows, cols],
                                              kp=(KT[2 + c][rows, pcols] if s > 0 else None),
                                              vc=VA1[vsl][:, h, 0:64], vp=(VA1[psl][:, h, 0:64] if s > 0 else None),
                                              od=OT[2 + c][rows, r4:512:4], dd=DT[2 + c][rows, r4:512:4],
                                              od_full=OT[2 + c][:, r4:512:4], dd_full=DT[2 + c][:, r4:512:4],
                                              ok=("OT", 2 + c, r4, hh), dk=("DT", 2 + c, r4, hh)))
                        attn_pair(0, heads, PT, [("QT", 2 + c)], [("KT", 2 + c, 0), ("KT", 2 + c, 1)], [("VA1", vsl), ("VA1", psl)])
                attn_flush()
                bg_issue(2)
                for u in range(2):
                    dma("sp", xres[u][:], xp[(4 * s + u) * 128:(4 * s + u + 1) * 128, :], (), [("xres", u)])
                nat = lambda a: a.rearrange("p (r m) -> p m r", r=16)
                v3 = lambda a: a.rearrange("p (m r) -> p m r", r=16)
                for cc in range(2):
                    k01 = [("DT", c, x, hh) for c in (cc, 2 + cc) for x in range(4) for hh in range(2)]
                    ds = dsum2[cc]
                    tt("dve", ds[:], DT[cc][:], DT[2 + cc][:], ALU.add, k01, [("dsum", cc)])
                    tt("pool", v3(ds[:]), v3(ds[:]), nat(G2in[2 + cc][:]), ALU.add, [("dsum", cc), ("G2in", 2 + cc)], [("dsum", cc)])
                    act(ds[:], ds[:], AF.Ln, [("dsum", cc)], [("dsum", cc)])
                    act(ds[:], ds[:], AF.Exp, [("dsum", cc)], [("dsum", cc)], scale=-1.0)
                    for g in range(3):
                        c6 = 2 * g + cc
                        ot = otmp2[g % 2]
                        ok_ = ("otmp", g % 2)
                        if g < 2:
                            sk = [("OT", c6, x, hh) for x in range(4) for hh in range(2)]
                            tt("dve", ot[:], OT[c6][:], ds[:], ALU.mult, sk + [("dsum", cc)], [ok_])
                        else:
                            tt("pool", v3(ot[:]), nat(G2in[cc][:]), v3(ds[:]), ALU.mult, [("G2in", cc), ("dsum", cc)], [ok_])
                        tt("dve" if g == 2 else "pool", oTf[c6][:], ot[:], gT[c6][:], ALU.mult, [ok_, ("gT", c6)], [("oTf", c6)])
                for u in range(4):
                    i = 4 * s + u
                    out_proj_tile(i, u, oTf, [("oTf", c) for c in range(6)], Woa, 6, ggbc[0], xres, 64, x1s, "x1s", t12)
                    if u + 2 < 4:
                        dma("sp", xres[i % 2][:], xp[(i + 2) * 128:(i + 3) * 128, :], (), [("xres", i % 2)])
            P.flush()
        if os.environ.get("KSTOP") == "B":
            return nc

        with ExitStack() as st:
            Wq = sbt(st, "Wq", [128, 8, 2048], BF16)
            Wob = sbt(st, "Wob", [128, 8, 1024], BF16)
            Wkv = sbt(st, "Wkv", [128, 8, 384], BF16)
            wst = [sbt(st, "wstC%d" % i, [128, 1024], F32) for i in range(2)]
            xin = [sbt(st, "xinC%d" % i, [128, D], F32) for i in range(4)]
            xres = [sbt(st, "xresC%d" % i, [128, D], F32) for i in range(2)]
            xn = [sbt(st, "xnC%d" % i, [128, D], BF16) for i in range(2)]
            hT = sbt(st, "hTC", [128, 8, 512], BF16)
            hkT = sbt(st, "hkTC", [128, 8, 512], BF16)
            QT = [sbt(st, "QTC%d" % c, [128, 512], BF16) for c in range(8)]
            KT = [sbt(st, "KTC%d" % c, [128, 1024], BF16) for c in range(2)]
            VT = sbt(st, "VTC", [128, 1024], BF16)
            gT = [sbt(st, "gTC%d" % c, [128, 512], BF16) for c in range(8)]
            OT = [sbt(st, "OTC%d" % c, [128, 512], BF16) for c in range(8)]
            DT = [sbt(st, "DTC%d" % c, [128, 512], BF16) for c in range(8)]
            dsum = [sbt(st, "dsumC%d" % i, [128, 512], F32) for i in range(2)]
            otmp = [sbt(st, "otmpC%d" % i, [128, 512], BF16) for i in range(2)]
            ctab = [sbt(st, "ctabC%d" % i, [128, 512], F32) for i in range(2)]
            stab = [sbt(st, "stabC%d" % i, [128, 512], F32) for i in range(2)]
            qraw = [sbt(st, "qrawC%d" % i, [128, 512], BF16) for i in range(2)]
            t12 = sbt(st, "t12C", [128, 4, 512], F32)
            kvf = [sbt(st, "kvfC%d" % i, [128, 512], F32) for i in range(2)]
            stage = [sbt(st, "stageC%d" % i, [128, 256], F32) for i in range(2)]
            PT = [sbt(st, "PTC%d" % i, [128, 512], BF16) for i in range(2)]
            VAc = [sbt(st, "VAc%d" % i, [128, 2, 128], BF16) for i in range(4)]
            print("[phaseC] sbuf remaining", nc.sbuf_bytes_remaining)
            load_w(wst, Wq, w_in_b, 1024, 0, 2048, 0, "w")
            load_w(wst, Wob, w_o_b, 1024, 0, 1024, 0, "w")
            load_w(wst, Wkv, w_kvx, 1024, 0, 384, 0, "w")
            for v in VAc:
                memset("pool", v[:, :, 64:128], 1.0, [("w", "va")])
            P.flush()
            for u in range(4):
                dma("sp", xin[u][:], x1s[u * 128:(u + 1) * 128, :], (), [("xin", u)])
            for s in range(NS):
                hs = s % 2
                ts_ = hs * 512
                tk = slice(512 * s, 512 * (s + 1))
                ring = slice(ts_, ts_ + 512)
                dma("sp", ctab[hs][:], cosT_d[:, tk], (), [("tab", hs)])
                dma("sp", stab[hs][:], sinT_d[:, tk], (), [("tab", hs)])
                bg_issue(2)
                norm_stats([4 * s + u for u in range(4)], xin, 0)
                for u in range(4):
                    i = 4 * s + u
                    norm_tile(i, u, xin, xn, 0, [(hT, "hT", A1, SH1), (hkT, "hkT", AKV, SHKV)], 0, stats=False)
                    if i + 4 < NT:
                        dma("sp", xin[i % 4][:], x1s[(i + 4) * 128:(i + 5) * 128, :], (), [("xin", i % 4)])
                need_f = (s == 7)
                slot = project(hkT, "hkT", 0, Wkv, slice(0, 128))
                rope(slot, ctab[hs][:], stab[hs][:], ("tab", hs), qraw, t12, KT[0][:, ring], ("KT", 0, hs),
                     kvf[0][:] if need_f else None, ("kvf", 0))
                slot = project(hkT, "hkT", 0, Wkv, slice(256, 384))
                rope(slot, ctab[hs][:], stab[hs][:], ("tab", hs), qraw, t12, KT[1][:, ring], ("KT", 1, hs))
                slot = project(hkT, "hkT", 0, Wkv, slice(128, 256))
                cp("act", VT[:, ring], psP[:, slot, :], [("psP", slot)], [("VT", hs)])
                if need_f:
                    cp("dve", kvf[1][:], psP[:, slot, :], [("psP", slot)], [("kvf", 1)])
                    cache_out(kvf, [("kvf", 0), ("kvf", 1)], 3, 256, stage, bp[:, :], "stage")
                for cc in range(8):
                    slot = project(hT, "hT", 0, Wq, slice(cc * 128, (cc + 1) * 128))
                    rope(slot, ctab[hs][:], stab[hs][:], ("tab", hs), qraw, t12, QT[cc][:], ("QT", cc))
                bg_issue(2)
                for cc in range(8):
                    slot = project(hT, "hT", 0, Wq, slice(1024 + cc * 128, 1024 + (cc + 1) * 128))
                    act(gT[cc][:], psP[:, slot, :], AF.Silu, [("psP", slot)], [("gT", cc)])
                for u in range(4):
                    i = 4 * s + u
                    vsl = i % 4
                    ptv = psT[0]
                    cols = slice(ts_ + u * 128, ts_ + (u + 1) * 128)
                    if u > 0:
                        pcols = slice(ts_ + (u - 1) * 128, ts_ + u * 128)
                    else:
                        pcols = slice((1 - hs) * 512 + 384, (1 - hs) * 512 + 512)
                    tr(ptv[:, 0, :], VT[:, cols], identb[:], [("VT", hs)], [("psT", 0)])
                    cp(evac_eng(), VAc[vsl][:, :, 0:64], ptv[:, 0, :].rearrange("p (h d) -> p h d", h=2),
                       [("psT", 0)], [("VAc", vsl)])
                    for c in range(8):
                        kvh = c // 4
                        heads = []
                        for hh in range(2):
                            rows = slice(hh * 64, (hh + 1) * 64)
                            ksrc = KT[0] if (kvh == hh) else KT[1]
                            heads.append(dict(q=QT[c][rows, u * 128:(u + 1) * 128], kc=ksrc[rows, cols],
                                              kp=(ksrc[rows, pcols] if i > 0 else None),
                                              vc=VAc[vsl][:, kvh, 0:64], vp=(VAc[(i - 1) % 4][:, kvh, 0:64] if i > 0 else None),
                                              od=OT[c][rows, u * 128:(u + 1) * 128], dd=DT[c][rows, u * 128:(u + 1) * 128],
                                              od_full=OT[c][:, u * 128:(u + 1) * 128], dd_full=DT[c][:, u * 128:(u + 1) * 128],
                                              ok=("OT", c, u, hh), dk=("DT", c, u, hh)))
                        attn_pair(1, heads, PT, [("QT", c)], [("KT", 0, 0), ("KT", 0, 1), ("KT", 1, 0), ("KT", 1, 1)],
                                  [("VAc", vsl), ("VAc", (i - 1) % 4)])
                attn_flush()
                bg_issue(2)
                for u in range(2):
                    dma("sp", xres[u][:], x1s[(4 * s + u) * 128:(4 * s + u + 1) * 128, :], (), [("xres", u)])
                for c in range(8):
                    dk = [("DT", c, x, hh) for x in range(4) for hh in range(2)]
                    okk = [("OT", c, x, hh) for x in range(4) for hh in range(2)]
                    b2 = c % 2
                    act(dsum[b2][:], DT[c][:], AF.Ln, dk, [("dsum", b2)], bias=esink[:, c:c + 1])
                    act(dsum[b2][:], dsum[b2][:], AF.Exp, [("dsum", b2)], [("dsum", b2)], scale=-1.0)
                    tt("dve", otmp[b2][:], OT[c][:], dsum[b2][:], ALU.mult, okk + [("dsum", b2)], [("otmp", b2)])
                    tt("pool" if c % 2 else "dve", OT[c][:], otmp[b2][:], gT[c][:], ALU.mult, [("otmp", b2), ("gT", c)] + okk, [("oTf", c)] + okk)
                bg_issue(2 if s < NS - 1 else 200)
                for u in range(4):
                    i = 4 * s + u
                    out_proj_tile(i, u, OT, [("oTf", c) for c in range(8)], Wob, 8, ggbc[1], xres, 32, yp, None, t12)
                    if u + 2 < 4:
                        dma("sp", xres[i % 2][:], x1s[(i + 2) * 128:(i + 3) * 128, :], (), [("xres", i % 2)])
            P.flush(final=True)
        print("[kernel] ops=%d waits=%d sig=%s" % (P.nops, P.nwait, P.cnt))
    return nc


_NC_CACHE = {}


def kernel(x_prompt, x_sample, c_prompt, c_sample, cache_a_kv_g0, cache_a_kv_g1, cache_a_kv_g2, cache_b_kv,
           ada_w, ada_b, g_pre, g_post, w_in_a, w_o_a, w_in_b, w_o_b, sinks_b, ada_kv_w, ada_kv_b, g_kv, w_kv,
           _cores=None):
    f = lambda a: np.ascontiguousarray(np.asarray(a, dtype=np.float32))
    x_prompt, x_sample, c_prompt, c_sample = f(x_prompt), f(x_sample), f(c_prompt), f(c_sample)
    ca = [f(cache_a_kv_g0), f(cache_a_kv_g1), f(cache_a_kv_g2)]
    cbk = f(cache_b_kv)
    ada_w, ada_b, g_pre, g_post = f(ada_w), f(ada_b), f(g_pre), f(g_post)
    w_in_a, w_o_a, w_in_b, w_o_b = f(w_in_a), f(w_o_a), f(w_in_b), f(w_o_b)
    sinks_b, ada_kv_w, ada_kv_b, g_kv, w_kv = f(sinks_b), f(ada_kv_w), f(ada_kv_b), f(g_kv), f(w_kv)
    cst = _consts()
    cores = list(range(NCORES)) if _cores is None else list(_cores)
    shared = dict(
        ada_w0=f(ada_w[0]), ada_w1=f(ada_w[1]), ada_kv_w=ada_kv_w,
        brow=f(np.concatenate([ada_b[0], ada_b[1], ada_kv_b])[None, :]),
        vecs=f(np.concatenate([g_pre[0].reshape(8, 128), g_pre[1].reshape(8, 128), g_post[0].reshape(8, 128),
                               g_post[1].reshape(8, 128), g_kv.reshape(8, 128)], axis=0)),
        gpost_rows=g_post, w_in_a=f(w_in_a[0]), w_o_a=f(w_o_a[0]), w_in_b=f(w_in_b[0]), w_o_b=f(w_o_b[0]),
        w_kvx=f(np.concatenate([w_kv[:, 0:256], w_kv[:, 64:128], w_kv[:, 0:64]], axis=1)),
        sink_fm=f(sinks_b[0].reshape(8, 2).T.repeat(64, axis=0)),
        sink_tm=f(np.tile(sinks_b[0][None, :], (DB, 1))),
        ident=cst["ident"], rmat=cst["rmat"], cosT=cst["cosT"], sinT=cst["sinT"], masks=cst["masks"], mbias=cst["mbias"],
        cs_tok=cst["cs_tok"], sn_tok=cst["sn_tok"], rowb_in=cst["rowb"], w_kv=w_kv,
        gvec_tm=f(np.stack([np.tile(v[None, :], (DB, 1)) for v in (g_pre[0], g_pre[1], g_post[0], g_post[1], g_kv)])),
    )
    in_maps = []
    for b in cores:
        crows = np.zeros((33, D), np.float32)
        crows[0:DB] = c_sample[DB * b:DB * (b + 1)]
        crows[32] = c_prompt[b]
        m = dict(shared)
        m.update(xp=x_prompt[b], crows=crows, xs=f(x_sample[DB * b:DB * (b + 1), 0, :]),
                 ca0=ca[0][0, DB * b:DB * (b + 1)].reshape(DB, 128, 512),
                 ca1=ca[1][0, DB * b:DB * (b + 1)].reshape(DB, 512, 512),
                 ca2=ca[2][0, DB * b:DB * (b + 1)].reshape(DB, 2048, 512),
                 cb=cbk[DB * b:DB * (b + 1)].reshape(DB, 128, 256))
        in_maps.append(m)
    if "nc" not in _NC_CACHE:
        _NC_CACHE["nc"] = build_nc()
    nc = _NC_CACHE["nc"]
    res = run_bass_kernel_spmd(nc, in_maps, core_ids=list(range(len(cores))))
    R = res.results
    n = len(cores)
    y_prompt = np.stack([R[k]["yp"] for k in range(n)])
    y_sample = np.concatenate([R[k]["ys"] for k in range(n)])[:, None, :]
    a_p = [np.stack([R[k]["a%dp" % g] for k in range(n)]).reshape(1, n, -1, 2, 4, 64) for g in range(3)]
    b_p = np.stack([R[k]["bp"] for k in range(n)]).reshape(n, 128, 2, 2, 64)
    a_s = [np.concatenate([R[k]["a%ds" % g] for k in range(n)]).reshape(1, n * DB, -1, 2, 4, 64) for g in range(3)]
    b_s = np.concatenate([R[k]["bs"] for k in range(n)]).reshape(n * DB, 128, 2, 2, 64)
    return (y_prompt, y_sample, a_p[0], a_p[1], a_p[2], b_p, a_s[0], a_s[1], a_s[2], b_s)
```
